# Optimizing a Trainium2 kernel written in Bass

```python
import jax
import jax.numpy as jnp
from jax import lax
import numpy as np

D_MODEL = 1024
BATCH = 2
SEQ = 8192
DEPTH = 2

HEAD_DIM = 64
D_MIX = D_MODEL
FOX_HEADS = D_MIX // (4 * HEAD_DIM)
FOX_WIDTH = FOX_HEADS * HEAD_DIM
CONV_CH = D_MIX // 4
CONV_TAPS = 31
NSA_HEADS = D_MIX // (2 * HEAD_DIM)
NSA_WIDTH = NSA_HEADS * HEAD_DIM
NSA_KV_GROUPS = 2
NSA_HPG = NSA_HEADS // NSA_KV_GROUPS
NSA_KV_WIDTH = NSA_KV_GROUPS * HEAD_DIM
N_BRANCH = 3
CMP_LEN = 32
CMP_STRIDE = 16
CMP_HIDDEN = 256
SLC_LEN = 64
SLC_TOPK = 16
WINDOW = 512
Q_BLOCK = 128
ROPE_THETA = 500000.0
ROPE_DIM = HEAD_DIM // 4
EPS = 1e-6
NEG_INF = -1e30
FORCED_SCORE = 1e4
IN_SIZES = (FOX_WIDTH, FOX_WIDTH, FOX_WIDTH, FOX_HEADS, FOX_WIDTH,
            2 * CONV_CH, CONV_CH,
            NSA_WIDTH, NSA_KV_WIDTH, NSA_KV_WIDTH, NSA_KV_WIDTH, NSA_KV_WIDTH,
            NSA_KV_WIDTH, NSA_KV_WIDTH, NSA_HEADS * N_BRANCH, NSA_WIDTH)
D_IN = sum(IN_SIZES)

kernel_name = 'hybrid_fox_conformer_nsa'


def rms_norm(x, g):
    xf = x.astype(jnp.float32)
    y = xf * lax.rsqrt(jnp.mean(xf * xf, axis=-1, keepdims=True) + EPS)
    return (y * g.astype(jnp.float32)).astype(x.dtype)


def masked_softmax(s, mask):
    s = jnp.where(mask, s.astype(jnp.float32), NEG_INF)
    return jnp.where(mask, jax.nn.softmax(s, axis=-1), 0.0)


def rope(x, pos):
    half = ROPE_DIM // 2
    inv = ROPE_THETA ** (-jnp.arange(0, ROPE_DIM, 2, dtype=jnp.float32) / ROPE_DIM)
    ang = pos.astype(jnp.float32)[:, None] * inv[None, :]
    cos = jnp.cos(ang)[None, :, None, :].astype(x.dtype)
    sin = jnp.sin(ang)[None, :, None, :].astype(x.dtype)
    x1, x2, rest = x[..., :half], x[..., half:ROPE_DIM], x[..., ROPE_DIM:]
    return jnp.concatenate([x1 * cos - x2 * sin, x1 * sin + x2 * cos, rest], axis=-1)


def fox_attention(q, k, v, f_logit):
    B, T, H, Dh = q.shape
    c = jnp.cumsum(jax.nn.log_sigmoid(f_logit.astype(jnp.float32)), axis=1)
    c = jnp.transpose(c, (0, 2, 1))
    scale = Dh ** -0.5
    kpos = jnp.arange(T)

    def block(i):
        qs = i * Q_BLOCK
        tq = qs + jnp.arange(Q_BLOCK)
        qb = lax.dynamic_slice_in_dim(q, qs, Q_BLOCK, axis=1)
        cb = lax.dynamic_slice_in_dim(c, qs, Q_BLOCK, axis=2)
        s = jnp.einsum('bqhd,bkhd->bhqk', qb, k).astype(jnp.float32) * scale
        s = s + cb[..., :, None] - c[:, :, None, :]
        p = masked_softmax(s, tq[:, None] >= kpos[None, :])
        return jnp.einsum('bhqk,bkhd->bqhd', p.astype(v.dtype), v)

    o = lax.map(block, jnp.arange(T // Q_BLOCK))
    return jnp.moveaxis(o, 0, 1).reshape(B, T, H * Dh)


def conformer_conv(u, conv_w, conv_b, ln_g, ln_b, w_pw):
    a, b = jnp.split(u, 2, axis=-1)
    y = a * jax.nn.sigmoid(b)
    C = y.shape[-1]
    y = lax.conv_general_dilated(
        y, conv_w[:, None, :].astype(y.dtype), window_strides=(1,),
        padding=[(CONV_TAPS - 1, 0)], dimension_numbers=('NWC', 'WIO', 'NWC'),
        feature_group_count=C) + conv_b
    yf = y.astype(jnp.float32)
    mu = jnp.mean(yf, axis=-1, keepdims=True)
    var = jnp.mean(jnp.square(yf - mu), axis=-1, keepdims=True)
    yn = ((yf - mu) * lax.rsqrt(var + EPS) * ln_g + ln_b).astype(y.dtype)
    return jax.nn.silu(yn) @ w_pw


def nsa_attention(q, kc, vc, ks, vs, kw, vw, gate_logit,
                  pe_k, pe_v, k_w1, k_w2, v_w1, v_w2):
    B, T, H, Dh = q.shape
    G = NSA_KV_GROUPS
    dt = q.dtype
    scale = Dh ** -0.5
    pos = jnp.arange(T)
    q = rope(q, pos)
    ks = rope(ks, pos)
    kw = rope(kw, pos)

    n_cmp = (T - CMP_LEN) // CMP_STRIDE + 1
    cmp_start = jnp.arange(n_cmp) * CMP_STRIDE
    cmp_end = cmp_start + CMP_LEN - 1
    cmp_idx = cmp_start[:, None] + jnp.arange(CMP_LEN)[None, :]

    def compress(x, pe, w1, w2):
        blk = x[:, cmp_idx] + pe[None, None, :, None, :]
        blk = jnp.transpose(blk, (0, 1, 3, 2, 4)).reshape(B, n_cmp, G, CMP_LEN * Dh)
        hid = jax.nn.gelu(jnp.einsum('bngi,ih->bngh', blk, w1))
        return jnp.einsum('bngh,ho->bngo', hid, w2)

    k_cmp = rope(compress(kc, pe_k, k_w1, k_w2), cmp_end)
    v_cmp = compress(vc, pe_v, v_w1, v_w2)

    n_slc = T // SLC_LEN
    k_sel = jnp.transpose(ks.reshape(B, n_slc, SLC_LEN, G, Dh), (0, 3, 1, 2, 4))
    v_sel = jnp.transpose(vs.reshape(B, n_slc, SLC_LEN, G, Dh), (0, 3, 1, 2, 4))
    slc_start = jnp.arange(n_slc) * SLC_LEN
    overlap = ((cmp_start[:, None] < slc_start[None, :] + SLC_LEN) &
               (cmp_end[:, None] >= slc_start[None, :])).astype(jnp.float32)
    topk = min(SLC_TOPK, n_slc)
    bi = jnp.arange(B)[:, None, None, None]
    gi = jnp.arange(G)[None, :, None, None]

    kw_p = jnp.pad(kw, ((0, 0), (WINDOW, 0), (0, 0), (0, 0)))
    vw_p = jnp.pad(vw, ((0, 0), (WINDOW, 0), (0, 0), (0, 0)))

    gates = jax.nn.sigmoid(gate_logit.astype(jnp.float32)).reshape(B, T, G, NSA_HPG, N_BRANCH)
    qg = q.reshape(B, T, G, NSA_HPG, Dh)

    def block(i):
        qs = i * Q_BLOCK
        tq = qs + jnp.arange(Q_BLOCK)
        qb = lax.dynamic_slice_in_dim(qg, qs, Q_BLOCK, axis=1)

        s = jnp.einsum('bqghd,bngd->bghqn', qb, k_cmp) * scale
        p_cmp = masked_softmax(s, cmp_end[None, :] <= tq[:, None])
        o_cmp = jnp.einsum('bghqn,bngd->bqghd', p_cmp.astype(dt), v_cmp)

        imp = jnp.einsum('bghqn,nj->bgqj', p_cmp, overlap)
        cur = tq // SLC_LEN
        j = jnp.arange(n_slc)[None, :]
        forced = (j == 0) | (j == cur[:, None]) | (j == cur[:, None] - 1)
        imp = jnp.where(forced, FORCED_SCORE,
                        jnp.where(slc_start[None, :] <= tq[:, None], imp, NEG_INF))
        _, idx = lax.top_k(imp, topk)
        k_g = k_sel[bi, gi, idx]
        v_g = v_sel[bi, gi, idx].reshape(B, G, Q_BLOCK, topk * SLC_LEN, Dh)
        s = jnp.einsum('bqghd,bgqksd->bghqks', qb, k_g) * scale
        s = s.reshape(B, G, NSA_HPG, Q_BLOCK, topk * SLC_LEN)
        key_pos = (idx[..., None] * SLC_LEN + jnp.arange(SLC_LEN)).reshape(
            B, G, 1, Q_BLOCK, topk * SLC_LEN)
        p_slc = masked_softmax(s, key_pos <= tq[:, None])
        o_slc = jnp.einsum('bghqm,bgqmd->bqghd', p_slc.astype(dt), v_g)

        kwb = lax.dynamic_slice_in_dim(kw_p, qs, WINDOW + Q_BLOCK, axis=1)
        vwb = lax.dynamic_slice_in_dim(vw_p, qs, WINDOW + Q_BLOCK, axis=1)
        sp = qs - WINDOW + jnp.arange(WINDOW + Q_BLOCK)
        wmask = ((sp[None, :] >= 0) & (sp[None, :] <= tq[:, None]) &
                 (tq[:, None] - sp[None, :] < WINDOW))
        s = jnp.einsum('bqghd,bkgd->bghqk', qb, kwb) * scale
        o_win = jnp.einsum('bghqk,bkgd->bqghd', masked_softmax(s, wmask).astype(dt), vwb)

        gb = lax.dynamic_slice_in_dim(gates, qs, Q_BLOCK, axis=1)
        o = gb[..., 0:1] * o_cmp + gb[..., 1:2] * o_slc + gb[..., 2:3] * o_win
        return o.astype(dt)

    o = lax.map(block, jnp.arange(T // Q_BLOCK))
    return jnp.moveaxis(o, 0, 1).reshape(B, T, H * Dh)


def hybrid_layer(x, norm_g, w_in, fox_b, conv_w, conv_b, conv_ln_g, conv_ln_b,
                 conv_pw, cmp_pe_k, cmp_pe_v, cmp_k_w1, cmp_k_w2, cmp_v_w1,
                 cmp_v_w2, w_out):
    B, T, _ = x.shape
    h = rms_norm(x, norm_g)
    z = h @ w_in
    split_points = np.cumsum(IN_SIZES)[:-1].tolist()
    (fq, fk, fv, ff, f_gate, glu_in, c_gate,
     nq, nkc, nvc, nks, nvs, nkw, nvw, n_gl, n_gate) = jnp.split(z, split_points, axis=-1)

    def heads(t, n):
        return t.reshape(B, T, n, HEAD_DIM)

    o_a = fox_attention(heads(fq, FOX_HEADS), heads(fk, FOX_HEADS),
                        heads(fv, FOX_HEADS), ff + fox_b) * jax.nn.silu(f_gate)
    o_b = conformer_conv(glu_in, conv_w, conv_b, conv_ln_g, conv_ln_b,
                         conv_pw) * jax.nn.silu(c_gate)
    o_c = nsa_attention(heads(nq, NSA_HEADS), heads(nkc, NSA_KV_GROUPS),
                        heads(nvc, NSA_KV_GROUPS), heads(nks, NSA_KV_GROUPS),
                        heads(nvs, NSA_KV_GROUPS), heads(nkw, NSA_KV_GROUPS),
                        heads(nvw, NSA_KV_GROUPS), n_gl, cmp_pe_k, cmp_pe_v,
                        cmp_k_w1, cmp_k_w2, cmp_v_w1, cmp_v_w2) * jax.nn.silu(n_gate)
    mixed = jnp.concatenate([o_a, o_b, o_c], axis=-1)
    return x + mixed @ w_out


def setup_inputs(seed: int = 0) -> dict:
    key = jax.random.key(seed)
    k = jax.random.split(key, 18)
    L = DEPTH

    def nrm(kk, shape, scale):
        return scale * jax.random.normal(kk, shape, jnp.float32)

    return {
        'x': nrm(k[0], (BATCH, SEQ, D_MODEL), 1.0),
        'norm_g': 1.0 + nrm(k[1], (L, D_MODEL), 0.02),
        'w_in': nrm(k[2], (L, D_MODEL, D_IN), D_MODEL ** -0.5),
        'fox_b': 3.0 + nrm(k[3], (L, FOX_HEADS), 0.5),
        'conv_w': nrm(k[4], (L, CONV_TAPS, CONV_CH), CONV_TAPS ** -0.5),
        'conv_b': nrm(k[5], (L, CONV_CH), 0.02),
        'conv_ln_g': 1.0 + nrm(k[6], (L, CONV_CH), 0.02),
        'conv_ln_b': nrm(k[7], (L, CONV_CH), 0.02),
        'conv_pw': nrm(k[8], (L, CONV_CH, CONV_CH), CONV_CH ** -0.5),
        'cmp_pe_k': nrm(k[9], (L, CMP_LEN, HEAD_DIM), 0.1),
        'cmp_pe_v': nrm(k[10], (L, CMP_LEN, HEAD_DIM), 0.1),
        'cmp_k_w1': nrm(k[11], (L, CMP_LEN * HEAD_DIM, CMP_HIDDEN), (CMP_LEN * HEAD_DIM) ** -0.5),
        'cmp_k_w2': nrm(k[12], (L, CMP_HIDDEN, HEAD_DIM), CMP_HIDDEN ** -0.5),
        'cmp_v_w1': nrm(k[13], (L, CMP_LEN * HEAD_DIM, CMP_HIDDEN), (CMP_LEN * HEAD_DIM) ** -0.5),
        'cmp_v_w2': nrm(k[14], (L, CMP_HIDDEN, HEAD_DIM), CMP_HIDDEN ** -0.5),
        'w_out': nrm(k[15], (L, D_MIX, D_MODEL), D_MIX ** -0.5),
        'final_g': 1.0 + nrm(k[16], (D_MODEL,), 0.02),
    }


def reference(x, norm_g, w_in, fox_b, conv_w, conv_b, conv_ln_g, conv_ln_b,
              conv_pw, cmp_pe_k, cmp_pe_v, cmp_k_w1, cmp_k_w2, cmp_v_w1,
              cmp_v_w2, w_out, final_g):
    for l in range(DEPTH):
        x = hybrid_layer(x, norm_g[l], w_in[l], fox_b[l], conv_w[l], conv_b[l],
                         conv_ln_g[l], conv_ln_b[l], conv_pw[l], cmp_pe_k[l],
                         cmp_pe_v[l], cmp_k_w1[l], cmp_k_w2[l], cmp_v_w1[l],
                         cmp_v_w2[l], w_out[l])
    return rms_norm(x, final_g)
```

```python
import numpy as np
import ml_dtypes
from contextlib import ExitStack
import concourse.bass as bass
import concourse.mybir as mybir
from concourse.bass_utils import run_bass_kernel_spmd

F32 = mybir.dt.float32
BF16 = mybir.dt.bfloat16
AF = mybir.ActivationFunctionType
ALU = mybir.AluOpType
AX = mybir.AxisListType
NPBF = ml_dtypes.bfloat16

ENGINES = ("pe", "act", "dve", "pool", "sp")


class Buf:
    __slots__ = ("name", "last_w", "readers")

    def __init__(self, name):
        self.name = name
        self.last_w = None
        self.readers = []


class Op:
    __slots__ = ("eng", "fn", "reads", "writes", "dma_key", "deps", "need_inc", "tok", "idx")

    def __init__(self, eng, fn, reads, writes, dma_key):
        self.eng = eng
        self.fn = fn
        self.reads = reads
        self.writes = writes
        self.dma_key = dma_key
        self.deps = []
        self.need_inc = False
        self.tok = None


class Sched:
    def __init__(self, nc, es):
        self.nc = nc
        self.es = es
        self.ops = []
        self.bufs = {}
        self.batch = set()
        self.cnt = {}
        self.sems = {}
        self.waited = {e: {} for e in ENGINES}

    def buf(self, name):
        b = self.bufs.get(name)
        if b is None:
            b = Buf(name)
            self.bufs[name] = b
        return b

    def _norm(self, lst):
        return [self.buf(b) if isinstance(b, str) else b for b in (lst or ())]

    def add(self, eng, meth, kw, reads=(), writes=(), dma_key=None):
        op = Op(eng, (meth, kw), self._norm(reads), self._norm(writes), dma_key)
        op.idx = len(self.ops)
        self.ops.append(op)
        return op

    def pe(self, meth, r=(), w=(), **kw):
        return self.add("pe", meth, kw, r, w)

    def act(self, meth, r=(), w=(), **kw):
        return self.add("act", meth, kw, r, w)

    def dve(self, meth, r=(), w=(), **kw):
        return self.add("dve", meth, kw, r, w)

    def pool(self, meth, r=(), w=(), **kw):
        return self.add("pool", meth, kw, r, w)

    def dma(self, key, r=(), w=(), eng="sp", batch=False, **kw):
        if batch:
            self.batch.add(key)
        return self.add(eng, "dma_start", kw, r, w, dma_key=key)

    def analyze(self):
        for op in self.ops:
            deps = set()
            for b in op.reads:
                if b.last_w is not None:
                    deps.add(b.last_w)
            for b in op.writes:
                if b.last_w is not None:
                    deps.add(b.last_w)
                for r in b.readers:
                    deps.add(r)
            deps.discard(op.idx)
            keep = []
            for d in deps:
                dop = self.ops[d]
                if dop.dma_key is not None and dop.dma_key == op.dma_key and dop.dma_key in self.batch:
                    continue
                if dop.dma_key is None and dop.eng == op.eng:
                    if op.eng == "pe" or op.dma_key is not None:
                        continue
                keep.append(d)
            op.deps = sorted(keep)
            for d in op.deps:
                self.ops[d].need_inc = True
            for b in op.reads:
                b.readers.append(op.idx)
            for b in op.writes:
                b.last_w = op.idx
                b.readers = []

    def flush(self):
        nc = self.nc
        self.analyze()
        per = {e: [op for op in self.ops if op.eng == e] for e in ENGINES}
        for e in ENGINES:
            comp = [op for op in per[e] if op.dma_key is None]
            if comp:
                comp[-1].need_inc = True
        cnt = self.cnt
        for op in self.ops:
            if op.dma_key is not None:
                k = ("dma", op.dma_key)
                cnt[k] = cnt.get(k, 0) + 16
                op.tok = (k, cnt[k])
            elif op.need_inc:
                k = ("eng", op.eng)
                cnt[k] = cnt.get(k, 0) + 1
                op.tok = (k, cnt[k])
        for op in self.ops:
            if op.dma_key is not None and op.dma_key in self.batch:
                op.tok = (op.tok[0], cnt[op.tok[0]])
        for k in sorted(cnt.keys()):
            if k not in self.sems:
                self.sems[k] = self.es.enter_context(nc.semaphore("s_%s_%s" % k))
        sems = self.sems
        ops = self.ops
        totals = dict(cnt)

        def run(eng_name, h):
            waited = self.waited[eng_name]
            for op in per[eng_name]:
                for d in op.deps:
                    k, v = ops[d].tok
                    if waited.get(k, 0) >= v:
                        continue
                    h.wait_ge(sems[k], v)
                    waited[k] = v
                ins = getattr(h, op.fn[0])(**op.fn[1])
                if op.tok is not None:
                    ins.then_inc(sems[op.tok[0]], 16 if op.dma_key is not None else 1)
            for k in sorted(totals.keys()):
                if waited.get(k, 0) < totals[k]:
                    h.wait_ge(sems[k], totals[k])
                    waited[k] = totals[k]

        with nc.Block() as block:
            block.sync(lambda h: run("sp", h))
            block.tensor(lambda h: run("pe", h))
            block.scalar(lambda h: run("act", h))
            block.vector(lambda h: run("dve", h))
            block.gpsimd(lambda h: run("pool", h))
        self.ops = []
        self.bufs = {}


D = 1024
DIN = 3612
NT = 2048
T = 8192
OFF = dict(fq=0, fk=256, fv=512, ff=768, fg=772, glu=1028, cg=1540, nq=1796, nkc=2308, nvc=2436,
           nks=2564, nvs=2692, nkw=2820, nvw=2948, ngl=3076, ng=3100)

A_OUTS = dict(qf=([2, 128, NT], BF16), kf=([2, 128, NT], BF16), yT=([2, 128, NT], BF16), qn=([4, 128, NT], BF16),
              kc=([128, NT], BF16), vc=([128, NT], BF16), ks=([128, NT], BF16), kw=([128, NT], BF16),
              vf=([NT, 320], BF16), vs=([NT, 160], BF16), vw=([NT, 160], BF16), lf=([NT, 4], F32),
              gt=([NT, D], BF16), gl=([NT, 24], F32))


def phase_a(S, nc, es, io, pfx="a"):
    P = lambda n: pfx + n
    sb = lambda n, s, d: es.enter_context(nc.sbuf_tensor(P(n), s, d))
    PS0b = es.enter_context(nc.psum_tensor(P("ps0"), [128, 1024], BF16))
    PS = [None] + [es.enter_context(nc.psum_tensor(P("ps%d" % i), [128, 512], F32)) for i in range(1, 8)]
    Wb = sb("Wb", [128, 8, DIN], BF16)
    wst = [sb("wst%d" % i, [128, DIN], F32) for i in range(2)]
    gcol = sb("gcol", [128, 8], F32)
    fb = sb("fb", [128, 4], F32)
    cosT = sb("cosT", [128, NT], F32)
    sinT = sb("sinT", [128, NT], F32)
    rmat_f = sb("rmat_f", [128, 128], F32)
    rmat = sb("rmat", [128, 128], BF16)
    idf = sb("idf", [128, 128], F32)
    idb = sb("idb", [128, 128], BF16)
    xt = [sb("xt%d" % i, [128, D], F32) for i in range(2)]
    sq = sb("sq", [128, D], BF16)
    ss = [sb("ss%d" % i, [128, 1], F32) for i in range(2)]
    hb = [sb("hb%d" % i, [128, D], BF16) for i in range(2)]
    hT = [sb("hT%d" % i, [128, 8, 512], BF16) for i in range(2)]
    fo = [sb("fo%d" % i, [128, 512], BF16) for i in range(4)]
    zc = [sb("zc%d" % i, [128, 512], BF16) for i in range(2)]
    t1 = [sb("t1%d" % i, [128, 512], F32) for i in range(2)]
    t2 = [sb("t2%d" % i, [128, 512], F32) for i in range(2)]
    sg = [sb("sg%d" % i, [128, 512], F32) for i in range(2)]
    vfo = [sb("vfo%d" % i, [128, 4, 80], BF16) for i in range(2)]
    vso = [sb("vso%d" % i, [128, 2, 80], BF16) for i in range(2)]
    vwo = [sb("vwo%d" % i, [128, 2, 80], BF16) for i in range(2)]
    lfo = [sb("lfo%d" % i, [128, 4], F32) for i in range(2)]
    lft = [sb("lft%d" % i, [128, 4], F32) for i in range(2)]
    glo = [sb("glo%d" % i, [128, 24], F32) for i in range(2)]
    gto = [sb("gto%d" % i, [128, D], BF16) for i in range(2)]
    out_keys = set()

    def store(src_name, dst, src):
        k = P("k_" + src_name)
        out_keys.add(k)
        S.dma(k, r=[P(src_name)], out=dst, in_=src)

    S.dma(P("c"), batch=True, w=[P("gcol")], out=gcol[:], in_=io["norm_g"])
    S.dma(P("c"), batch=True, w=[P("fb")], out=fb[:], in_=io["fox_b"].partition_broadcast(128))
    S.dma(P("c"), batch=True, w=[P("cos")], out=cosT[:], in_=io["cos"])
    S.dma(P("c"), batch=True, w=[P("sin")], out=sinT[:], in_=io["sin"])
    S.dma(P("c"), batch=True, w=[P("rmf")], out=rmat_f[:], in_=io["rmat"])
    S.dma(P("c"), batch=True, w=[P("idf")], out=idf[:], in_=io["ident"])
    S.dve("tensor_copy", r=[P("rmf")], w=[P("rmat")], out=rmat[:], in_=rmat_f[:])
    S.dve("tensor_copy", r=[P("idf")], w=[P("idb")], out=idb[:], in_=idf[:])
    for i in range(2):
        S.dve("memset", w=[P("vfo%d" % i)], ap=vfo[i][:], constant=1.0)
        S.dve("memset", w=[P("vso%d" % i)], ap=vso[i][:], constant=1.0)
        S.dve("memset", w=[P("vwo%d" % i)], ap=vwo[i][:], constant=1.0)
    for c in range(8):
        w = wst[c % 2]
        wn = P("wst%d" % (c % 2))
        S.dma(P("w%d" % (c % 2)), w=[wn], out=w[:], in_=io["w_in"][c * 128:(c + 1) * 128, :])
        eng = S.dve if c % 2 == 0 else S.pool
        eng("tensor_scalar", r=[wn, P("gcol")], w=[P("Wb")], out=Wb[:, c, :], in0=w[:], scalar1=gcol[:, c:c + 1],
            scalar2=None, op0=ALU.mult)

    psrr = [1]

    def next_ps():
        k = psrr[0]
        psrr[0] = 1 + (psrr[0] % 7)
        return k

    foi = [0]
    ri = [0]
    for s in range(4):
        hTs = hT[s % 2]
        hTn = P("hT%d" % (s % 2))
        for q in range(4):
            tn = s * 4 + q
            p = tn % 2
            x_t = xt[p]
            xn = P("xt%d" % p)
            ssn = P("ss%d" % p)
            S.dma(P("x%d" % p), w=[xn], out=x_t[:], in_=io["x"][tn * 128:(tn + 1) * 128, :])
            S.act("activation", r=[xn], w=[P("sq"), ssn], out=sq[:], in_=x_t[:], func=AF.Square, accum_out=ss[p][:, 0:1])
            S.dve("tensor_scalar", r=[ssn], w=[ssn], out=ss[p][:], in0=ss[p][:], scalar1=1.0 / D, scalar2=1e-6,
                  op0=ALU.mult, op1=ALU.add)
            S.act("activation", r=[ssn], w=[ssn], out=ss[p][:], in_=ss[p][:], func=AF.Sqrt)
            S.dve("reciprocal", r=[ssn], w=[ssn], out=ss[p][:], in_=ss[p][:])
            S.dve("tensor_scalar", r=[xn, ssn], w=[P("hb%d" % p)], out=hb[p][:], in0=x_t[:], scalar1=ss[p][:, 0:1],
                  scalar2=None, op0=ALU.mult)
            for half in range(2):
                for cc in range(4):
                    c = half * 4 + cc
                    S.pe("transpose", r=[P("hb%d" % p), P("idb")], w=[P("ps0")], out=PS0b[:, cc * 128:(cc + 1) * 128],
                         in_=hb[p][:, c * 128:(c + 1) * 128], identity=idb[:])
                S.act("activation", r=[P("ps0")], w=[hTn], out=hTs[:, half * 4:(half + 1) * 4, q * 128:(q + 1) * 128],
                      in_=PS0b[:, 0:512].rearrange("p (c t) -> p c t", c=4), func=AF.Copy)
        cols = slice(s * 512, (s + 1) * 512)

        def fm(coff, k):
            for c in range(8):
                S.pe("matmul", r=[P("Wb"), hTn], w=[P("ps%d" % k)], out=PS[k][:, :], lhsT=Wb[:, c, coff:coff + 128],
                     rhs=hTs[:, c, :], start=(c == 0), stop=(c == 7))

        def plain_out(coff, dst):
            k = next_ps()
            fm(coff, k)
            i = foi[0] % 4
            foi[0] += 1
            S.act("activation", r=[P("ps%d" % k)], w=[P("fo%d" % i)], out=fo[i][:], in_=PS[k][:], func=AF.Copy)
            store("fo%d" % i, dst, fo[i][:])

        def rope_out(coff, dst):
            k = next_ps()
            fm(coff, k)
            k2 = next_ps()
            j = ri[0] % 2
            ri[0] += 1
            i = foi[0] % 4
            foi[0] += 1
            S.act("activation", r=[P("ps%d" % k)], w=[P("zc%d" % j)], out=zc[j][:], in_=PS[k][:], func=AF.Copy)
            S.pe("matmul", r=[P("rmat"), P("zc%d" % j)], w=[P("ps%d" % k2)], out=PS[k2][:], lhsT=rmat[:], rhs=zc[j][:],
                 start=True, stop=True)
            S.act("activation", r=[P("ps%d" % k)], w=[P("t1%d" % j)], out=t1[j][:], in_=PS[k][:], func=AF.Copy)
            S.act("activation", r=[P("ps%d" % k2)], w=[P("t2%d" % j)], out=t2[j][:], in_=PS[k2][:], func=AF.Copy)
            S.dve("tensor_tensor", r=[P("t1%d" % j), P("cos")], w=[P("t1%d" % j)], out=t1[j][:], in0=t1[j][:],
                  in1=cosT[:, cols], op=ALU.mult)
            S.pool("tensor_tensor", r=[P("t2%d" % j), P("sin")], w=[P("t2%d" % j)], out=t2[j][:], in0=t2[j][:],
                   in1=sinT[:, cols], op=ALU.mult)
            S.dve("tensor_tensor", r=[P("t1%d" % j), P("t2%d" % j)], w=[P("fo%d" % i)], out=fo[i][:], in0=t1[j][:],
                   in1=t2[j][:], op=ALU.add)
            store("fo%d" % i, dst, fo[i][:])

        for h2 in range(2):
            plain_out(OFF["fq"] + 128 * h2, io["qf"][h2, :, cols])
            plain_out(OFF["fk"] + 128 * h2, io["kf"][h2, :, cols])
        for h2 in range(2):
            ka = next_ps()
            fm(OFF["glu"] + 128 * h2, ka)
            kb = next_ps()
            fm(OFF["glu"] + 256 + 128 * h2, kb)
            i = foi[0] % 4
            foi[0] += 1
            S.act("activation", r=[P("ps%d" % kb)], w=[P("sg%d" % h2)], out=sg[h2][:], in_=PS[kb][:], func=AF.Sigmoid)
            S.act("activation", r=[P("ps%d" % ka)], w=[P("t1%d" % h2)], out=t1[h2][:], in_=PS[ka][:], func=AF.Copy)
            S.dve("tensor_tensor", r=[P("t1%d" % h2), P("sg%d" % h2)], w=[P("fo%d" % i)], out=fo[i][:], in0=t1[h2][:],
                  in1=sg[h2][:], op=ALU.mult)
            store("fo%d" % i, io["yT"][h2, :, cols], fo[i][:])
        for c4 in range(4):
            rope_out(OFF["nq"] + 128 * c4, io["qn"][c4, :, cols])
        plain_out(OFF["nkc"], io["kc"][:, cols])
        plain_out(OFF["nvc"], io["vc"][:, cols])
        rope_out(OFF["nks"], io["ks"][:, cols])
        rope_out(OFF["nkw"], io["kw"][:, cols])

        for q in range(4):
            tn = s * 4 + q
            p = tn % 2
            rows = slice(tn * 128, (tn + 1) * 128)

            def tm(c0, c1, k):
                for c in range(8):
                    S.pe("matmul", r=[P("Wb"), hTn], w=[P("ps%d" % k)], out=PS[k][:, 0:c1 - c0],
                         lhsT=hTs[:, c, q * 128:(q + 1) * 128], rhs=Wb[:, c, c0:c1], start=(c == 0), stop=(c == 7))

            k = next_ps()
            tm(512, 772, k)
            S.act("activation", r=[P("ps%d" % k)], w=[P("vfo%d" % p)], out=vfo[p][:, :, 0:64],
                  in_=PS[k][:, 0:256].rearrange("p (h d) -> p h d", h=4), func=AF.Copy)
            store("vfo%d" % p, io["vf"][rows, :], vfo[p][:].rearrange("p h d -> p (h d)"))
            lt = P("lft%d" % p)
            S.act("activation", r=[P("ps%d" % k)], w=[lt], out=lft[p][:], in_=PS[k][:, 256:260], func=AF.Copy)
            S.dve("tensor_tensor", r=[lt, P("fb")], w=[lt], out=lft[p][:], in0=lft[p][:], in1=fb[:], op=ALU.add)
            S.act("activation", r=[lt], w=[lt], out=lft[p][:], in_=lft[p][:], func=AF.Exp, scale=-1.0)
            S.act("activation", r=[lt], w=[lt], out=lft[p][:], in_=lft[p][:], func=AF.Ln, bias=1.0, scale=1.0)
            S.dve("tensor_scalar", r=[lt], w=[P("lfo%d" % p)], out=lfo[p][:], in0=lft[p][:], scalar1=-1.0, scalar2=None, op0=ALU.mult)
            store("lfo%d" % p, io["lf"][rows, :], lfo[p][:])
            for (c0, g0) in ((772, 0), (1540, 256)):
                k = next_ps()
                tm(c0, c0 + 256, k)
                S.act("activation", r=[P("ps%d" % k)], w=[P("gto%d" % p)], out=gto[p][:, g0:g0 + 256], in_=PS[k][:, 0:256], func=AF.Silu)
            k = next_ps()
            tm(3100, 3612, k)
            S.act("activation", r=[P("ps%d" % k)], w=[P("gto%d" % p)], out=gto[p][:, 512:1024], in_=PS[k][:, 0:512], func=AF.Silu)
            store("gto%d" % p, io["gt"][rows, :], gto[p][:])
            k = next_ps()
            tm(2692, 2820, k)
            S.act("activation", r=[P("ps%d" % k)], w=[P("vso%d" % p)], out=vso[p][:, :, 0:64],
                  in_=PS[k][:, 0:128].rearrange("p (h d) -> p h d", h=2), func=AF.Copy)
            store("vso%d" % p, io["vs"][rows, :], vso[p][:].rearrange("p h d -> p (h d)"))
            k = next_ps()
            tm(2948, 3100, k)
            S.act("activation", r=[P("ps%d" % k)], w=[P("vwo%d" % p)], out=vwo[p][:, :, 0:64],
                  in_=PS[k][:, 0:128].rearrange("p (h d) -> p h d", h=2), func=AF.Copy)
            store("vwo%d" % p, io["vw"][rows, :], vwo[p][:].rearrange("p h d -> p (h d)"))
            S.act("activation", r=[P("ps%d" % k)], w=[P("glo%d" % p)], out=glo[p][:], in_=PS[k][:, 128:152], func=AF.Sigmoid)
            store("glo%d" % p, io["gl"][rows, :], glo[p][:])
    return sorted(out_keys)


def own_positions(r):
    m = np.arange(16)[:, None]
    tl = np.arange(128)[None, :]
    return ((4 * m + r) * 128 + tl).reshape(-1)


def rope_tables(pos):
    inv = 500000.0 ** (-np.arange(0, 16, 2, dtype=np.float32) / 16.0)
    ang = pos.astype(np.float32)[None, :] * np.tile(inv, 2)[:, None].astype(np.float32)
    cos = np.ones((64, pos.size), np.float32)
    sin = np.zeros((64, pos.size), np.float32)
    cos[:16] = np.cos(ang)
    sin[:16] = np.sin(ang)
    return np.tile(cos, (2, 1)), np.tile(sin, (2, 1))


def rope_rmat():
    R = np.zeros((128, 128), np.float32)
    for hh in range(2):
        for d in range(8):
            R[hh * 64 + d + 8, hh * 64 + d] = -1.0
            R[hh * 64 + d, hh * 64 + d + 8] = 1.0
    return R


def build_a():
    nc = bass.Bass("TRN2", target_bir_lowering=False)
    io = {}
    io["x"] = nc.dram_tensor("x", [NT, D], F32, kind="ExternalInput").ap()
    io["w_in"] = nc.dram_tensor("w_in", [D, DIN], F32, kind="ExternalInput").ap()
    io["norm_g"] = nc.dram_tensor("norm_g", [128, 8], F32, kind="ExternalInput").ap()
    io["fox_b"] = nc.dram_tensor("fox_b", [4], F32, kind="ExternalInput").ap()
    io["cos"] = nc.dram_tensor("cos", [128, NT], F32, kind="ExternalInput").ap()
    io["sin"] = nc.dram_tensor("sin", [128, NT], F32, kind="ExternalInput").ap()
    io["rmat"] = nc.dram_tensor("rmat", [128, 128], F32, kind="ExternalInput").ap()
    io["ident"] = nc.dram_tensor("ident", [128, 128], F32, kind="ExternalInput").ap()
    for n, (shp, dt) in A_OUTS.items():
        io[n] = nc.dram_tensor(n, shp, dt, kind="ExternalOutput").ap()
    with ExitStack() as top:
        S = Sched(nc, top)
        with ExitStack() as es:
            phase_a(S, nc, es, io)
            S.flush()
    return nc


def out_scope(S, nc, es, io, MIXB, last, pfx="o"):
    P = lambda n: pfx + n
    sb = lambda n, s_, d: es.enter_context(nc.sbuf_tensor(P(n), s_, d))
    PS = [es.enter_context(nc.psum_tensor(P("ps%d" % i), [128, 512], F32)) for i in range(4)]
    PSTb = es.enter_context(nc.psum_tensor(P("pstb"), [128, 1024], BF16))
    WO = sb("WO", [128, 8, D], BF16)
    wst = [sb("wst%d" % i, [128, D], F32) for i in range(2)]
    idf = sb("idf", [128, 128], F32)
    idb = sb("idb", [128, 128], BF16)
    MXT = [sb("MXT%d" % i, [128, 8, 128], BF16) for i in range(2)]
    xt = [sb("xt%d" % i, [128, D], F32) for i in range(2)]
    yo = [sb("yo%d" % i, [128, D], F32) for i in range(2)]
    sq = sb("sq", [128, D], F32)
    ss = sb("ss", [128, 1], F32)
    fg = sb("fg", [128, D], F32)
    S.dma(P("c"), batch=True, w=[P("idf")], out=idf[:], in_=io["ident"])
    if last:
        S.dma(P("c"), batch=True, w=[P("fg")], out=fg[:], in_=io["final_g"].partition_broadcast(128))
    S.dve("tensor_copy", r=[P("idf")], w=[P("idb")], out=idb[:], in_=idf[:])
    for c in range(8):
        w, wn = wst[c % 2], P("wst%d" % (c % 2))
        S.dma(P("w%d" % (c % 2)), w=[wn], out=w[:], in_=io["w_out"][c * 128:(c + 1) * 128, :])
        (S.dve if c % 2 == 0 else S.pool)("tensor_copy", r=[wn], w=[P("WO")], out=WO[:, c, :], in_=w[:])
    for m in range(16):
        p = m % 2
        mx, mxn, x_t, xn, y_t, yn = MXT[p], P("MXT%d" % p), xt[p], P("xt%d" % p), yo[p], P("yo%d" % p)
        S.dma(P("x%d" % p), w=[xn], out=x_t[:], in_=io["x"][m * 128:(m + 1) * 128, :])
        for c in range(8):
            S.pe("transpose", r=["MIXB", P("idb")], w=[P("pstb")], out=PSTb[:, c * 128:(c + 1) * 128], in_=MIXB[:, m, c * 128:(c + 1) * 128],
                 identity=idb[:])
        S.act("activation", r=[P("pstb")], w=[mxn], out=mx[:].rearrange("f c q -> f (c q)"), in_=PSTb[:], func=AF.Copy)
        for n2 in range(2):
            k = (2 * m + n2) % 4
            for c in range(8):
                S.pe("matmul", r=[mxn, P("WO")], w=[P("ps%d" % k)], out=PS[k][:], lhsT=mx[:, c, :], rhs=WO[:, c, n2 * 512:(n2 + 1) * 512],
                     start=(c == 0), stop=(c == 7))
            S.act("activation", r=[P("ps%d" % k)], w=[yn], out=y_t[:, n2 * 512:(n2 + 1) * 512], in_=PS[k][:], func=AF.Copy)
        S.dve("tensor_tensor", r=[yn, xn], w=[yn], out=y_t[:], in0=y_t[:], in1=x_t[:], op=ALU.add)
        if last:
            S.act("activation", r=[yn], w=[P("sq"), P("ss")], out=sq[:], in_=y_t[:], func=AF.Square, accum_out=ss[:, 0:1])
            S.dve("tensor_scalar", r=[P("ss")], w=[P("ss")], out=ss[:], in0=ss[:], scalar1=1.0 / D, scalar2=1e-6, op0=ALU.mult, op1=ALU.add)
            S.act("activation", r=[P("ss")], w=[P("ss")], out=ss[:], in_=ss[:], func=AF.Sqrt)
            S.dve("reciprocal", r=[P("ss")], w=[P("ss")], out=ss[:], in_=ss[:])
            S.dve("scalar_tensor_tensor", r=[yn, P("ss"), P("fg")], w=[yn], out=y_t[:], in0=y_t[:], scalar=ss[:, 0:1], in1=fg[:],
                  op0=ALU.mult, op1=ALU.mult)
        S.dma(P("o%d" % p), r=[yn], out=io["xo"][m * 128:(m + 1) * 128, :], in_=y_t[:])


B_INS = dict(kf_g=([2, 128, T], BF16), vf_g=([T, 320], BF16), lf_g=([T, 4], F32), qf=([2, 128, NT], BF16),
             kc_g=([128, T], BF16), vc_g=([128, T], BF16), ks_g=([128, T], BF16), kw_g=([128, T], BF16),
             vs_g=([T, 160], BF16), vw_g=([T, 160], BF16), qn=([4, 128, NT], BF16), gl=([NT, 24], F32), gt=([NT, D], BF16),
             yT=([2, 128, NT], BF16), tails_g=([4, 2, 128, 16, 30], BF16), x=([NT, D], F32), w_out=([D, D], F32), final_g=([D], F32),
             tri=([128, 128], F32), ones128=([128, 128], F32), onehot=([128, 4], F32), dmask=([128, 4, 128], BF16),
             wmask=([128, 8, 128], BF16), cmask=([16, 128, 4, 128], BF16), force=([16, 128, 128], F32), ee=([128, T], BF16),
             ident=([128, 128], F32), rmat=([128, 128], F32), cosC=([128, 512], F32), sinC=([128, 512], F32),
             ovl=([128, 4, 128], BF16), peT=([64, 2, 32], F32), w2=([128, 2, 2, 64], F32), w1k=([2048, 256], F32),
             w1v=([2048, 256], F32), conv_w=([128, 2, 31], F32), conv_b=([128, 2], F32), ln_g=([128, 2], F32), ln_b=([128, 2], F32),
             conv_pw=([256, 256], F32), ohprev=([128, 4], F32))


def phase_b(S, nc, top, io, last):
    KCMP = top.enter_context(nc.sbuf_tensor("KCMP", [128, 512], BF16))
    VCA = top.enter_context(nc.sbuf_tensor("VCA", [128, 4, 2, 208], BF16))
    MIXB = top.enter_context(nc.sbuf_tensor("MIXB", [128, 16, D], BF16))
    for scope in (lambda es: cmp_scope(S, nc, es, io, KCMP, VCA), lambda es: conv_scope(S, nc, es, io, MIXB),
                  lambda es: fox_scope(S, nc, es, io, MIXB), lambda es: nsa_scope(S, nc, es, io, KCMP, VCA, MIXB),
                  lambda es: out_scope(S, nc, es, io, MIXB, last)):
        with ExitStack() as es:
            scope(es)
            S.flush()


def build_b(last):
    nc = bass.Bass("TRN2", target_bir_lowering=False)
    io = {n: nc.dram_tensor(n, shp, dt, kind="ExternalInput").ap() for n, (shp, dt) in B_INS.items()}
    io["xo"] = nc.dram_tensor("xo", [NT, D], F32, kind="ExternalOutput").ap()
    with ExitStack() as top:
        S = Sched(nc, top)
        phase_b(S, nc, top, io, last)
    return nc


def a_inputs(inputs, layer, x_own):
    maps = []
    for core in range(8):
        cos, sin = rope_tables(own_positions(core % 4))
        maps.append(dict(x=np.ascontiguousarray(x_own[core], dtype=np.float32), w_in=np.ascontiguousarray(inputs["w_in"][layer]),
                         norm_g=np.ascontiguousarray(inputs["norm_g"][layer].reshape(8, 128).T), fox_b=np.ascontiguousarray(inputs["fox_b"][layer]),
                         cos=cos, sin=sin, rmat=rope_rmat(), ident=np.eye(128, dtype=np.float32)))
    return maps


def b_inputs(inputs, layer, resA, x_own):
    maps = []
    cc = common_consts()
    ee = ee_const()
    cmpc = cmp_consts(inputs, layer)
    for core in range(8):
        b, r = core // 4, core % 4
        R = [resA[4 * b + rr] for rr in range(4)]
        cat = lambda n, ax: np.concatenate([np.asarray(R[rr][n]) for rr in range(4)], axis=ax)
        d = dict(kf_g=cat("kf", 2), vf_g=cat("vf", 0), lf_g=cat("lf", 0), kc_g=cat("kc", 1), vc_g=cat("vc", 1), ks_g=cat("ks", 1),
                 kw_g=cat("kw", 1), vs_g=cat("vs", 0), vw_g=cat("vw", 0))
        d["tails_g"] = np.ascontiguousarray(np.stack([np.asarray(R[rr]["yT"]).reshape(2, 128, 16, 128)[:, :, :, 98:128] for rr in range(4)], axis=0))
        own = resA[core]
        for n in ("qf", "qn", "gl", "gt", "yT"):
            d[n] = np.asarray(own[n])
        d["x"] = np.ascontiguousarray(x_own[core], dtype=np.float32)
        d["w_out"] = np.ascontiguousarray(inputs["w_out"][layer])
        d["final_g"] = np.ascontiguousarray(inputs["final_g"])
        d.update(cc)
        d.update(rank_consts(r))
        d.update(nsa_rank_consts(r))
        d.update(cmpc)
        d.update(conv_consts(inputs, layer, r))
        d["ee"] = ee
        d["ident"] = np.eye(128, dtype=np.float32)
        maps.append(d)
    return maps


def kernel(**inputs):
    inputs = {k: np.asarray(v) for k, v in inputs.items()}
    x = inputs["x"].astype(np.float32)
    x_own = [x[core // 4][own_positions(core % 4)] for core in range(8)]
    ncA = build_a()
    for layer in range(2):
        resA = run_bass_kernel_spmd(ncA, a_inputs(inputs, layer, x_own), core_ids=list(range(8))).results
        ncB = build_b(last=(layer == 1))
        resB = run_bass_kernel_spmd(ncB, b_inputs(inputs, layer, resA, x_own), core_ids=list(range(8))).results
        x_own = [np.asarray(resB[core]["xo"]) for core in range(8)]
    out = np.empty((2, T, D), np.float32)
    for core in range(8):
        out[core // 4][own_positions(core % 4)] = x_own[core]
    return out


def cidx(j):
    return (j % 4) * 16 + j // 4


def gcol(j):
    return cidx(j) * 128


def fox_scope(S, nc, es, io, MIXB, pfx="f"):
    P = lambda n: pfx + n
    sb = lambda n, s, d: es.enter_context(nc.sbuf_tensor(P(n), s, d))
    PS = [es.enter_context(nc.psum_tensor(P("ps%d" % i), [128, 512], F32)) for i in range(4)]
    KF = sb("KF", [128, 2, T], BF16)
    VF = sb("VF", [128, 64, 320], BF16)
    QF = sb("QF", [128, 4, NT], BF16)
    LF = sb("LF", [128, 64, 4], F32)
    tri = sb("tri", [128, 128], F32)
    ones = sb("ones", [128, 128], F32)
    oneh = sb("oneh", [128, 4], F32)
    dm = sb("dm", [128, 4, 128], BF16)
    dm4 = sb("dm4", [128, 4, 4, 128], BF16)
    WI = sb("WI", [128, 64, 4], F32)
    TOT = sb("TOT", [128, 64, 4], F32)
    SA = sb("SA", [128, 64, 4], F32)
    SB_ = sb("SB", [128, 64, 4], F32)
    NC_ = sb("NC", [128, 64, 4], F32)
    ROWN = sb("ROWN", [128, 16, 4], F32)
    CB = [sb("CB%d" % i, [128, 64, 4], F32) for i in range(2)]
    PT = [sb("PT%d" % i, [128, 512], BF16) for i in range(3)]
    ofs = sb("ofs", [128, 4, 65], F32)
    rden = sb("rden", [128, 4], F32)
    onrm = sb("onrm", [128, 256], F32)
    gtt = [sb("gtt%d" % i, [128, 256], BF16) for i in range(2)]

    ld = P("ld")
    S.pool("memset", w=[P("QF")], ap=QF[:].rearrange("p h t -> p (h t)"), constant=0.0)
    for h2 in range(2):
        S.dma(ld, batch=True, w=[P("KF")], out=KF[:, h2, :], in_=io["kf_g"][h2])
    for h in range(4):
        pb = 64 * (h % 2)
        S.dma(ld, batch=True, w=[P("QF")], out=QF[pb:pb + 64, h, :], in_=io["qf"][h // 2, pb:pb + 64, :])
    for r in range(4):
        S.dma(ld, batch=True, w=[P("VF")], out=VF[:, r * 16:(r + 1) * 16, :],
              in_=io["vf_g"][r * NT:(r + 1) * NT, :].rearrange("(c t) w -> t c w", t=128))
    for r in range(4):
        S.dma(ld, batch=True, w=[P("LF")], out=LF[:].rearrange("t (m r) h -> t m r h", r=4)[:, :, r, :],
              in_=io["lf_g"][r * NT:(r + 1) * NT, :].rearrange("(m t) h -> t m h", t=128))
    S.dma(ld, batch=True, w=[P("tri")], out=tri[:], in_=io["tri"])
    S.dma(ld, batch=True, w=[P("ones")], out=ones[:], in_=io["ones128"])
    S.dma(ld, batch=True, w=[P("oneh")], out=oneh[:], in_=io["onehot"])
    S.dma(ld, batch=True, w=[P("dm")], out=dm[:], in_=io["dmask"])
    for h in range(4):
        S.pool("tensor_copy", r=[P("dm")], w=[P("dm4")], out=dm4[:, :, h, :], in_=dm[:])

    LFf = LF[:].rearrange("t j h -> t (j h)")
    S.pe("matmul", r=[P("tri"), P("LF")], w=[P("ps0")], out=PS[0][:, 0:256], lhsT=tri[:], rhs=LFf, start=True, stop=True)
    S.pe("matmul", r=[P("ones"), P("LF")], w=[P("ps1")], out=PS[1][:, 0:256], lhsT=ones[:], rhs=LFf, start=True, stop=True)
    S.act("activation", r=[P("ps0")], w=[P("WI")], out=WI[:].rearrange("t j h -> t (j h)"), in_=PS[0][:, 0:256], func=AF.Copy)
    S.act("activation", r=[P("ps1")], w=[P("TOT")], out=TOT[:].rearrange("t j h -> t (j h)"), in_=PS[1][:, 0:256], func=AF.Copy)
    src, srcn = TOT, P("TOT")
    d = 1
    pp = [(SA, P("SA")), (SB_, P("SB"))]
    k = 0
    while d < 64:
        dst, dstn = pp[k % 2]
        S.dve("tensor_tensor", r=[srcn], w=[dstn], out=dst[:, d:, :], in0=src[:, d:, :], in1=src[:, :64 - d, :], op=ALU.add)
        S.dve("tensor_copy", r=[srcn], w=[dstn], out=dst[:, :d, :], in_=src[:, :d, :])
        src, srcn = dst, dstn
        d *= 2
        k += 1
    INCL, INCLn = src, srcn
    S.dve("tensor_tensor", r=[P("WI"), INCLn], w=[P("NC")], out=NC_[:], in0=WI[:], in1=INCL[:], op=ALU.add)
    S.dve("tensor_tensor", r=[P("NC"), P("TOT")], w=[P("NC")], out=NC_[:], in0=TOT[:], in1=NC_[:], op=ALU.subtract)
    I4 = INCL[:].rearrange("t (m r) h -> t m r h", r=4)
    S.dve("tensor_scalar", r=[INCLn, P("oneh")], w=[P("ROWN")], out=ROWN[:], in0=I4[:, :, 0, :], scalar1=oneh[:, 0:1],
          scalar2=None, op0=ALU.mult)
    for r in range(1, 4):
        S.dve("scalar_tensor_tensor", r=[INCLn, P("oneh"), P("ROWN")], w=[P("ROWN")], out=ROWN[:], in0=I4[:, :, r, :],
              scalar=oneh[:, r:r + 1], in1=ROWN[:], op0=ALU.mult, op1=ALU.add)

    st_i = [0]
    for m in range(16):
        J = 4 * m + 4
        cb, cbn = CB[m % 2], P("CB%d" % (m % 2))
        for h in range(4):
            S.dve("tensor_scalar", r=[P("NC"), P("ROWN")], w=[cbn], out=cb[:, 0:J, h], in0=NC_[:, 0:J, h],
                  scalar1=ROWN[:, m, h:h + 1], scalar2=0.0, op0=ALU.add, op1=ALU.min)
        for j in range(J):
            sp = st_i[0] % 2
            pt, ptn = PT[st_i[0] % 3], P("PT%d" % (st_i[0] % 3))
            st_i[0] += 1
            psn = P("ps%d" % sp)
            for h in range(4):
                S.pe("matmul", r=[P("KF"), P("QF")], w=[psn], out=PS[sp][:, h * 128:(h + 1) * 128],
                     lhsT=KF[:, h // 2, gcol(j):gcol(j) + 128], rhs=QF[:, h, m * 128:(m + 1) * 128], start=True, stop=True)
            for h in range(4):
                S.act("activation", r=[psn, cbn], w=[ptn], out=pt[:, h * 128:(h + 1) * 128], in_=PS[sp][:, h * 128:(h + 1) * 128],
                      func=AF.Exp, bias=cb[:, j, h:h + 1], scale=0.125)
            if j >= 4 * m:
                S.dve("tensor_tensor", r=[ptn, P("dm4")], w=[ptn], out=pt[:], in0=pt[:],
                      in1=dm4[:, j - 4 * m, :, :].rearrange("k h q -> k (h q)"), op=ALU.mult)
            for h in range(4):
                S.pe("matmul", r=[ptn, P("VF")], w=[P("ps2")], out=PS[2][:, h * 65:(h + 1) * 65], lhsT=pt[:, h * 128:(h + 1) * 128],
                     rhs=VF[:, cidx(j), h * 80:h * 80 + 65], start=(j == 0 and h == 0), stop=(j == J - 1 and h == 3))
        S.act("activation", r=[P("ps2")], w=[P("ofs")], out=ofs[:].rearrange("q h d -> q (h d)"), in_=PS[2][:, 0:260], func=AF.Copy)
        S.dve("reciprocal", r=[P("ofs")], w=[P("rden")], out=rden[:], in_=ofs[:, :, 64])
        for h in range(4):
            S.dve("tensor_scalar", r=[P("ofs"), P("rden")], w=[P("onrm")], out=onrm[:, h * 64:(h + 1) * 64], in0=ofs[:, h, 0:64],
                  scalar1=rden[:, h:h + 1], scalar2=None, op0=ALU.mult)
        gp = m % 2
        S.dma(P("gt%d" % gp), w=[P("gtt%d" % gp)], out=gtt[gp][:], in_=io["gt"][m * 128:(m + 1) * 128, 0:256])
        S.dve("tensor_tensor", r=[P("onrm"), P("gtt%d" % gp)], w=["MIXB"], out=MIXB[:, m, 0:256], in0=onrm[:], in1=gtt[gp][:], op=ALU.mult)


def rank_consts(r):
    k = np.arange(128)[:, None]
    q = np.arange(128)[None, :]
    dmask = np.zeros((128, 4, 128), np.float32)
    for c in range(4):
        if c < r:
            dmask[:, c, :] = 1.0
        elif c == r:
            dmask[:, c, :] = (k <= q)
    onehot = np.zeros((128, 4), np.float32)
    onehot[:, r] = 1.0
    return dict(dmask=dmask.astype(NPBF), onehot=onehot)


def common_consts():
    k = np.arange(128)[:, None]
    q = np.arange(128)[None, :]
    return dict(tri=(k <= q).astype(np.float32), ones128=np.ones((128, 128), np.float32))


def cmp_scope(S, nc, es, io, KCMP, VCA, pfx="c"):
    P = lambda n: pfx + n
    sb = lambda n, s, d: es.enter_context(nc.sbuf_tensor(P(n), s, d))
    PS = [es.enter_context(nc.psum_tensor(P("ps%d" % i), [128, 512], F32)) for i in range(4)]
    KCN = [sb("KCN%d" % i, [128, T], BF16) for i in range(2)]
    W1st = sb("W1st", [128, 32, 256], F32)
    W1Z = [sb("W1Z%d" % g, [128, 32, 256], BF16) for g in range(2)]
    PEf = sb("PEf", [128, 2, 32], F32)
    PEB = sb("PEB", [128, 2, 32], BF16)
    W2f = sb("W2f", [128, 2, 2, 64], F32)
    W2KZ = sb("W2KZ", [128, 2, 2, 128], BF16)
    W2V = sb("W2V", [128, 2, 64], BF16)
    BH = sb("BH", [128, 2], F32)
    HID = [[sb("HID%d%d" % (kv, g), [128, 2, 512], BF16) for g in range(2)] for kv in range(2)]
    X = sb("X", [128, 512], F32)
    X2 = sb("X2", [128, 512], F32)
    U = sb("U", [128, 512], F32)
    SG = sb("SG", [128, 512], F32)
    KC0 = sb("KC0", [128, 512], F32)
    KZ = sb("KZ", [128, 512], BF16)
    RT = sb("RT", [128, 512], F32)
    cosC = sb("cosC", [128, 512], F32)
    sinC = sb("sinC", [128, 512], F32)
    rmf = sb("rmf", [128, 128], F32)
    rmb = sb("rmb", [128, 128], BF16)
    ovf = sb("ovf", [128, 4, 128], BF16)

    ld = P("ld")
    for kv, nm in ((0, "kc_g"), (1, "vc_g")):
        for r in range(4):
            S.dma(ld, batch=True, w=[P("KCN%d" % kv)], out=KCN[kv][:].rearrange("d (m r t) -> d m r t", r=4, t=128)[:, :, r, :],
                  in_=io[nm][:, r * NT:(r + 1) * NT].rearrange("d (m t) -> d m t", t=128))
    S.dma(ld, batch=True, w=[P("PEf")], out=PEf[0:64, :, :], in_=io["peT"])
    S.dma(ld, batch=True, w=[P("PEf")], out=PEf[64:128, :, :], in_=io["peT"])
    S.dma(ld, batch=True, w=[P("W2f")], out=W2f[:], in_=io["w2"])
    S.dma(ld, batch=True, w=[P("cosC")], out=cosC[:], in_=io["cosC"])
    S.dma(ld, batch=True, w=[P("sinC")], out=sinC[:], in_=io["sinC"])
    S.dma(ld, batch=True, w=[P("rmf")], out=rmf[:], in_=io["rmat"])
    S.dma(ld, batch=True, w=[P("ovf")], out=ovf[:], in_=io["ovl"])
    S.dve("tensor_copy", r=[P("PEf")], w=[P("PEB")], out=PEB[:], in_=PEf[:])
    S.dve("tensor_copy", r=[P("rmf")], w=[P("rmb")], out=rmb[:], in_=rmf[:])
    S.pool("memset", w=[P("W2KZ")], ap=W2KZ[:], constant=0.0)
    for g in range(2):
        S.pool("memset", w=[P("W1Z%d" % g)], ap=W1Z[g][:], constant=0.0)
        S.dve("tensor_copy", r=[P("W2f"), P("W2KZ")], w=[P("W2KZ")], out=W2KZ[:, g, :, 64 * g:64 * g + 64], in_=W2f[:, 0, :, :])
        for kv in range(2):
            S.pool("memset", w=[P("HID%d%d" % (kv, g))], ap=HID[kv][g][:], constant=0.0)
    S.dve("tensor_copy", r=[P("W2f")], w=[P("W2V")], out=W2V[:], in_=W2f[:, 1, :, :])
    S.pool("memset", w=[P("KC0")], ap=KC0[:], constant=0.0)
    S.pool("memset", w=["VCA"], ap=VCA[:], constant=1.0)
    for g in range(2):
        S.pool("tensor_copy", r=[P("ovf"), "VCA"], w=["VCA"], out=VCA[:, :, g, 80:208], in_=ovf[:])

    for kv, wn in ((0, "w1k"), (1, "w1v")):
        for half in range(2):
            S.dma(P("w1"), w=[P("W1st")], out=W1st[64 * half:64 * half + 64, :, :],
                  in_=io[wn].rearrange("(l d) h -> d l h", d=64))
        for g in range(2):
            S.dve("tensor_copy", r=[P("W1st"), P("W1Z%d" % g)], w=[P("W1Z%d" % g)], out=W1Z[g][64 * g:64 * g + 64, :, :],
                  in_=W1st[64 * g:64 * g + 64, :, :])
        for hc in range(2):
            for l in range(32):
                S.pe("matmul", r=[P("W1Z0"), P("PEB")], w=[P("ps3")], out=PS[3][:, hc:hc + 1], lhsT=W1Z[0][:, l, hc * 128:(hc + 1) * 128],
                     rhs=PEB[:, kv, l:l + 1], start=(hc == 0 and l == 0), stop=(hc == 1 and l == 31))
        S.act("activation", r=[P("ps3")], w=[P("BH")], out=BH[:], in_=PS[3][:, 0:2], func=AF.Copy)
        for g in range(2):
            for hc in range(2):
                k = (g * 2 + hc) % 3
                psn = P("ps%d" % k)
                for l in range(32):
                    S.pe("matmul", r=[P("W1Z%d" % g), P("KCN%d" % kv)], w=[psn], out=PS[k][:, 0:511],
                         lhsT=W1Z[g][:, l, hc * 128:(hc + 1) * 128], rhs=KCN[kv][:, l:l + 16 * 510 + 1:16], start=(l == 0), stop=(l == 31))
                S.act("activation", r=[psn, P("BH")], w=[P("X")], out=X[:, 0:511], in_=PS[k][:, 0:511], func=AF.Identity,
                      bias=BH[:, hc:hc + 1], scale=1.0)
                S.dve("tensor_tensor", r=[P("X")], w=[P("X2")], out=X2[:, 0:511], in0=X[:, 0:511], in1=X[:, 0:511], op=ALU.mult)
                S.dve("tensor_scalar", r=[P("X2")], w=[P("X2")], out=X2[:, 0:511], in0=X2[:, 0:511], scalar1=0.044715, scalar2=1.0,
                      op0=ALU.mult, op1=ALU.add)
                S.dve("tensor_tensor", r=[P("X2"), P("X")], w=[P("U")], out=U[:, 0:511], in0=X2[:, 0:511], in1=X[:, 0:511], op=ALU.mult)
                S.act("activation", r=[P("U")], w=[P("SG")], out=SG[:, 0:511], in_=U[:, 0:511], func=AF.Sigmoid, scale=1.5957691216057308)
                S.dve("tensor_tensor", r=[P("X"), P("SG")], w=[P("HID%d%d" % (kv, g))], out=HID[kv][g][:, hc, 0:511], in0=X[:, 0:511],
                      in1=SG[:, 0:511], op=ALU.mult)
    n = 0
    for g in range(2):
        for hc in range(2):
            S.pe("matmul", r=[P("W2KZ"), P("HID0%d" % g)], w=[P("ps0")], out=PS[0][:, 0:511], lhsT=W2KZ[:, g, hc, :],
                 rhs=HID[0][g][:, hc, 0:511], start=(n == 0), stop=(n == 3))
            n += 1
    S.act("activation", r=[P("ps0")], w=[P("KC0")], out=KC0[:, 0:511], in_=PS[0][:, 0:511], func=AF.Copy)
    S.dve("tensor_copy", r=[P("KC0")], w=[P("KZ")], out=KZ[:], in_=KC0[:])
    S.pe("matmul", r=[P("rmb"), P("KZ")], w=[P("ps1")], out=PS[1][:], lhsT=rmb[:], rhs=KZ[:], start=True, stop=True)
    S.act("activation", r=[P("ps1")], w=[P("RT")], out=RT[:], in_=PS[1][:], func=AF.Copy)
    S.dve("tensor_tensor", r=[P("KC0"), P("cosC")], w=[P("KC0")], out=KC0[:], in0=KC0[:], in1=cosC[:], op=ALU.mult)
    S.dve("tensor_tensor", r=[P("RT"), P("sinC")], w=[P("RT")], out=RT[:], in0=RT[:], in1=sinC[:], op=ALU.mult)
    S.dve("tensor_tensor", r=[P("KC0"), P("RT")], w=["KCMP"], out=KCMP[:], in0=KC0[:], in1=RT[:], op=ALU.add)
    for c in range(4):
        for g in range(2):
            k = 2 + (c * 2 + g) % 2
            for hc in range(2):
                S.pe("matmul", r=[P("HID1%d" % g), P("W2V")], w=[P("ps%d" % k)], out=PS[k][:, 0:64], lhsT=HID[1][g][:, hc, c * 128:(c + 1) * 128],
                     rhs=W2V[:, hc, :], start=(hc == 0), stop=(hc == 1))
            S.act("activation", r=[P("ps%d" % k), "VCA"], w=["VCA"], out=VCA[:, c, g, 0:64], in_=PS[k][:, 0:64], func=AF.Copy)


def cmp_consts(inputs, layer):
    n = np.arange(512)
    cosC, sinC = rope_tables(16 * n + 31)
    nn = np.arange(512)[:, None]
    bb = np.arange(128)[None, :]
    ovl = ((16 * nn < 64 * bb + 64) & (16 * nn + 31 >= 64 * bb) & (nn < 511)).astype(np.float32)
    peT = np.stack([inputs["cmp_pe_k"][layer].T, inputs["cmp_pe_v"][layer].T], axis=1)
    w2 = np.stack([inputs["cmp_k_w2"][layer].reshape(2, 128, 64), inputs["cmp_v_w2"][layer].reshape(2, 128, 64)], axis=0)
    w2 = np.ascontiguousarray(w2.transpose(2, 0, 1, 3))
    return dict(cosC=cosC, sinC=sinC, ovl=np.ascontiguousarray(ovl.reshape(4, 128, 128).transpose(1, 0, 2)).astype(NPBF),
                peT=np.ascontiguousarray(peT, dtype=np.float32), w2=w2.astype(np.float32),
                w1k=np.ascontiguousarray(inputs["cmp_k_w1"][layer]), w1v=np.ascontiguousarray(inputs["cmp_v_w1"][layer]),
                rmat=rope_rmat())


def nsa_scope(S, nc, es, io, KCMP, VCA, MIXB, pfx="n"):
    P = lambda n: pfx + n
    sb = lambda n, s, d: es.enter_context(nc.sbuf_tensor(P(n), s, d))
    PS = [es.enter_context(nc.psum_tensor(P("ps%d" % i), [128, 512], F32)) for i in range(6)]
    PSTb = es.enter_context(nc.psum_tensor(P("pstb"), [128, 1024], BF16))
    KS = sb("KS", [128, T], BF16)
    VS = sb("VS", [128, 64, 160], BF16)
    EE = sb("EE", [128, T], BF16)
    QNZ = sb("QNZ", [128, 2, 4, NT], BF16)
    KWm = [sb("KWm%d" % i, [128, 8, 128], BF16) for i in range(2)]
    VWm = [sb("VWm%d" % i, [128, 8, 160], BF16) for i in range(2)]
    dm = sb("dm", [128, 4, 128], BF16)
    wm = sb("wm", [128, 8, 128], BF16)
    cm = [sb("cm%d" % i, [128, 4, 128], BF16) for i in range(2)]
    fc = [sb("fc%d" % i, [128, 128], F32) for i in range(2)]
    glt = [sb("glt%d" % i, [128, 24], F32) for i in range(2)]
    gtt = [sb("gtt%d" % i, [128, 512], BF16) for i in range(2)]
    idf = sb("idf", [128, 128], F32)
    idb = sb("idb", [128, 128], BF16)
    PT = [sb("PT%d" % i, [128, 512], BF16) for i in range(3)]
    ocs = sb("ocs", [128, 4, 208], F32)
    oss = sb("oss", [128, 4, 65], F32)
    ows = sb("ows", [128, 4, 65], F32)
    rd = sb("rd", [128, 3, 4], F32)
    fac = sb("fac", [128, 3, 4], F32)
    imp = sb("imp", [128, 128], F32)
    wk = sb("wk", [128, 128], F32)
    mx = sb("mx", [128, 8], F32)
    MBq = sb("MBq", [128, 128], BF16)
    MBT = sb("MBT", [128, 4, 128], BF16)
    acc = sb("acc", [128, 4, 64], F32)
    tmp = sb("tmp", [128, 4, 64], F32)

    def bc(ap, h=4):
        return ap.unsqueeze(1).to_broadcast([128, h, ap.shape[-1]])

    ld = P("ld")
    S.dma(ld, batch=True, w=[P("KS")], out=KS[:], in_=io["ks_g"])
    S.dma(ld, batch=True, w=[P("EE")], out=EE[:], in_=io["ee"])
    for r in range(4):
        S.dma(ld, batch=True, w=[P("VS")], out=VS[:, r * 16:(r + 1) * 16, :],
              in_=io["vs_g"][r * NT:(r + 1) * NT, :].rearrange("(c t) w -> t c w", t=128))
    S.pool("memset", w=[P("QNZ")], ap=QNZ[:].rearrange("p g h t -> p (g h t)"), constant=0.0)
    for hh in range(8):
        g, h = hh // 4, hh % 4
        src = 64 * (hh % 2)
        S.dma(ld, batch=True, w=[P("QNZ")], out=QNZ[64 * g:64 * g + 64, g, h, :], in_=io["qn"][hh // 2, src:src + 64, :])
    S.dma(ld, batch=True, w=[P("dm")], out=dm[:], in_=io["dmask"])
    S.dma(ld, batch=True, w=[P("wm")], out=wm[:], in_=io["wmask"])
    S.dma(ld, batch=True, w=[P("idf")], out=idf[:], in_=io["ident"])
    S.dve("tensor_copy", r=[P("idf")], w=[P("idb")], out=idb[:], in_=idf[:])

    st_i = [0]

    def unit(lhsT, rk, qrhs, mask, vrhs, vk, obank, ocols, first, last, bias_mm=None):
        sp = st_i[0] % 2
        pt, ptn = PT[st_i[0] % 3], P("PT%d" % (st_i[0] % 3))
        st_i[0] += 1
        psn = P("ps%d" % sp)
        S.pe("matmul", r=list(rk) + [P("QNZ")], w=[psn], out=PS[sp][:], lhsT=lhsT, rhs=qrhs, start=True, stop=(bias_mm is None))
        if bias_mm is not None:
            S.pe("matmul", r=[P("EE"), P("MBT")], w=[psn], out=PS[sp][:], lhsT=bias_mm, rhs=MBT[:].rearrange("b h q -> b (h q)"),
                 start=False, stop=True)
        S.act("activation", r=[psn], w=[ptn], out=pt[:], in_=PS[sp][:], func=AF.Exp, scale=0.125)
        if mask is not None:
            mk, mn = mask
            S.pool("tensor_tensor", r=[ptn, mn], w=[ptn], out=pt[:].rearrange("k (h q) -> k h q", h=4),
                   in0=pt[:].rearrange("k (h q) -> k h q", h=4), in1=bc(mk), op=ALU.mult)
        for h in range(4):
            ob, oc = obank(h), ocols(h)
            S.pe("matmul", r=[ptn] + list(vk), w=[P("ps%d" % ob)], out=PS[ob][:, oc[0]:oc[1]], lhsT=pt[:, h * 128:(h + 1) * 128],
                 rhs=vrhs, start=(first and (h == 0 or (ob != obank(0) and h == 2))), stop=(last and (h == 3 or (ob != obank(3) and h == 1))))

    for m in range(16):
        p = m % 2
        cmn, fcn, gln, gtn, kwn, vwn = P("cm%d" % p), P("fc%d" % p), P("glt%d" % p), P("gtt%d" % p), P("KWm%d" % p), P("VWm%d" % p)
        S.dma(P("cm%d" % p), w=[cmn], out=cm[p][:], in_=io["cmask"][m])
        S.dma(P("fc%d" % p), w=[fcn], out=fc[p][:], in_=io["force"][m])
        S.dma(P("gl%d" % p), w=[gln], out=glt[p][:], in_=io["gl"][m * 128:(m + 1) * 128, :])
        S.dma(P("gt%d" % p), w=[gtn], out=gtt[p][:], in_=io["gt"][m * 128:(m + 1) * 128, 512:1024])
        for mm in ((m - 1, m) if m > 0 else (m,)):
            s0 = 4 * (mm - m + 1)
            S.dma(P("kw%d" % p), w=[kwn], out=KWm[p][:, s0:s0 + 4, :],
                  in_=io["kw_g"].rearrange("d (r c t) -> d r c t", r=4, t=128)[:, :, mm, :])
            S.dma(P("vw%d" % p), w=[vwn], out=VWm[p][:, s0:s0 + 4, :],
                  in_=io["vw_g"].rearrange("(r c t) w -> t r c w", r=4, t=128)[:, :, mm, :])
        qs = slice(m * 128, (m + 1) * 128)
        for g in range(2):
            qrhs = QNZ[:, g, :, qs]
            for c in range(4):
                unit(KCMP[:, c * 128:(c + 1) * 128], ["KCMP"], qrhs, (cm[p][:, c, :], cmn), VCA[:, c, g, :], ["VCA"],
                     lambda h: 2 + h // 2, lambda h: ((h % 2) * 208, (h % 2) * 208 + 208), c == 0, c == 3)
            S.act("activation", r=[P("ps2")], w=[P("ocs")], out=ocs[:, 0:2, :].rearrange("q h w -> q (h w)"), in_=PS[2][:, 0:416], func=AF.Copy)
            S.act("activation", r=[P("ps3")], w=[P("ocs")], out=ocs[:, 2:4, :].rearrange("q h w -> q (h w)"), in_=PS[3][:, 0:416], func=AF.Copy)
            S.dve("tensor_scalar", r=[P("ocs")], w=[P("rd")], out=rd[:, 0, :], in0=ocs[:, :, 64], scalar1=1e-30, scalar2=None, op0=ALU.max)
            S.dve("reciprocal", r=[P("rd")], w=[P("rd")], out=rd[:, 0, :], in_=rd[:, 0, :])
            S.dve("tensor_scalar", r=[P("ocs"), P("rd")], w=[P("imp")], out=imp[:], in0=ocs[:, 0, 80:208], scalar1=rd[:, 0, 0:1],
                  scalar2=None, op0=ALU.mult)
            for h in range(1, 4):
                S.dve("scalar_tensor_tensor", r=[P("ocs"), P("rd"), P("imp")], w=[P("imp")], out=imp[:], in0=ocs[:, h, 80:208],
                      scalar=rd[:, 0, h:h + 1], in1=imp[:], op0=ALU.mult, op1=ALU.add)
            S.dve("tensor_tensor", r=[P("imp"), fcn], w=[P("imp")], out=imp[:], in0=imp[:], in1=fc[p][:], op=ALU.max)
            S.dve("max", r=[P("imp")], w=[P("mx")], out=mx[:], in_=imp[:])
            S.dve("match_replace", r=[P("mx"), P("imp")], w=[P("wk")], out=wk[:], in_to_replace=mx[:], in_values=imp[:], imm_value=-1.0)
            S.dve("max", r=[P("wk")], w=[P("mx")], out=mx[:], in_=wk[:])
            S.dve("tensor_scalar", r=[P("imp"), P("mx")], w=[P("wk")], out=wk[:], in0=imp[:], scalar1=mx[:, 7:8], scalar2=None, op0=ALU.is_ge)
            S.dve("tensor_scalar", r=[P("wk")], w=[P("MBq")], out=MBq[:], in0=wk[:], scalar1=-1.0, scalar2=30000.0, op0=ALU.add, op1=ALU.mult)
            S.pe("transpose", r=[P("MBq"), P("idb")], w=[P("pstb")], out=PSTb[:, 0:128], in_=MBq[:], identity=idb[:])
            S.act("activation", r=[P("pstb")], w=[P("MBT")], out=MBT[:], in_=bc(PSTb[:, 0:128]), func=AF.Copy)
            J = 4 * m + 4
            for j in range(J):
                unit(KS[:, gcol(j):gcol(j) + 128], [P("KS")], qrhs, (dm[:, j - 4 * m, :], P("dm")) if j >= 4 * m else None,
                     VS[:, cidx(j), g * 80:g * 80 + 65], [P("VS")], lambda h: 4, lambda h: (h * 65, h * 65 + 65), j == 0, j == J - 1,
                     bias_mm=EE[:, gcol(j):gcol(j) + 128])
            cl = list(range(8)) if m > 0 else list(range(4, 8))
            for c in cl:
                unit(KWm[p][:, c, :], [kwn], qrhs, (wm[:, c, :], P("wm")), VWm[p][:, c, g * 80:g * 80 + 65], [vwn],
                     lambda h: 5, lambda h: (h * 65, h * 65 + 65), c == cl[0], c == cl[-1])
            S.act("activation", r=[P("ps4")], w=[P("oss")], out=oss[:].rearrange("q h w -> q (h w)"), in_=PS[4][:, 0:260], func=AF.Copy)
            S.act("activation", r=[P("ps5")], w=[P("ows")], out=ows[:].rearrange("q h w -> q (h w)"), in_=PS[5][:, 0:260], func=AF.Copy)
            S.dve("reciprocal", r=[P("oss")], w=[P("rd")], out=rd[:, 1, :], in_=oss[:, :, 64])
            S.dve("reciprocal", r=[P("ows")], w=[P("rd")], out=rd[:, 2, :], in_=ows[:, :, 64])
            S.dve("tensor_tensor", r=[P("rd"), gln], w=[P("fac")], out=fac[:], in0=rd[:],
                  in1=glt[p][:, g * 12:g * 12 + 12].rearrange("q (h b) -> q b h", b=3), op=ALU.mult)
            srcs = ((ocs, P("ocs")), (oss, P("oss")), (ows, P("ows")))
            for br in range(3):
                o_t, o_n = srcs[br]
                dst, dstn = (acc, P("acc")) if br == 0 else (tmp, P("tmp"))
                S.dve("tensor_tensor", r=[o_n, P("fac")], w=[dstn], out=dst[:], in0=o_t[:, :, 0:64],
                      in1=fac[:, br, :].unsqueeze(2).to_broadcast([128, 4, 64]), op=ALU.mult)
                if br > 0:
                    S.dve("tensor_tensor", r=[P("acc"), P("tmp")], w=[P("acc")], out=acc[:], in0=acc[:], in1=tmp[:], op=ALU.add)
            S.dve("tensor_tensor", r=[P("acc"), gtn], w=["MIXB"], out=MIXB[:, m, 512 + g * 256:512 + g * 256 + 256],
                  in0=acc[:].rearrange("q h d -> q (h d)"), in1=gtt[p][:, g * 256:g * 256 + 256], op=ALU.mult)


def nsa_rank_consts(r):
    k = np.arange(128)[:, None]
    q = np.arange(128)[None, :]
    wmask = np.zeros((128, 8, 128), np.float32)
    for c in range(8):
        d = (4 + r) - c
        if d == 4:
            wmask[:, c, :] = (k > q)
        elif 1 <= d <= 3:
            wmask[:, c, :] = 1.0
        elif d == 0:
            wmask[:, c, :] = (k <= q)
    cmask = np.zeros((16, 128, 4, 128), np.float32)
    force = np.zeros((16, 128, 128), np.float32)
    nl = np.arange(128)[:, None]
    b = np.arange(128)[None, :]
    for m in range(16):
        tq = (4 * m + r) * 128 + np.arange(128)
        for c in range(4):
            n = 128 * c + nl
            cmask[m, :, c, :] = ((16 * n + 31) <= tq[None, :]) & (n < 511)
        cur = (tq // 64)[:, None]
        force[m] = 1e4 * (b == 0) + 2e4 * (b == cur) + 3e4 * (b == cur - 1)
    return dict(wmask=wmask.astype(NPBF), cmask=cmask.astype(NPBF), force=force.astype(np.float32))


def ee_const():
    ee = np.zeros((128, T), np.float32)
    for j in range(64):
        for half in range(2):
            ee[2 * j + half, gcol(j) + 64 * half: gcol(j) + 64 * half + 64] = 1.0
    return ee.astype(NPBF)


def conv_scope(S, nc, es, io, MIXB, pfx="v"):
    P = lambda n: pfx + n
    sb = lambda n, s, d: es.enter_context(nc.sbuf_tensor(P(n), s, d))
    PS = [es.enter_context(nc.psum_tensor(P("ps%d" % i), [128, 512], F32)) for i in range(4)]
    YB = sb("YB", [128, 2, NT], BF16)
    TLB = sb("TLB", [128, 2, 4, 16, 30], BF16)
    YE = sb("YE", [128, 2, 16, 158], F32)
    accA = sb("accA", [128, 2, 16, 128], F32)
    accB = sb("accB", [128, 2, 16, 128], F32)
    tmpP = sb("tmpP", [128, 16, 128], F32)
    SQ = sb("SQ", [128, 2, NT], F32)
    MEAN = sb("MEAN", [128, NT], F32)
    MSQ = sb("MSQ", [128, NT], F32)
    CW = sb("CW", [128, 2, 31], F32)
    CBs = sb("CBs", [128, 2], F32)
    LG = sb("LG", [128, 2], F32)
    LB = sb("LB", [128, 2], F32)
    ohp = sb("ohp", [128, 4], F32)
    onesd = sb("onesd", [128, 128], F32)
    WPf = sb("WPf", [128, 2, 256], F32)
    WPW = sb("WPW", [128, 2, 256], BF16)
    ACTT = sb("ACTT", [128, 2, NT], BF16)
    ob = [sb("ob%d" % i, [128, 256], F32) for i in range(2)]
    gtt = [sb("gtt%d" % i, [128, 256], BF16) for i in range(2)]

    ld = P("ld")
    for cc in range(2):
        S.dma(ld, batch=True, w=[P("YB")], out=YB[:, cc, :], in_=io["yT"][cc])
        S.dma(ld, batch=True, w=[P("WPf")], out=WPf[:, cc, :], in_=io["conv_pw"][cc * 128:(cc + 1) * 128, :])
        for r in range(4):
            S.dma(ld, batch=True, w=[P("TLB")], out=TLB[:, cc, r, :, :], in_=io["tails_g"][r, cc])
    S.dma(ld, batch=True, w=[P("CW")], out=CW[:], in_=io["conv_w"])
    S.dma(ld, batch=True, w=[P("CBs")], out=CBs[:], in_=io["conv_b"])
    S.dma(ld, batch=True, w=[P("LG")], out=LG[:], in_=io["ln_g"])
    S.dma(ld, batch=True, w=[P("LB")], out=LB[:], in_=io["ln_b"])
    S.dma(ld, batch=True, w=[P("ohp")], out=ohp[:], in_=io["ohprev"])
    S.dve("memset", w=[P("onesd")], ap=onesd[:], constant=1.0 / 256.0)
    S.dve("tensor_copy", r=[P("WPf")], w=[P("WPW")], out=WPW[:], in_=WPf[:])
    for cc in range(2):
        S.act("activation", r=[P("YB")], w=[P("YE")], out=YE[:, cc, :, 30:158], in_=YB[:, cc, :].rearrange("p (m t) -> p m t", t=128), func=AF.Copy)
        S.dve("tensor_scalar", r=[P("TLB"), P("ohp")], w=[P("YE")], out=YE[:, cc, :, 0:30], in0=TLB[:, cc, 0, :, :], scalar1=ohp[:, 0:1],
              scalar2=None, op0=ALU.mult)
        for r in (1, 2):
            S.dve("scalar_tensor_tensor", r=[P("TLB"), P("ohp"), P("YE")], w=[P("YE")], out=YE[:, cc, :, 0:30], in0=TLB[:, cc, r, :, :],
                  scalar=ohp[:, r:r + 1], in1=YE[:, cc, :, 0:30], op0=ALU.mult, op1=ALU.add)
        S.dve("scalar_tensor_tensor", r=[P("TLB"), P("ohp"), P("YE")], w=[P("YE")], out=YE[:, cc, 1:16, 0:30], in0=TLB[:, cc, 3, 0:15, :],
              scalar=ohp[:, 3:4], in1=YE[:, cc, 1:16, 0:30], op0=ALU.mult, op1=ALU.add)
    for cc in range(2):
        an, bn = P("accA%d" % cc), P("accB%d" % cc)
        for tp in range(0, 10):
            src = YE[:, cc, :, tp:tp + 128]
            if tp == 0:
                S.pool("tensor_scalar", r=[P("YE"), P("CW"), P("CBs")], w=[an], out=accA[:, cc], in0=src, scalar1=CW[:, cc, 0:1],
                       scalar2=CBs[:, cc:cc + 1], op0=ALU.mult, op1=ALU.add)
            else:
                S.pool("tensor_scalar", r=[P("YE"), P("CW")], w=[P("tmpP")], out=tmpP[:], in0=src, scalar1=CW[:, cc, tp:tp + 1],
                       scalar2=None, op0=ALU.mult)
                S.pool("tensor_tensor", r=[an, P("tmpP")], w=[an], out=accA[:, cc], in0=accA[:, cc], in1=tmpP[:], op=ALU.add)
        for tp in range(10, 31):
            src = YE[:, cc, :, tp:tp + 128]
            if tp == 10:
                S.dve("tensor_scalar", r=[P("YE"), P("CW")], w=[bn], out=accB[:, cc], in0=src, scalar1=CW[:, cc, tp:tp + 1],
                      scalar2=None, op0=ALU.mult)
            else:
                S.dve("scalar_tensor_tensor", r=[P("YE"), P("CW"), bn], w=[bn], out=accB[:, cc], in0=src, scalar=CW[:, cc, tp:tp + 1],
                      in1=accB[:, cc], op0=ALU.mult, op1=ALU.add)
        S.dve("tensor_tensor", r=[an, bn], w=[an], out=accA[:, cc], in0=accA[:, cc], in1=accB[:, cc], op=ALU.add)
        S.act("activation", r=[an], w=[P("SQ")], out=SQ[:, cc, :], in_=accA[:, cc].rearrange("p m t -> p (m t)"), func=AF.Square)
    for pc in range(4):
        cs = slice(pc * 512, (pc + 1) * 512)
        for cc in range(2):
            S.pe("matmul", r=[P("onesd"), P("accA%d" % cc)], w=[P("ps0")], out=PS[0][:], lhsT=onesd[:],
                 rhs=accA[:, cc].rearrange("p m t -> p (m t)")[:, cs], start=(cc == 0), stop=(cc == 1))
        for cc in range(2):
            S.pe("matmul", r=[P("onesd"), P("SQ")], w=[P("ps1")], out=PS[1][:], lhsT=onesd[:], rhs=SQ[:, cc, cs], start=(cc == 0), stop=(cc == 1))
        S.act("activation", r=[P("ps0")], w=[P("MEAN")], out=MEAN[:, cs], in_=PS[0][:], func=AF.Copy)
        S.act("activation", r=[P("ps1")], w=[P("MSQ")], out=MSQ[:, cs], in_=PS[1][:], func=AF.Copy)
    S.dve("tensor_tensor", r=[P("MEAN")], w=[P("SQ")], out=SQ[:, 0, :], in0=MEAN[:], in1=MEAN[:], op=ALU.mult)
    S.dve("tensor_tensor", r=[P("MSQ"), P("SQ")], w=[P("MSQ")], out=MSQ[:], in0=MSQ[:], in1=SQ[:, 0, :], op=ALU.subtract)
    S.dve("tensor_scalar", r=[P("MSQ")], w=[P("MSQ")], out=MSQ[:], in0=MSQ[:], scalar1=1e-6, scalar2=None, op0=ALU.add)
    S.act("activation", r=[P("MSQ")], w=[P("MSQ")], out=MSQ[:], in_=MSQ[:], func=AF.Sqrt)
    S.dve("reciprocal", r=[P("MSQ")], w=[P("MSQ")], out=MSQ[:], in_=MSQ[:])
    for cc in range(2):
        an = P("accA%d" % cc)
        a2 = accA[:, cc].rearrange("p m t -> p (m t)")
        S.dve("tensor_tensor", r=[an, P("MEAN")], w=[an], out=a2, in0=a2, in1=MEAN[:], op=ALU.subtract)
        S.dve("tensor_tensor", r=[an, P("MSQ")], w=[an], out=a2, in0=a2, in1=MSQ[:], op=ALU.mult)
        S.act("activation", r=[an, P("LG"), P("LB")], w=[P("ACTT")], out=ACTT[:, cc, :], in_=a2, func=AF.Silu, scale=LG[:, cc:cc + 1], bias=LB[:, cc:cc + 1])
    for m in range(16):
        p = m % 2
        k = 2 + p
        S.dma(P("gt%d" % p), w=[P("gtt%d" % p)], out=gtt[p][:], in_=io["gt"][m * 128:(m + 1) * 128, 256:512])
        for cc in range(2):
            S.pe("matmul", r=[P("ACTT"), P("WPW")], w=[P("ps%d" % k)], out=PS[k][:, 0:256], lhsT=ACTT[:, cc, m * 128:(m + 1) * 128], rhs=WPW[:, cc, :],
                 start=(cc == 0), stop=(cc == 1))
        S.act("activation", r=[P("ps%d" % k)], w=[P("ob%d" % p)], out=ob[p][:], in_=PS[k][:, 0:256], func=AF.Copy)
        S.dve("tensor_tensor", r=[P("ob%d" % p), P("gtt%d" % p)], w=["MIXB"], out=MIXB[:, m, 256:512], in0=ob[p][:], in1=gtt[p][:], op=ALU.mult)


def conv_consts(inputs, layer, r):
    cw = inputs["conv_w"][layer]
    col = lambda v: np.ascontiguousarray(v.reshape(2, 128).T, dtype=np.float32)
    ohp = np.zeros((128, 4), np.float32)
    ohp[:, (r - 1) % 4] = 1.0
    return dict(conv_w=np.ascontiguousarray(cw.T.reshape(2, 128, 31).transpose(1, 0, 2), dtype=np.float32),
                conv_b=col(inputs["conv_b"][layer]), ln_g=col(inputs["conv_ln_g"][layer]), ln_b=col(inputs["conv_ln_b"][layer]),
                conv_pw=np.ascontiguousarray(inputs["conv_pw"][layer]), ohprev=ohp)
```

```python
import numpy as np
import ml_dtypes
from contextlib import ExitStack
import concourse.bass as bass
import concourse.mybir as mybir
from concourse.bass_utils import run_bass_kernel_spmd

F32 = mybir.dt.float32
BF16 = mybir.dt.bfloat16
AF = mybir.ActivationFunctionType
ALU = mybir.AluOpType
AX = mybir.AxisListType
NPBF = ml_dtypes.bfloat16

ENGINES = ("pe", "act", "dve", "pool", "sp")


class Buf:
    __slots__ = ("name", "last_w", "readers")

    def __init__(self, name):
        self.name = name
        self.last_w = None
        self.readers = []


class Op:
    __slots__ = ("eng", "fn", "reads", "writes", "dma_key", "deps", "need_inc", "tok", "idx")

    def __init__(self, eng, fn, reads, writes, dma_key):
        self.eng = eng
        self.fn = fn
        self.reads = reads
        self.writes = writes
        self.dma_key = dma_key
        self.deps = []
        self.need_inc = False
        self.tok = None


class Sched:
    def __init__(self, nc, es):
        self.nc = nc
        self.es = es
        self.ops = []
        self.bufs = {}
        self.batch = set()
        self.cnt = {}
        self.sems = {}
        self.waited = {e: {} for e in ENGINES}

    def buf(self, name):
        b = self.bufs.get(name)
        if b is None:
            b = Buf(name)
            self.bufs[name] = b
        return b

    def _norm(self, lst):
        return [self.buf(b) if isinstance(b, str) else b for b in (lst or ())]

    def add(self, eng, meth, kw, reads=(), writes=(), dma_key=None):
        op = Op(eng, (meth, kw), self._norm(reads), self._norm(writes), dma_key)
        op.idx = len(self.ops)
        self.ops.append(op)
        return op

    def pe(self, meth, r=(), w=(), **kw):
        return self.add("pe", meth, kw, r, w)

    def act(self, meth, r=(), w=(), **kw):
        return self.add("act", meth, kw, r, w)

    def dve(self, meth, r=(), w=(), **kw):
        return self.add("dve", meth, kw, r, w)

    def pool(self, meth, r=(), w=(), **kw):
        return self.add("pool", meth, kw, r, w)

    def dma(self, key, r=(), w=(), eng="sp", batch=False, **kw):
        if batch:
            self.batch.add(key)
        return self.add(eng, "dma_start", kw, r, w, dma_key=key)

    def analyze(self):
        for op in self.ops:
            deps = set()
            for b in op.reads:
                if b.last_w is not None:
                    deps.add(b.last_w)
            for b in op.writes:
                if b.last_w is not None:
                    deps.add(b.last_w)
                for r in b.readers:
                    deps.add(r)
            deps.discard(op.idx)
            keep = []
            for d in deps:
                dop = self.ops[d]
                if dop.dma_key is not None and dop.dma_key == op.dma_key and dop.dma_key in self.batch:
                    continue
                if dop.dma_key is None and dop.eng == op.eng:
                    if op.eng == "pe" or op.dma_key is not None:
                        continue
                keep.append(d)
            op.deps = sorted(keep)
            for d in op.deps:
                self.ops[d].need_inc = True
            for b in op.reads:
                b.readers.append(op.idx)
            for b in op.writes:
                b.last_w = op.idx
                b.readers = []

    def flush(self):
        nc = self.nc
        self.analyze()
        per = {e: [op for op in self.ops if op.eng == e] for e in ENGINES}
        for e in ENGINES:
            comp = [op for op in per[e] if op.dma_key is None]
            if comp:
                comp[-1].need_inc = True
        cnt = self.cnt
        for op in self.ops:
            if op.dma_key is not None:
                k = ("dma", op.dma_key)
                cnt[k] = cnt.get(k, 0) + 16
                op.tok = (k, cnt[k])
            elif op.need_inc:
                k = ("eng", op.eng)
                cnt[k] = cnt.get(k, 0) + 1
                op.tok = (k, cnt[k])
        for op in self.ops:
            if op.dma_key is not None and op.dma_key in self.batch:
                op.tok = (op.tok[0], cnt[op.tok[0]])
        for k in sorted(cnt.keys()):
            if k not in self.sems:
                self.sems[k] = self.es.enter_context(nc.semaphore("s_%s_%s" % k))
        sems = self.sems
        ops = self.ops
        totals = dict(cnt)

        def run(eng_name, h):
            waited = self.waited[eng_name]
            for op in per[eng_name]:
                for d in op.deps:
                    k, v = ops[d].tok
                    if waited.get(k, 0) >= v:
                        continue
                    h.wait_ge(sems[k], v)
                    waited[k] = v
                ins = getattr(h, op.fn[0])(**op.fn[1])
                if op.tok is not None:
                    ins.then_inc(sems[op.tok[0]], 16 if op.dma_key is not None else 1)
            for k in sorted(totals.keys()):
                if waited.get(k, 0) < totals[k]:
                    h.wait_ge(sems[k], totals[k])
                    waited[k] = totals[k]

        with nc.Block() as block:
            block.sync(lambda h: run("sp", h))
            block.tensor(lambda h: run("pe", h))
            block.scalar(lambda h: run("act", h))
            block.vector(lambda h: run("dve", h))
            block.gpsimd(lambda h: run("pool", h))
        self.ops = []
        self.bufs = {}


D = 1024
DIN = 3612
NT = 2048
T = 8192
OFF = dict(fq=0, fk=256, fv=512, ff=768, fg=772, glu=1028, cg=1540, nq=1796, nkc=2308, nvc=2436,
           nks=2564, nvs=2692, nkw=2820, nvw=2948, ngl=3076, ng=3100)

A_OUTS = dict(qf=([2, 128, NT], BF16), kf=([2, 128, NT], BF16), yT=([2, 128, NT], BF16), qn=([4, 128, NT], BF16),
              kc=([128, NT], BF16), vc=([128, NT], BF16), ks=([128, NT], BF16), kw=([128, NT], BF16),
              vf=([NT, 320], BF16), vs=([NT, 160], BF16), vw=([NT, 160], BF16), lf=([NT, 4], F32),
              gt=([NT, D], BF16), gl=([NT, 24], F32))


def phase_a(S, nc, es, io, pfx="a"):
    P = lambda n: pfx + n
    sb = lambda n, s, d: es.enter_context(nc.sbuf_tensor(P(n), s, d))
    PS0b = es.enter_context(nc.psum_tensor(P("ps0"), [128, 1024], BF16))
    PS = [None] + [es.enter_context(nc.psum_tensor(P("ps%d" % i), [128, 512], F32)) for i in range(1, 8)]
    Wb = sb("Wb", [128, 8, DIN], BF16)
    wst = [sb("wst%d" % i, [128, DIN], F32) for i in range(2)]
    gcol = sb("gcol", [128, 8], F32)
    fb = sb("fb", [128, 4], F32)
    cosT = sb("cosT", [128, NT], F32)
    sinT = sb("sinT", [128, NT], F32)
    rmat_f = sb("rmat_f", [128, 128], F32)
    rmat = sb("rmat", [128, 128], BF16)
    idf = sb("idf", [128, 128], F32)
    idb = sb("idb", [128, 128], BF16)
    xt = [sb("xt%d" % i, [128, D], F32) for i in range(2)]
    sq = sb("sq", [128, D], BF16)
    ss = [sb("ss%d" % i, [128, 1], F32) for i in range(2)]
    hb = [sb("hb%d" % i, [128, D], BF16) for i in range(2)]
    hT = [sb("hT%d" % i, [128, 8, 512], BF16) for i in range(2)]
    fo = [sb("fo%d" % i, [128, 512], BF16) for i in range(4)]
    zc = [sb("zc%d" % i, [128, 512], BF16) for i in range(2)]
    t1 = [sb("t1%d" % i, [128, 512], F32) for i in range(2)]
    t2 = [sb("t2%d" % i, [128, 512], F32) for i in range(2)]
    sg = [sb("sg%d" % i, [128, 512], F32) for i in range(2)]
    vfo = [sb("vfo%d" % i, [128, 4, 80], BF16) for i in range(2)]
    vso = [sb("vso%d" % i, [128, 2, 80], BF16) for i in range(2)]
    vwo = [sb("vwo%d" % i, [128, 2, 80], BF16) for i in range(2)]
    lfo = [sb("lfo%d" % i, [128, 4], F32) for i in range(2)]
    lft = [sb("lft%d" % i, [128, 4], F32) for i in range(2)]
    glo = [sb("glo%d" % i, [128, 24], F32) for i in range(2)]
    gto = [sb("gto%d" % i, [128, D], BF16) for i in range(2)]
    out_keys = set()

    def store(src_name, dst, src):
        k = P("k_" + src_name)
        out_keys.add(k)
        S.dma(k, r=[P(src_name)], out=dst, in_=src)

    S.dma(P("c"), batch=True, w=[P("gcol")], out=gcol[:], in_=io["norm_g"])
    S.dma(P("c"), batch=True, w=[P("fb")], out=fb[:], in_=io["fox_b"].partition_broadcast(128))
    S.dma(P("c"), batch=True, w=[P("cos")], out=cosT[:], in_=io["cos"])
    S.dma(P("c"), batch=True, w=[P("sin")], out=sinT[:], in_=io["sin"])
    S.dma(P("c"), batch=True, w=[P("rmf")], out=rmat_f[:], in_=io["rmat"])
    S.dma(P("c"), batch=True, w=[P("idf")], out=idf[:], in_=io["ident"])
    S.dve("tensor_copy", r=[P("rmf")], w=[P("rmat")], out=rmat[:], in_=rmat_f[:])
    S.dve("tensor_copy", r=[P("idf")], w=[P("idb")], out=idb[:], in_=idf[:])
    for i in range(2):
        S.dve("memset", w=[P("vfo%d" % i)], ap=vfo[i][:], constant=1.0)
        S.dve("memset", w=[P("vso%d" % i)], ap=vso[i][:], constant=1.0)
        S.dve("memset", w=[P("vwo%d" % i)], ap=vwo[i][:], constant=1.0)
    for c in range(8):
        w = wst[c % 2]
        wn = P("wst%d" % (c % 2))
        S.dma(P("w%d" % (c % 2)), w=[wn], out=w[:], in_=io["w_in"][c * 128:(c + 1) * 128, :])
        eng = S.dve if c % 2 == 0 else S.pool
        eng("tensor_scalar", r=[wn, P("gcol")], w=[P("Wb")], out=Wb[:, c, :], in0=w[:], scalar1=gcol[:, c:c + 1],
            scalar2=None, op0=ALU.mult)

    psrr = [1]

    def next_ps():
        k = psrr[0]
        psrr[0] = 1 + (psrr[0] % 7)
        return k

    foi = [0]
    ri = [0]
    for s in range(4):
        hTs = hT[s % 2]
        hTn = P("hT%d" % (s % 2))
        for q in range(4):
            tn = s * 4 + q
            p = tn % 2
            x_t = xt[p]
            xn = P("xt%d" % p)
            ssn = P("ss%d" % p)
            S.dma(P("x%d" % p), w=[xn], out=x_t[:], in_=io["x"][tn * 128:(tn + 1) * 128, :])
            S.act("activation", r=[xn], w=[P("sq"), ssn], out=sq[:], in_=x_t[:], func=AF.Square, accum_out=ss[p][:, 0:1])
            S.dve("tensor_scalar", r=[ssn], w=[ssn], out=ss[p][:], in0=ss[p][:], scalar1=1.0 / D, scalar2=1e-6,
                  op0=ALU.mult, op1=ALU.add)
            S.act("activation", r=[ssn], w=[ssn], out=ss[p][:], in_=ss[p][:], func=AF.Sqrt)
            S.dve("reciprocal", r=[ssn], w=[ssn], out=ss[p][:], in_=ss[p][:])
            S.dve("tensor_scalar", r=[xn, ssn], w=[P("hb%d" % p)], out=hb[p][:], in0=x_t[:], scalar1=ss[p][:, 0:1],
                  scalar2=None, op0=ALU.mult)
            for half in range(2):
                for cc in range(4):
                    c = half * 4 + cc
                    S.pe("transpose", r=[P("hb%d" % p), P("idb")], w=[P("ps0")], out=PS0b[:, cc * 128:(cc + 1) * 128],
                         in_=hb[p][:, c * 128:(c + 1) * 128], identity=idb[:])
                S.act("activation", r=[P("ps0")], w=[hTn], out=hTs[:, half * 4:(half + 1) * 4, q * 128:(q + 1) * 128],
                      in_=PS0b[:, 0:512].rearrange("p (c t) -> p c t", c=4), func=AF.Copy)
        cols = slice(s * 512, (s + 1) * 512)

        def fm(coff, k):
            for c in range(8):
                S.pe("matmul", r=[P("Wb"), hTn], w=[P("ps%d" % k)], out=PS[k][:, :], lhsT=Wb[:, c, coff:coff + 128],
                     rhs=hTs[:, c, :], start=(c == 0), stop=(c == 7))

        def plain_out(coff, dst):
            k = next_ps()
            fm(coff, k)
            i = foi[0] % 4
            foi[0] += 1
            S.act("activation", r=[P("ps%d" % k)], w=[P("fo%d" % i)], out=fo[i][:], in_=PS[k][:], func=AF.Copy)
            store("fo%d" % i, dst, fo[i][:])

        def rope_out(coff, dst):
            k = next_ps()
            fm(coff, k)
            k2 = next_ps()
            j = ri[0] % 2
            ri[0] += 1
            i = foi[0] % 4
            foi[0] += 1
            S.act("activation", r=[P("ps%d" % k)], w=[P("zc%d" % j)], out=zc[j][:], in_=PS[k][:], func=AF.Copy)
            S.pe("matmul", r=[P("rmat"), P("zc%d" % j)], w=[P("ps%d" % k2)], out=PS[k2][:], lhsT=rmat[:], rhs=zc[j][:],
                 start=True, stop=True)
            S.act("activation", r=[P("ps%d" % k)], w=[P("t1%d" % j)], out=t1[j][:], in_=PS[k][:], func=AF.Copy)
            S.act("activation", r=[P("ps%d" % k2)], w=[P("t2%d" % j)], out=t2[j][:], in_=PS[k2][:], func=AF.Copy)
            S.dve("tensor_tensor", r=[P("t1%d" % j), P("cos")], w=[P("t1%d" % j)], out=t1[j][:], in0=t1[j][:],
                  in1=cosT[:, cols], op=ALU.mult)
            S.pool("tensor_tensor", r=[P("t2%d" % j), P("sin")], w=[P("t2%d" % j)], out=t2[j][:], in0=t2[j][:],
                   in1=sinT[:, cols], op=ALU.mult)
            S.dve("tensor_tensor", r=[P("t1%d" % j), P("t2%d" % j)], w=[P("fo%d" % i)], out=fo[i][:], in0=t1[j][:],
                   in1=t2[j][:], op=ALU.add)
            store("fo%d" % i, dst, fo[i][:])

        for h2 in range(2):
            plain_out(OFF["fq"] + 128 * h2, io["qf"][h2, :, cols])
            plain_out(OFF["fk"] + 128 * h2, io["kf"][h2, :, cols])
        for h2 in range(2):
            ka = next_ps()
            fm(OFF["glu"] + 128 * h2, ka)
            kb = next_ps()
            fm(OFF["glu"] + 256 + 128 * h2, kb)
            i = foi[0] % 4
            foi[0] += 1
            S.act("activation", r=[P("ps%d" % kb)], w=[P("sg%d" % h2)], out=sg[h2][:], in_=PS[kb][:], func=AF.Sigmoid)
            S.act("activation", r=[P("ps%d" % ka)], w=[P("t1%d" % h2)], out=t1[h2][:], in_=PS[ka][:], func=AF.Copy)
            S.dve("tensor_tensor", r=[P("t1%d" % h2), P("sg%d" % h2)], w=[P("fo%d" % i)], out=fo[i][:], in0=t1[h2][:],
                  in1=sg[h2][:], op=ALU.mult)
            store("fo%d" % i, io["yT"][h2, :, cols], fo[i][:])
        for c4 in range(4):
            rope_out(OFF["nq"] + 128 * c4, io["qn"][c4, :, cols])
        plain_out(OFF["nkc"], io["kc"][:, cols])
        plain_out(OFF["nvc"], io["vc"][:, cols])
        rope_out(OFF["nks"], io["ks"][:, cols])
        rope_out(OFF["nkw"], io["kw"][:, cols])

        for q in range(4):
            tn = s * 4 + q
            p = tn % 2
            rows = slice(tn * 128, (tn + 1) * 128)

            def tm(c0, c1, k):
                for c in range(8):
                    S.pe("matmul", r=[P("Wb"), hTn], w=[P("ps%d" % k)], out=PS[k][:, 0:c1 - c0],
                         lhsT=hTs[:, c, q * 128:(q + 1) * 128], rhs=Wb[:, c, c0:c1], start=(c == 0), stop=(c == 7))

            k = next_ps()
            tm(512, 772, k)
            S.act("activation", r=[P("ps%d" % k)], w=[P("vfo%d" % p)], out=vfo[p][:, :, 0:64],
                  in_=PS[k][:, 0:256].rearrange("p (h d) -> p h d", h=4), func=AF.Copy)
            store("vfo%d" % p, io["vf"][rows, :], vfo[p][:].rearrange("p h d -> p (h d)"))
            lt = P("lft%d" % p)
            S.act("activation", r=[P("ps%d" % k)], w=[lt], out=lft[p][:], in_=PS[k][:, 256:260], func=AF.Copy)
            S.dve("tensor_tensor", r=[lt, P("fb")], w=[lt], out=lft[p][:], in0=lft[p][:], in1=fb[:], op=ALU.add)
            S.act("activation", r=[lt], w=[lt], out=lft[p][:], in_=lft[p][:], func=AF.Exp, scale=-1.0)
            S.act("activation", r=[lt], w=[lt], out=lft[p][:], in_=lft[p][:], func=AF.Ln, bias=1.0, scale=1.0)
            S.dve("tensor_scalar", r=[lt], w=[P("lfo%d" % p)], out=lfo[p][:], in0=lft[p][:], scalar1=-1.0, scalar2=None, op0=ALU.mult)
            store("lfo%d" % p, io["lf"][rows, :], lfo[p][:])
            for (c0, g0) in ((772, 0), (1540, 256)):
                k = next_ps()
                tm(c0, c0 + 256, k)
                S.act("activation", r=[P("ps%d" % k)], w=[P("gto%d" % p)], out=gto[p][:, g0:g0 + 256], in_=PS[k][:, 0:256], func=AF.Silu)
            k = next_ps()
            tm(3100, 3612, k)
            S.act("activation", r=[P("ps%d" % k)], w=[P("gto%d" % p)], out=gto[p][:, 512:1024], in_=PS[k][:, 0:512], func=AF.Silu)
            store("gto%d" % p, io["gt"][rows, :], gto[p][:])
            k = next_ps()
            tm(2692, 2820, k)
            S.act("activation", r=[P("ps%d" % k)], w=[P("vso%d" % p)], out=vso[p][:, :, 0:64],
                  in_=PS[k][:, 0:128].rearrange("p (h d) -> p h d", h=2), func=AF.Copy)
            store("vso%d" % p, io["vs"][rows, :], vso[p][:].rearrange("p h d -> p (h d)"))
            k = next_ps()
            tm(2948, 3100, k)
            S.act("activation", r=[P("ps%d" % k)], w=[P("vwo%d" % p)], out=vwo[p][:, :, 0:64],
                  in_=PS[k][:, 0:128].rearrange("p (h d) -> p h d", h=2), func=AF.Copy)
            store("vwo%d" % p, io["vw"][rows, :], vwo[p][:].rearrange("p h d -> p (h d)"))
            S.act("activation", r=[P("ps%d" % k)], w=[P("glo%d" % p)], out=glo[p][:], in_=PS[k][:, 128:152], func=AF.Sigmoid)
            store("glo%d" % p, io["gl"][rows, :], glo[p][:])
    return sorted(out_keys)


def own_positions(r):
    m = np.arange(16)[:, None]
    tl = np.arange(128)[None, :]
    return ((4 * m + r) * 128 + tl).reshape(-1)


def rope_tables(pos):
    inv = 500000.0 ** (-np.arange(0, 16, 2, dtype=np.float32) / 16.0)
    ang = pos.astype(np.float32)[None, :] * np.tile(inv, 2)[:, None].astype(np.float32)
    cos = np.ones((64, pos.size), np.float32)
    sin = np.zeros((64, pos.size), np.float32)
    cos[:16] = np.cos(ang)
    sin[:16] = np.sin(ang)
    return np.tile(cos, (2, 1)), np.tile(sin, (2, 1))


def rope_rmat():
    R = np.zeros((128, 128), np.float32)
    for hh in range(2):
        for d in range(8):
            R[hh * 64 + d + 8, hh * 64 + d] = -1.0
            R[hh * 64 + d, hh * 64 + d + 8] = 1.0
    return R


def build_a():
    nc = bass.Bass("TRN2", target_bir_lowering=False)
    io = {}
    io["x"] = nc.dram_tensor("x", [NT, D], F32, kind="ExternalInput").ap()
    io["w_in"] = nc.dram_tensor("w_in", [D, DIN], F32, kind="ExternalInput").ap()
    io["norm_g"] = nc.dram_tensor("norm_g", [128, 8], F32, kind="ExternalInput").ap()
    io["fox_b"] = nc.dram_tensor("fox_b", [4], F32, kind="ExternalInput").ap()
    io["cos"] = nc.dram_tensor("cos", [128, NT], F32, kind="ExternalInput").ap()
    io["sin"] = nc.dram_tensor("sin", [128, NT], F32, kind="ExternalInput").ap()
    io["rmat"] = nc.dram_tensor("rmat", [128, 128], F32, kind="ExternalInput").ap()
    io["ident"] = nc.dram_tensor("ident", [128, 128], F32, kind="ExternalInput").ap()
    for n, (shp, dt) in A_OUTS.items():
        io[n] = nc.dram_tensor(n, shp, dt, kind="ExternalOutput").ap()
    with ExitStack() as top:
        S = Sched(nc, top)
        with ExitStack() as es:
            phase_a(S, nc, es, io)
            S.flush()
    return nc


def out_scope(S, nc, es, io, MIXB, last, pfx="o"):
    P = lambda n: pfx + n
    sb = lambda n, s_, d: es.enter_context(nc.sbuf_tensor(P(n), s_, d))
    PS = [es.enter_context(nc.psum_tensor(P("ps%d" % i), [128, 512], F32)) for i in range(4)]
    PSTb = es.enter_context(nc.psum_tensor(P("pstb"), [128, 1024], BF16))
    WO = sb("WO", [128, 8, D], BF16)
    wst = [sb("wst%d" % i, [128, D], F32) for i in range(2)]
    idf = sb("idf", [128, 128], F32)
    idb = sb("idb", [128, 128], BF16)
    MXT = [sb("MXT%d" % i, [128, 8, 128], BF16) for i in range(2)]
    xt = [sb("xt%d" % i, [128, D], F32) for i in range(2)]
    yo = [sb("yo%d" % i, [128, D], F32) for i in range(2)]
    sq = sb("sq", [128, D], F32)
    ss = sb("ss", [128, 1], F32)
    fg = sb("fg", [128, D], F32)
    S.dma(P("c"), batch=True, w=[P("idf")], out=idf[:], in_=io["ident"])
    if last:
        S.dma(P("c"), batch=True, w=[P("fg")], out=fg[:], in_=io["final_g"].partition_broadcast(128))
    S.dve("tensor_copy", r=[P("idf")], w=[P("idb")], out=idb[:], in_=idf[:])
    for c in range(8):
        w, wn = wst[c % 2], P("wst%d" % (c % 2))
        S.dma(P("w%d" % (c % 2)), w=[wn], out=w[:], in_=io["w_out"][c * 128:(c + 1) * 128, :])
        (S.dve if c % 2 == 0 else S.pool)("tensor_copy", r=[wn], w=[P("WO")], out=WO[:, c, :], in_=w[:])
    for m in range(16):
        p = m % 2
        mx, mxn, x_t, xn, y_t, yn = MXT[p], P("MXT%d" % p), xt[p], P("xt%d" % p), yo[p], P("yo%d" % p)
        S.dma(P("x%d" % p), w=[xn], out=x_t[:], in_=io["x"][m * 128:(m + 1) * 128, :])
        for c in range(8):
            S.pe("transpose", r=["MIXB", P("idb")], w=[P("pstb")], out=PSTb[:, c * 128:(c + 1) * 128], in_=MIXB[:, m, c * 128:(c + 1) * 128],
                 identity=idb[:])
        S.act("activation", r=[P("pstb")], w=[mxn], out=mx[:].rearrange("f c q -> f (c q)"), in_=PSTb[:], func=AF.Copy)
        for n2 in range(2):
            k = (2 * m + n2) % 4
            for c in range(8):
                S.pe("matmul", r=[mxn, P("WO")], w=[P("ps%d" % k)], out=PS[k][:], lhsT=mx[:, c, :], rhs=WO[:, c, n2 * 512:(n2 + 1) * 512],
                     start=(c == 0), stop=(c == 7))
            S.act("activation", r=[P("ps%d" % k)], w=[yn], out=y_t[:, n2 * 512:(n2 + 1) * 512], in_=PS[k][:], func=AF.Copy)
        S.dve("tensor_tensor", r=[yn, xn], w=[yn], out=y_t[:], in0=y_t[:], in1=x_t[:], op=ALU.add)
        if last:
            S.act("activation", r=[yn], w=[P("sq"), P("ss")], out=sq[:], in_=y_t[:], func=AF.Square, accum_out=ss[:, 0:1])
            S.dve("tensor_scalar", r=[P("ss")], w=[P("ss")], out=ss[:], in0=ss[:], scalar1=1.0 / D, scalar2=1e-6, op0=ALU.mult, op1=ALU.add)
            S.act("activation", r=[P("ss")], w=[P("ss")], out=ss[:], in_=ss[:], func=AF.Sqrt)
            S.dve("reciprocal", r=[P("ss")], w=[P("ss")], out=ss[:], in_=ss[:])
            S.dve("scalar_tensor_tensor", r=[yn, P("ss"), P("fg")], w=[yn], out=y_t[:], in0=y_t[:], scalar=ss[:, 0:1], in1=fg[:],
                  op0=ALU.mult, op1=ALU.mult)
        S.dma(P("o%d" % p), r=[yn], out=io["xo"][m * 128:(m + 1) * 128, :], in_=y_t[:])


B_INS = dict(kf_g=([2, 128, T], BF16), vf_g=([T, 320], BF16), lf_g=([T, 4], F32), qf=([2, 128, NT], BF16),
             kc_g=([128, T], BF16), vc_g=([128, T], BF16), ks_g=([128, T], BF16), kw_g=([128, T], BF16),
             vs_g=([T, 160], BF16), vw_g=([T, 160], BF16), qn=([4, 128, NT], BF16), gl=([NT, 24], F32), gt=([NT, D], BF16),
             yT=([2, 128, NT], BF16), tails_g=([4, 2, 128, 16, 30], BF16), x=([NT, D], F32), w_out=([D, D], F32), final_g=([D], F32),
             tri=([128, 128], F32), ones128=([128, 128], F32), onehot=([128, 4], F32), dmask=([128, 4, 128], BF16),
             wmask=([128, 8, 128], BF16), cmask=([16, 128, 4, 128], BF16), force=([16, 128, 128], F32), ee=([128, T], BF16),
             ident=([128, 128], F32), rmat=([128, 128], F32), cosC=([128, 512], F32), sinC=([128, 512], F32),
             ovl=([128, 4, 128], BF16), peT=([64, 2, 32], F32), w2=([128, 2, 2, 64], F32), w1k=([2048, 256], F32),
             w1v=([2048, 256], F32), conv_w=([128, 2, 31], F32), conv_b=([128, 2], F32), ln_g=([128, 2], F32), ln_b=([128, 2], F32),
             conv_pw=([256, 256], F32), ohprev=([128, 4], F32))


def phase_b(S, nc, top_unused, io, last):
  with ExitStack() as top:
    KCMP = top.enter_context(nc.sbuf_tensor("KCMP", [128, 512], BF16))
    VCA = top.enter_context(nc.sbuf_tensor("VCA", [128, 4, 2, 208], BF16))
    MIXB = top.enter_context(nc.sbuf_tensor("MIXB", [128, 16, D], BF16))
    for scope in (lambda es: cmp_scope(S, nc, es, io, KCMP, VCA), lambda es: conv_scope(S, nc, es, io, MIXB),
                  lambda es: fox_scope(S, nc, es, io, MIXB), lambda es: nsa_scope(S, nc, es, io, KCMP, VCA, MIXB),
                  lambda es: out_scope(S, nc, es, io, MIXB, last)):
        with ExitStack() as es:
            scope(es)
            S.flush()


def build_b(last):
    nc = bass.Bass("TRN2", target_bir_lowering=False)
    io = {n: nc.dram_tensor(n, shp, dt, kind="ExternalInput").ap() for n, (shp, dt) in B_INS.items()}
    io["xo"] = nc.dram_tensor("xo", [NT, D], F32, kind="ExternalOutput").ap()
    with ExitStack() as top:
        S = Sched(nc, top)
        phase_b(S, nc, top, io, last)
    return nc


def build_ba():
    nc = bass.Bass("TRN2", target_bir_lowering=False)
    io = {n: nc.dram_tensor(n, shp, dt, kind="ExternalInput").ap() for n, (shp, dt) in B_INS.items()}
    io["xo"] = nc.dram_tensor("xo", [NT, D], F32, kind="ExternalOutput").ap()
    ioa = dict(x=io["xo"], rmat=io["rmat"], ident=io["ident"])
    ioa["w_in"] = nc.dram_tensor("a_w_in", [D, DIN], F32, kind="ExternalInput").ap()
    ioa["norm_g"] = nc.dram_tensor("a_norm_g", [128, 8], F32, kind="ExternalInput").ap()
    ioa["fox_b"] = nc.dram_tensor("a_fox_b", [4], F32, kind="ExternalInput").ap()
    ioa["cos"] = nc.dram_tensor("a_cos", [128, NT], F32, kind="ExternalInput").ap()
    ioa["sin"] = nc.dram_tensor("a_sin", [128, NT], F32, kind="ExternalInput").ap()
    for n, (shp, dt) in A_OUTS.items():
        ioa[n] = nc.dram_tensor("a_" + n, shp, dt, kind="ExternalOutput").ap()
    with ExitStack() as top:
        S = Sched(nc, top)
        phase_b(S, nc, top, io, False)
        with ExitStack() as es:
            phase_a(S, nc, es, ioa, pfx="A")
            S.flush()
    return nc


def a_inputs(inputs, layer, x_own):
    maps = []
    for core in range(8):
        cos, sin = rope_tables(own_positions(core % 4))
        maps.append(dict(x=np.ascontiguousarray(x_own[core], dtype=np.float32), w_in=np.ascontiguousarray(inputs["w_in"][layer]),
                         norm_g=np.ascontiguousarray(inputs["norm_g"][layer].reshape(8, 128).T), fox_b=np.ascontiguousarray(inputs["fox_b"][layer]),
                         cos=cos, sin=sin, rmat=rope_rmat(), ident=np.eye(128, dtype=np.float32)))
    return maps


def b_inputs(inputs, layer, resA, x_own):
    maps = []
    cc = common_consts()
    ee = ee_const()
    cmpc = cmp_consts(inputs, layer)
    for core in range(8):
        b, r = core // 4, core % 4
        R = [resA[4 * b + rr] for rr in range(4)]
        cat = lambda n, ax: np.concatenate([np.asarray(R[rr][n]) for rr in range(4)], axis=ax)
        d = dict(kf_g=cat("kf", 2), vf_g=cat("vf", 0), lf_g=cat("lf", 0), kc_g=cat("kc", 1), vc_g=cat("vc", 1), ks_g=cat("ks", 1),
                 kw_g=cat("kw", 1), vs_g=cat("vs", 0), vw_g=cat("vw", 0))
        d["tails_g"] = np.ascontiguousarray(np.stack([np.asarray(R[rr]["yT"]).reshape(2, 128, 16, 128)[:, :, :, 98:128] for rr in range(4)], axis=0))
        own = resA[core]
        for n in ("qf", "qn", "gl", "gt", "yT"):
            d[n] = np.asarray(own[n])
        d["x"] = np.ascontiguousarray(x_own[core], dtype=np.float32)
        d["w_out"] = np.ascontiguousarray(inputs["w_out"][layer])
        d["final_g"] = np.ascontiguousarray(inputs["final_g"])
        d.update(cc)
        d.update(rank_consts(r))
        d.update(nsa_rank_consts(r))
        d.update(cmpc)
        d.update(conv_consts(inputs, layer, r))
        d["ee"] = ee
        d["ident"] = np.eye(128, dtype=np.float32)
        maps.append(d)
    return maps


def kernel(**inputs):
    inputs = {k: np.asarray(v) for k, v in inputs.items()}
    x = inputs["x"].astype(np.float32)
    x_own = [x[core // 4][own_positions(core % 4)] for core in range(8)]
    cores = list(range(8))
    resA0 = run_bass_kernel_spmd(build_a(), a_inputs(inputs, 0, x_own), core_ids=cores).results
    maps = b_inputs(inputs, 0, resA0, x_own)
    a1 = a_inputs(inputs, 1, x_own)
    for core in cores:
        for k in ("w_in", "norm_g", "fox_b", "cos", "sin"):
            maps[core]["a_" + k] = a1[core][k]
    resBA = run_bass_kernel_spmd(build_ba(), maps, core_ids=cores).results
    x1_own = [np.asarray(resBA[core]["xo"]) for core in cores]
    resA1 = [{k[2:]: v for k, v in resBA[core].items() if k.startswith("a_")} for core in cores]
    resB1 = run_bass_kernel_spmd(build_b(last=True), b_inputs(inputs, 1, resA1, x1_own), core_ids=cores).results
    out = np.empty((2, T, D), np.float32)
    for core in cores:
        out[core // 4][own_positions(core % 4)] = np.asarray(resB1[core]["xo"])
    return out


def cidx(j):
    return (j % 4) * 16 + j // 4


def gcol(j):
    return cidx(j) * 128


def fox_scope(S, nc, es, io, MIXB, pfx="f"):
    P = lambda n: pfx + n
    sb = lambda n, s, d: es.enter_context(nc.sbuf_tensor(P(n), s, d))
    PS = [es.enter_context(nc.psum_tensor(P("ps%d" % i), [128, 512], F32)) for i in range(4)]
    KF = sb("KF", [128, 2, T], BF16)
    VF = sb("VF", [128, 64, 320], BF16)
    QF = sb("QF", [128, 4, NT], BF16)
    LF = sb("LF", [128, 64, 4], F32)
    tri = sb("tri", [128, 128], F32)
    ones = sb("ones", [128, 128], F32)
    oneh = sb("oneh", [128, 4], F32)
    dm = sb("dm", [128, 4, 128], BF16)
    dm4 = sb("dm4", [128, 4, 4, 128], BF16)
    WI = sb("WI", [128, 64, 4], F32)
    TOT = sb("TOT", [128, 64, 4], F32)
    SA = sb("SA", [128, 64, 4], F32)
    SB_ = sb("SB", [128, 64, 4], F32)
    NC_ = sb("NC", [128, 64, 4], F32)
    ROWN = sb("ROWN", [128, 16, 4], F32)
    CB = [sb("CB%d" % i, [128, 64, 4], F32) for i in range(2)]
    PT = [sb("PT%d" % i, [128, 512], BF16) for i in range(3)]
    ofs = sb("ofs", [128, 4, 65], F32)
    rden = sb("rden", [128, 4], F32)
    onrm = sb("onrm", [128, 256], F32)
    gtt = [sb("gtt%d" % i, [128, 256], BF16) for i in range(2)]

    ld = P("ld")
    S.pool("memset", w=[P("QF")], ap=QF[:].rearrange("p h t -> p (h t)"), constant=0.0)
    for h2 in range(2):
        S.dma(ld, batch=True, w=[P("KF")], out=KF[:, h2, :], in_=io["kf_g"][h2])
    for h in range(4):
        pb = 64 * (h % 2)
        S.dma(ld, batch=True, w=[P("QF")], out=QF[pb:pb + 64, h, :], in_=io["qf"][h // 2, pb:pb + 64, :])
    for r in range(4):
        S.dma(ld, batch=True, w=[P("VF")], out=VF[:, r * 16:(r + 1) * 16, :],
              in_=io["vf_g"][r * NT:(r + 1) * NT, :].rearrange("(c t) w -> t c w", t=128))
    for r in range(4):
        S.dma(ld, batch=True, w=[P("LF")], out=LF[:].rearrange("t (m r) h -> t m r h", r=4)[:, :, r, :],
              in_=io["lf_g"][r * NT:(r + 1) * NT, :].rearrange("(m t) h -> t m h", t=128))
    S.dma(ld, batch=True, w=[P("tri")], out=tri[:], in_=io["tri"])
    S.dma(ld, batch=True, w=[P("ones")], out=ones[:], in_=io["ones128"])
    S.dma(ld, batch=True, w=[P("oneh")], out=oneh[:], in_=io["onehot"])
    S.dma(ld, batch=True, w=[P("dm")], out=dm[:], in_=io["dmask"])
    for h in range(4):
        S.pool("tensor_copy", r=[P("dm")], w=[P("dm4")], out=dm4[:, :, h, :], in_=dm[:])

    LFf = LF[:].rearrange("t j h -> t (j h)")
    S.pe("matmul", r=[P("tri"), P("LF")], w=[P("ps0")], out=PS[0][:, 0:256], lhsT=tri[:], rhs=LFf, start=True, stop=True)
    S.pe("matmul", r=[P("ones"), P("LF")], w=[P("ps1")], out=PS[1][:, 0:256], lhsT=ones[:], rhs=LFf, start=True, stop=True)
    S.act("activation", r=[P("ps0")], w=[P("WI")], out=WI[:].rearrange("t j h -> t (j h)"), in_=PS[0][:, 0:256], func=AF.Copy)
    S.act("activation", r=[P("ps1")], w=[P("TOT")], out=TOT[:].rearrange("t j h -> t (j h)"), in_=PS[1][:, 0:256], func=AF.Copy)
    src, srcn = TOT, P("TOT")
    d = 1
    pp = [(SA, P("SA")), (SB_, P("SB"))]
    k = 0
    while d < 64:
        dst, dstn = pp[k % 2]
        S.dve("tensor_tensor", r=[srcn], w=[dstn], out=dst[:, d:, :], in0=src[:, d:, :], in1=src[:, :64 - d, :], op=ALU.add)
        S.dve("tensor_copy", r=[srcn], w=[dstn], out=dst[:, :d, :], in_=src[:, :d, :])
        src, srcn = dst, dstn
        d *= 2
        k += 1
    INCL, INCLn = src, srcn
    S.dve("tensor_tensor", r=[P("WI"), INCLn], w=[P("NC")], out=NC_[:], in0=WI[:], in1=INCL[:], op=ALU.add)
    S.dve("tensor_tensor", r=[P("NC"), P("TOT")], w=[P("NC")], out=NC_[:], in0=TOT[:], in1=NC_[:], op=ALU.subtract)
    I4 = INCL[:].rearrange("t (m r) h -> t m r h", r=4)
    S.dve("tensor_scalar", r=[INCLn, P("oneh")], w=[P("ROWN")], out=ROWN[:], in0=I4[:, :, 0, :], scalar1=oneh[:, 0:1],
          scalar2=None, op0=ALU.mult)
    for r in range(1, 4):
        S.dve("scalar_tensor_tensor", r=[INCLn, P("oneh"), P("ROWN")], w=[P("ROWN")], out=ROWN[:], in0=I4[:, :, r, :],
              scalar=oneh[:, r:r + 1], in1=ROWN[:], op0=ALU.mult, op1=ALU.add)

    st_i = [0]
    for m in range(16):
        J = 4 * m + 4
        cb, cbn = CB[m % 2], P("CB%d" % (m % 2))
        for h in range(4):
            S.dve("tensor_scalar", r=[P("NC"), P("ROWN")], w=[cbn + "_%d" % h], out=cb[:, 0:J, h], in0=NC_[:, 0:J, h],
                  scalar1=ROWN[:, m, h:h + 1], scalar2=0.0, op0=ALU.add, op1=ALU.min)
        for j in range(J):
            sp = st_i[0] % 2
            pt, ptn = PT[st_i[0] % 3], P("PT%d" % (st_i[0] % 3))
            st_i[0] += 1
            psn = P("ps%d" % sp)
            for h in range(4):
                S.pe("matmul", r=[P("KF"), P("QF")], w=[psn], out=PS[sp][:, h * 128:(h + 1) * 128],
                     lhsT=KF[:, h // 2, gcol(j):gcol(j) + 128], rhs=QF[:, h, m * 128:(m + 1) * 128], start=True, stop=True)
            pth = [ptn + "_%d" % h for h in range(4)]
            for h in range(4):
                S.act("activation", r=[psn, cbn + "_%d" % h], w=[pth[h]], out=pt[:, h * 128:(h + 1) * 128], in_=PS[sp][:, h * 128:(h + 1) * 128],
                      func=AF.Exp, bias=cb[:, j, h:h + 1], scale=0.125)
            if j >= 4 * m:
                S.dve("tensor_tensor", r=pth + [P("dm4")], w=pth, out=pt[:], in0=pt[:],
                      in1=dm4[:, j - 4 * m, :, :].rearrange("k h q -> k (h q)"), op=ALU.mult)
            for h in range(4):
                S.pe("matmul", r=[pth[h], P("VF")], w=[P("ps2")], out=PS[2][:, h * 65:(h + 1) * 65], lhsT=pt[:, h * 128:(h + 1) * 128],
                     rhs=VF[:, cidx(j), h * 80:h * 80 + 65], start=(j == 0 and h == 0), stop=(j == J - 1 and h == 3))
        S.act("activation", r=[P("ps2")], w=[P("ofs")], out=ofs[:].rearrange("q h d -> q (h d)"), in_=PS[2][:, 0:260], func=AF.Copy)
        S.dve("reciprocal", r=[P("ofs")], w=[P("rden")], out=rden[:], in_=ofs[:, :, 64])
        for h in range(4):
            S.dve("tensor_scalar", r=[P("ofs"), P("rden")], w=[P("onrm%d" % h)], out=onrm[:, h * 64:(h + 1) * 64], in0=ofs[:, h, 0:64],
                  scalar1=rden[:, h:h + 1], scalar2=None, op0=ALU.mult)
        gp = m % 2
        S.dma(P("gt%d" % gp), w=[P("gtt%d" % gp)], out=gtt[gp][:], in_=io["gt"][m * 128:(m + 1) * 128, 0:256])
        S.dve("tensor_tensor", r=[P("onrm%d" % h) for h in range(4)] + [P("gtt%d" % gp)], w=["MIXB"], out=MIXB[:, m, 0:256], in0=onrm[:],
              in1=gtt[gp][:], op=ALU.mult)


def rank_consts(r):
    k = np.arange(128)[:, None]
    q = np.arange(128)[None, :]
    dmask = np.zeros((128, 4, 128), np.float32)
    for c in range(4):
        if c < r:
            dmask[:, c, :] = 1.0
        elif c == r:
            dmask[:, c, :] = (k <= q)
    onehot = np.zeros((128, 4), np.float32)
    onehot[:, r] = 1.0
    return dict(dmask=dmask.astype(NPBF), onehot=onehot)


def common_consts():
    k = np.arange(128)[:, None]
    q = np.arange(128)[None, :]
    return dict(tri=(k <= q).astype(np.float32), ones128=np.ones((128, 128), np.float32))


def cmp_scope(S, nc, es, io, KCMP, VCA, pfx="c"):
    P = lambda n: pfx + n
    sb = lambda n, s, d: es.enter_context(nc.sbuf_tensor(P(n), s, d))
    PS = [es.enter_context(nc.psum_tensor(P("ps%d" % i), [128, 512], F32)) for i in range(4)]
    KCN = [sb("KCN%d" % i, [128, T], BF16) for i in range(2)]
    W1st = sb("W1st", [128, 32, 256], F32)
    W1Z = [sb("W1Z%d" % g, [128, 32, 256], BF16) for g in range(2)]
    PEf = sb("PEf", [128, 2, 32], F32)
    PEB = sb("PEB", [128, 2, 32], BF16)
    W2f = sb("W2f", [128, 2, 2, 64], F32)
    W2KZ = sb("W2KZ", [128, 2, 2, 128], BF16)
    W2V = sb("W2V", [128, 2, 64], BF16)
    BH = sb("BH", [128, 2], F32)
    HID = [[sb("HID%d%d" % (kv, g), [128, 2, 512], BF16) for g in range(2)] for kv in range(2)]
    X = sb("X", [128, 512], F32)
    X2 = sb("X2", [128, 512], F32)
    U = sb("U", [128, 512], F32)
    SG = sb("SG", [128, 512], F32)
    KC0 = sb("KC0", [128, 512], F32)
    KZ = sb("KZ", [128, 512], BF16)
    RT = sb("RT", [128, 512], F32)
    cosC = sb("cosC", [128, 512], F32)
    sinC = sb("sinC", [128, 512], F32)
    rmf = sb("rmf", [128, 128], F32)
    rmb = sb("rmb", [128, 128], BF16)
    ovf = sb("ovf", [128, 4, 128], BF16)

    ld = P("ld")
    for kv, nm in ((0, "kc_g"), (1, "vc_g")):
        for r in range(4):
            S.dma(ld, batch=True, w=[P("KCN%d" % kv)], out=KCN[kv][:].rearrange("d (m r t) -> d m r t", r=4, t=128)[:, :, r, :],
                  in_=io[nm][:, r * NT:(r + 1) * NT].rearrange("d (m t) -> d m t", t=128))
    S.dma(ld, batch=True, w=[P("PEf")], out=PEf[0:64, :, :], in_=io["peT"])
    S.dma(ld, batch=True, w=[P("PEf")], out=PEf[64:128, :, :], in_=io["peT"])
    S.dma(ld, batch=True, w=[P("W2f")], out=W2f[:], in_=io["w2"])
    S.dma(ld, batch=True, w=[P("cosC")], out=cosC[:], in_=io["cosC"])
    S.dma(ld, batch=True, w=[P("sinC")], out=sinC[:], in_=io["sinC"])
    S.dma(ld, batch=True, w=[P("rmf")], out=rmf[:], in_=io["rmat"])
    S.dma(ld, batch=True, w=[P("ovf")], out=ovf[:], in_=io["ovl"])
    S.dve("tensor_copy", r=[P("PEf")], w=[P("PEB")], out=PEB[:], in_=PEf[:])
    S.dve("tensor_copy", r=[P("rmf")], w=[P("rmb")], out=rmb[:], in_=rmf[:])
    S.pool("memset", w=[P("W2KZ")], ap=W2KZ[:], constant=0.0)
    for g in range(2):
        S.pool("memset", w=[P("W1Z%d" % g)], ap=W1Z[g][:], constant=0.0)
        S.dve("tensor_copy", r=[P("W2f"), P("W2KZ")], w=[P("W2KZ")], out=W2KZ[:, g, :, 64 * g:64 * g + 64], in_=W2f[:, 0, :, :])
        for kv in range(2):
            S.pool("memset", w=[P("HID%d%d" % (kv, g))], ap=HID[kv][g][:], constant=0.0)
    S.dve("tensor_copy", r=[P("W2f")], w=[P("W2V")], out=W2V[:], in_=W2f[:, 1, :, :])
    S.pool("memset", w=[P("KC0")], ap=KC0[:], constant=0.0)
    S.pool("memset", w=["VCA"], ap=VCA[:], constant=1.0)
    for g in range(2):
        S.pool("tensor_copy", r=[P("ovf"), "VCA"], w=["VCA"], out=VCA[:, :, g, 80:208], in_=ovf[:])

    for kv, wn in ((0, "w1k"), (1, "w1v")):
        for half in range(2):
            S.dma(P("w1"), w=[P("W1st")], out=W1st[64 * half:64 * half + 64, :, :],
                  in_=io[wn].rearrange("(l d) h -> d l h", d=64))
        for g in range(2):
            S.dve("tensor_copy", r=[P("W1st"), P("W1Z%d" % g)], w=[P("W1Z%d" % g)], out=W1Z[g][64 * g:64 * g + 64, :, :],
                  in_=W1st[64 * g:64 * g + 64, :, :])
        for hc in range(2):
            for l in range(32):
                S.pe("matmul", r=[P("W1Z0"), P("PEB")], w=[P("ps3")], out=PS[3][:, hc:hc + 1], lhsT=W1Z[0][:, l, hc * 128:(hc + 1) * 128],
                     rhs=PEB[:, kv, l:l + 1], start=(hc == 0 and l == 0), stop=(hc == 1 and l == 31))
        S.act("activation", r=[P("ps3")], w=[P("BH")], out=BH[:], in_=PS[3][:, 0:2], func=AF.Copy)
        for g in range(2):
            for hc in range(2):
                k = (g * 2 + hc) % 3
                psn = P("ps%d" % k)
                for l in range(32):
                    S.pe("matmul", r=[P("W1Z%d" % g), P("KCN%d" % kv)], w=[psn], out=PS[k][:, 0:511],
                         lhsT=W1Z[g][:, l, hc * 128:(hc + 1) * 128], rhs=KCN[kv][:, l:l + 16 * 510 + 1:16], start=(l == 0), stop=(l == 31))
                S.act("activation", r=[psn, P("BH")], w=[P("X")], out=X[:, 0:511], in_=PS[k][:, 0:511], func=AF.Identity,
                      bias=BH[:, hc:hc + 1], scale=1.0)
                S.dve("tensor_tensor", r=[P("X")], w=[P("X2")], out=X2[:, 0:511], in0=X[:, 0:511], in1=X[:, 0:511], op=ALU.mult)
                S.dve("tensor_scalar", r=[P("X2")], w=[P("X2")], out=X2[:, 0:511], in0=X2[:, 0:511], scalar1=0.044715, scalar2=1.0,
                      op0=ALU.mult, op1=ALU.add)
                S.dve("tensor_tensor", r=[P("X2"), P("X")], w=[P("U")], out=U[:, 0:511], in0=X2[:, 0:511], in1=X[:, 0:511], op=ALU.mult)
                S.act("activation", r=[P("U")], w=[P("SG")], out=SG[:, 0:511], in_=U[:, 0:511], func=AF.Sigmoid, scale=1.5957691216057308)
                S.dve("tensor_tensor", r=[P("X"), P("SG")], w=[P("HID%d%d" % (kv, g))], out=HID[kv][g][:, hc, 0:511], in0=X[:, 0:511],
                      in1=SG[:, 0:511], op=ALU.mult)
    n = 0
    for g in range(2):
        for hc in range(2):
            S.pe("matmul", r=[P("W2KZ"), P("HID0%d" % g)], w=[P("ps0")], out=PS[0][:, 0:511], lhsT=W2KZ[:, g, hc, :],
                 rhs=HID[0][g][:, hc, 0:511], start=(n == 0), stop=(n == 3))
            n += 1
    S.act("activation", r=[P("ps0")], w=[P("KC0")], out=KC0[:, 0:511], in_=PS[0][:, 0:511], func=AF.Copy)
    S.dve("tensor_copy", r=[P("KC0")], w=[P("KZ")], out=KZ[:], in_=KC0[:])
    S.pe("matmul", r=[P("rmb"), P("KZ")], w=[P("ps1")], out=PS[1][:], lhsT=rmb[:], rhs=KZ[:], start=True, stop=True)
    S.act("activation", r=[P("ps1")], w=[P("RT")], out=RT[:], in_=PS[1][:], func=AF.Copy)
    S.dve("tensor_tensor", r=[P("KC0"), P("cosC")], w=[P("KC0")], out=KC0[:], in0=KC0[:], in1=cosC[:], op=ALU.mult)
    S.dve("tensor_tensor", r=[P("RT"), P("sinC")], w=[P("RT")], out=RT[:], in0=RT[:], in1=sinC[:], op=ALU.mult)
    S.dve("tensor_tensor", r=[P("KC0"), P("RT")], w=["KCMP"], out=KCMP[:], in0=KC0[:], in1=RT[:], op=ALU.add)
    for c in range(4):
        for g in range(2):
            k = 2 + (c * 2 + g) % 2
            for hc in range(2):
                S.pe("matmul", r=[P("HID1%d" % g), P("W2V")], w=[P("ps%d" % k)], out=PS[k][:, 0:64], lhsT=HID[1][g][:, hc, c * 128:(c + 1) * 128],
                     rhs=W2V[:, hc, :], start=(hc == 0), stop=(hc == 1))
            S.act("activation", r=[P("ps%d" % k), "VCA"], w=["VCA"], out=VCA[:, c, g, 0:64], in_=PS[k][:, 0:64], func=AF.Copy)


def cmp_consts(inputs, layer):
    n = np.arange(512)
    cosC, sinC = rope_tables(16 * n + 31)
    nn = np.arange(512)[:, None]
    bb = np.arange(128)[None, :]
    ovl = ((16 * nn < 64 * bb + 64) & (16 * nn + 31 >= 64 * bb) & (nn < 511)).astype(np.float32)
    peT = np.stack([inputs["cmp_pe_k"][layer].T, inputs["cmp_pe_v"][layer].T], axis=1)
    w2 = np.stack([inputs["cmp_k_w2"][layer].reshape(2, 128, 64), inputs["cmp_v_w2"][layer].reshape(2, 128, 64)], axis=0)
    w2 = np.ascontiguousarray(w2.transpose(2, 0, 1, 3))
    return dict(cosC=cosC, sinC=sinC, ovl=np.ascontiguousarray(ovl.reshape(4, 128, 128).transpose(1, 0, 2)).astype(NPBF),
                peT=np.ascontiguousarray(peT, dtype=np.float32), w2=w2.astype(np.float32),
                w1k=np.ascontiguousarray(inputs["cmp_k_w1"][layer]), w1v=np.ascontiguousarray(inputs["cmp_v_w1"][layer]),
                rmat=rope_rmat())


def nsa_scope(S, nc, es, io, KCMP, VCA, MIXB, pfx="n"):
    P = lambda n: pfx + n
    sb = lambda n, s, d: es.enter_context(nc.sbuf_tensor(P(n), s, d))
    PS = [es.enter_context(nc.psum_tensor(P("ps%d" % i), [128, 512], F32)) for i in range(6)]
    PSTb = es.enter_context(nc.psum_tensor(P("pstb"), [128, 1024], BF16))
    KS = sb("KS", [128, T], BF16)
    VS = sb("VS", [128, 64, 160], BF16)
    EE = sb("EE", [128, T], BF16)
    QNZ = sb("QNZ", [128, 2, 4, NT], BF16)
    KWm = [sb("KWm%d" % i, [128, 8, 128], BF16) for i in range(2)]
    VWm = [sb("VWm%d" % i, [128, 8, 160], BF16) for i in range(2)]
    dm = sb("dm", [128, 4, 128], BF16)
    wm = sb("wm", [128, 8, 128], BF16)
    cm = [sb("cm%d" % i, [128, 4, 128], BF16) for i in range(2)]
    fc = [sb("fc%d" % i, [128, 128], F32) for i in range(2)]
    glt = [sb("glt%d" % i, [128, 24], F32) for i in range(2)]
    gtt = [sb("gtt%d" % i, [128, 512], BF16) for i in range(2)]
    idf = sb("idf", [128, 128], F32)
    idb = sb("idb", [128, 128], BF16)
    PT = [sb("PT%d" % i, [128, 512], BF16) for i in range(3)]
    ocs = sb("ocs", [128, 4, 208], F32)
    oss = sb("oss", [128, 4, 65], F32)
    ows = sb("ows", [128, 4, 65], F32)
    rd = sb("rd", [128, 3, 4], F32)
    fac = sb("fac", [128, 3, 4], F32)
    imp = sb("imp", [128, 128], F32)
    wk = sb("wk", [128, 128], F32)
    mx = sb("mx", [128, 8], F32)
    MBq = sb("MBq", [128, 128], BF16)
    MBT = sb("MBT", [128, 4, 128], BF16)
    acc = sb("acc", [128, 4, 64], F32)
    tmp = sb("tmp", [128, 4, 64], F32)

    def bc(ap, h=4):
        return ap.unsqueeze(1).to_broadcast([128, h, ap.shape[-1]])

    ld = P("ld")
    S.dma(ld, batch=True, w=[P("KS")], out=KS[:], in_=io["ks_g"])
    S.dma(ld, batch=True, w=[P("EE")], out=EE[:], in_=io["ee"])
    for r in range(4):
        S.dma(ld, batch=True, w=[P("VS")], out=VS[:, r * 16:(r + 1) * 16, :],
              in_=io["vs_g"][r * NT:(r + 1) * NT, :].rearrange("(c t) w -> t c w", t=128))
    S.pool("memset", w=[P("QNZ")], ap=QNZ[:].rearrange("p g h t -> p (g h t)"), constant=0.0)
    for hh in range(8):
        g, h = hh // 4, hh % 4
        src = 64 * (hh % 2)
        S.dma(ld, batch=True, w=[P("QNZ")], out=QNZ[64 * g:64 * g + 64, g, h, :], in_=io["qn"][hh // 2, src:src + 64, :])
    S.dma(ld, batch=True, w=[P("dm")], out=dm[:], in_=io["dmask"])
    S.dma(ld, batch=True, w=[P("wm")], out=wm[:], in_=io["wmask"])
    S.dma(ld, batch=True, w=[P("idf")], out=idf[:], in_=io["ident"])
    S.dve("tensor_copy", r=[P("idf")], w=[P("idb")], out=idb[:], in_=idf[:])

    st_i = [0]

    def unit(lhsT, rk, qrhs, mask, vrhs, vk, obank, ocols, first, last, bias_mm=None):
        sp = st_i[0] % 2
        pt, ptn = PT[st_i[0] % 3], P("PT%d" % (st_i[0] % 3))
        st_i[0] += 1
        psn = P("ps%d" % sp)
        S.pe("matmul", r=list(rk) + [P("QNZ")], w=[psn], out=PS[sp][:], lhsT=lhsT, rhs=qrhs, start=True, stop=(bias_mm is None))
        if bias_mm is not None:
            S.pe("matmul", r=[P("EE"), P("MBT")], w=[psn], out=PS[sp][:], lhsT=bias_mm, rhs=MBT[:].rearrange("b h q -> b (h q)"),
                 start=False, stop=True)
        S.act("activation", r=[psn], w=[ptn], out=pt[:], in_=PS[sp][:], func=AF.Exp, scale=0.125)
        if mask is not None:
            mk, mn = mask
            S.pool("tensor_tensor", r=[ptn, mn], w=[ptn], out=pt[:].rearrange("k (h q) -> k h q", h=4),
                   in0=pt[:].rearrange("k (h q) -> k h q", h=4), in1=bc(mk), op=ALU.mult)
        for h in range(4):
            ob, oc = obank(h), ocols(h)
            S.pe("matmul", r=[ptn] + list(vk), w=[P("ps%d" % ob)], out=PS[ob][:, oc[0]:oc[1]], lhsT=pt[:, h * 128:(h + 1) * 128],
                 rhs=vrhs, start=(first and (h == 0 or (ob != obank(0) and h == 2))), stop=(last and (h == 3 or (ob != obank(3) and h == 1))))

    for m in range(16):
        p = m % 2
        cmn, fcn, gln, gtn, kwn, vwn = P("cm%d" % p), P("fc%d" % p), P("glt%d" % p), P("gtt%d" % p), P("KWm%d" % p), P("VWm%d" % p)
        S.dma(P("cm%d" % p), w=[cmn], out=cm[p][:], in_=io["cmask"][m])
        S.dma(P("fc%d" % p), w=[fcn], out=fc[p][:], in_=io["force"][m])
        S.dma(P("gl%d" % p), w=[gln], out=glt[p][:], in_=io["gl"][m * 128:(m + 1) * 128, :])
        S.dma(P("gt%d" % p), w=[gtn], out=gtt[p][:], in_=io["gt"][m * 128:(m + 1) * 128, 512:1024])
        for mm in ((m - 1, m) if m > 0 else (m,)):
            s0 = 4 * (mm - m + 1)
            S.dma(P("kw%d" % p), w=[kwn], out=KWm[p][:, s0:s0 + 4, :],
                  in_=io["kw_g"].rearrange("d (r c t) -> d r c t", r=4, t=128)[:, :, mm, :])
            S.dma(P("vw%d" % p), w=[vwn], out=VWm[p][:, s0:s0 + 4, :],
                  in_=io["vw_g"].rearrange("(r c t) w -> t r c w", r=4, t=128)[:, :, mm, :])
        qs = slice(m * 128, (m + 1) * 128)
        for g in range(2):
            qrhs = QNZ[:, g, :, qs]
            for c in range(4):
                unit(KCMP[:, c * 128:(c + 1) * 128], ["KCMP"], qrhs, (cm[p][:, c, :], cmn), VCA[:, c, g, :], ["VCA"],
                     lambda h: 2 + h // 2, lambda h: ((h % 2) * 208, (h % 2) * 208 + 208), c == 0, c == 3)
            S.act("activation", r=[P("ps2")], w=[P("ocs")], out=ocs[:, 0:2, :].rearrange("q h w -> q (h w)"), in_=PS[2][:, 0:416], func=AF.Copy)
            S.act("activation", r=[P("ps3")], w=[P("ocs")], out=ocs[:, 2:4, :].rearrange("q h w -> q (h w)"), in_=PS[3][:, 0:416], func=AF.Copy)
            S.dve("tensor_scalar", r=[P("ocs")], w=[P("rd")], out=rd[:, 0, :], in0=ocs[:, :, 64], scalar1=1e-30, scalar2=None, op0=ALU.max)
            S.dve("reciprocal", r=[P("rd")], w=[P("rd")], out=rd[:, 0, :], in_=rd[:, 0, :])
            S.dve("tensor_scalar", r=[P("ocs"), P("rd")], w=[P("imp")], out=imp[:], in0=ocs[:, 0, 80:208], scalar1=rd[:, 0, 0:1],
                  scalar2=None, op0=ALU.mult)
            for h in range(1, 4):
                S.dve("scalar_tensor_tensor", r=[P("ocs"), P("rd"), P("imp")], w=[P("imp")], out=imp[:], in0=ocs[:, h, 80:208],
                      scalar=rd[:, 0, h:h + 1], in1=imp[:], op0=ALU.mult, op1=ALU.add)
            S.dve("tensor_tensor", r=[P("imp"), fcn], w=[P("imp")], out=imp[:], in0=imp[:], in1=fc[p][:], op=ALU.max)
            S.dve("max", r=[P("imp")], w=[P("mx")], out=mx[:], in_=imp[:])
            S.dve("match_replace", r=[P("mx"), P("imp")], w=[P("wk")], out=wk[:], in_to_replace=mx[:], in_values=imp[:], imm_value=-1.0)
            S.dve("max", r=[P("wk")], w=[P("mx")], out=mx[:], in_=wk[:])
            S.dve("tensor_scalar", r=[P("imp"), P("mx")], w=[P("wk")], out=wk[:], in0=imp[:], scalar1=mx[:, 7:8], scalar2=None, op0=ALU.is_ge)
            S.dve("tensor_scalar", r=[P("wk")], w=[P("MBq")], out=MBq[:], in0=wk[:], scalar1=-1.0, scalar2=30000.0, op0=ALU.add, op1=ALU.mult)
            S.pe("transpose", r=[P("MBq"), P("idb")], w=[P("pstb")], out=PSTb[:, 0:128], in_=MBq[:], identity=idb[:])
            S.act("activation", r=[P("pstb")], w=[P("MBT")], out=MBT[:], in_=bc(PSTb[:, 0:128]), func=AF.Copy)
            J = 4 * m + 4
            for j in range(J):
                unit(KS[:, gcol(j):gcol(j) + 128], [P("KS")], qrhs, (dm[:, j - 4 * m, :], P("dm")) if j >= 4 * m else None,
                     VS[:, cidx(j), g * 80:g * 80 + 65], [P("VS")], lambda h: 4, lambda h: (h * 65, h * 65 + 65), j == 0, j == J - 1,
                     bias_mm=EE[:, gcol(j):gcol(j) + 128])
            cl = list(range(8)) if m > 0 else list(range(4, 8))
            for c in cl:
                unit(KWm[p][:, c, :], [kwn], qrhs, (wm[:, c, :], P("wm")), VWm[p][:, c, g * 80:g * 80 + 65], [vwn],
                     lambda h: 5, lambda h: (h * 65, h * 65 + 65), c == cl[0], c == cl[-1])
            S.act("activation", r=[P("ps4")], w=[P("oss")], out=oss[:].rearrange("q h w -> q (h w)"), in_=PS[4][:, 0:260], func=AF.Copy)
            S.act("activation", r=[P("ps5")], w=[P("ows")], out=ows[:].rearrange("q h w -> q (h w)"), in_=PS[5][:, 0:260], func=AF.Copy)
            S.dve("reciprocal", r=[P("oss")], w=[P("rd")], out=rd[:, 1, :], in_=oss[:, :, 64])
            S.dve("reciprocal", r=[P("ows")], w=[P("rd")], out=rd[:, 2, :], in_=ows[:, :, 64])
            S.dve("tensor_tensor", r=[P("rd"), gln], w=[P("fac")], out=fac[:], in0=rd[:],
                  in1=glt[p][:, g * 12:g * 12 + 12].rearrange("q (h b) -> q b h", b=3), op=ALU.mult)
            srcs = ((ocs, P("ocs")), (oss, P("oss")), (ows, P("ows")))
            for br in range(3):
                o_t, o_n = srcs[br]
                dst, dstn = (acc, P("acc")) if br == 0 else (tmp, P("tmp"))
                S.dve("tensor_tensor", r=[o_n, P("fac")], w=[dstn], out=dst[:], in0=o_t[:, :, 0:64],
                      in1=fac[:, br, :].unsqueeze(2).to_broadcast([128, 4, 64]), op=ALU.mult)
                if br > 0:
                    S.dve("tensor_tensor", r=[P("acc"), P("tmp")], w=[P("acc")], out=acc[:], in0=acc[:], in1=tmp[:], op=ALU.add)
            S.dve("tensor_tensor", r=[P("acc"), gtn], w=["MIXB"], out=MIXB[:, m, 512 + g * 256:512 + g * 256 + 256],
                  in0=acc[:].rearrange("q h d -> q (h d)"), in1=gtt[p][:, g * 256:g * 256 + 256], op=ALU.mult)


def nsa_rank_consts(r):
    k = np.arange(128)[:, None]
    q = np.arange(128)[None, :]
    wmask = np.zeros((128, 8, 128), np.float32)
    for c in range(8):
        d = (4 + r) - c
        if d == 4:
            wmask[:, c, :] = (k > q)
        elif 1 <= d <= 3:
            wmask[:, c, :] = 1.0
        elif d == 0:
            wmask[:, c, :] = (k <= q)
    cmask = np.zeros((16, 128, 4, 128), np.float32)
    force = np.zeros((16, 128, 128), np.float32)
    nl = np.arange(128)[:, None]
    b = np.arange(128)[None, :]
    for m in range(16):
        tq = (4 * m + r) * 128 + np.arange(128)
        for c in range(4):
            n = 128 * c + nl
            cmask[m, :, c, :] = ((16 * n + 31) <= tq[None, :]) & (n < 511)
        cur = (tq // 64)[:, None]
        force[m] = 1e4 * (b == 0) + 2e4 * (b == cur) + 3e4 * (b == cur - 1)
    return dict(wmask=wmask.astype(NPBF), cmask=cmask.astype(NPBF), force=force.astype(np.float32))


def ee_const():
    ee = np.zeros((128, T), np.float32)
    for j in range(64):
        for half in range(2):
            ee[2 * j + half, gcol(j) + 64 * half: gcol(j) + 64 * half + 64] = 1.0
    return ee.astype(NPBF)


def conv_scope(S, nc, es, io, MIXB, pfx="v"):
    P = lambda n: pfx + n
    sb = lambda n, s, d: es.enter_context(nc.sbuf_tensor(P(n), s, d))
    PS = [es.enter_context(nc.psum_tensor(P("ps%d" % i), [128, 512], F32)) for i in range(4)]
    YB = sb("YB", [128, 2, NT], BF16)
    TLB = sb("TLB", [128, 2, 4, 16, 30], BF16)
    YE = sb("YE", [128, 2, 16, 158], F32)
    accA = sb("accA", [128, 2, 16, 128], F32)
    accB = sb("accB", [128, 2, 16, 128], F32)
    tmpP = sb("tmpP", [128, 16, 128], F32)
    SQ = sb("SQ", [128, 2, NT], F32)
    MEAN = sb("MEAN", [128, NT], F32)
    MSQ = sb("MSQ", [128, NT], F32)
    CW = sb("CW", [128, 2, 31], F32)
    CBs = sb("CBs", [128, 2], F32)
    LG = sb("LG", [128, 2], F32)
    LB = sb("LB", [128, 2], F32)
    ohp = sb("ohp", [128, 4], F32)
    onesd = sb("onesd", [128, 128], F32)
    WPf = sb("WPf", [128, 2, 256], F32)
    WPW = sb("WPW", [128, 2, 256], BF16)
    ACTT = sb("ACTT", [128, 2, NT], BF16)
    ob = [sb("ob%d" % i, [128, 256], F32) for i in range(2)]
    gtt = [sb("gtt%d" % i, [128, 256], BF16) for i in range(2)]

    ld = P("ld")
    for cc in range(2):
        S.dma(ld, batch=True, w=[P("YB")], out=YB[:, cc, :], in_=io["yT"][cc])
        S.dma(ld, batch=True, w=[P("WPf")], out=WPf[:, cc, :], in_=io["conv_pw"][cc * 128:(cc + 1) * 128, :])
        for r in range(4):
            S.dma(ld, batch=True, w=[P("TLB")], out=TLB[:, cc, r, :, :], in_=io["tails_g"][r, cc])
    S.dma(ld, batch=True, w=[P("CW")], out=CW[:], in_=io["conv_w"])
    S.dma(ld, batch=True, w=[P("CBs")], out=CBs[:], in_=io["conv_b"])
    S.dma(ld, batch=True, w=[P("LG")], out=LG[:], in_=io["ln_g"])
    S.dma(ld, batch=True, w=[P("LB")], out=LB[:], in_=io["ln_b"])
    S.dma(ld, batch=True, w=[P("ohp")], out=ohp[:], in_=io["ohprev"])
    S.dve("memset", w=[P("onesd")], ap=onesd[:], constant=1.0 / 256.0)
    S.dve("tensor_copy", r=[P("WPf")], w=[P("WPW")], out=WPW[:], in_=WPf[:])
    for cc in range(2):
        S.act("activation", r=[P("YB")], w=[P("YE")], out=YE[:, cc, :, 30:158], in_=YB[:, cc, :].rearrange("p (m t) -> p m t", t=128), func=AF.Copy)
        S.dve("tensor_scalar", r=[P("TLB"), P("ohp")], w=[P("YE")], out=YE[:, cc, :, 0:30], in0=TLB[:, cc, 0, :, :], scalar1=ohp[:, 0:1],
              scalar2=None, op0=ALU.mult)
        for r in (1, 2):
            S.dve("scalar_tensor_tensor", r=[P("TLB"), P("ohp"), P("YE")], w=[P("YE")], out=YE[:, cc, :, 0:30], in0=TLB[:, cc, r, :, :],
                  scalar=ohp[:, r:r + 1], in1=YE[:, cc, :, 0:30], op0=ALU.mult, op1=ALU.add)
        S.dve("scalar_tensor_tensor", r=[P("TLB"), P("ohp"), P("YE")], w=[P("YE")], out=YE[:, cc, 1:16, 0:30], in0=TLB[:, cc, 3, 0:15, :],
              scalar=ohp[:, 3:4], in1=YE[:, cc, 1:16, 0:30], op0=ALU.mult, op1=ALU.add)
    for cc in range(2):
        an, bn = P("accA%d" % cc), P("accB%d" % cc)
        for tp in range(0, 10):
            src = YE[:, cc, :, tp:tp + 128]
            if tp == 0:
                S.pool("tensor_scalar", r=[P("YE"), P("CW"), P("CBs")], w=[an], out=accA[:, cc], in0=src, scalar1=CW[:, cc, 0:1],
                       scalar2=CBs[:, cc:cc + 1], op0=ALU.mult, op1=ALU.add)
            else:
                S.pool("tensor_scalar", r=[P("YE"), P("CW")], w=[P("tmpP")], out=tmpP[:], in0=src, scalar1=CW[:, cc, tp:tp + 1],
                       scalar2=None, op0=ALU.mult)
                S.pool("tensor_tensor", r=[an, P("tmpP")], w=[an], out=accA[:, cc], in0=accA[:, cc], in1=tmpP[:], op=ALU.add)
        for tp in range(10, 31):
            src = YE[:, cc, :, tp:tp + 128]
            if tp == 10:
                S.dve("tensor_scalar", r=[P("YE"), P("CW")], w=[bn], out=accB[:, cc], in0=src, scalar1=CW[:, cc, tp:tp + 1],
                      scalar2=None, op0=ALU.mult)
            else:
                S.dve("scalar_tensor_tensor", r=[P("YE"), P("CW"), bn], w=[bn], out=accB[:, cc], in0=src, scalar=CW[:, cc, tp:tp + 1],
                      in1=accB[:, cc], op0=ALU.mult, op1=ALU.add)
        S.dve("tensor_tensor", r=[an, bn], w=[an], out=accA[:, cc], in0=accA[:, cc], in1=accB[:, cc], op=ALU.add)
        S.act("activation", r=[an], w=[P("SQ")], out=SQ[:, cc, :], in_=accA[:, cc].rearrange("p m t -> p (m t)"), func=AF.Square)
    for pc in range(4):
        cs = slice(pc * 512, (pc + 1) * 512)
        for cc in range(2):
            S.pe("matmul", r=[P("onesd"), P("accA%d" % cc)], w=[P("ps0")], out=PS[0][:], lhsT=onesd[:],
                 rhs=accA[:, cc].rearrange("p m t -> p (m t)")[:, cs], start=(cc == 0), stop=(cc == 1))
        for cc in range(2):
            S.pe("matmul", r=[P("onesd"), P("SQ")], w=[P("ps1")], out=PS[1][:], lhsT=onesd[:], rhs=SQ[:, cc, cs], start=(cc == 0), stop=(cc == 1))
        S.act("activation", r=[P("ps0")], w=[P("MEAN")], out=MEAN[:, cs], in_=PS[0][:], func=AF.Copy)
        S.act("activation", r=[P("ps1")], w=[P("MSQ")], out=MSQ[:, cs], in_=PS[1][:], func=AF.Copy)
    S.dve("tensor_tensor", r=[P("MEAN")], w=[P("SQ")], out=SQ[:, 0, :], in0=MEAN[:], in1=MEAN[:], op=ALU.mult)
    S.dve("tensor_tensor", r=[P("MSQ"), P("SQ")], w=[P("MSQ")], out=MSQ[:], in0=MSQ[:], in1=SQ[:, 0, :], op=ALU.subtract)
    S.dve("tensor_scalar", r=[P("MSQ")], w=[P("MSQ")], out=MSQ[:], in0=MSQ[:], scalar1=1e-6, scalar2=None, op0=ALU.add)
    S.act("activation", r=[P("MSQ")], w=[P("MSQ")], out=MSQ[:], in_=MSQ[:], func=AF.Sqrt)
    S.dve("reciprocal", r=[P("MSQ")], w=[P("MSQ")], out=MSQ[:], in_=MSQ[:])
    for cc in range(2):
        an = P("accA%d" % cc)
        a2 = accA[:, cc].rearrange("p m t -> p (m t)")
        S.dve("tensor_tensor", r=[an, P("MEAN")], w=[an], out=a2, in0=a2, in1=MEAN[:], op=ALU.subtract)
        S.dve("tensor_tensor", r=[an, P("MSQ")], w=[an], out=a2, in0=a2, in1=MSQ[:], op=ALU.mult)
        S.act("activation", r=[an, P("LG"), P("LB")], w=[P("ACTT")], out=ACTT[:, cc, :], in_=a2, func=AF.Silu, scale=LG[:, cc:cc + 1], bias=LB[:, cc:cc + 1])
    for m in range(16):
        p = m % 2
        k = 2 + p
        S.dma(P("gt%d" % p), w=[P("gtt%d" % p)], out=gtt[p][:], in_=io["gt"][m * 128:(m + 1) * 128, 256:512])
        for cc in range(2):
            S.pe("matmul", r=[P("ACTT"), P("WPW")], w=[P("ps%d" % k)], out=PS[k][:, 0:256], lhsT=ACTT[:, cc, m * 128:(m + 1) * 128], rhs=WPW[:, cc, :],
                 start=(cc == 0), stop=(cc == 1))
        S.act("activation", r=[P("ps%d" % k)], w=[P("ob%d" % p)], out=ob[p][:], in_=PS[k][:, 0:256], func=AF.Copy)
        S.dve("tensor_tensor", r=[P("ob%d" % p), P("gtt%d" % p)], w=["MIXB"], out=MIXB[:, m, 256:512], in0=ob[p][:], in1=gtt[p][:], op=ALU.mult)


def conv_consts(inputs, layer, r):
    cw = inputs["conv_w"][layer]
    col = lambda v: np.ascontiguousarray(v.reshape(2, 128).T, dtype=np.float32)
    ohp = np.zeros((128, 4), np.float32)
    ohp[:, (r - 1) % 4] = 1.0
    return dict(conv_w=np.ascontiguousarray(cw.T.reshape(2, 128, 31).transpose(1, 0, 2), dtype=np.float32),
                conv_b=col(inputs["conv_b"][layer]), ln_g=col(inputs["conv_ln_g"][layer]), ln_b=col(inputs["conv_ln_b"][layer]),
                conv_pw=np.ascontiguousarray(inputs["conv_pw"][layer]), ohprev=ohp)
```

```python
import numpy as np
import ml_dtypes
from contextlib import ExitStack
import concourse.bass as bass
import concourse.mybir as mybir
from concourse.bass_utils import run_bass_kernel_spmd

F32 = mybir.dt.float32
BF16 = mybir.dt.bfloat16
AF = mybir.ActivationFunctionType
ALU = mybir.AluOpType
AX = mybir.AxisListType
NPBF = ml_dtypes.bfloat16

ENGINES = ("pe", "act", "dve", "pool", "sp")


class Buf:
    __slots__ = ("name", "last_w", "readers")

    def __init__(self, name):
        self.name = name
        self.last_w = None
        self.readers = []


class Op:
    __slots__ = ("eng", "fn", "reads", "writes", "dma_key", "deps", "need_inc", "tok", "idx")

    def __init__(self, eng, fn, reads, writes, dma_key):
        self.eng = eng
        self.fn = fn
        self.reads = reads
        self.writes = writes
        self.dma_key = dma_key
        self.deps = []
        self.need_inc = False
        self.tok = None


class Sched:
    def __init__(self, nc, es):
        self.nc = nc
        self.es = es
        self.ops = []
        self.bufs = {}
        self.batch = set()
        self.cnt = {}
        self.sems = {}
        self.waited = {e: {} for e in ENGINES}

    def buf(self, name):
        b = self.bufs.get(name)
        if b is None:
            b = Buf(name)
            self.bufs[name] = b
        return b

    def _norm(self, lst):
        return [self.buf(b) if isinstance(b, str) else b for b in (lst or ())]

    def add(self, eng, meth, kw, reads=(), writes=(), dma_key=None):
        op = Op(eng, (meth, kw), self._norm(reads), self._norm(writes), dma_key)
        op.idx = len(self.ops)
        self.ops.append(op)
        return op

    def pe(self, meth, r=(), w=(), **kw):
        return self.add("pe", meth, kw, r, w)

    def act(self, meth, r=(), w=(), **kw):
        return self.add("act", meth, kw, r, w)

    def dve(self, meth, r=(), w=(), **kw):
        return self.add("dve", meth, kw, r, w)

    def pool(self, meth, r=(), w=(), **kw):
        return self.add("pool", meth, kw, r, w)

    def dma(self, key, r=(), w=(), eng="sp", batch=False, **kw):
        if batch:
            self.batch.add(key)
        return self.add(eng, "dma_start", kw, r, w, dma_key=key)

    def analyze(self):
        for op in self.ops:
            deps = set()
            for b in op.reads:
                if b.last_w is not None:
                    deps.add(b.last_w)
            for b in op.writes:
                if b.last_w is not None:
                    deps.add(b.last_w)
                for r in b.readers:
                    deps.add(r)
            deps.discard(op.idx)
            keep = []
            for d in deps:
                dop = self.ops[d]
                if dop.dma_key is not None and dop.dma_key == op.dma_key and dop.dma_key in self.batch:
                    continue
                if dop.dma_key is None and dop.eng == op.eng:
                    if op.eng == "pe" or op.dma_key is not None:
                        continue
                keep.append(d)
            op.deps = sorted(keep)
            for d in op.deps:
                self.ops[d].need_inc = True
            for b in op.reads:
                b.readers.append(op.idx)
            for b in op.writes:
                b.last_w = op.idx
                b.readers = []

    def flush(self):
        nc = self.nc
        self.analyze()
        per = {e: [op for op in self.ops if op.eng == e] for e in ENGINES}
        for e in ENGINES:
            comp = [op for op in per[e] if op.dma_key is None]
            if comp:
                comp[-1].need_inc = True
        cnt = self.cnt
        for op in self.ops:
            if op.dma_key is not None:
                k = ("dma", op.dma_key)
                cnt[k] = cnt.get(k, 0) + 16
                op.tok = (k, cnt[k])
            elif op.need_inc:
                k = ("eng", op.eng)
                cnt[k] = cnt.get(k, 0) + 1
                op.tok = (k, cnt[k])
        for op in self.ops:
            if op.dma_key is not None and op.dma_key in self.batch:
                op.tok = (op.tok[0], cnt[op.tok[0]])
        for k in sorted(cnt.keys()):
            if k not in self.sems:
                self.sems[k] = self.es.enter_context(nc.semaphore("s_%s_%s" % k))
        sems = self.sems
        ops = self.ops
        totals = dict(cnt)

        def run(eng_name, h):
            waited = self.waited[eng_name]
            for op in per[eng_name]:
                for d in op.deps:
                    k, v = ops[d].tok
                    if waited.get(k, 0) >= v:
                        continue
                    h.wait_ge(sems[k], v)
                    waited[k] = v
                ins = getattr(h, op.fn[0])(**op.fn[1])
                if op.tok is not None:
                    ins.then_inc(sems[op.tok[0]], 16 if op.dma_key is not None else 1)
            for k in sorted(totals.keys()):
                if waited.get(k, 0) < totals[k]:
                    h.wait_ge(sems[k], totals[k])
                    waited[k] = totals[k]

        with nc.Block() as block:
            block.sync(lambda h: run("sp", h))
            block.tensor(lambda h: run("pe", h))
            block.scalar(lambda h: run("act", h))
            block.vector(lambda h: run("dve", h))
            block.gpsimd(lambda h: run("pool", h))
        self.ops = []
        self.bufs = {}


D = 1024
DIN = 3612
NT = 2048
T = 8192
OFF = dict(fq=0, fk=256, fv=512, ff=768, fg=772, glu=1028, cg=1540, nq=1796, nkc=2308, nvc=2436,
           nks=2564, nvs=2692, nkw=2820, nvw=2948, ngl=3076, ng=3100)

A_OUTS = dict(qf=([2, 128, NT], BF16), kf=([2, 128, NT], BF16), yT=([2, 128, NT], BF16), qn=([4, 128, NT], BF16),
              kc=([128, NT], BF16), vc=([128, NT], BF16), ks=([128, NT], BF16), kw=([128, NT], BF16),
              vf=([NT, 320], BF16), vs=([NT, 160], BF16), vw=([NT, 160], BF16), lf=([NT, 4], F32),
              gt=([NT, D], BF16), gl=([NT, 24], F32))


def phase_a(S, nc, es, io, pfx="a"):
    P = lambda n: pfx + n
    sb = lambda n, s, d: es.enter_context(nc.sbuf_tensor(P(n), s, d))
    PS0b = es.enter_context(nc.psum_tensor(P("ps0"), [128, 1024], BF16))
    PS = [None] + [es.enter_context(nc.psum_tensor(P("ps%d" % i), [128, 512], F32)) for i in range(1, 8)]
    Wb = sb("Wb", [128, 8, DIN], BF16)
    wst = [sb("wst%d" % i, [128, DIN], F32) for i in range(2)]
    gcol = sb("gcol", [128, 8], F32)
    fb = sb("fb", [128, 4], F32)
    cosT = sb("cosT", [128, NT], F32)
    sinT = sb("sinT", [128, NT], F32)
    rmat_f = sb("rmat_f", [128, 128], F32)
    rmat = sb("rmat", [128, 128], BF16)
    idf = sb("idf", [128, 128], F32)
    idb = sb("idb", [128, 128], BF16)
    xt = [sb("xt%d" % i, [128, D], F32) for i in range(2)]
    sq = sb("sq", [128, D], BF16)
    ss = [sb("ss%d" % i, [128, 1], F32) for i in range(2)]
    hb = [sb("hb%d" % i, [128, D], BF16) for i in range(2)]
    hT = [sb("hT%d" % i, [128, 8, 512], BF16) for i in range(2)]
    fo = [sb("fo%d" % i, [128, 512], BF16) for i in range(4)]
    zc = [sb("zc%d" % i, [128, 512], BF16) for i in range(2)]
    t1 = [sb("t1%d" % i, [128, 512], F32) for i in range(2)]
    t2 = [sb("t2%d" % i, [128, 512], F32) for i in range(2)]
    sg = [sb("sg%d" % i, [128, 512], F32) for i in range(2)]
    vfo = [sb("vfo%d" % i, [128, 4, 80], BF16) for i in range(2)]
    vso = [sb("vso%d" % i, [128, 2, 80], BF16) for i in range(2)]
    vwo = [sb("vwo%d" % i, [128, 2, 80], BF16) for i in range(2)]
    lfo = [sb("lfo%d" % i, [128, 4], F32) for i in range(2)]
    lft = [sb("lft%d" % i, [128, 4], F32) for i in range(2)]
    glo = [sb("glo%d" % i, [128, 24], F32) for i in range(2)]
    gto = [sb("gto%d" % i, [128, D], BF16) for i in range(2)]
    out_keys = set()

    def store(src_name, dst, src):
        k = P("k_" + src_name)
        out_keys.add(k)
        S.dma(k, r=[P(src_name)], out=dst, in_=src)

    S.dma(P("c"), batch=True, w=[P("gcol")], out=gcol[:], in_=io["norm_g"])
    S.dma(P("c"), batch=True, w=[P("fb")], out=fb[:], in_=io["fox_b"].partition_broadcast(128))
    S.dma(P("c"), batch=True, w=[P("cos")], out=cosT[:], in_=io["cos"])
    S.dma(P("c"), batch=True, w=[P("sin")], out=sinT[:], in_=io["sin"])
    S.dma(P("c"), batch=True, w=[P("rmf")], out=rmat_f[:], in_=io["rmat"])
    S.dma(P("c"), batch=True, w=[P("idf")], out=idf[:], in_=io["ident"])
    S.dve("tensor_copy", r=[P("rmf")], w=[P("rmat")], out=rmat[:], in_=rmat_f[:])
    S.dve("tensor_copy", r=[P("idf")], w=[P("idb")], out=idb[:], in_=idf[:])
    for i in range(2):
        S.dve("memset", w=[P("vfo%d" % i)], ap=vfo[i][:], constant=1.0)
        S.dve("memset", w=[P("vso%d" % i)], ap=vso[i][:], constant=1.0)
        S.dve("memset", w=[P("vwo%d" % i)], ap=vwo[i][:], constant=1.0)
    for c in range(8):
        w = wst[c % 2]
        wn = P("wst%d" % (c % 2))
        S.dma(P("w%d" % (c % 2)), w=[wn], out=w[:], in_=io["w_in"][c * 128:(c + 1) * 128, :])
        eng = S.dve if c % 2 == 0 else S.pool
        eng("tensor_scalar", r=[wn, P("gcol")], w=[P("Wb")], out=Wb[:, c, :], in0=w[:], scalar1=gcol[:, c:c + 1],
            scalar2=None, op0=ALU.mult)

    psrr = [1]

    def next_ps():
        k = psrr[0]
        psrr[0] = 1 + (psrr[0] % 7)
        return k

    foi = [0]
    ri = [0]
    for s in range(4):
        hTs = hT[s % 2]
        hTn = P("hT%d" % (s % 2))
        for q in range(4):
            tn = s * 4 + q
            p = tn % 2
            x_t = xt[p]
            xn = P("xt%d" % p)
            ssn = P("ss%d" % p)
            S.dma(P("x%d" % p), w=[xn], out=x_t[:], in_=io["x"][tn * 128:(tn + 1) * 128, :])
            S.act("activation", r=[xn], w=[P("sq"), ssn], out=sq[:], in_=x_t[:], func=AF.Square, accum_out=ss[p][:, 0:1])
            S.dve("tensor_scalar", r=[ssn], w=[ssn], out=ss[p][:], in0=ss[p][:], scalar1=1.0 / D, scalar2=1e-6,
                  op0=ALU.mult, op1=ALU.add)
            S.act("activation", r=[ssn], w=[ssn], out=ss[p][:], in_=ss[p][:], func=AF.Sqrt)
            S.dve("reciprocal", r=[ssn], w=[ssn], out=ss[p][:], in_=ss[p][:])
            S.dve("tensor_scalar", r=[xn, ssn], w=[P("hb%d" % p)], out=hb[p][:], in0=x_t[:], scalar1=ss[p][:, 0:1],
                  scalar2=None, op0=ALU.mult)
            for half in range(2):
                for cc in range(4):
                    c = half * 4 + cc
                    S.pe("transpose", r=[P("hb%d" % p), P("idb")], w=[P("ps0")], out=PS0b[:, cc * 128:(cc + 1) * 128],
                         in_=hb[p][:, c * 128:(c + 1) * 128], identity=idb[:])
                S.act("activation", r=[P("ps0")], w=[hTn], out=hTs[:, half * 4:(half + 1) * 4, q * 128:(q + 1) * 128],
                      in_=PS0b[:, 0:512].rearrange("p (c t) -> p c t", c=4), func=AF.Copy)
        cols = slice(s * 512, (s + 1) * 512)

        def fm(coff, k):
            for c in range(8):
                S.pe("matmul", r=[P("Wb"), hTn], w=[P("ps%d" % k)], out=PS[k][:, :], lhsT=Wb[:, c, coff:coff + 128],
                     rhs=hTs[:, c, :], start=(c == 0), stop=(c == 7))

        def plain_out(coff, dst):
            k = next_ps()
            fm(coff, k)
            i = foi[0] % 4
            foi[0] += 1
            S.act("activation", r=[P("ps%d" % k)], w=[P("fo%d" % i)], out=fo[i][:], in_=PS[k][:], func=AF.Copy)
            store("fo%d" % i, dst, fo[i][:])

        def rope_out(coff, dst):
            k = next_ps()
            fm(coff, k)
            k2 = next_ps()
            j = ri[0] % 2
            ri[0] += 1
            i = foi[0] % 4
            foi[0] += 1
            S.act("activation", r=[P("ps%d" % k)], w=[P("zc%d" % j)], out=zc[j][:], in_=PS[k][:], func=AF.Copy)
            S.pe("matmul", r=[P("rmat"), P("zc%d" % j)], w=[P("ps%d" % k2)], out=PS[k2][:], lhsT=rmat[:], rhs=zc[j][:],
                 start=True, stop=True)
            S.act("activation", r=[P("ps%d" % k)], w=[P("t1%d" % j)], out=t1[j][:], in_=PS[k][:], func=AF.Copy)
            S.act("activation", r=[P("ps%d" % k2)], w=[P("t2%d" % j)], out=t2[j][:], in_=PS[k2][:], func=AF.Copy)
            S.dve("tensor_tensor", r=[P("t1%d" % j), P("cos")], w=[P("t1%d" % j)], out=t1[j][:], in0=t1[j][:],
                  in1=cosT[:, cols], op=ALU.mult)
            S.pool("tensor_tensor", r=[P("t2%d" % j), P("sin")], w=[P("t2%d" % j)], out=t2[j][:], in0=t2[j][:],
                   in1=sinT[:, cols], op=ALU.mult)
            S.dve("tensor_tensor", r=[P("t1%d" % j), P("t2%d" % j)], w=[P("fo%d" % i)], out=fo[i][:], in0=t1[j][:],
                   in1=t2[j][:], op=ALU.add)
            store("fo%d" % i, dst, fo[i][:])

        for h2 in range(2):
            plain_out(OFF["fq"] + 128 * h2, io["qf"][h2, :, cols])
            plain_out(OFF["fk"] + 128 * h2, io["kf"][h2, :, cols])
        for h2 in range(2):
            ka = next_ps()
            fm(OFF["glu"] + 128 * h2, ka)
            kb = next_ps()
            fm(OFF["glu"] + 256 + 128 * h2, kb)
            i = foi[0] % 4
            foi[0] += 1
            S.act("activation", r=[P("ps%d" % kb)], w=[P("sg%d" % h2)], out=sg[h2][:], in_=PS[kb][:], func=AF.Sigmoid)
            S.act("activation", r=[P("ps%d" % ka)], w=[P("t1%d" % h2)], out=t1[h2][:], in_=PS[ka][:], func=AF.Copy)
            S.dve("tensor_tensor", r=[P("t1%d" % h2), P("sg%d" % h2)], w=[P("fo%d" % i)], out=fo[i][:], in0=t1[h2][:],
                  in1=sg[h2][:], op=ALU.mult)
            store("fo%d" % i, io["yT"][h2, :, cols], fo[i][:])
        for c4 in range(4):
            rope_out(OFF["nq"] + 128 * c4, io["qn"][c4, :, cols])
        plain_out(OFF["nkc"], io["kc"][:, cols])
        plain_out(OFF["nvc"], io["vc"][:, cols])
        rope_out(OFF["nks"], io["ks"][:, cols])
        rope_out(OFF["nkw"], io["kw"][:, cols])

        for q in range(4):
            tn = s * 4 + q
            p = tn % 2
            rows = slice(tn * 128, (tn + 1) * 128)

            def tm(c0, c1, k):
                for c in range(8):
                    S.pe("matmul", r=[P("Wb"), hTn], w=[P("ps%d" % k)], out=PS[k][:, 0:c1 - c0],
                         lhsT=hTs[:, c, q * 128:(q + 1) * 128], rhs=Wb[:, c, c0:c1], start=(c == 0), stop=(c == 7))

            k = next_ps()
            tm(512, 772, k)
            S.act("activation", r=[P("ps%d" % k)], w=[P("vfo%d" % p)], out=vfo[p][:, :, 0:64],
                  in_=PS[k][:, 0:256].rearrange("p (h d) -> p h d", h=4), func=AF.Copy)
            store("vfo%d" % p, io["vf"][rows, :], vfo[p][:].rearrange("p h d -> p (h d)"))
            lt = P("lft%d" % p)
            S.act("activation", r=[P("ps%d" % k)], w=[lt], out=lft[p][:], in_=PS[k][:, 256:260], func=AF.Copy)
            S.dve("tensor_tensor", r=[lt, P("fb")], w=[lt], out=lft[p][:], in0=lft[p][:], in1=fb[:], op=ALU.add)
            S.act("activation", r=[lt], w=[lt], out=lft[p][:], in_=lft[p][:], func=AF.Exp, scale=-1.0)
            S.act("activation", r=[lt], w=[lt], out=lft[p][:], in_=lft[p][:], func=AF.Ln, bias=1.0, scale=1.0)
            S.dve("tensor_scalar", r=[lt], w=[P("lfo%d" % p)], out=lfo[p][:], in0=lft[p][:], scalar1=-1.0, scalar2=None, op0=ALU.mult)
            store("lfo%d" % p, io["lf"][rows, :], lfo[p][:])
            for (c0, g0) in ((772, 0), (1540, 256)):
                k = next_ps()
                tm(c0, c0 + 256, k)
                S.act("activation", r=[P("ps%d" % k)], w=[P("gto%d" % p)], out=gto[p][:, g0:g0 + 256], in_=PS[k][:, 0:256], func=AF.Silu)
            k = next_ps()
            tm(3100, 3612, k)
            S.act("activation", r=[P("ps%d" % k)], w=[P("gto%d" % p)], out=gto[p][:, 512:1024], in_=PS[k][:, 0:512], func=AF.Silu)
            store("gto%d" % p, io["gt"][rows, :], gto[p][:])
            k = next_ps()
            tm(2692, 2820, k)
            S.act("activation", r=[P("ps%d" % k)], w=[P("vso%d" % p)], out=vso[p][:, :, 0:64],
                  in_=PS[k][:, 0:128].rearrange("p (h d) -> p h d", h=2), func=AF.Copy)
            store("vso%d" % p, io["vs"][rows, :], vso[p][:].rearrange("p h d -> p (h d)"))
            k = next_ps()
            tm(2948, 3100, k)
            S.act("activation", r=[P("ps%d" % k)], w=[P("vwo%d" % p)], out=vwo[p][:, :, 0:64],
                  in_=PS[k][:, 0:128].rearrange("p (h d) -> p h d", h=2), func=AF.Copy)
            store("vwo%d" % p, io["vw"][rows, :], vwo[p][:].rearrange("p h d -> p (h d)"))
            S.act("activation", r=[P("ps%d" % k)], w=[P("glo%d" % p)], out=glo[p][:], in_=PS[k][:, 128:152], func=AF.Sigmoid)
            store("glo%d" % p, io["gl"][rows, :], glo[p][:])
    return sorted(out_keys)


def own_positions(r):
    m = np.arange(16)[:, None]
    tl = np.arange(128)[None, :]
    return ((4 * m + r) * 128 + tl).reshape(-1)


def rope_tables(pos):
    inv = 500000.0 ** (-np.arange(0, 16, 2, dtype=np.float32) / 16.0)
    ang = pos.astype(np.float32)[None, :] * np.tile(inv, 2)[:, None].astype(np.float32)
    cos = np.ones((64, pos.size), np.float32)
    sin = np.zeros((64, pos.size), np.float32)
    cos[:16] = np.cos(ang)
    sin[:16] = np.sin(ang)
    return np.tile(cos, (2, 1)), np.tile(sin, (2, 1))


def rope_rmat():
    R = np.zeros((128, 128), np.float32)
    for hh in range(2):
        for d in range(8):
            R[hh * 64 + d + 8, hh * 64 + d] = -1.0
            R[hh * 64 + d, hh * 64 + d + 8] = 1.0
    return R


def build_a():
    nc = bass.Bass("TRN2", target_bir_lowering=False)
    io = {}
    io["x"] = nc.dram_tensor("x", [NT, D], F32, kind="ExternalInput").ap()
    io["w_in"] = nc.dram_tensor("w_in", [D, DIN], F32, kind="ExternalInput").ap()
    io["norm_g"] = nc.dram_tensor("norm_g", [128, 8], F32, kind="ExternalInput").ap()
    io["fox_b"] = nc.dram_tensor("fox_b", [4], F32, kind="ExternalInput").ap()
    io["cos"] = nc.dram_tensor("cos", [128, NT], F32, kind="ExternalInput").ap()
    io["sin"] = nc.dram_tensor("sin", [128, NT], F32, kind="ExternalInput").ap()
    io["rmat"] = nc.dram_tensor("rmat", [128, 128], F32, kind="ExternalInput").ap()
    io["ident"] = nc.dram_tensor("ident", [128, 128], F32, kind="ExternalInput").ap()
    for n, (shp, dt) in A_OUTS.items():
        io[n] = nc.dram_tensor(n, shp, dt, kind="ExternalOutput").ap()
    with ExitStack() as top:
        S = Sched(nc, top)
        with ExitStack() as es:
            phase_a(S, nc, es, io)
            S.flush()
    return nc


def out_scope(S, nc, es, io, MIXB, last, pfx="o"):
    P = lambda n: pfx + n
    sb = lambda n, s_, d: es.enter_context(nc.sbuf_tensor(P(n), s_, d))
    PS = [es.enter_context(nc.psum_tensor(P("ps%d" % i), [128, 512], F32)) for i in range(4)]
    PSTb = es.enter_context(nc.psum_tensor(P("pstb"), [128, 1024], BF16))
    WO = sb("WO", [128, 8, D], BF16)
    wst = [sb("wst%d" % i, [128, D], F32) for i in range(2)]
    idf = sb("idf", [128, 128], F32)
    idb = sb("idb", [128, 128], BF16)
    MXT = [sb("MXT%d" % i, [128, 8, 128], BF16) for i in range(2)]
    xt = [sb("xt%d" % i, [128, D], F32) for i in range(2)]
    yo = [sb("yo%d" % i, [128, D], F32) for i in range(2)]
    sq = sb("sq", [128, D], F32)
    ss = sb("ss", [128, 1], F32)
    fg = sb("fg", [128, D], F32)
    S.dma(P("c"), batch=True, w=[P("idf")], out=idf[:], in_=io["ident"])
    if last:
        S.dma(P("c"), batch=True, w=[P("fg")], out=fg[:], in_=io["final_g"].partition_broadcast(128))
    S.dve("tensor_copy", r=[P("idf")], w=[P("idb")], out=idb[:], in_=idf[:])
    for c in range(8):
        w, wn = wst[c % 2], P("wst%d" % (c % 2))
        S.dma(P("w%d" % (c % 2)), w=[wn], out=w[:], in_=io["w_out"][c * 128:(c + 1) * 128, :])
        (S.dve if c % 2 == 0 else S.pool)("tensor_copy", r=[wn], w=[P("WO")], out=WO[:, c, :], in_=w[:])
    for m in range(16):
        p = m % 2
        mx, mxn, x_t, xn, y_t, yn = MXT[p], P("MXT%d" % p), xt[p], P("xt%d" % p), yo[p], P("yo%d" % p)
        S.dma(P("x%d" % p), w=[xn], out=x_t[:], in_=io["x"][m * 128:(m + 1) * 128, :])
        for c in range(8):
            S.pe("transpose", r=["MIXB", P("idb")], w=[P("pstb")], out=PSTb[:, c * 128:(c + 1) * 128], in_=MIXB[:, m, c * 128:(c + 1) * 128],
                 identity=idb[:])
        S.act("activation", r=[P("pstb")], w=[mxn], out=mx[:].rearrange("f c q -> f (c q)"), in_=PSTb[:], func=AF.Copy)
        for n2 in range(2):
            k = (2 * m + n2) % 4
            for c in range(8):
                S.pe("matmul", r=[mxn, P("WO")], w=[P("ps%d" % k)], out=PS[k][:], lhsT=mx[:, c, :], rhs=WO[:, c, n2 * 512:(n2 + 1) * 512],
                     start=(c == 0), stop=(c == 7))
            S.act("activation", r=[P("ps%d" % k)], w=[yn], out=y_t[:, n2 * 512:(n2 + 1) * 512], in_=PS[k][:], func=AF.Copy)
        S.dve("tensor_tensor", r=[yn, xn], w=[yn], out=y_t[:], in0=y_t[:], in1=x_t[:], op=ALU.add)
        if last:
            S.act("activation", r=[yn], w=[P("sq"), P("ss")], out=sq[:], in_=y_t[:], func=AF.Square, accum_out=ss[:, 0:1])
            S.dve("tensor_scalar", r=[P("ss")], w=[P("ss")], out=ss[:], in0=ss[:], scalar1=1.0 / D, scalar2=1e-6, op0=ALU.mult, op1=ALU.add)
            S.act("activation", r=[P("ss")], w=[P("ss")], out=ss[:], in_=ss[:], func=AF.Sqrt)
            S.dve("reciprocal", r=[P("ss")], w=[P("ss")], out=ss[:], in_=ss[:])
            S.dve("scalar_tensor_tensor", r=[yn, P("ss"), P("fg")], w=[yn], out=y_t[:], in0=y_t[:], scalar=ss[:, 0:1], in1=fg[:],
                  op0=ALU.mult, op1=ALU.mult)
        S.dma(P("o%d" % p), r=[yn], out=io["xo"][m * 128:(m + 1) * 128, :], in_=y_t[:])


B_INS = dict(kf_g=([2, 128, T], BF16), vf_g=([T, 320], BF16), lf_g=([T, 4], F32), qf=([2, 128, NT], BF16),
             kc_g=([128, T], BF16), vc_g=([128, T], BF16), ks_g=([128, T], BF16), kw_g=([128, T], BF16),
             vs_g=([T, 160], BF16), vw_g=([T, 160], BF16), qn=([4, 128, NT], BF16), gl=([NT, 24], F32), gt=([NT, D], BF16),
             yT=([2, 128, NT], BF16), tails_g=([4, 2, 128, 16, 30], BF16), x=([NT, D], F32), w_out=([D, D], F32), final_g=([D], F32),
             tri=([128, 128], F32), ones128=([128, 128], F32), onehot=([128, 4], F32), dmask=([128, 4, 128], BF16),
             wmask=([128, 8, 128], BF16), cmask=([16, 128, 4, 128], BF16), force=([16, 128, 128], F32), ee=([128, T], BF16),
             ident=([128, 128], F32), rmat=([128, 128], F32), cosC=([128, 512], F32), sinC=([128, 512], F32),
             ovl=([128, 4, 128], BF16), peT=([64, 2, 32], F32), w2=([128, 2, 2, 64], F32), w1k=([2048, 256], F32),
             w1v=([2048, 256], F32), conv_w=([128, 2, 31], F32), conv_b=([128, 2], F32), ln_g=([128, 2], F32), ln_b=([128, 2], F32),
             conv_pw=([256, 256], F32), ohprev=([128, 4], F32))


def phase_b(S, nc, top_unused, io, last):
  with ExitStack() as top:
    KCMP = top.enter_context(nc.sbuf_tensor("KCMP", [128, 512], BF16))
    VCA = top.enter_context(nc.sbuf_tensor("VCA", [128, 4, 2, 208], BF16))
    MIXB = top.enter_context(nc.sbuf_tensor("MIXB", [128, 16, D], BF16))
    for scope in (lambda es: cmp_scope(S, nc, es, io, KCMP, VCA), lambda es: conv_scope(S, nc, es, io, MIXB),
                  lambda es: fox_scope(S, nc, es, io, MIXB), lambda es: nsa_scope(S, nc, es, io, KCMP, VCA, MIXB),
                  lambda es: out_scope(S, nc, es, io, MIXB, last)):
        with ExitStack() as es:
            scope(es)
            S.flush()


def build_b(last):
    nc = bass.Bass("TRN2", target_bir_lowering=False)
    io = {n: nc.dram_tensor(n, shp, dt, kind="ExternalInput").ap() for n, (shp, dt) in B_INS.items()}
    io["xo"] = nc.dram_tensor("xo", [NT, D], F32, kind="ExternalOutput").ap()
    with ExitStack() as top:
        S = Sched(nc, top)
        phase_b(S, nc, top, io, last)
    return nc


def build_ba():
    nc = bass.Bass("TRN2", target_bir_lowering=False)
    io = {n: nc.dram_tensor(n, shp, dt, kind="ExternalInput").ap() for n, (shp, dt) in B_INS.items()}
    io["xo"] = nc.dram_tensor("xo", [NT, D], F32, kind="ExternalOutput").ap()
    ioa = dict(x=io["xo"], rmat=io["rmat"], ident=io["ident"])
    ioa["w_in"] = nc.dram_tensor("a_w_in", [D, DIN], F32, kind="ExternalInput").ap()
    ioa["norm_g"] = nc.dram_tensor("a_norm_g", [128, 8], F32, kind="ExternalInput").ap()
    ioa["fox_b"] = nc.dram_tensor("a_fox_b", [4], F32, kind="ExternalInput").ap()
    ioa["cos"] = nc.dram_tensor("a_cos", [128, NT], F32, kind="ExternalInput").ap()
    ioa["sin"] = nc.dram_tensor("a_sin", [128, NT], F32, kind="ExternalInput").ap()
    for n, (shp, dt) in A_OUTS.items():
        ioa[n] = nc.dram_tensor("a_" + n, shp, dt, kind="ExternalOutput").ap()
    with ExitStack() as top:
        S = Sched(nc, top)
        phase_b(S, nc, top, io, False)
        with ExitStack() as es:
            phase_a(S, nc, es, ioa, pfx="A")
            S.flush()
    return nc


def a_inputs(inputs, layer, x_own):
    maps = []
    for core in range(8):
        cos, sin = rope_tables(own_positions(core % 4))
        maps.append(dict(x=np.ascontiguousarray(x_own[core], dtype=np.float32), w_in=np.ascontiguousarray(inputs["w_in"][layer]),
                         norm_g=np.ascontiguousarray(inputs["norm_g"][layer].reshape(8, 128).T), fox_b=np.ascontiguousarray(inputs["fox_b"][layer]),
                         cos=cos, sin=sin, rmat=rope_rmat(), ident=np.eye(128, dtype=np.float32)))
    return maps


def b_inputs(inputs, layer, resA, x_own):
    maps = []
    cc = common_consts()
    ee = ee_const()
    cmpc = cmp_consts(inputs, layer)
    for core in range(8):
        b, r = core // 4, core % 4
        R = [resA[4 * b + rr] for rr in range(4)]
        cat = lambda n, ax: np.concatenate([np.asarray(R[rr][n]) for rr in range(4)], axis=ax)
        d = dict(kf_g=cat("kf", 2), vf_g=cat("vf", 0), lf_g=cat("lf", 0), kc_g=cat("kc", 1), vc_g=cat("vc", 1), ks_g=cat("ks", 1),
                 kw_g=cat("kw", 1), vs_g=cat("vs", 0), vw_g=cat("vw", 0))
        d["tails_g"] = np.ascontiguousarray(np.stack([np.asarray(R[rr]["yT"]).reshape(2, 128, 16, 128)[:, :, :, 98:128] for rr in range(4)], axis=0))
        own = resA[core]
        for n in ("qf", "qn", "gl", "gt", "yT"):
            d[n] = np.asarray(own[n])
        d["x"] = np.ascontiguousarray(x_own[core], dtype=np.float32)
        d["w_out"] = np.ascontiguousarray(inputs["w_out"][layer])
        d["final_g"] = np.ascontiguousarray(inputs["final_g"])
        d.update(cc)
        d.update(rank_consts(r))
        d.update(nsa_rank_consts(r))
        d.update(cmpc)
        d.update(conv_consts(inputs, layer, r))
        d["ee"] = ee
        d["ident"] = np.eye(128, dtype=np.float32)
        maps.append(d)
    return maps


def kernel(**inputs):
    inputs = {k: np.asarray(v) for k, v in inputs.items()}
    x = inputs["x"].astype(np.float32)
    x_own = [x[core // 4][own_positions(core % 4)] for core in range(8)]
    cores = list(range(8))
    resA0 = run_bass_kernel_spmd(build_a(), a_inputs(inputs, 0, x_own), core_ids=cores).results
    maps = b_inputs(inputs, 0, resA0, x_own)
    a1 = a_inputs(inputs, 1, x_own)
    for core in cores:
        for k in ("w_in", "norm_g", "fox_b", "cos", "sin"):
            maps[core]["a_" + k] = a1[core][k]
    resBA = run_bass_kernel_spmd(build_ba(), maps, core_ids=cores).results
    x1_own = [np.asarray(resBA[core]["xo"]) for core in cores]
    resA1 = [{k[2:]: v for k, v in resBA[core].items() if k.startswith("a_")} for core in cores]
    resB1 = run_bass_kernel_spmd(build_b(last=True), b_inputs(inputs, 1, resA1, x1_own), core_ids=cores).results
    out = np.empty((2, T, D), np.float32)
    for core in cores:
        out[core // 4][own_positions(core % 4)] = np.asarray(resB1[core]["xo"])
    return out


def cidx(j):
    return (j % 4) * 16 + j // 4


def gcol(j):
    return cidx(j) * 128


def fox_scope(S, nc, es, io, MIXB, pfx="f"):
    P = lambda n: pfx + n
    sb = lambda n, s, d: es.enter_context(nc.sbuf_tensor(P(n), s, d))
    PS = [es.enter_context(nc.psum_tensor(P("ps%d" % i), [128, 512], F32)) for i in range(4)]
    KF = sb("KF", [128, 2, T], BF16)
    VF = sb("VF", [128, 64, 320], BF16)
    QF = sb("QF", [128, 4, NT], BF16)
    LF = sb("LF", [128, 64, 4], F32)
    tri = sb("tri", [128, 128], F32)
    ones = sb("ones", [128, 128], F32)
    oneh = sb("oneh", [128, 4], F32)
    dm = sb("dm", [128, 4, 128], BF16)
    dm4 = sb("dm4", [128, 4, 4, 128], BF16)
    WI = sb("WI", [128, 64, 4], F32)
    TOT = sb("TOT", [128, 64, 4], F32)
    SA = sb("SA", [128, 64, 4], F32)
    SB_ = sb("SB", [128, 64, 4], F32)
    NC_ = sb("NC", [128, 64, 4], F32)
    ROWN = sb("ROWN", [128, 16, 4], F32)
    CB = [sb("CB%d" % i, [128, 64, 4], F32) for i in range(2)]
    PT = [sb("PT%d" % i, [128, 512], BF16) for i in range(3)]
    ofs = sb("ofs", [128, 4, 65], F32)
    rden = sb("rden", [128, 4], F32)
    onrm = sb("onrm", [128, 256], F32)
    gtt = [sb("gtt%d" % i, [128, 256], BF16) for i in range(2)]

    ld = P("ld")
    S.pool("memset", w=[P("QF")], ap=QF[:].rearrange("p h t -> p (h t)"), constant=0.0)
    for h2 in range(2):
        S.dma(ld, batch=True, w=[P("KF")], out=KF[:, h2, :], in_=io["kf_g"][h2])
    for h in range(4):
        pb = 64 * (h % 2)
        S.dma(ld, batch=True, w=[P("QF")], out=QF[pb:pb + 64, h, :], in_=io["qf"][h // 2, pb:pb + 64, :])
    for r in range(4):
        S.dma(ld, batch=True, w=[P("VF")], out=VF[:, r * 16:(r + 1) * 16, :],
              in_=io["vf_g"][r * NT:(r + 1) * NT, :].rearrange("(c t) w -> t c w", t=128))
    for r in range(4):
        S.dma(ld, batch=True, w=[P("LF")], out=LF[:].rearrange("t (m r) h -> t m r h", r=4)[:, :, r, :],
              in_=io["lf_g"][r * NT:(r + 1) * NT, :].rearrange("(m t) h -> t m h", t=128))
    S.dma(ld, batch=True, w=[P("tri")], out=tri[:], in_=io["tri"])
    S.dma(ld, batch=True, w=[P("ones")], out=ones[:], in_=io["ones128"])
    S.dma(ld, batch=True, w=[P("oneh")], out=oneh[:], in_=io["onehot"])
    S.dma(ld, batch=True, w=[P("dm")], out=dm[:], in_=io["dmask"])
    for h in range(4):
        S.pool("tensor_copy", r=[P("dm")], w=[P("dm4")], out=dm4[:, :, h, :], in_=dm[:])

    LFf = LF[:].rearrange("t j h -> t (j h)")
    S.pe("matmul", r=[P("tri"), P("LF")], w=[P("ps0")], out=PS[0][:, 0:256], lhsT=tri[:], rhs=LFf, start=True, stop=True)
    S.pe("matmul", r=[P("ones"), P("LF")], w=[P("ps1")], out=PS[1][:, 0:256], lhsT=ones[:], rhs=LFf, start=True, stop=True)
    S.act("activation", r=[P("ps0")], w=[P("WI")], out=WI[:].rearrange("t j h -> t (j h)"), in_=PS[0][:, 0:256], func=AF.Copy)
    S.act("activation", r=[P("ps1")], w=[P("TOT")], out=TOT[:].rearrange("t j h -> t (j h)"), in_=PS[1][:, 0:256], func=AF.Copy)
    src, srcn = TOT, P("TOT")
    d = 1
    pp = [(SA, P("SA")), (SB_, P("SB"))]
    k = 0
    while d < 64:
        dst, dstn = pp[k % 2]
        S.dve("tensor_tensor", r=[srcn], w=[dstn], out=dst[:, d:, :], in0=src[:, d:, :], in1=src[:, :64 - d, :], op=ALU.add)
        S.dve("tensor_copy", r=[srcn], w=[dstn], out=dst[:, :d, :], in_=src[:, :d, :])
        src, srcn = dst, dstn
        d *= 2
        k += 1
    INCL, INCLn = src, srcn
    S.dve("tensor_tensor", r=[P("WI"), INCLn], w=[P("NC")], out=NC_[:], in0=WI[:], in1=INCL[:], op=ALU.add)
    S.dve("tensor_tensor", r=[P("NC"), P("TOT")], w=[P("NC")], out=NC_[:], in0=TOT[:], in1=NC_[:], op=ALU.subtract)
    I4 = INCL[:].rearrange("t (m r) h -> t m r h", r=4)
    S.dve("tensor_scalar", r=[INCLn, P("oneh")], w=[P("ROWN")], out=ROWN[:], in0=I4[:, :, 0, :], scalar1=oneh[:, 0:1],
          scalar2=None, op0=ALU.mult)
    for r in range(1, 4):
        S.dve("scalar_tensor_tensor", r=[INCLn, P("oneh"), P("ROWN")], w=[P("ROWN")], out=ROWN[:], in0=I4[:, :, r, :],
              scalar=oneh[:, r:r + 1], in1=ROWN[:], op0=ALU.mult, op1=ALU.add)

    st_i = [0]
    for m in range(16):
        J = 4 * m + 4
        cb, cbn = CB[m % 2], P("CB%d" % (m % 2))
        for h in range(4):
            S.dve("tensor_scalar", r=[P("NC"), P("ROWN")], w=[cbn + "_%d" % h], out=cb[:, 0:J, h], in0=NC_[:, 0:J, h],
                  scalar1=ROWN[:, m, h:h + 1], scalar2=0.0, op0=ALU.add, op1=ALU.min)
        for j in range(J):
            sp = st_i[0] % 2
            pt, ptn = PT[st_i[0] % 3], P("PT%d" % (st_i[0] % 3))
            st_i[0] += 1
            psn = P("ps%d" % sp)
            for h in range(4):
                S.pe("matmul", r=[P("KF"), P("QF")], w=[psn], out=PS[sp][:, h * 128:(h + 1) * 128],
                     lhsT=KF[:, h // 2, gcol(j):gcol(j) + 128], rhs=QF[:, h, m * 128:(m + 1) * 128], start=True, stop=True)
            pth = [ptn + "_%d" % h for h in range(4)]
            for h in range(4):
                S.act("activation", r=[psn, cbn + "_%d" % h], w=[pth[h]], out=pt[:, h * 128:(h + 1) * 128], in_=PS[sp][:, h * 128:(h + 1) * 128],
                      func=AF.Exp, bias=cb[:, j, h:h + 1], scale=0.125)
            if j >= 4 * m:
                S.dve("tensor_tensor", r=pth + [P("dm4")], w=pth, out=pt[:], in0=pt[:],
                      in1=dm4[:, j - 4 * m, :, :].rearrange("k h q -> k (h q)"), op=ALU.mult)
            for h in range(4):
                S.pe("matmul", r=[pth[h], P("VF")], w=[P("ps2")], out=PS[2][:, h * 65:(h + 1) * 65], lhsT=pt[:, h * 128:(h + 1) * 128],
                     rhs=VF[:, cidx(j), h * 80:h * 80 + 65], start=(j == 0 and h == 0), stop=(j == J - 1 and h == 3))
        S.act("activation", r=[P("ps2")], w=[P("ofs")], out=ofs[:].rearrange("q h d -> q (h d)"), in_=PS[2][:, 0:260], func=AF.Copy)
        S.dve("reciprocal", r=[P("ofs")], w=[P("rden")], out=rden[:], in_=ofs[:, :, 64])
        for h in range(4):
            S.dve("tensor_scalar", r=[P("ofs"), P("rden")], w=[P("onrm%d" % h)], out=onrm[:, h * 64:(h + 1) * 64], in0=ofs[:, h, 0:64],
                  scalar1=rden[:, h:h + 1], scalar2=None, op0=ALU.mult)
        gp = m % 2
        S.dma(P("gt%d" % gp), w=[P("gtt%d" % gp)], out=gtt[gp][:], in_=io["gt"][m * 128:(m + 1) * 128, 0:256])
        S.dve("tensor_tensor", r=[P("onrm%d" % h) for h in range(4)] + [P("gtt%d" % gp)], w=["MIXB"], out=MIXB[:, m, 0:256], in0=onrm[:],
              in1=gtt[gp][:], op=ALU.mult)


def rank_consts(r):
    k = np.arange(128)[:, None]
    q = np.arange(128)[None, :]
    dmask = np.zeros((128, 4, 128), np.float32)
    for c in range(4):
        if c < r:
            dmask[:, c, :] = 1.0
        elif c == r:
            dmask[:, c, :] = (k <= q)
    onehot = np.zeros((128, 4), np.float32)
    onehot[:, r] = 1.0
    return dict(dmask=dmask.astype(NPBF), onehot=onehot)


def common_consts():
    k = np.arange(128)[:, None]
    q = np.arange(128)[None, :]
    return dict(tri=(k <= q).astype(np.float32), ones128=np.ones((128, 128), np.float32))


def cmp_scope(S, nc, es, io, KCMP, VCA, pfx="c"):
    P = lambda n: pfx + n
    sb = lambda n, s, d: es.enter_context(nc.sbuf_tensor(P(n), s, d))
    PS = [es.enter_context(nc.psum_tensor(P("ps%d" % i), [128, 512], F32)) for i in range(4)]
    KCN = [sb("KCN%d" % i, [128, T], BF16) for i in range(2)]
    W1st = sb("W1st", [128, 32, 256], F32)
    W1Z = [sb("W1Z%d" % g, [128, 32, 256], BF16) for g in range(2)]
    PEf = sb("PEf", [128, 2, 32], F32)
    PEB = sb("PEB", [128, 2, 32], BF16)
    W2f = sb("W2f", [128, 2, 2, 64], F32)
    W2KZ = sb("W2KZ", [128, 2, 2, 128], BF16)
    W2V = sb("W2V", [128, 2, 64], BF16)
    BH = sb("BH", [128, 2], F32)
    HID = [[sb("HID%d%d" % (kv, g), [128, 2, 512], BF16) for g in range(2)] for kv in range(2)]
    X = sb("X", [128, 512], F32)
    X2 = sb("X2", [128, 512], F32)
    U = sb("U", [128, 512], F32)
    SG = sb("SG", [128, 512], F32)
    KC0 = sb("KC0", [128, 512], F32)
    KZ = sb("KZ", [128, 512], BF16)
    RT = sb("RT", [128, 512], F32)
    cosC = sb("cosC", [128, 512], F32)
    sinC = sb("sinC", [128, 512], F32)
    rmf = sb("rmf", [128, 128], F32)
    rmb = sb("rmb", [128, 128], BF16)
    ovf = sb("ovf", [128, 4, 128], BF16)

    ld = P("ld")
    for kv, nm in ((0, "kc_g"), (1, "vc_g")):
        for r in range(4):
            S.dma(ld, batch=True, w=[P("KCN%d" % kv)], out=KCN[kv][:].rearrange("d (m r t) -> d m r t", r=4, t=128)[:, :, r, :],
                  in_=io[nm][:, r * NT:(r + 1) * NT].rearrange("d (m t) -> d m t", t=128))
    S.dma(ld, batch=True, w=[P("PEf")], out=PEf[0:64, :, :], in_=io["peT"])
    S.dma(ld, batch=True, w=[P("PEf")], out=PEf[64:128, :, :], in_=io["peT"])
    S.dma(ld, batch=True, w=[P("W2f")], out=W2f[:], in_=io["w2"])
    S.dma(ld, batch=True, w=[P("cosC")], out=cosC[:], in_=io["cosC"])
    S.dma(ld, batch=True, w=[P("sinC")], out=sinC[:], in_=io["sinC"])
    S.dma(ld, batch=True, w=[P("rmf")], out=rmf[:], in_=io["rmat"])
    S.dma(ld, batch=True, w=[P("ovf")], out=ovf[:], in_=io["ovl"])
    S.dve("tensor_copy", r=[P("PEf")], w=[P("PEB")], out=PEB[:], in_=PEf[:])
    S.dve("tensor_copy", r=[P("rmf")], w=[P("rmb")], out=rmb[:], in_=rmf[:])
    S.pool("memset", w=[P("W2KZ")], ap=W2KZ[:], constant=0.0)
    for g in range(2):
        S.pool("memset", w=[P("W1Z%d" % g)], ap=W1Z[g][:], constant=0.0)
        S.dve("tensor_copy", r=[P("W2f"), P("W2KZ")], w=[P("W2KZ")], out=W2KZ[:, g, :, 64 * g:64 * g + 64], in_=W2f[:, 0, :, :])
        for kv in range(2):
            S.pool("memset", w=[P("HID%d%d" % (kv, g))], ap=HID[kv][g][:], constant=0.0)
    S.dve("tensor_copy", r=[P("W2f")], w=[P("W2V")], out=W2V[:], in_=W2f[:, 1, :, :])
    S.pool("memset", w=[P("KC0")], ap=KC0[:], constant=0.0)
    S.pool("memset", w=["VCA"], ap=VCA[:], constant=1.0)
    for g in range(2):
        S.pool("tensor_copy", r=[P("ovf"), "VCA"], w=["VCA"], out=VCA[:, :, g, 80:208], in_=ovf[:])

    for kv, wn in ((0, "w1k"), (1, "w1v")):
        for half in range(2):
            S.dma(P("w1"), w=[P("W1st")], out=W1st[64 * half:64 * half + 64, :, :],
                  in_=io[wn].rearrange("(l d) h -> d l h", d=64))
        for g in range(2):
            S.dve("tensor_copy", r=[P("W1st"), P("W1Z%d" % g)], w=[P("W1Z%d" % g)], out=W1Z[g][64 * g:64 * g + 64, :, :],
                  in_=W1st[64 * g:64 * g + 64, :, :])
        for hc in range(2):
            for l in range(32):
                S.pe("matmul", r=[P("W1Z0"), P("PEB")], w=[P("ps3")], out=PS[3][:, hc:hc + 1], lhsT=W1Z[0][:, l, hc * 128:(hc + 1) * 128],
                     rhs=PEB[:, kv, l:l + 1], start=(hc == 0 and l == 0), stop=(hc == 1 and l == 31))
        S.act("activation", r=[P("ps3")], w=[P("BH")], out=BH[:], in_=PS[3][:, 0:2], func=AF.Copy)
        for g in range(2):
            for hc in range(2):
                k = (g * 2 + hc) % 3
                psn = P("ps%d" % k)
                for l in range(32):
                    S.pe("matmul", r=[P("W1Z%d" % g), P("KCN%d" % kv)], w=[psn], out=PS[k][:, 0:511],
                         lhsT=W1Z[g][:, l, hc * 128:(hc + 1) * 128], rhs=KCN[kv][:, l:l + 16 * 510 + 1:16], start=(l == 0), stop=(l == 31))
                S.act("activation", r=[psn, P("BH")], w=[P("X")], out=X[:, 0:511], in_=PS[k][:, 0:511], func=AF.Identity,
                      bias=BH[:, hc:hc + 1], scale=1.0)
                S.dve("tensor_tensor", r=[P("X")], w=[P("X2")], out=X2[:, 0:511], in0=X[:, 0:511], in1=X[:, 0:511], op=ALU.mult)
                S.dve("tensor_scalar", r=[P("X2")], w=[P("X2")], out=X2[:, 0:511], in0=X2[:, 0:511], scalar1=0.044715, scalar2=1.0,
                      op0=ALU.mult, op1=ALU.add)
                S.dve("tensor_tensor", r=[P("X2"), P("X")], w=[P("U")], out=U[:, 0:511], in0=X2[:, 0:511], in1=X[:, 0:511], op=ALU.mult)
                S.act("activation", r=[P("U")], w=[P("SG")], out=SG[:, 0:511], in_=U[:, 0:511], func=AF.Sigmoid, scale=1.5957691216057308)
                S.dve("tensor_tensor", r=[P("X"), P("SG")], w=[P("HID%d%d" % (kv, g))], out=HID[kv][g][:, hc, 0:511], in0=X[:, 0:511],
                      in1=SG[:, 0:511], op=ALU.mult)
    n = 0
    for g in range(2):
        for hc in range(2):
            S.pe("matmul", r=[P("W2KZ"), P("HID0%d" % g)], w=[P("ps0")], out=PS[0][:, 0:511], lhsT=W2KZ[:, g, hc, :],
                 rhs=HID[0][g][:, hc, 0:511], start=(n == 0), stop=(n == 3))
            n += 1
    S.act("activation", r=[P("ps0")], w=[P("KC0")], out=KC0[:, 0:511], in_=PS[0][:, 0:511], func=AF.Copy)
    S.dve("tensor_copy", r=[P("KC0")], w=[P("KZ")], out=KZ[:], in_=KC0[:])
    S.pe("matmul", r=[P("rmb"), P("KZ")], w=[P("ps1")], out=PS[1][:], lhsT=rmb[:], rhs=KZ[:], start=True, stop=True)
    S.act("activation", r=[P("ps1")], w=[P("RT")], out=RT[:], in_=PS[1][:], func=AF.Copy)
    S.dve("tensor_tensor", r=[P("KC0"), P("cosC")], w=[P("KC0")], out=KC0[:], in0=KC0[:], in1=cosC[:], op=ALU.mult)
    S.dve("tensor_tensor", r=[P("RT"), P("sinC")], w=[P("RT")], out=RT[:], in0=RT[:], in1=sinC[:], op=ALU.mult)
    S.dve("tensor_tensor", r=[P("KC0"), P("RT")], w=["KCMP"], out=KCMP[:], in0=KC0[:], in1=RT[:], op=ALU.add)
    for c in range(4):
        for g in range(2):
            k = 2 + (c * 2 + g) % 2
            for hc in range(2):
                S.pe("matmul", r=[P("HID1%d" % g), P("W2V")], w=[P("ps%d" % k)], out=PS[k][:, 0:64], lhsT=HID[1][g][:, hc, c * 128:(c + 1) * 128],
                     rhs=W2V[:, hc, :], start=(hc == 0), stop=(hc == 1))
            S.act("activation", r=[P("ps%d" % k), "VCA"], w=["VCA"], out=VCA[:, c, g, 0:64], in_=PS[k][:, 0:64], func=AF.Copy)


def cmp_consts(inputs, layer):
    n = np.arange(512)
    cosC, sinC = rope_tables(16 * n + 31)
    nn = np.arange(512)[:, None]
    bb = np.arange(128)[None, :]
    ovl = ((16 * nn < 64 * bb + 64) & (16 * nn + 31 >= 64 * bb) & (nn < 511)).astype(np.float32)
    peT = np.stack([inputs["cmp_pe_k"][layer].T, inputs["cmp_pe_v"][layer].T], axis=1)
    w2 = np.stack([inputs["cmp_k_w2"][layer].reshape(2, 128, 64), inputs["cmp_v_w2"][layer].reshape(2, 128, 64)], axis=0)
    w2 = np.ascontiguousarray(w2.transpose(2, 0, 1, 3))
    return dict(cosC=cosC, sinC=sinC, ovl=np.ascontiguousarray(ovl.reshape(4, 128, 128).transpose(1, 0, 2)).astype(NPBF),
                peT=np.ascontiguousarray(peT, dtype=np.float32), w2=w2.astype(np.float32),
                w1k=np.ascontiguousarray(inputs["cmp_k_w1"][layer]), w1v=np.ascontiguousarray(inputs["cmp_v_w1"][layer]),
                rmat=rope_rmat())


def nsa_scope(S, nc, es, io, KCMP, VCA, MIXB, pfx="n"):
    P = lambda n: pfx + n
    sb = lambda n, s, d: es.enter_context(nc.sbuf_tensor(P(n), s, d))
    PS = [es.enter_context(nc.psum_tensor(P("ps%d" % i), [128, 512], F32)) for i in range(6)]
    PSTb = es.enter_context(nc.psum_tensor(P("pstb"), [128, 1024], BF16))
    KS = sb("KS", [128, T], BF16)
    VS = sb("VS", [128, 64, 160], BF16)
    EE = sb("EE", [128, T], BF16)
    QNZ = sb("QNZ", [128, 2, 4, NT], BF16)
    KWm = [sb("KWm%d" % i, [128, 8, 128], BF16) for i in range(2)]
    VWm = [sb("VWm%d" % i, [128, 8, 160], BF16) for i in range(2)]
    dm = sb("dm", [128, 4, 128], BF16)
    wm = sb("wm", [128, 8, 128], BF16)
    cm = [sb("cm%d" % i, [128, 4, 128], BF16) for i in range(2)]
    fc = [sb("fc%d" % i, [128, 128], F32) for i in range(2)]
    glt = [sb("glt%d" % i, [128, 24], F32) for i in range(2)]
    gtt = [sb("gtt%d" % i, [128, 512], BF16) for i in range(2)]
    idf = sb("idf", [128, 128], F32)
    idb = sb("idb", [128, 128], BF16)
    PT = [sb("PT%d" % i, [128, 512], BF16) for i in range(3)]
    ocs = sb("ocs", [128, 4, 208], F32)
    oss = sb("oss", [128, 4, 65], F32)
    ows = sb("ows", [128, 4, 65], F32)
    rd = sb("rd", [128, 3, 4], F32)
    fac = sb("fac", [128, 3, 4], F32)
    imp = sb("imp", [128, 128], F32)
    wk = sb("wk", [128, 128], F32)
    mx = sb("mx", [128, 8], F32)
    MBq = sb("MBq", [128, 128], BF16)
    MBT = sb("MBT", [128, 4, 128], BF16)
    acc = sb("acc", [128, 4, 64], F32)
    tmp = sb("tmp", [128, 4, 64], F32)

    def bc(ap, h=4):
        return ap.unsqueeze(1).to_broadcast([128, h, ap.shape[-1]])

    ld = P("ld")
    S.dma(ld, batch=True, w=[P("KS")], out=KS[:], in_=io["ks_g"])
    S.dma(ld, batch=True, w=[P("EE")], out=EE[:], in_=io["ee"])
    for r in range(4):
        S.dma(ld, batch=True, w=[P("VS")], out=VS[:, r * 16:(r + 1) * 16, :],
              in_=io["vs_g"][r * NT:(r + 1) * NT, :].rearrange("(c t) w -> t c w", t=128))
    S.pool("memset", w=[P("QNZ")], ap=QNZ[:].rearrange("p g h t -> p (g h t)"), constant=0.0)
    for hh in range(8):
        g, h = hh // 4, hh % 4
        src = 64 * (hh % 2)
        S.dma(ld, batch=True, w=[P("QNZ")], out=QNZ[64 * g:64 * g + 64, g, h, :], in_=io["qn"][hh // 2, src:src + 64, :])
    S.dma(ld, batch=True, w=[P("dm")], out=dm[:], in_=io["dmask"])
    S.dma(ld, batch=True, w=[P("wm")], out=wm[:], in_=io["wmask"])
    S.dma(ld, batch=True, w=[P("idf")], out=idf[:], in_=io["ident"])
    S.dve("tensor_copy", r=[P("idf")], w=[P("idb")], out=idb[:], in_=idf[:])

    units = []
    load_fns = []

    def add_unit(**kw):
        units.append(kw)

    def emit_s(i, u):
        sp = i % 2
        psn = P("ps%d" % sp)
        for fn in u.get("pre", ()):
            fn()
        S.pe("matmul", r=list(u["rk"]) + [P("QNZ")], w=[psn], out=PS[sp][:], lhsT=u["lhsT"], rhs=u["qrhs"], start=True,
             stop=(u.get("bias_mm") is None))
        if u.get("bias_mm") is not None:
            S.pe("matmul", r=[P("EE"), P("MBT")], w=[psn], out=PS[sp][:], lhsT=u["bias_mm"], rhs=MBT[:].rearrange("b h q -> b (h q)"),
                 start=False, stop=True)

    def emit_rest(i, u):
        sp = i % 2
        psn = P("ps%d" % sp)
        pt, ptn = PT[i % 3], P("PT%d" % (i % 3))
        S.act("activation", r=[psn], w=[ptn], out=pt[:], in_=PS[sp][:], func=AF.Exp, scale=0.125)
        if u.get("mask") is not None:
            mk, mn = u["mask"]
            S.pool("tensor_tensor", r=[ptn, mn], w=[ptn], out=pt[:].rearrange("k (h q) -> k h q", h=4),
                   in0=pt[:].rearrange("k (h q) -> k h q", h=4), in1=bc(mk), op=ALU.mult)
        obank, ocols, first, last = u["obank"], u["ocols"], u["first"], u["last"]
        for h in range(4):
            ob, oc = obank(h), ocols(h)
            S.pe("matmul", r=[ptn] + list(u["vk"]), w=[P("ps%d" % ob)], out=PS[ob][:, oc[0]:oc[1]], lhsT=pt[:, h * 128:(h + 1) * 128],
                 rhs=u["vrhs"], start=(first and (h == 0 or (ob != obank(0) and h == 2))),
                 stop=(last and (h == 3 or (ob != obank(3) and h == 1))))
        for fn in u.get("post", ()):
            fn()

    for m in range(16):
        p = m % 2
        cmn, fcn, gln, gtn, kwn, vwn = P("cm%d" % p), P("fc%d" % p), P("glt%d" % p), P("gtt%d" % p), P("KWm%d" % p), P("VWm%d" % p)

        def loads(m=m, p=p, cmn=cmn, fcn=fcn, gln=gln, gtn=gtn, kwn=kwn, vwn=vwn):
            S.dma(P("cm%d" % p), w=[cmn], out=cm[p][:], in_=io["cmask"][m])
            S.dma(P("fc%d" % p), w=[fcn], out=fc[p][:], in_=io["force"][m])
            S.dma(P("gl%d" % p), w=[gln], out=glt[p][:], in_=io["gl"][m * 128:(m + 1) * 128, :])
            S.dma(P("gt%d" % p), w=[gtn], out=gtt[p][:], in_=io["gt"][m * 128:(m + 1) * 128, 512:1024])
            for mm in ((m - 1, m) if m > 0 else (m,)):
                s0 = 4 * (mm - m + 1)
                S.dma(P("kw%d" % p), w=[kwn], out=KWm[p][:, s0:s0 + 4, :],
                      in_=io["kw_g"].rearrange("d (r c t) -> d r c t", r=4, t=128)[:, :, mm, :])
                S.dma(P("vw%d" % p), w=[vwn], out=VWm[p][:, s0:s0 + 4, :],
                      in_=io["vw_g"].rearrange("(r c t) w -> t r c w", r=4, t=128)[:, :, mm, :])

        load_fns.append(loads)
    load_fns[0]()
    for m in range(16):
        p = m % 2
        cmn, fcn, gln, gtn, kwn, vwn = P("cm%d" % p), P("fc%d" % p), P("glt%d" % p), P("gtt%d" % p), P("KWm%d" % p), P("VWm%d" % p)
        nxt_loads = load_fns[m + 1] if m + 1 < 16 else None
        qs = slice(m * 128, (m + 1) * 128)
        for g in range(2):
            qrhs = QNZ[:, g, :, qs]

            def cmp_post(p=p, fcn=fcn):
                S.act("activation", r=[P("ps2")], w=[P("ocs")], out=ocs[:, 0:2, :].rearrange("q h w -> q (h w)"), in_=PS[2][:, 0:416], func=AF.Copy)
                S.act("activation", r=[P("ps3")], w=[P("ocs")], out=ocs[:, 2:4, :].rearrange("q h w -> q (h w)"), in_=PS[3][:, 0:416], func=AF.Copy)
                S.dve("tensor_scalar", r=[P("ocs")], w=[P("rd0")], out=rd[:, 0, :], in0=ocs[:, :, 64], scalar1=1e-30, scalar2=None, op0=ALU.max)
                S.dve("reciprocal", r=[P("rd0")], w=[P("rd0")], out=rd[:, 0, :], in_=rd[:, 0, :])
                S.dve("tensor_scalar", r=[P("ocs"), P("rd0")], w=[P("imp")], out=imp[:], in0=ocs[:, 0, 80:208], scalar1=rd[:, 0, 0:1],
                      scalar2=None, op0=ALU.mult)
                for h in range(1, 4):
                    S.dve("scalar_tensor_tensor", r=[P("ocs"), P("rd0"), P("imp")], w=[P("imp")], out=imp[:], in0=ocs[:, h, 80:208],
                          scalar=rd[:, 0, h:h + 1], in1=imp[:], op0=ALU.mult, op1=ALU.add)
                S.dve("tensor_tensor", r=[P("imp"), fcn], w=[P("imp")], out=imp[:], in0=imp[:], in1=fc[p][:], op=ALU.max)
                S.dve("max", r=[P("imp")], w=[P("mx")], out=mx[:], in_=imp[:])
                S.dve("match_replace", r=[P("mx"), P("imp")], w=[P("wk")], out=wk[:], in_to_replace=mx[:], in_values=imp[:], imm_value=-1.0)
                S.dve("max", r=[P("wk")], w=[P("mx")], out=mx[:], in_=wk[:])
                S.dve("tensor_scalar", r=[P("imp"), P("mx")], w=[P("wk")], out=wk[:], in0=imp[:], scalar1=mx[:, 7:8], scalar2=None, op0=ALU.is_ge)
                S.dve("tensor_scalar", r=[P("wk")], w=[P("MBq")], out=MBq[:], in0=wk[:], scalar1=-1.0, scalar2=30000.0, op0=ALU.add, op1=ALU.mult)

            def slc_pre():
                S.pe("transpose", r=[P("MBq"), P("idb")], w=[P("pstb")], out=PSTb[:, 0:128], in_=MBq[:], identity=idb[:])
                S.act("activation", r=[P("pstb")], w=[P("MBT")], out=MBT[:], in_=bc(PSTb[:, 0:128]), func=AF.Copy)

            def win_post():
                S.act("activation", r=[P("ps5")], w=[P("ows")], out=ows[:].rearrange("q h w -> q (h w)"), in_=PS[5][:, 0:260], func=AF.Copy)
                S.dve("reciprocal", r=[P("ows")], w=[P("rd2")], out=rd[:, 2, :], in_=ows[:, :, 64])

            def slc_post(m=m, g=g, p=p, gln=gln, gtn=gtn):
                S.act("activation", r=[P("ps4")], w=[P("oss")], out=oss[:].rearrange("q h w -> q (h w)"), in_=PS[4][:, 0:260], func=AF.Copy)
                S.dve("reciprocal", r=[P("oss")], w=[P("rd1")], out=rd[:, 1, :], in_=oss[:, :, 64])
                S.dve("tensor_tensor", r=[P("rd0"), P("rd1"), P("rd2"), gln], w=[P("fac")], out=fac[:], in0=rd[:],
                      in1=glt[p][:, g * 12:g * 12 + 12].rearrange("q (h b) -> q b h", b=3), op=ALU.mult)
                srcs = ((ocs, P("ocs")), (oss, P("oss")), (ows, P("ows")))
                for br in range(3):
                    o_t, o_n = srcs[br]
                    dst, dstn = (acc, P("acc")) if br == 0 else (tmp, P("tmp"))
                    S.dve("tensor_tensor", r=[o_n, P("fac")], w=[dstn], out=dst[:], in0=o_t[:, :, 0:64],
                          in1=fac[:, br, :].unsqueeze(2).to_broadcast([128, 4, 64]), op=ALU.mult)
                    if br > 0:
                        S.dve("tensor_tensor", r=[P("acc"), P("tmp")], w=[P("acc")], out=acc[:], in0=acc[:], in1=tmp[:], op=ALU.add)
                S.dve("tensor_tensor", r=[P("acc"), gtn], w=["MIXB"], out=MIXB[:, m, 512 + g * 256:512 + g * 256 + 256],
                      in0=acc[:].rearrange("q h d -> q (h d)"), in1=gtt[p][:, g * 256:g * 256 + 256], op=ALU.mult)

            for c in range(4):
                add_unit(lhsT=KCMP[:, c * 128:(c + 1) * 128], rk=["KCMP"], qrhs=qrhs, mask=(cm[p][:, c, :], cmn), vrhs=VCA[:, c, g, :], vk=["VCA"],
                         obank=(lambda h: 2 + h // 2), ocols=(lambda h: ((h % 2) * 208, (h % 2) * 208 + 208)), first=(c == 0), last=(c == 3),
                         post=(([nxt_loads] if (g == 0 and c == 0 and nxt_loads is not None) else []) + ([cmp_post] if c == 3 else [])))
            cl = list(range(8)) if m > 0 else list(range(4, 8))
            for c in cl:
                add_unit(lhsT=KWm[p][:, c, :], rk=[kwn], qrhs=qrhs, mask=(wm[:, c, :], P("wm")), vrhs=VWm[p][:, c, g * 80:g * 80 + 65], vk=[vwn],
                         obank=(lambda h: 5), ocols=(lambda h: (h * 65, h * 65 + 65)), first=(c == cl[0]), last=(c == cl[-1]),
                         post=([win_post] if c == cl[-1] else []))
            J = 4 * m + 4
            for j in range(J):
                add_unit(lhsT=KS[:, gcol(j):gcol(j) + 128], rk=[P("KS")], qrhs=qrhs,
                         mask=((dm[:, j - 4 * m, :], P("dm")) if j >= 4 * m else None), vrhs=VS[:, cidx(j), g * 80:g * 80 + 65], vk=[P("VS")],
                         obank=(lambda h: 4), ocols=(lambda h: (h * 65, h * 65 + 65)), first=(j == 0), last=(j == J - 1),
                         bias_mm=EE[:, gcol(j):gcol(j) + 128], pre=([slc_pre] if j == 0 else []), post=([slc_post] if j == J - 1 else []))

    for i, u in enumerate(units):
        emit_s(i, u)
        if i > 0:
            emit_rest(i - 1, units[i - 1])
    emit_rest(len(units) - 1, units[-1])


def nsa_rank_consts(r):
    k = np.arange(128)[:, None]
    q = np.arange(128)[None, :]
    wmask = np.zeros((128, 8, 128), np.float32)
    for c in range(8):
        d = (4 + r) - c
        if d == 4:
            wmask[:, c, :] = (k > q)
        elif 1 <= d <= 3:
            wmask[:, c, :] = 1.0
        elif d == 0:
            wmask[:, c, :] = (k <= q)
    cmask = np.zeros((16, 128, 4, 128), np.float32)
    force = np.zeros((16, 128, 128), np.float32)
    nl = np.arange(128)[:, None]
    b = np.arange(128)[None, :]
    for m in range(16):
        tq = (4 * m + r) * 128 + np.arange(128)
        for c in range(4):
            n = 128 * c + nl
            cmask[m, :, c, :] = ((16 * n + 31) <= tq[None, :]) & (n < 511)
        cur = (tq // 64)[:, None]
        force[m] = 1e4 * (b == 0) + 2e4 * (b == cur) + 3e4 * (b == cur - 1)
    return dict(wmask=wmask.astype(NPBF), cmask=cmask.astype(NPBF), force=force.astype(np.float32))


def ee_const():
    ee = np.zeros((128, T), np.float32)
    for j in range(64):
        for half in range(2):
            ee[2 * j + half, gcol(j) + 64 * half: gcol(j) + 64 * half + 64] = 1.0
    return ee.astype(NPBF)


def conv_scope(S, nc, es, io, MIXB, pfx="v"):
    P = lambda n: pfx + n
    sb = lambda n, s, d: es.enter_context(nc.sbuf_tensor(P(n), s, d))
    PS = [es.enter_context(nc.psum_tensor(P("ps%d" % i), [128, 512], F32)) for i in range(4)]
    YB = sb("YB", [128, 2, NT], BF16)
    TLB = sb("TLB", [128, 2, 4, 16, 30], BF16)
    YE = sb("YE", [128, 2, 16, 158], F32)
    accA = sb("accA", [128, 2, 16, 128], F32)
    accB = sb("accB", [128, 2, 16, 128], F32)
    tmpP = sb("tmpP", [128, 16, 128], F32)
    SQ = sb("SQ", [128, 2, NT], F32)
    MEAN = sb("MEAN", [128, NT], F32)
    MSQ = sb("MSQ", [128, NT], F32)
    CW = sb("CW", [128, 2, 31], F32)
    CBs = sb("CBs", [128, 2], F32)
    LG = sb("LG", [128, 2], F32)
    LB = sb("LB", [128, 2], F32)
    ohp = sb("ohp", [128, 4], F32)
    onesd = sb("onesd", [128, 128], F32)
    WPf = sb("WPf", [128, 2, 256], F32)
    WPW = sb("WPW", [128, 2, 256], BF16)
    ACTT = sb("ACTT", [128, 2, NT], BF16)
    ob = [sb("ob%d" % i, [128, 256], F32) for i in range(2)]
    gtt = [sb("gtt%d" % i, [128, 256], BF16) for i in range(2)]

    ld = P("ld")
    for cc in range(2):
        S.dma(ld, batch=True, w=[P("YB")], out=YB[:, cc, :], in_=io["yT"][cc])
        S.dma(ld, batch=True, w=[P("WPf")], out=WPf[:, cc, :], in_=io["conv_pw"][cc * 128:(cc + 1) * 128, :])
        for r in range(4):
            S.dma(ld, batch=True, w=[P("TLB")], out=TLB[:, cc, r, :, :], in_=io["tails_g"][r, cc])
    S.dma(ld, batch=True, w=[P("CW")], out=CW[:], in_=io["conv_w"])
    S.dma(ld, batch=True, w=[P("CBs")], out=CBs[:], in_=io["conv_b"])
    S.dma(ld, batch=True, w=[P("LG")], out=LG[:], in_=io["ln_g"])
    S.dma(ld, batch=True, w=[P("LB")], out=LB[:], in_=io["ln_b"])
    S.dma(ld, batch=True, w=[P("ohp")], out=ohp[:], in_=io["ohprev"])
    S.dve("memset", w=[P("onesd")], ap=onesd[:], constant=1.0 / 256.0)
    S.dve("tensor_copy", r=[P("WPf")], w=[P("WPW")], out=WPW[:], in_=WPf[:])
    for cc in range(2):
        S.act("activation", r=[P("YB")], w=[P("YE")], out=YE[:, cc, :, 30:158], in_=YB[:, cc, :].rearrange("p (m t) -> p m t", t=128), func=AF.Copy)
        S.dve("tensor_scalar", r=[P("TLB"), P("ohp")], w=[P("YE")], out=YE[:, cc, :, 0:30], in0=TLB[:, cc, 0, :, :], scalar1=ohp[:, 0:1],
              scalar2=None, op0=ALU.mult)
        for r in (1, 2):
            S.dve("scalar_tensor_tensor", r=[P("TLB"), P("ohp"), P("YE")], w=[P("YE")], out=YE[:, cc, :, 0:30], in0=TLB[:, cc, r, :, :],
                  scalar=ohp[:, r:r + 1], in1=YE[:, cc, :, 0:30], op0=ALU.mult, op1=ALU.add)
        S.dve("scalar_tensor_tensor", r=[P("TLB"), P("ohp"), P("YE")], w=[P("YE")], out=YE[:, cc, 1:16, 0:30], in0=TLB[:, cc, 3, 0:15, :],
              scalar=ohp[:, 3:4], in1=YE[:, cc, 1:16, 0:30], op0=ALU.mult, op1=ALU.add)
    for cc in range(2):
        an, bn = P("accA%d" % cc), P("accB%d" % cc)
        for tp in range(0, 10):
            src = YE[:, cc, :, tp:tp + 128]
            if tp == 0:
                S.pool("tensor_scalar", r=[P("YE"), P("CW"), P("CBs")], w=[an], out=accA[:, cc], in0=src, scalar1=CW[:, cc, 0:1],
                       scalar2=CBs[:, cc:cc + 1], op0=ALU.mult, op1=ALU.add)
            else:
                S.pool("tensor_scalar", r=[P("YE"), P("CW")], w=[P("tmpP")], out=tmpP[:], in0=src, scalar1=CW[:, cc, tp:tp + 1],
                       scalar2=None, op0=ALU.mult)
                S.pool("tensor_tensor", r=[an, P("tmpP")], w=[an], out=accA[:, cc], in0=accA[:, cc], in1=tmpP[:], op=ALU.add)
        for tp in range(10, 31):
            src = YE[:, cc, :, tp:tp + 128]
            if tp == 10:
                S.dve("tensor_scalar", r=[P("YE"), P("CW")], w=[bn], out=accB[:, cc], in0=src, scalar1=CW[:, cc, tp:tp + 1],
                      scalar2=None, op0=ALU.mult)
            else:
                S.dve("scalar_tensor_tensor", r=[P("YE"), P("CW"), bn], w=[bn], out=accB[:, cc], in0=src, scalar=CW[:, cc, tp:tp + 1],
                      in1=accB[:, cc], op0=ALU.mult, op1=ALU.add)
        S.dve("tensor_tensor", r=[an, bn], w=[an], out=accA[:, cc], in0=accA[:, cc], in1=accB[:, cc], op=ALU.add)
        S.act("activation", r=[an], w=[P("SQ")], out=SQ[:, cc, :], in_=accA[:, cc].rearrange("p m t -> p (m t)"), func=AF.Square)
    for pc in range(4):
        cs = slice(pc * 512, (pc + 1) * 512)
        for cc in range(2):
            S.pe("matmul", r=[P("onesd"), P("accA%d" % cc)], w=[P("ps0")], out=PS[0][:], lhsT=onesd[:],
                 rhs=accA[:, cc].rearrange("p m t -> p (m t)")[:, cs], start=(cc == 0), stop=(cc == 1))
        for cc in range(2):
            S.pe("matmul", r=[P("onesd"), P("SQ")], w=[P("ps1")], out=PS[1][:], lhsT=onesd[:], rhs=SQ[:, cc, cs], start=(cc == 0), stop=(cc == 1))
        S.act("activation", r=[P("ps0")], w=[P("MEAN")], out=MEAN[:, cs], in_=PS[0][:], func=AF.Copy)
        S.act("activation", r=[P("ps1")], w=[P("MSQ")], out=MSQ[:, cs], in_=PS[1][:], func=AF.Copy)
    S.dve("tensor_tensor", r=[P("MEAN")], w=[P("SQ")], out=SQ[:, 0, :], in0=MEAN[:], in1=MEAN[:], op=ALU.mult)
    S.dve("tensor_tensor", r=[P("MSQ"), P("SQ")], w=[P("MSQ")], out=MSQ[:], in0=MSQ[:], in1=SQ[:, 0, :], op=ALU.subtract)
    S.dve("tensor_scalar", r=[P("MSQ")], w=[P("MSQ")], out=MSQ[:], in0=MSQ[:], scalar1=1e-6, scalar2=None, op0=ALU.add)
    S.act("activation", r=[P("MSQ")], w=[P("MSQ")], out=MSQ[:], in_=MSQ[:], func=AF.Sqrt)
    S.dve("reciprocal", r=[P("MSQ")], w=[P("MSQ")], out=MSQ[:], in_=MSQ[:])
    for cc in range(2):
        an = P("accA%d" % cc)
        a2 = accA[:, cc].rearrange("p m t -> p (m t)")
        S.dve("tensor_tensor", r=[an, P("MEAN")], w=[an], out=a2, in0=a2, in1=MEAN[:], op=ALU.subtract)
        S.dve("tensor_tensor", r=[an, P("MSQ")], w=[an], out=a2, in0=a2, in1=MSQ[:], op=ALU.mult)
        S.act("activation", r=[an, P("LG"), P("LB")], w=[P("ACTT")], out=ACTT[:, cc, :], in_=a2, func=AF.Silu, scale=LG[:, cc:cc + 1], bias=LB[:, cc:cc + 1])
    for m in range(16):
        p = m % 2
        k = 2 + p
        S.dma(P("gt%d" % p), w=[P("gtt%d" % p)], out=gtt[p][:], in_=io["gt"][m * 128:(m + 1) * 128, 256:512])
        for cc in range(2):
            S.pe("matmul", r=[P("ACTT"), P("WPW")], w=[P("ps%d" % k)], out=PS[k][:, 0:256], lhsT=ACTT[:, cc, m * 128:(m + 1) * 128], rhs=WPW[:, cc, :],
                 start=(cc == 0), stop=(cc == 1))
        S.act("activation", r=[P("ps%d" % k)], w=[P("ob%d" % p)], out=ob[p][:], in_=PS[k][:, 0:256], func=AF.Copy)
        S.dve("tensor_tensor", r=[P("ob%d" % p), P("gtt%d" % p)], w=["MIXB"], out=MIXB[:, m, 256:512], in0=ob[p][:], in1=gtt[p][:], op=ALU.mult)


def conv_consts(inputs, layer, r):
    cw = inputs["conv_w"][layer]
    col = lambda v: np.ascontiguousarray(v.reshape(2, 128).T, dtype=np.float32)
    ohp = np.zeros((128, 4), np.float32)
    ohp[:, (r - 1) % 4] = 1.0
    return dict(conv_w=np.ascontiguousarray(cw.T.reshape(2, 128, 31).transpose(1, 0, 2), dtype=np.float32),
                conv_b=col(inputs["conv_b"][layer]), ln_g=col(inputs["conv_ln_g"][layer]), ln_b=col(inputs["conv_ln_b"][layer]),
                conv_pw=np.ascontiguousarray(inputs["conv_pw"][layer]), ohprev=ohp)
```

```python
import numpy as np
import ml_dtypes
from contextlib import ExitStack
import concourse.bass as bass
import concourse.mybir as mybir
from concourse.bass_utils import run_bass_kernel_spmd

F32 = mybir.dt.float32
BF16 = mybir.dt.bfloat16
AF = mybir.ActivationFunctionType
ALU = mybir.AluOpType
AX = mybir.AxisListType
NPBF = ml_dtypes.bfloat16

ENGINES = ("pe", "act", "dve", "pool", "sp")


class Buf:
    __slots__ = ("name", "last_w", "readers")

    def __init__(self, name):
        self.name = name
        self.last_w = None
        self.readers = []


class Op:
    __slots__ = ("eng", "fn", "reads", "writes", "dma_key", "deps", "need_inc", "tok", "idx")

    def __init__(self, eng, fn, reads, writes, dma_key):
        self.eng = eng
        self.fn = fn
        self.reads = reads
        self.writes = writes
        self.dma_key = dma_key
        self.deps = []
        self.need_inc = False
        self.tok = None


class Sched:
    def __init__(self, nc, es):
        self.nc = nc
        self.es = es
        self.ops = []
        self.bufs = {}
        self.batch = set()
        self.cnt = {}
        self.sems = {}
        self.waited = {e: {} for e in ENGINES}

    def buf(self, name):
        b = self.bufs.get(name)
        if b is None:
            b = Buf(name)
            self.bufs[name] = b
        return b

    def _norm(self, lst):
        return [self.buf(b) if isinstance(b, str) else b for b in (lst or ())]

    def add(self, eng, meth, kw, reads=(), writes=(), dma_key=None):
        op = Op(eng, (meth, kw), self._norm(reads), self._norm(writes), dma_key)
        op.idx = len(self.ops)
        self.ops.append(op)
        return op

    def pe(self, meth, r=(), w=(), **kw):
        return self.add("pe", meth, kw, r, w)

    def act(self, meth, r=(), w=(), **kw):
        return self.add("act", meth, kw, r, w)

    def dve(self, meth, r=(), w=(), **kw):
        return self.add("dve", meth, kw, r, w)

    def pool(self, meth, r=(), w=(), **kw):
        return self.add("pool", meth, kw, r, w)

    def dma(self, key, r=(), w=(), eng="sp", batch=False, **kw):
        if batch:
            self.batch.add(key)
        return self.add(eng, "dma_start", kw, r, w, dma_key=key)

    def analyze(self):
        for op in self.ops:
            deps = set()
            for b in op.reads:
                if b.last_w is not None:
                    deps.add(b.last_w)
            for b in op.writes:
                if b.last_w is not None:
                    deps.add(b.last_w)
                for r in b.readers:
                    deps.add(r)
            deps.discard(op.idx)
            keep = []
            for d in deps:
                dop = self.ops[d]
                if dop.dma_key is not None and dop.dma_key == op.dma_key and dop.dma_key in self.batch:
                    continue
                if dop.dma_key is None and dop.eng == op.eng:
                    if op.eng == "pe" or op.dma_key is not None:
                        continue
                keep.append(d)
            op.deps = sorted(keep)
            for d in op.deps:
                self.ops[d].need_inc = True
            for b in op.reads:
                b.readers.append(op.idx)
            for b in op.writes:
                b.last_w = op.idx
                b.readers = []

    def flush(self):
        nc = self.nc
        self.analyze()
        per = {e: [op for op in self.ops if op.eng == e] for e in ENGINES}
        for e in ENGINES:
            comp = [op for op in per[e] if op.dma_key is None]
            if comp:
                comp[-1].need_inc = True
        cnt = self.cnt
        for op in self.ops:
            if op.dma_key is not None:
                k = ("dma", op.dma_key)
                cnt[k] = cnt.get(k, 0) + 16
                op.tok = (k, cnt[k])
            elif op.need_inc:
                k = ("eng", op.eng)
                cnt[k] = cnt.get(k, 0) + 1
                op.tok = (k, cnt[k])
        for op in self.ops:
            if op.dma_key is not None and op.dma_key in self.batch:
                op.tok = (op.tok[0], cnt[op.tok[0]])
        for k in sorted(cnt.keys()):
            if k not in self.sems:
                self.sems[k] = self.es.enter_context(nc.semaphore("s_%s_%s" % k))
        sems = self.sems
        ops = self.ops
        totals = dict(cnt)

        def run(eng_name, h):
            waited = self.waited[eng_name]
            for op in per[eng_name]:
                for d in op.deps:
                    k, v = ops[d].tok
                    if waited.get(k, 0) >= v:
                        continue
                    h.wait_ge(sems[k], v)
                    waited[k] = v
                ins = getattr(h, op.fn[0])(**op.fn[1])
                if op.tok is not None:
                    ins.then_inc(sems[op.tok[0]], 16 if op.dma_key is not None else 1)
            for k in sorted(totals.keys()):
                if waited.get(k, 0) < totals[k]:
                    h.wait_ge(sems[k], totals[k])
                    waited[k] = totals[k]

        with nc.Block() as block:
            block.sync(lambda h: run("sp", h))
            block.tensor(lambda h: run("pe", h))
            block.scalar(lambda h: run("act", h))
            block.vector(lambda h: run("dve", h))
            block.gpsimd(lambda h: run("pool", h))
        self.ops = []
        self.bufs = {}


D = 1024
DIN = 3612
NT = 2048
T = 8192
OFF = dict(fq=0, fk=256, fv=512, ff=768, fg=772, glu=1028, cg=1540, nq=1796, nkc=2308, nvc=2436,
           nks=2564, nvs=2692, nkw=2820, nvw=2948, ngl=3076, ng=3100)

A_OUTS = dict(qf=([2, 128, NT], BF16), kf=([2, 128, NT], BF16), yT=([2, 128, NT], BF16), qn=([4, 128, NT], BF16),
              kc=([128, NT], BF16), vc=([128, NT], BF16), ks=([128, NT], BF16), kw=([128, NT], BF16),
              vf=([NT, 320], BF16), vs=([NT, 160], BF16), vw=([NT, 160], BF16), lf=([NT, 4], F32),
              gt=([NT, D], BF16), gl=([NT, 24], F32))


def phase_a(S, nc, es, io, pfx="a"):
    P = lambda n: pfx + n
    sb = lambda n, s, d: es.enter_context(nc.sbuf_tensor(P(n), s, d))
    PS0b = es.enter_context(nc.psum_tensor(P("ps0"), [128, 1024], BF16))
    PS = [None] + [es.enter_context(nc.psum_tensor(P("ps%d" % i), [128, 512], F32)) for i in range(1, 8)]
    Wb = sb("Wb", [128, 8, DIN], BF16)
    wst = [sb("wst%d" % i, [128, DIN], F32) for i in range(2)]
    gcol = sb("gcol", [128, 8], F32)
    fb = sb("fb", [128, 4], F32)
    cosT = sb("cosT", [128, NT], F32)
    sinT = sb("sinT", [128, NT], F32)
    rmat_f = sb("rmat_f", [128, 128], F32)
    rmat = sb("rmat", [128, 128], BF16)
    idf = sb("idf", [128, 128], F32)
    idb = sb("idb", [128, 128], BF16)
    xt = [sb("xt%d" % i, [128, D], F32) for i in range(2)]
    sq = sb("sq", [128, D], BF16)
    ss = [sb("ss%d" % i, [128, 1], F32) for i in range(2)]
    hb = [sb("hb%d" % i, [128, D], BF16) for i in range(2)]
    hT = [sb("hT%d" % i, [128, 8, 512], BF16) for i in range(2)]
    fo = [sb("fo%d" % i, [128, 512], BF16) for i in range(4)]
    zc = [sb("zc%d" % i, [128, 512], BF16) for i in range(2)]
    t1 = [sb("t1%d" % i, [128, 512], F32) for i in range(2)]
    t2 = [sb("t2%d" % i, [128, 512], F32) for i in range(2)]
    sg = [sb("sg%d" % i, [128, 512], F32) for i in range(2)]
    vfo = [sb("vfo%d" % i, [128, 4, 80], BF16) for i in range(2)]
    vso = [sb("vso%d" % i, [128, 2, 80], BF16) for i in range(2)]
    vwo = [sb("vwo%d" % i, [128, 2, 80], BF16) for i in range(2)]
    lfo = [sb("lfo%d" % i, [128, 4], F32) for i in range(2)]
    lft = [sb("lft%d" % i, [128, 4], F32) for i in range(2)]
    glo = [sb("glo%d" % i, [128, 24], F32) for i in range(2)]
    gto = [sb("gto%d" % i, [128, D], BF16) for i in range(2)]
    out_keys = set()

    def store(src_name, dst, src):
        k = P("k_" + src_name)
        out_keys.add(k)
        S.dma(k, r=[P(src_name)], out=dst, in_=src)

    S.dma(P("c"), batch=True, w=[P("gcol")], out=gcol[:], in_=io["norm_g"])
    S.dma(P("c"), batch=True, w=[P("fb")], out=fb[:], in_=io["fox_b"].partition_broadcast(128))
    S.dma(P("c"), batch=True, w=[P("cos")], out=cosT[:], in_=io["cos"])
    S.dma(P("c"), batch=True, w=[P("sin")], out=sinT[:], in_=io["sin"])
    S.dma(P("c"), batch=True, w=[P("rmf")], out=rmat_f[:], in_=io["rmat"])
    S.dma(P("c"), batch=True, w=[P("idf")], out=idf[:], in_=io["ident"])
    S.dve("tensor_copy", r=[P("rmf")], w=[P("rmat")], out=rmat[:], in_=rmat_f[:])
    S.dve("tensor_copy", r=[P("idf")], w=[P("idb")], out=idb[:], in_=idf[:])
    for i in range(2):
        S.dve("memset", w=[P("vfo%d" % i)], ap=vfo[i][:], constant=1.0)
        S.dve("memset", w=[P("vso%d" % i)], ap=vso[i][:], constant=1.0)
        S.dve("memset", w=[P("vwo%d" % i)], ap=vwo[i][:], constant=1.0)
    for c in range(8):
        w = wst[c % 2]
        wn = P("wst%d" % (c % 2))
        S.dma(P("w%d" % (c % 2)), w=[wn], out=w[:], in_=io["w_in"][c * 128:(c + 1) * 128, :])
        eng = S.dve if c % 2 == 0 else S.pool
        eng("tensor_scalar", r=[wn, P("gcol")], w=[P("Wb")], out=Wb[:, c, :], in0=w[:], scalar1=gcol[:, c:c + 1],
            scalar2=None, op0=ALU.mult)

    psrr = [1]

    def next_ps():
        k = psrr[0]
        psrr[0] = 1 + (psrr[0] % 7)
        return k

    foi = [0]
    ri = [0]
    for s in range(4):
        hTs = hT[s % 2]
        hTn = P("hT%d" % (s % 2))
        for q in range(4):
            tn = s * 4 + q
            p = tn % 2
            x_t = xt[p]
            xn = P("xt%d" % p)
            ssn = P("ss%d" % p)
            S.dma(P("x%d" % p), w=[xn], out=x_t[:], in_=io["x"][tn * 128:(tn + 1) * 128, :])
            S.act("activation", r=[xn], w=[P("sq"), ssn], out=sq[:], in_=x_t[:], func=AF.Square, accum_out=ss[p][:, 0:1])
            S.dve("tensor_scalar", r=[ssn], w=[ssn], out=ss[p][:], in0=ss[p][:], scalar1=1.0 / D, scalar2=1e-6,
                  op0=ALU.mult, op1=ALU.add)
            S.act("activation", r=[ssn], w=[ssn], out=ss[p][:], in_=ss[p][:], func=AF.Sqrt)
            S.dve("reciprocal", r=[ssn], w=[ssn], out=ss[p][:], in_=ss[p][:])
            S.dve("tensor_scalar", r=[xn, ssn], w=[P("hb%d" % p)], out=hb[p][:], in0=x_t[:], scalar1=ss[p][:, 0:1],
                  scalar2=None, op0=ALU.mult)
            for half in range(2):
                for cc in range(4):
                    c = half * 4 + cc
                    S.pe("transpose", r=[P("hb%d" % p), P("idb")], w=[P("ps0")], out=PS0b[:, cc * 128:(cc + 1) * 128],
                         in_=hb[p][:, c * 128:(c + 1) * 128], identity=idb[:])
                S.act("activation", r=[P("ps0")], w=[hTn], out=hTs[:, half * 4:(half + 1) * 4, q * 128:(q + 1) * 128],
                      in_=PS0b[:, 0:512].rearrange("p (c t) -> p c t", c=4), func=AF.Copy)
        cols = slice(s * 512, (s + 1) * 512)

        def fm(coff, k):
            for c in range(8):
                S.pe("matmul", r=[P("Wb"), hTn], w=[P("ps%d" % k)], out=PS[k][:, :], lhsT=Wb[:, c, coff:coff + 128],
                     rhs=hTs[:, c, :], start=(c == 0), stop=(c == 7))

        def plain_out(coff, dst):
            k = next_ps()
            fm(coff, k)
            i = foi[0] % 4
            foi[0] += 1
            S.act("activation", r=[P("ps%d" % k)], w=[P("fo%d" % i)], out=fo[i][:], in_=PS[k][:], func=AF.Copy)
            store("fo%d" % i, dst, fo[i][:])

        def rope_out(coff, dst):
            k = next_ps()
            fm(coff, k)
            k2 = next_ps()
            j = ri[0] % 2
            ri[0] += 1
            i = foi[0] % 4
            foi[0] += 1
            S.act("activation", r=[P("ps%d" % k)], w=[P("zc%d" % j)], out=zc[j][:], in_=PS[k][:], func=AF.Copy)
            S.pe("matmul", r=[P("rmat"), P("zc%d" % j)], w=[P("ps%d" % k2)], out=PS[k2][:], lhsT=rmat[:], rhs=zc[j][:],
                 start=True, stop=True)
            S.act("activation", r=[P("ps%d" % k)], w=[P("t1%d" % j)], out=t1[j][:], in_=PS[k][:], func=AF.Copy)
            S.act("activation", r=[P("ps%d" % k2)], w=[P("t2%d" % j)], out=t2[j][:], in_=PS[k2][:], func=AF.Copy)
            S.dve("tensor_tensor", r=[P("t1%d" % j), P("cos")], w=[P("t1%d" % j)], out=t1[j][:], in0=t1[j][:],
                  in1=cosT[:, cols], op=ALU.mult)
            S.pool("tensor_tensor", r=[P("t2%d" % j), P("sin")], w=[P("t2%d" % j)], out=t2[j][:], in0=t2[j][:],
                   in1=sinT[:, cols], op=ALU.mult)
            S.dve("tensor_tensor", r=[P("t1%d" % j), P("t2%d" % j)], w=[P("fo%d" % i)], out=fo[i][:], in0=t1[j][:],
                   in1=t2[j][:], op=ALU.add)
            store("fo%d" % i, dst, fo[i][:])

        for h2 in range(2):
            plain_out(OFF["fq"] + 128 * h2, io["qf"][h2, :, cols])
            plain_out(OFF["fk"] + 128 * h2, io["kf"][h2, :, cols])
        for h2 in range(2):
            ka = next_ps()
            fm(OFF["glu"] + 128 * h2, ka)
            kb = next_ps()
            fm(OFF["glu"] + 256 + 128 * h2, kb)
            i = foi[0] % 4
            foi[0] += 1
            S.act("activation", r=[P("ps%d" % kb)], w=[P("sg%d" % h2)], out=sg[h2][:], in_=PS[kb][:], func=AF.Sigmoid)
            S.act("activation", r=[P("ps%d" % ka)], w=[P("t1%d" % h2)], out=t1[h2][:], in_=PS[ka][:], func=AF.Copy)
            S.dve("tensor_tensor", r=[P("t1%d" % h2), P("sg%d" % h2)], w=[P("fo%d" % i)], out=fo[i][:], in0=t1[h2][:],
                  in1=sg[h2][:], op=ALU.mult)
            store("fo%d" % i, io["yT"][h2, :, cols], fo[i][:])
        for c4 in range(4):
            rope_out(OFF["nq"] + 128 * c4, io["qn"][c4, :, cols])
        plain_out(OFF["nkc"], io["kc"][:, cols])
        plain_out(OFF["nvc"], io["vc"][:, cols])
        rope_out(OFF["nks"], io["ks"][:, cols])
        rope_out(OFF["nkw"], io["kw"][:, cols])

        for q in range(4):
            tn = s * 4 + q
            p = tn % 2
            rows = slice(tn * 128, (tn + 1) * 128)

            def tm(c0, c1, k):
                for c in range(8):
                    S.pe("matmul", r=[P("Wb"), hTn], w=[P("ps%d" % k)], out=PS[k][:, 0:c1 - c0],
                         lhsT=hTs[:, c, q * 128:(q + 1) * 128], rhs=Wb[:, c, c0:c1], start=(c == 0), stop=(c == 7))

            k = next_ps()
            tm(512, 772, k)
            S.act("activation", r=[P("ps%d" % k)], w=[P("vfo%d" % p)], out=vfo[p][:, :, 0:64],
                  in_=PS[k][:, 0:256].rearrange("p (h d) -> p h d", h=4), func=AF.Copy)
            store("vfo%d" % p, io["vf"][rows, :], vfo[p][:].rearrange("p h d -> p (h d)"))
            lt = P("lft%d" % p)
            S.act("activation", r=[P("ps%d" % k)], w=[lt], out=lft[p][:], in_=PS[k][:, 256:260], func=AF.Copy)
            S.dve("tensor_tensor", r=[lt, P("fb")], w=[lt], out=lft[p][:], in0=lft[p][:], in1=fb[:], op=ALU.add)
            S.act("activation", r=[lt], w=[lt], out=lft[p][:], in_=lft[p][:], func=AF.Exp, scale=-1.0)
            S.act("activation", r=[lt], w=[lt], out=lft[p][:], in_=lft[p][:], func=AF.Ln, bias=1.0, scale=1.0)
            S.dve("tensor_scalar", r=[lt], w=[P("lfo%d" % p)], out=lfo[p][:], in0=lft[p][:], scalar1=-1.0, scalar2=None, op0=ALU.mult)
            store("lfo%d" % p, io["lf"][rows, :], lfo[p][:])
            for (c0, g0) in ((772, 0), (1540, 256)):
                k = next_ps()
                tm(c0, c0 + 256, k)
                S.act("activation", r=[P("ps%d" % k)], w=[P("gto%d" % p)], out=gto[p][:, g0:g0 + 256], in_=PS[k][:, 0:256], func=AF.Silu)
            k = next_ps()
            tm(3100, 3612, k)
            S.act("activation", r=[P("ps%d" % k)], w=[P("gto%d" % p)], out=gto[p][:, 512:1024], in_=PS[k][:, 0:512], func=AF.Silu)
            store("gto%d" % p, io["gt"][rows, :], gto[p][:])
            k = next_ps()
            tm(2692, 2820, k)
            S.act("activation", r=[P("ps%d" % k)], w=[P("vso%d" % p)], out=vso[p][:, :, 0:64],
                  in_=PS[k][:, 0:128].rearrange("p (h d) -> p h d", h=2), func=AF.Copy)
            store("vso%d" % p, io["vs"][rows, :], vso[p][:].rearrange("p h d -> p (h d)"))
            k = next_ps()
            tm(2948, 3100, k)
            S.act("activation", r=[P("ps%d" % k)], w=[P("vwo%d" % p)], out=vwo[p][:, :, 0:64],
                  in_=PS[k][:, 0:128].rearrange("p (h d) -> p h d", h=2), func=AF.Copy)
            store("vwo%d" % p, io["vw"][rows, :], vwo[p][:].rearrange("p h d -> p (h d)"))
            S.act("activation", r=[P("ps%d" % k)], w=[P("glo%d" % p)], out=glo[p][:], in_=PS[k][:, 128:152], func=AF.Sigmoid)
            store("glo%d" % p, io["gl"][rows, :], glo[p][:])
    return sorted(out_keys)


def own_positions(r):
    m = np.arange(16)[:, None]
    tl = np.arange(128)[None, :]
    return ((4 * m + r) * 128 + tl).reshape(-1)


def rope_tables(pos):
    inv = 500000.0 ** (-np.arange(0, 16, 2, dtype=np.float32) / 16.0)
    ang = pos.astype(np.float32)[None, :] * np.tile(inv, 2)[:, None].astype(np.float32)
    cos = np.ones((64, pos.size), np.float32)
    sin = np.zeros((64, pos.size), np.float32)
    cos[:16] = np.cos(ang)
    sin[:16] = np.sin(ang)
    return np.tile(cos, (2, 1)), np.tile(sin, (2, 1))


def rope_rmat():
    R = np.zeros((128, 128), np.float32)
    for hh in range(2):
        for d in range(8):
            R[hh * 64 + d + 8, hh * 64 + d] = -1.0
            R[hh * 64 + d, hh * 64 + d + 8] = 1.0
    return R


def build_a():
    nc = bass.Bass("TRN2", target_bir_lowering=False)
    io = {}
    io["x"] = nc.dram_tensor("x", [NT, D], F32, kind="ExternalInput").ap()
    io["w_in"] = nc.dram_tensor("w_in", [D, DIN], F32, kind="ExternalInput").ap()
    io["norm_g"] = nc.dram_tensor("norm_g", [128, 8], F32, kind="ExternalInput").ap()
    io["fox_b"] = nc.dram_tensor("fox_b", [4], F32, kind="ExternalInput").ap()
    io["cos"] = nc.dram_tensor("cos", [128, NT], F32, kind="ExternalInput").ap()
    io["sin"] = nc.dram_tensor("sin", [128, NT], F32, kind="ExternalInput").ap()
    io["rmat"] = nc.dram_tensor("rmat", [128, 128], F32, kind="ExternalInput").ap()
    io["ident"] = nc.dram_tensor("ident", [128, 128], F32, kind="ExternalInput").ap()
    for n, (shp, dt) in A_OUTS.items():
        io[n] = nc.dram_tensor(n, shp, dt, kind="ExternalOutput").ap()
    with ExitStack() as top:
        S = Sched(nc, top)
        with ExitStack() as es:
            phase_a(S, nc, es, io)
            S.flush()
    return nc


def out_scope(S, nc, es, io, MIXB, last, pfx="o"):
    P = lambda n: pfx + n
    sb = lambda n, s_, d: es.enter_context(nc.sbuf_tensor(P(n), s_, d))
    PS = [es.enter_context(nc.psum_tensor(P("ps%d" % i), [128, 512], F32)) for i in range(4)]
    PSTb = es.enter_context(nc.psum_tensor(P("pstb"), [128, 1024], BF16))
    WO = sb("WO", [128, 8, D], BF16)
    wst = [sb("wst%d" % i, [128, D], F32) for i in range(2)]
    idf = sb("idf", [128, 128], F32)
    idb = sb("idb", [128, 128], BF16)
    MXT = [sb("MXT%d" % i, [128, 8, 128], BF16) for i in range(2)]
    xt = [sb("xt%d" % i, [128, D], F32) for i in range(2)]
    yo = [sb("yo%d" % i, [128, D], F32) for i in range(2)]
    sq = sb("sq", [128, D], F32)
    ss = sb("ss", [128, 1], F32)
    fg = sb("fg", [128, D], F32)
    S.dma(P("c"), batch=True, w=[P("idf")], out=idf[:], in_=io["ident"])
    if last:
        S.dma(P("c"), batch=True, w=[P("fg")], out=fg[:], in_=io["final_g"].partition_broadcast(128))
    S.dve("tensor_copy", r=[P("idf")], w=[P("idb")], out=idb[:], in_=idf[:])
    for c in range(8):
        w, wn = wst[c % 2], P("wst%d" % (c % 2))
        S.dma(P("w%d" % (c % 2)), w=[wn], out=w[:], in_=io["w_out"][c * 128:(c + 1) * 128, :])
        (S.dve if c % 2 == 0 else S.pool)("tensor_copy", r=[wn], w=[P("WO")], out=WO[:, c, :], in_=w[:])
    for m in range(16):
        p = m % 2
        mx, mxn, x_t, xn, y_t, yn = MXT[p], P("MXT%d" % p), xt[p], P("xt%d" % p), yo[p], P("yo%d" % p)
        S.dma(P("x%d" % p), w=[xn], out=x_t[:], in_=io["x"][m * 128:(m + 1) * 128, :])
        for c in range(8):
            S.pe("transpose", r=["MIXB", P("idb")], w=[P("pstb")], out=PSTb[:, c * 128:(c + 1) * 128], in_=MIXB[:, m, c * 128:(c + 1) * 128],
                 identity=idb[:])
        S.act("activation", r=[P("pstb")], w=[mxn], out=mx[:].rearrange("f c q -> f (c q)"), in_=PSTb[:], func=AF.Copy)
        for n2 in range(2):
            k = (2 * m + n2) % 4
            for c in range(8):
                S.pe("matmul", r=[mxn, P("WO")], w=[P("ps%d" % k)], out=PS[k][:], lhsT=mx[:, c, :], rhs=WO[:, c, n2 * 512:(n2 + 1) * 512],
                     start=(c == 0), stop=(c == 7))
            S.act("activation", r=[P("ps%d" % k)], w=[yn], out=y_t[:, n2 * 512:(n2 + 1) * 512], in_=PS[k][:], func=AF.Copy)
        S.dve("tensor_tensor", r=[yn, xn], w=[yn], out=y_t[:], in0=y_t[:], in1=x_t[:], op=ALU.add)
        if last:
            S.act("activation", r=[yn], w=[P("sq"), P("ss")], out=sq[:], in_=y_t[:], func=AF.Square, accum_out=ss[:, 0:1])
            S.dve("tensor_scalar", r=[P("ss")], w=[P("ss")], out=ss[:], in0=ss[:], scalar1=1.0 / D, scalar2=1e-6, op0=ALU.mult, op1=ALU.add)
            S.act("activation", r=[P("ss")], w=[P("ss")], out=ss[:], in_=ss[:], func=AF.Sqrt)
            S.dve("reciprocal", r=[P("ss")], w=[P("ss")], out=ss[:], in_=ss[:])
            S.dve("scalar_tensor_tensor", r=[yn, P("ss"), P("fg")], w=[yn], out=y_t[:], in0=y_t[:], scalar=ss[:, 0:1], in1=fg[:],
                  op0=ALU.mult, op1=ALU.mult)
        S.dma(P("o%d" % p), r=[yn], out=io["xo"][m * 128:(m + 1) * 128, :], in_=y_t[:])


B_INS = dict(kf_g=([2, 128, T], BF16), vf_g=([T, 320], BF16), lf_g=([T, 4], F32), qf=([2, 128, NT], BF16),
             kc_g=([128, T], BF16), vc_g=([128, T], BF16), ks_g=([128, T], BF16), kw_g=([128, T], BF16),
             vs_g=([T, 160], BF16), vw_g=([T, 160], BF16), qn=([4, 128, NT], BF16), gl=([NT, 24], F32), gt=([NT, D], BF16),
             yT=([2, 128, NT], BF16), tails_g=([4, 2, 128, 16, 30], BF16), x=([NT, D], F32), w_out=([D, D], F32), final_g=([D], F32),
             tri=([128, 128], F32), ones128=([128, 128], F32), onehot=([128, 4], F32), dmask=([128, 4, 128], BF16),
             wmask=([128, 8, 128], BF16), cmask=([16, 128, 4, 128], BF16), force=([16, 128, 128], F32), ee=([128, T], BF16),
             ident=([128, 128], F32), rmat=([128, 128], F32), cosC=([128, 512], F32), sinC=([128, 512], F32),
             ovl=([128, 4, 128], BF16), peT=([64, 2, 32], F32), w2=([128, 2, 2, 64], F32), w1k=([2048, 256], F32),
             w1v=([2048, 256], F32), conv_w=([128, 2, 31], F32), conv_b=([128, 2], F32), ln_g=([128, 2], F32), ln_b=([128, 2], F32),
             conv_pw=([256, 256], F32), ohprev=([128, 4], F32))


def phase_b(S, nc, top_unused, io, last):
  with ExitStack() as top:
    KCMP = top.enter_context(nc.sbuf_tensor("KCMP", [128, 512], BF16))
    VCA = top.enter_context(nc.sbuf_tensor("VCA", [128, 4, 2, 208], BF16))
    MIXB = top.enter_context(nc.sbuf_tensor("MIXB", [128, 16, D], BF16))
    for scope in (lambda es: cmp_scope(S, nc, es, io, KCMP, VCA), lambda es: conv_scope(S, nc, es, io, MIXB),
                  lambda es: fox_scope(S, nc, es, io, MIXB), lambda es: nsa_scope(S, nc, es, io, KCMP, VCA, MIXB),
                  lambda es: out_scope(S, nc, es, io, MIXB, last)):
        with ExitStack() as es:
            scope(es)
            S.flush()


def build_b(last):
    nc = bass.Bass("TRN2", target_bir_lowering=False)
    io = {n: nc.dram_tensor(n, shp, dt, kind="ExternalInput").ap() for n, (shp, dt) in B_INS.items()}
    io["xo"] = nc.dram_tensor("xo", [NT, D], F32, kind="ExternalOutput").ap()
    with ExitStack() as top:
        S = Sched(nc, top)
        phase_b(S, nc, top, io, last)
    return nc


def build_ba():
    nc = bass.Bass("TRN2", target_bir_lowering=False)
    io = {n: nc.dram_tensor(n, shp, dt, kind="ExternalInput").ap() for n, (shp, dt) in B_INS.items()}
    io["xo"] = nc.dram_tensor("xo", [NT, D], F32, kind="ExternalOutput").ap()
    ioa = dict(x=io["xo"], rmat=io["rmat"], ident=io["ident"])
    ioa["w_in"] = nc.dram_tensor("a_w_in", [D, DIN], F32, kind="ExternalInput").ap()
    ioa["norm_g"] = nc.dram_tensor("a_norm_g", [128, 8], F32, kind="ExternalInput").ap()
    ioa["fox_b"] = nc.dram_tensor("a_fox_b", [4], F32, kind="ExternalInput").ap()
    ioa["cos"] = nc.dram_tensor("a_cos", [128, NT], F32, kind="ExternalInput").ap()
    ioa["sin"] = nc.dram_tensor("a_sin", [128, NT], F32, kind="ExternalInput").ap()
    for n, (shp, dt) in A_OUTS.items():
        ioa[n] = nc.dram_tensor("a_" + n, shp, dt, kind="ExternalOutput").ap()
    with ExitStack() as top:
        S = Sched(nc, top)
        phase_b(S, nc, top, io, False)
        with ExitStack() as es:
            phase_a(S, nc, es, ioa, pfx="A")
            S.flush()
    return nc


def a_inputs(inputs, layer, x_own):
    maps = []
    for core in range(8):
        cos, sin = rope_tables(own_positions(core % 4))
        maps.append(dict(x=np.ascontiguousarray(x_own[core], dtype=np.float32), w_in=np.ascontiguousarray(inputs["w_in"][layer]),
                         norm_g=np.ascontiguousarray(inputs["norm_g"][layer].reshape(8, 128).T), fox_b=np.ascontiguousarray(inputs["fox_b"][layer]),
                         cos=cos, sin=sin, rmat=rope_rmat(), ident=np.eye(128, dtype=np.float32)))
    return maps


def b_inputs(inputs, layer, resA, x_own):
    maps = []
    cc = common_consts()
    ee = ee_const()
    cmpc = cmp_consts(inputs, layer)
    for core in range(8):
        b, r = core // 4, core % 4
        R = [resA[4 * b + rr] for rr in range(4)]
        cat = lambda n, ax: np.concatenate([np.asarray(R[rr][n]) for rr in range(4)], axis=ax)
        d = dict(kf_g=cat("kf", 2), vf_g=cat("vf", 0), lf_g=cat("lf", 0), kc_g=cat("kc", 1), vc_g=cat("vc", 1), ks_g=cat("ks", 1),
                 kw_g=cat("kw", 1), vs_g=cat("vs", 0), vw_g=cat("vw", 0))
        d["tails_g"] = np.ascontiguousarray(np.stack([np.asarray(R[rr]["yT"]).reshape(2, 128, 16, 128)[:, :, :, 98:128] for rr in range(4)], axis=0))
        own = resA[core]
        for n in ("qf", "qn", "gl", "gt", "yT"):
            d[n] = np.asarray(own[n])
        d["x"] = np.ascontiguousarray(x_own[core], dtype=np.float32)
        d["w_out"] = np.ascontiguousarray(inputs["w_out"][layer])
        d["final_g"] = np.ascontiguousarray(inputs["final_g"])
        d.update(cc)
        d.update(rank_consts(r))
        d.update(nsa_rank_consts(r))
        d.update(cmpc)
        d.update(conv_consts(inputs, layer, r))
        d["ee"] = ee
        d["ident"] = np.eye(128, dtype=np.float32)
        maps.append(d)
    return maps


def kernel(**inputs):
    inputs = {k: np.asarray(v) for k, v in inputs.items()}
    x = inputs["x"].astype(np.float32)
    x_own = [x[core // 4][own_positions(core % 4)] for core in range(8)]
    cores = list(range(8))
    resA0 = run_bass_kernel_spmd(build_a(), a_inputs(inputs, 0, x_own), core_ids=cores).results
    maps = b_inputs(inputs, 0, resA0, x_own)
    a1 = a_inputs(inputs, 1, x_own)
    for core in cores:
        for k in ("w_in", "norm_g", "fox_b", "cos", "sin"):
            maps[core]["a_" + k] = a1[core][k]
    resBA = run_bass_kernel_spmd(build_ba(), maps, core_ids=cores).results
    x1_own = [np.asarray(resBA[core]["xo"]) for core in cores]
    resA1 = [{k[2:]: v for k, v in resBA[core].items() if k.startswith("a_")} for core in cores]
    resB1 = run_bass_kernel_spmd(build_b(last=True), b_inputs(inputs, 1, resA1, x1_own), core_ids=cores).results
    out = np.empty((2, T, D), np.float32)
    for core in cores:
        out[core // 4][own_positions(core % 4)] = np.asarray(resB1[core]["xo"])
    return out


def cidx(j):
    return (j % 4) * 16 + j // 4


def gcol(j):
    return cidx(j) * 128


def fox_scope(S, nc, es, io, MIXB, pfx="f"):
    P = lambda n: pfx + n
    sb = lambda n, s, d: es.enter_context(nc.sbuf_tensor(P(n), s, d))
    PS = [es.enter_context(nc.psum_tensor(P("ps%d" % i), [128, 512], F32)) for i in range(4)]
    KF = sb("KF", [128, 2, T], BF16)
    VF = sb("VF", [128, 64, 320], BF16)
    QF = sb("QF", [128, 4, NT], BF16)
    LF = sb("LF", [128, 64, 4], F32)
    tri = sb("tri", [128, 128], F32)
    ones = sb("ones", [128, 128], F32)
    oneh = sb("oneh", [128, 4], F32)
    dm = sb("dm", [128, 4, 128], BF16)
    dm4 = sb("dm4", [128, 4, 4, 128], BF16)
    WI = sb("WI", [128, 64, 4], F32)
    TOT = sb("TOT", [128, 64, 4], F32)
    SA = sb("SA", [128, 64, 4], F32)
    SB_ = sb("SB", [128, 64, 4], F32)
    NC_ = sb("NC", [128, 64, 4], F32)
    ROWN = sb("ROWN", [128, 16, 4], F32)
    CB = [sb("CB%d" % i, [128, 64, 4], F32) for i in range(2)]
    PT = [sb("PT%d" % i, [128, 512], BF16) for i in range(3)]
    ofs = sb("ofs", [128, 4, 65], F32)
    rden = sb("rden", [128, 4], F32)
    onrm = sb("onrm", [128, 256], F32)
    gtt = [sb("gtt%d" % i, [128, 256], BF16) for i in range(2)]

    ld = P("ld")
    S.pool("memset", w=[P("QF")], ap=QF[:].rearrange("p h t -> p (h t)"), constant=0.0)
    for h2 in range(2):
        S.dma(ld, batch=True, w=[P("KF")], out=KF[:, h2, :], in_=io["kf_g"][h2])
    for h in range(4):
        pb = 64 * (h % 2)
        S.dma(ld, batch=True, w=[P("QF")], out=QF[pb:pb + 64, h, :], in_=io["qf"][h // 2, pb:pb + 64, :])
    for r in range(4):
        S.dma(ld, batch=True, w=[P("VF")], out=VF[:, r * 16:(r + 1) * 16, :],
              in_=io["vf_g"][r * NT:(r + 1) * NT, :].rearrange("(c t) w -> t c w", t=128))
    for r in range(4):
        S.dma(ld, batch=True, w=[P("LF")], out=LF[:].rearrange("t (m r) h -> t m r h", r=4)[:, :, r, :],
              in_=io["lf_g"][r * NT:(r + 1) * NT, :].rearrange("(m t) h -> t m h", t=128))
    S.dma(ld, batch=True, w=[P("tri")], out=tri[:], in_=io["tri"])
    S.dma(ld, batch=True, w=[P("ones")], out=ones[:], in_=io["ones128"])
    S.dma(ld, batch=True, w=[P("oneh")], out=oneh[:], in_=io["onehot"])
    S.dma(ld, batch=True, w=[P("dm")], out=dm[:], in_=io["dmask"])
    for h in range(4):
        S.pool("tensor_copy", r=[P("dm")], w=[P("dm4")], out=dm4[:, :, h, :], in_=dm[:])

    LFf = LF[:].rearrange("t j h -> t (j h)")
    S.pe("matmul", r=[P("tri"), P("LF")], w=[P("ps0")], out=PS[0][:, 0:256], lhsT=tri[:], rhs=LFf, start=True, stop=True)
    S.pe("matmul", r=[P("ones"), P("LF")], w=[P("ps1")], out=PS[1][:, 0:256], lhsT=ones[:], rhs=LFf, start=True, stop=True)
    S.act("activation", r=[P("ps0")], w=[P("WI")], out=WI[:].rearrange("t j h -> t (j h)"), in_=PS[0][:, 0:256], func=AF.Copy)
    S.act("activation", r=[P("ps1")], w=[P("TOT")], out=TOT[:].rearrange("t j h -> t (j h)"), in_=PS[1][:, 0:256], func=AF.Copy)
    src, srcn = TOT, P("TOT")
    d = 1
    pp = [(SA, P("SA")), (SB_, P("SB"))]
    k = 0
    while d < 64:
        dst, dstn = pp[k % 2]
        S.dve("tensor_tensor", r=[srcn], w=[dstn], out=dst[:, d:, :], in0=src[:, d:, :], in1=src[:, :64 - d, :], op=ALU.add)
        S.dve("tensor_copy", r=[srcn], w=[dstn], out=dst[:, :d, :], in_=src[:, :d, :])
        src, srcn = dst, dstn
        d *= 2
        k += 1
    INCL, INCLn = src, srcn
    S.dve("tensor_tensor", r=[P("WI"), INCLn], w=[P("NC")], out=NC_[:], in0=WI[:], in1=INCL[:], op=ALU.add)
    S.dve("tensor_tensor", r=[P("NC"), P("TOT")], w=[P("NC")], out=NC_[:], in0=TOT[:], in1=NC_[:], op=ALU.subtract)
    I4 = INCL[:].rearrange("t (m r) h -> t m r h", r=4)
    S.dve("tensor_scalar", r=[INCLn, P("oneh")], w=[P("ROWN")], out=ROWN[:], in0=I4[:, :, 0, :], scalar1=oneh[:, 0:1],
          scalar2=None, op0=ALU.mult)
    for r in range(1, 4):
        S.dve("scalar_tensor_tensor", r=[INCLn, P("oneh"), P("ROWN")], w=[P("ROWN")], out=ROWN[:], in0=I4[:, :, r, :],
              scalar=oneh[:, r:r + 1], in1=ROWN[:], op0=ALU.mult, op1=ALU.add)

    st_i = [0]
    for m in range(16):
        J = 4 * m + 4
        cb, cbn = CB[m % 2], P("CB%d" % (m % 2))
        for h in range(4):
            S.dve("tensor_scalar", r=[P("NC"), P("ROWN")], w=[cbn + "_%d" % h], out=cb[:, 0:J, h], in0=NC_[:, 0:J, h],
                  scalar1=ROWN[:, m, h:h + 1], scalar2=0.0, op0=ALU.add, op1=ALU.min)
        for j in range(J):
            sp = st_i[0] % 2
            pt, ptn = PT[st_i[0] % 3], P("PT%d" % (st_i[0] % 3))
            st_i[0] += 1
            psn = P("ps%d" % sp)
            for h in range(4):
                S.pe("matmul", r=[P("KF"), P("QF")], w=[psn], out=PS[sp][:, h * 128:(h + 1) * 128],
                     lhsT=KF[:, h // 2, gcol(j):gcol(j) + 128], rhs=QF[:, h, m * 128:(m + 1) * 128], start=True, stop=True)
            pth = [ptn + "_%d" % h for h in range(4)]
            for h in range(4):
                S.act("activation", r=[psn, cbn + "_%d" % h], w=[pth[h]], out=pt[:, h * 128:(h + 1) * 128], in_=PS[sp][:, h * 128:(h + 1) * 128],
                      func=AF.Exp, bias=cb[:, j, h:h + 1], scale=0.125)
            if j >= 4 * m:
                S.dve("tensor_tensor", r=pth + [P("dm4")], w=pth, out=pt[:], in0=pt[:],
                      in1=dm4[:, j - 4 * m, :, :].rearrange("k h q -> k (h q)"), op=ALU.mult)
            for h in range(4):
                S.pe("matmul", r=[pth[h], P("VF")], w=[P("ps2")], out=PS[2][:, h * 65:(h + 1) * 65], lhsT=pt[:, h * 128:(h + 1) * 128],
                     rhs=VF[:, cidx(j), h * 80:h * 80 + 65], start=(j == 0 and h == 0), stop=(j == J - 1 and h == 3))
        S.act("activation", r=[P("ps2")], w=[P("ofs")], out=ofs[:].rearrange("q h d -> q (h d)"), in_=PS[2][:, 0:260], func=AF.Copy)
        S.dve("reciprocal", r=[P("ofs")], w=[P("rden")], out=rden[:], in_=ofs[:, :, 64])
        for h in range(4):
            S.dve("tensor_scalar", r=[P("ofs"), P("rden")], w=[P("onrm%d" % h)], out=onrm[:, h * 64:(h + 1) * 64], in0=ofs[:, h, 0:64],
                  scalar1=rden[:, h:h + 1], scalar2=None, op0=ALU.mult)
        gp = m % 2
        S.dma(P("gt%d" % gp), w=[P("gtt%d" % gp)], out=gtt[gp][:], in_=io["gt"][m * 128:(m + 1) * 128, 0:256])
        S.dve("tensor_tensor", r=[P("onrm%d" % h) for h in range(4)] + [P("gtt%d" % gp)], w=["MIXB"], out=MIXB[:, m, 0:256], in0=onrm[:],
              in1=gtt[gp][:], op=ALU.mult)


def rank_consts(r):
    k = np.arange(128)[:, None]
    q = np.arange(128)[None, :]
    dmask = np.zeros((128, 4, 128), np.float32)
    for c in range(4):
        if c < r:
            dmask[:, c, :] = 1.0
        elif c == r:
            dmask[:, c, :] = (k <= q)
    onehot = np.zeros((128, 4), np.float32)
    onehot[:, r] = 1.0
    return dict(dmask=dmask.astype(NPBF), onehot=onehot)


def common_consts():
    k = np.arange(128)[:, None]
    q = np.arange(128)[None, :]
    return dict(tri=(k <= q).astype(np.float32), ones128=np.ones((128, 128), np.float32))


def cmp_scope(S, nc, es, io, KCMP, VCA, pfx="c"):
    P = lambda n: pfx + n
    sb = lambda n, s, d: es.enter_context(nc.sbuf_tensor(P(n), s, d))
    PS = [es.enter_context(nc.psum_tensor(P("ps%d" % i), [128, 512], F32)) for i in range(4)]
    KCN = [sb("KCN%d" % i, [128, T], BF16) for i in range(2)]
    W1st = sb("W1st", [128, 32, 256], F32)
    W1Z = [sb("W1Z%d" % g, [128, 32, 256], BF16) for g in range(2)]
    PEf = sb("PEf", [128, 2, 32], F32)
    PEB = sb("PEB", [128, 2, 32], BF16)
    W2f = sb("W2f", [128, 2, 2, 64], F32)
    W2KZ = sb("W2KZ", [128, 2, 2, 128], BF16)
    W2V = sb("W2V", [128, 2, 64], BF16)
    BH = sb("BH", [128, 2], F32)
    HID = [[sb("HID%d%d" % (kv, g), [128, 2, 512], BF16) for g in range(2)] for kv in range(2)]
    X = sb("X", [128, 512], F32)
    X2 = sb("X2", [128, 512], F32)
    U = sb("U", [128, 512], F32)
    SG = sb("SG", [128, 512], F32)
    KC0 = sb("KC0", [128, 512], F32)
    KZ = sb("KZ", [128, 512], BF16)
    RT = sb("RT", [128, 512], F32)
    cosC = sb("cosC", [128, 512], F32)
    sinC = sb("sinC", [128, 512], F32)
    rmf = sb("rmf", [128, 128], F32)
    rmb = sb("rmb", [128, 128], BF16)
    ovf = sb("ovf", [128, 4, 128], BF16)

    ld = P("ld")
    for kv, nm in ((0, "kc_g"), (1, "vc_g")):
        for r in range(4):
            S.dma(ld, batch=True, w=[P("KCN%d" % kv)], out=KCN[kv][:].rearrange("d (m r t) -> d m r t", r=4, t=128)[:, :, r, :],
                  in_=io[nm][:, r * NT:(r + 1) * NT].rearrange("d (m t) -> d m t", t=128))
    S.dma(ld, batch=True, w=[P("PEf")], out=PEf[0:64, :, :], in_=io["peT"])
    S.dma(ld, batch=True, w=[P("PEf")], out=PEf[64:128, :, :], in_=io["peT"])
    S.dma(ld, batch=True, w=[P("W2f")], out=W2f[:], in_=io["w2"])
    S.dma(ld, batch=True, w=[P("cosC")], out=cosC[:], in_=io["cosC"])
    S.dma(ld, batch=True, w=[P("sinC")], out=sinC[:], in_=io["sinC"])
    S.dma(ld, batch=True, w=[P("rmf")], out=rmf[:], in_=io["rmat"])
    S.dma(ld, batch=True, w=[P("ovf")], out=ovf[:], in_=io["ovl"])
    S.dve("tensor_copy", r=[P("PEf")], w=[P("PEB")], out=PEB[:], in_=PEf[:])
    S.dve("tensor_copy", r=[P("rmf")], w=[P("rmb")], out=rmb[:], in_=rmf[:])
    S.pool("memset", w=[P("W2KZ")], ap=W2KZ[:], constant=0.0)
    for g in range(2):
        S.pool("memset", w=[P("W1Z%d" % g)], ap=W1Z[g][:], constant=0.0)
        S.dve("tensor_copy", r=[P("W2f"), P("W2KZ")], w=[P("W2KZ")], out=W2KZ[:, g, :, 64 * g:64 * g + 64], in_=W2f[:, 0, :, :])
        for kv in range(2):
            S.pool("memset", w=[P("HID%d%d" % (kv, g))], ap=HID[kv][g][:], constant=0.0)
    S.dve("tensor_copy", r=[P("W2f")], w=[P("W2V")], out=W2V[:], in_=W2f[:, 1, :, :])
    S.pool("memset", w=[P("KC0")], ap=KC0[:], constant=0.0)
    S.pool("memset", w=["VCA"], ap=VCA[:], constant=1.0)
    for g in range(2):
        S.pool("tensor_copy", r=[P("ovf"), "VCA"], w=["VCA"], out=VCA[:, :, g, 80:208], in_=ovf[:])

    for kv, wn in ((0, "w1k"), (1, "w1v")):
        for half in range(2):
            S.dma(P("w1"), w=[P("W1st")], out=W1st[64 * half:64 * half + 64, :, :],
                  in_=io[wn].rearrange("(l d) h -> d l h", d=64))
        for g in range(2):
            S.dve("tensor_copy", r=[P("W1st"), P("W1Z%d" % g)], w=[P("W1Z%d" % g)], out=W1Z[g][64 * g:64 * g + 64, :, :],
                  in_=W1st[64 * g:64 * g + 64, :, :])
        for hc in range(2):
            for l in range(32):
                S.pe("matmul", r=[P("W1Z0"), P("PEB")], w=[P("ps3")], out=PS[3][:, hc:hc + 1], lhsT=W1Z[0][:, l, hc * 128:(hc + 1) * 128],
                     rhs=PEB[:, kv, l:l + 1], start=(hc == 0 and l == 0), stop=(hc == 1 and l == 31))
        S.act("activation", r=[P("ps3")], w=[P("BH")], out=BH[:], in_=PS[3][:, 0:2], func=AF.Copy)
        for g in range(2):
            for hc in range(2):
                k = (g * 2 + hc) % 3
                psn = P("ps%d" % k)
                for l in range(32):
                    S.pe("matmul", r=[P("W1Z%d" % g), P("KCN%d" % kv)], w=[psn], out=PS[k][:, 0:511],
                         lhsT=W1Z[g][:, l, hc * 128:(hc + 1) * 128], rhs=KCN[kv][:, l:l + 16 * 510 + 1:16], start=(l == 0), stop=(l == 31))
                S.act("activation", r=[psn, P("BH")], w=[P("X")], out=X[:, 0:511], in_=PS[k][:, 0:511], func=AF.Identity,
                      bias=BH[:, hc:hc + 1], scale=1.0)
                S.dve("tensor_tensor", r=[P("X")], w=[P("X2")], out=X2[:, 0:511], in0=X[:, 0:511], in1=X[:, 0:511], op=ALU.mult)
                S.dve("tensor_scalar", r=[P("X2")], w=[P("X2")], out=X2[:, 0:511], in0=X2[:, 0:511], scalar1=0.044715, scalar2=1.0,
                      op0=ALU.mult, op1=ALU.add)
                S.dve("tensor_tensor", r=[P("X2"), P("X")], w=[P("U")], out=U[:, 0:511], in0=X2[:, 0:511], in1=X[:, 0:511], op=ALU.mult)
                S.act("activation", r=[P("U")], w=[P("SG")], out=SG[:, 0:511], in_=U[:, 0:511], func=AF.Sigmoid, scale=1.5957691216057308)
                S.dve("tensor_tensor", r=[P("X"), P("SG")], w=[P("HID%d%d" % (kv, g))], out=HID[kv][g][:, hc, 0:511], in0=X[:, 0:511],
                      in1=SG[:, 0:511], op=ALU.mult)
    n = 0
    for g in range(2):
        for hc in range(2):
            S.pe("matmul", r=[P("W2KZ"), P("HID0%d" % g)], w=[P("ps0")], out=PS[0][:, 0:511], lhsT=W2KZ[:, g, hc, :],
                 rhs=HID[0][g][:, hc, 0:511], start=(n == 0), stop=(n == 3))
            n += 1
    S.act("activation", r=[P("ps0")], w=[P("KC0")], out=KC0[:, 0:511], in_=PS[0][:, 0:511], func=AF.Copy)
    S.dve("tensor_copy", r=[P("KC0")], w=[P("KZ")], out=KZ[:], in_=KC0[:])
    S.pe("matmul", r=[P("rmb"), P("KZ")], w=[P("ps1")], out=PS[1][:], lhsT=rmb[:], rhs=KZ[:], start=True, stop=True)
    S.act("activation", r=[P("ps1")], w=[P("RT")], out=RT[:], in_=PS[1][:], func=AF.Copy)
    S.dve("tensor_tensor", r=[P("KC0"), P("cosC")], w=[P("KC0")], out=KC0[:], in0=KC0[:], in1=cosC[:], op=ALU.mult)
    S.dve("tensor_tensor", r=[P("RT"), P("sinC")], w=[P("RT")], out=RT[:], in0=RT[:], in1=sinC[:], op=ALU.mult)
    S.dve("tensor_tensor", r=[P("KC0"), P("RT")], w=["KCMP"], out=KCMP[:], in0=KC0[:], in1=RT[:], op=ALU.add)
    for c in range(4):
        for g in range(2):
            k = 2 + (c * 2 + g) % 2
            for hc in range(2):
                S.pe("matmul", r=[P("HID1%d" % g), P("W2V")], w=[P("ps%d" % k)], out=PS[k][:, 0:64], lhsT=HID[1][g][:, hc, c * 128:(c + 1) * 128],
                     rhs=W2V[:, hc, :], start=(hc == 0), stop=(hc == 1))
            S.act("activation", r=[P("ps%d" % k), "VCA"], w=["VCA"], out=VCA[:, c, g, 0:64], in_=PS[k][:, 0:64], func=AF.Copy)


def cmp_consts(inputs, layer):
    n = np.arange(512)
    cosC, sinC = rope_tables(16 * n + 31)
    nn = np.arange(512)[:, None]
    bb = np.arange(128)[None, :]
    ovl = ((16 * nn < 64 * bb + 64) & (16 * nn + 31 >= 64 * bb) & (nn < 511)).astype(np.float32)
    peT = np.stack([inputs["cmp_pe_k"][layer].T, inputs["cmp_pe_v"][layer].T], axis=1)
    w2 = np.stack([inputs["cmp_k_w2"][layer].reshape(2, 128, 64), inputs["cmp_v_w2"][layer].reshape(2, 128, 64)], axis=0)
    w2 = np.ascontiguousarray(w2.transpose(2, 0, 1, 3))
    return dict(cosC=cosC, sinC=sinC, ovl=np.ascontiguousarray(ovl.reshape(4, 128, 128).transpose(1, 0, 2)).astype(NPBF),
                peT=np.ascontiguousarray(peT, dtype=np.float32), w2=w2.astype(np.float32),
                w1k=np.ascontiguousarray(inputs["cmp_k_w1"][layer]), w1v=np.ascontiguousarray(inputs["cmp_v_w1"][layer]),
                rmat=rope_rmat())


def nsa_scope(S, nc, es, io, KCMP, VCA, MIXB, pfx="n"):
    P = lambda n: pfx + n
    sb = lambda n, s, d: es.enter_context(nc.sbuf_tensor(P(n), s, d))
    PS = [es.enter_context(nc.psum_tensor(P("ps%d" % i), [128, 512], F32)) for i in range(6)]
    PSTb = es.enter_context(nc.psum_tensor(P("pstb"), [128, 1024], BF16))
    KS = sb("KS", [128, T], BF16)
    VS = sb("VS", [128, 64, 160], BF16)
    EE = sb("EE", [128, T], BF16)
    QNZ = sb("QNZ", [128, 2, 4, NT], BF16)
    KWm = [sb("KWm%d" % i, [128, 8, 128], BF16) for i in range(2)]
    VWm = [sb("VWm%d" % i, [128, 8, 160], BF16) for i in range(2)]
    dm = sb("dm", [128, 4, 128], BF16)
    wm = sb("wm", [128, 8, 128], BF16)
    cm = [sb("cm%d" % i, [128, 4, 128], BF16) for i in range(2)]
    fc = [sb("fc%d" % i, [128, 128], F32) for i in range(2)]
    glt = [sb("glt%d" % i, [128, 24], F32) for i in range(2)]
    gtt = [sb("gtt%d" % i, [128, 512], BF16) for i in range(2)]
    idf = sb("idf", [128, 128], F32)
    idb = sb("idb", [128, 128], BF16)
    PT = [sb("PT%d" % i, [128, 512], BF16) for i in range(3)]
    ocs = sb("ocs", [128, 4, 208], F32)
    oss = sb("oss", [128, 4, 65], F32)
    ows = sb("ows", [128, 4, 65], F32)
    rd = sb("rd", [128, 3, 4], F32)
    fac = sb("fac", [128, 3, 4], F32)
    imp = sb("imp", [128, 128], F32)
    wk = sb("wk", [128, 128], F32)
    mx = sb("mx", [128, 8], F32)
    MBq = sb("MBq", [128, 128], BF16)
    MBT = sb("MBT", [128, 4, 128], BF16)
    acc = sb("acc", [128, 4, 64], F32)
    tmp = sb("tmp", [128, 4, 64], F32)

    def bc(ap, h=4):
        return ap.unsqueeze(1).to_broadcast([128, h, ap.shape[-1]])

    ld = P("ld")
    S.dma(ld, batch=True, w=[P("KS")], out=KS[:], in_=io["ks_g"])
    S.dma(ld, batch=True, w=[P("EE")], out=EE[:], in_=io["ee"])
    for r in range(4):
        S.dma(ld, batch=True, w=[P("VS")], out=VS[:, r * 16:(r + 1) * 16, :],
              in_=io["vs_g"][r * NT:(r + 1) * NT, :].rearrange("(c t) w -> t c w", t=128))
    S.pool("memset", w=[P("QNZ")], ap=QNZ[:].rearrange("p g h t -> p (g h t)"), constant=0.0)
    for hh in range(8):
        g, h = hh // 4, hh % 4
        src = 64 * (hh % 2)
        S.dma(ld, batch=True, w=[P("QNZ")], out=QNZ[64 * g:64 * g + 64, g, h, :], in_=io["qn"][hh // 2, src:src + 64, :])
    S.dma(ld, batch=True, w=[P("dm")], out=dm[:], in_=io["dmask"])
    S.dma(ld, batch=True, w=[P("wm")], out=wm[:], in_=io["wmask"])
    S.dma(ld, batch=True, w=[P("idf")], out=idf[:], in_=io["ident"])
    S.dve("tensor_copy", r=[P("idf")], w=[P("idb")], out=idb[:], in_=idf[:])

    units = []
    load_fns = []

    def add_unit(**kw):
        units.append(kw)

    def emit_s(i, u):
        sp = i % 2
        psn = P("ps%d" % sp)
        for fn in u.get("pre", ()):
            fn()
        S.pe("matmul", r=list(u["rk"]) + [P("QNZ")], w=[psn], out=PS[sp][:], lhsT=u["lhsT"], rhs=u["qrhs"], start=True,
             stop=(u.get("bias_mm") is None))
        if u.get("bias_mm") is not None:
            S.pe("matmul", r=[P("EE"), P("MBT")], w=[psn], out=PS[sp][:], lhsT=u["bias_mm"], rhs=MBT[:].rearrange("b h q -> b (h q)"),
                 start=False, stop=True)

    def emit_rest(i, u):
        sp = i % 2
        psn = P("ps%d" % sp)
        pt, ptn = PT[i % 3], P("PT%d" % (i % 3))
        S.act("activation", r=[psn], w=[ptn], out=pt[:], in_=PS[sp][:], func=AF.Exp, scale=0.125)
        if u.get("mask") is not None:
            mk, mn = u["mask"]
            S.pool("tensor_tensor", r=[ptn, mn], w=[ptn], out=pt[:].rearrange("k (h q) -> k h q", h=4),
                   in0=pt[:].rearrange("k (h q) -> k h q", h=4), in1=bc(mk), op=ALU.mult)
        obank, ocols, first, last = u["obank"], u["ocols"], u["first"], u["last"]
        for h in range(4):
            ob, oc = obank(h), ocols(h)
            S.pe("matmul", r=[ptn] + list(u["vk"]), w=[P("ps%d" % ob)], out=PS[ob][:, oc[0]:oc[1]], lhsT=pt[:, h * 128:(h + 1) * 128],
                 rhs=u["vrhs"], start=(first and (h == 0 or (ob != obank(0) and h == 2))),
                 stop=(last and (h == 3 or (ob != obank(3) and h == 1))))
        for fn in u.get("post", ()):
            fn()

    for m in range(16):
        p = m % 2
        cmn, fcn, gln, gtn, kwn, vwn = P("cm%d" % p), P("fc%d" % p), P("glt%d" % p), P("gtt%d" % p), P("KWm%d" % p), P("VWm%d" % p)

        def loads(m=m, p=p, cmn=cmn, fcn=fcn, gln=gln, gtn=gtn, kwn=kwn, vwn=vwn):
            S.dma(P("cm%d" % p), w=[cmn], out=cm[p][:], in_=io["cmask"][m])
            S.dma(P("fc%d" % p), w=[fcn], out=fc[p][:], in_=io["force"][m])
            S.dma(P("gl%d" % p), w=[gln], out=glt[p][:], in_=io["gl"][m * 128:(m + 1) * 128, :])
            S.dma(P("gt%d" % p), w=[gtn], out=gtt[p][:], in_=io["gt"][m * 128:(m + 1) * 128, 512:1024])
            for mm in ((m - 1, m) if m > 0 else (m,)):
                s0 = 4 * (mm - m + 1)
                S.dma(P("kw%d" % p), w=[kwn], out=KWm[p][:, s0:s0 + 4, :],
                      in_=io["kw_g"].rearrange("d (r c t) -> d r c t", r=4, t=128)[:, :, mm, :])
                S.dma(P("vw%d" % p), w=[vwn], out=VWm[p][:, s0:s0 + 4, :],
                      in_=io["vw_g"].rearrange("(r c t) w -> t r c w", r=4, t=128)[:, :, mm, :])

        load_fns.append(loads)
    load_fns[0]()
    for m in range(16):
        p = m % 2
        cmn, fcn, gln, gtn, kwn, vwn = P("cm%d" % p), P("fc%d" % p), P("glt%d" % p), P("gtt%d" % p), P("KWm%d" % p), P("VWm%d" % p)
        nxt_loads = load_fns[m + 1] if m + 1 < 16 else None
        qs = slice(m * 128, (m + 1) * 128)
        for g in range(2):
            qrhs = QNZ[:, g, :, qs]

            def cmp_post(p=p, fcn=fcn):
                S.act("activation", r=[P("ps2")], w=[P("ocs")], out=ocs[:, 0:2, :].rearrange("q h w -> q (h w)"), in_=PS[2][:, 0:416], func=AF.Copy)
                S.act("activation", r=[P("ps3")], w=[P("ocs")], out=ocs[:, 2:4, :].rearrange("q h w -> q (h w)"), in_=PS[3][:, 0:416], func=AF.Copy)
                S.dve("tensor_scalar", r=[P("ocs")], w=[P("rd0")], out=rd[:, 0, :], in0=ocs[:, :, 64], scalar1=1e-30, scalar2=None, op0=ALU.max)
                S.dve("reciprocal", r=[P("rd0")], w=[P("rd0")], out=rd[:, 0, :], in_=rd[:, 0, :])
                S.dve("tensor_scalar", r=[P("ocs"), P("rd0")], w=[P("imp")], out=imp[:], in0=ocs[:, 0, 80:208], scalar1=rd[:, 0, 0:1],
                      scalar2=None, op0=ALU.mult)
                for h in range(1, 4):
                    S.dve("scalar_tensor_tensor", r=[P("ocs"), P("rd0"), P("imp")], w=[P("imp")], out=imp[:], in0=ocs[:, h, 80:208],
                          scalar=rd[:, 0, h:h + 1], in1=imp[:], op0=ALU.mult, op1=ALU.add)
                S.dve("tensor_tensor", r=[P("imp"), fcn], w=[P("imp")], out=imp[:], in0=imp[:], in1=fc[p][:], op=ALU.max)
                S.dve("max", r=[P("imp")], w=[P("mx")], out=mx[:], in_=imp[:])
                S.dve("match_replace", r=[P("mx"), P("imp")], w=[P("wk")], out=wk[:], in_to_replace=mx[:], in_values=imp[:], imm_value=-1.0)
                S.dve("max", r=[P("wk")], w=[P("mx")], out=mx[:], in_=wk[:])
                S.dve("tensor_scalar", r=[P("imp"), P("mx")], w=[P("wk")], out=wk[:], in0=imp[:], scalar1=mx[:, 7:8], scalar2=None, op0=ALU.is_ge)
                S.dve("tensor_scalar", r=[P("wk")], w=[P("MBq")], out=MBq[:], in0=wk[:], scalar1=-1.0, scalar2=30000.0, op0=ALU.add, op1=ALU.mult)

            def slc_pre():
                S.pe("transpose", r=[P("MBq"), P("idb")], w=[P("pstb")], out=PSTb[:, 0:128], in_=MBq[:], identity=idb[:])
                S.act("activation", r=[P("pstb")], w=[P("MBT")], out=MBT[:], in_=bc(PSTb[:, 0:128]), func=AF.Copy)

            def win_post():
                S.act("activation", r=[P("ps5")], w=[P("ows")], out=ows[:].rearrange("q h w -> q (h w)"), in_=PS[5][:, 0:260], func=AF.Copy)
                S.dve("reciprocal", r=[P("ows")], w=[P("rd2")], out=rd[:, 2, :], in_=ows[:, :, 64])

            def slc_post(m=m, g=g, p=p, gln=gln, gtn=gtn):
                S.act("activation", r=[P("ps4")], w=[P("oss")], out=oss[:].rearrange("q h w -> q (h w)"), in_=PS[4][:, 0:260], func=AF.Copy)
                S.dve("reciprocal", r=[P("oss")], w=[P("rd1")], out=rd[:, 1, :], in_=oss[:, :, 64])
                S.dve("tensor_tensor", r=[P("rd0"), P("rd1"), P("rd2"), gln], w=[P("fac")], out=fac[:], in0=rd[:],
                      in1=glt[p][:, g * 12:g * 12 + 12].rearrange("q (h b) -> q b h", b=3), op=ALU.mult)
                srcs = ((ocs, P("ocs")), (oss, P("oss")), (ows, P("ows")))
                for br in range(3):
                    o_t, o_n = srcs[br]
                    dst, dstn = (acc, P("acc")) if br == 0 else (tmp, P("tmp"))
                    S.dve("tensor_tensor", r=[o_n, P("fac")], w=[dstn], out=dst[:], in0=o_t[:, :, 0:64],
                          in1=fac[:, br, :].unsqueeze(2).to_broadcast([128, 4, 64]), op=ALU.mult)
                    if br > 0:
                        S.dve("tensor_tensor", r=[P("acc"), P("tmp")], w=[P("acc")], out=acc[:], in0=acc[:], in1=tmp[:], op=ALU.add)
                S.dve("tensor_tensor", r=[P("acc"), gtn], w=["MIXB"], out=MIXB[:, m, 512 + g * 256:512 + g * 256 + 256],
                      in0=acc[:].rearrange("q h d -> q (h d)"), in1=gtt[p][:, g * 256:g * 256 + 256], op=ALU.mult)

            for c in range(4):
                add_unit(lhsT=KCMP[:, c * 128:(c + 1) * 128], rk=["KCMP"], qrhs=qrhs, mask=(cm[p][:, c, :], cmn), vrhs=VCA[:, c, g, :], vk=["VCA"],
                         obank=(lambda h: 2 + h // 2), ocols=(lambda h: ((h % 2) * 208, (h % 2) * 208 + 208)), first=(c == 0), last=(c == 3),
                         post=(([nxt_loads] if (g == 0 and c == 0 and nxt_loads is not None) else []) + ([cmp_post] if c == 3 else [])))
            cl = list(range(8)) if m > 0 else list(range(4, 8))
            for c in cl:
                add_unit(lhsT=KWm[p][:, c, :], rk=[kwn], qrhs=qrhs, mask=(wm[:, c, :], P("wm")), vrhs=VWm[p][:, c, g * 80:g * 80 + 65], vk=[vwn],
                         obank=(lambda h: 5), ocols=(lambda h: (h * 65, h * 65 + 65)), first=(c == cl[0]), last=(c == cl[-1]),
                         post=([win_post] if c == cl[-1] else []))
            J = 4 * m + 4
            for j in range(J):
                add_unit(lhsT=KS[:, gcol(j):gcol(j) + 128], rk=[P("KS")], qrhs=qrhs,
                         mask=((dm[:, j - 4 * m, :], P("dm")) if j >= 4 * m else None), vrhs=VS[:, cidx(j), g * 80:g * 80 + 65], vk=[P("VS")],
                         obank=(lambda h: 4), ocols=(lambda h: (h * 65, h * 65 + 65)), first=(j == 0), last=(j == J - 1),
                         bias_mm=EE[:, gcol(j):gcol(j) + 128], pre=([slc_pre] if j == 0 else []), post=([slc_post] if j == J - 1 else []))

    for i, u in enumerate(units):
        emit_s(i, u)
        if i > 0:
            emit_rest(i - 1, units[i - 1])
    emit_rest(len(units) - 1, units[-1])


def nsa_rank_consts(r):
    k = np.arange(128)[:, None]
    q = np.arange(128)[None, :]
    wmask = np.zeros((128, 8, 128), np.float32)
    for c in range(8):
        d = (4 + r) - c
        if d == 4:
            wmask[:, c, :] = (k > q)
        elif 1 <= d <= 3:
            wmask[:, c, :] = 1.0
        elif d == 0:
            wmask[:, c, :] = (k <= q)
    cmask = np.zeros((16, 128, 4, 128), np.float32)
    force = np.zeros((16, 128, 128), np.float32)
    nl = np.arange(128)[:, None]
    b = np.arange(128)[None, :]
    for m in range(16):
        tq = (4 * m + r) * 128 + np.arange(128)
        for c in range(4):
            n = 128 * c + nl
            cmask[m, :, c, :] = ((16 * n + 31) <= tq[None, :]) & (n < 511)
        cur = (tq // 64)[:, None]
        force[m] = 1e4 * (b == 0) + 2e4 * (b == cur) + 3e4 * (b == cur - 1)
    return dict(wmask=wmask.astype(NPBF), cmask=cmask.astype(NPBF), force=force.astype(np.float32))


def ee_const():
    ee = np.zeros((128, T), np.float32)
    for j in range(64):
        for half in range(2):
            ee[2 * j + half, gcol(j) + 64 * half: gcol(j) + 64 * half + 64] = 1.0
    return ee.astype(NPBF)


def conv_scope(S, nc, es, io, MIXB, pfx="v"):
    P = lambda n: pfx + n
    sb = lambda n, s, d: es.enter_context(nc.sbuf_tensor(P(n), s, d))
    PS = [es.enter_context(nc.psum_tensor(P("ps%d" % i), [128, 512], F32)) for i in range(4)]
    YB = sb("YB", [128, 2, NT], BF16)
    TLB = sb("TLB", [128, 2, 4, 16, 30], BF16)
    YE = sb("YE", [128, 2, 16, 158], BF16)
    DIAG = sb("DIAG", [128, 2, 31, 128], BF16)
    idf = sb("idf", [128, 128], F32)
    accA = sb("accA", [128, 2, 16, 128], F32)
    SQ = sb("SQ", [128, 2, NT], F32)
    MEAN = sb("MEAN", [128, NT], F32)
    MSQ = sb("MSQ", [128, NT], F32)
    CW = sb("CW", [128, 2, 31], F32)
    CBs = sb("CBs", [128, 2], F32)
    LG = sb("LG", [128, 2], F32)
    LB = sb("LB", [128, 2], F32)
    ohp = sb("ohp", [128, 4], F32)
    onesd = sb("onesd", [128, 128], F32)
    WPf = sb("WPf", [128, 2, 256], F32)
    WPW = sb("WPW", [128, 2, 256], BF16)
    ACTT = sb("ACTT", [128, 2, NT], BF16)
    ob = [sb("ob%d" % i, [128, 256], F32) for i in range(2)]
    gtt = [sb("gtt%d" % i, [128, 256], BF16) for i in range(2)]

    ld = P("ld")
    for cc in range(2):
        S.dma(ld, batch=True, w=[P("YB")], out=YB[:, cc, :], in_=io["yT"][cc])
        S.dma(ld, batch=True, w=[P("WPf")], out=WPf[:, cc, :], in_=io["conv_pw"][cc * 128:(cc + 1) * 128, :])
        for r in range(4):
            S.dma(ld, batch=True, w=[P("TLB")], out=TLB[:, cc, r, :, :], in_=io["tails_g"][r, cc])
    S.dma(ld, batch=True, w=[P("CW")], out=CW[:], in_=io["conv_w"])
    S.dma(ld, batch=True, w=[P("CBs")], out=CBs[:], in_=io["conv_b"])
    S.dma(ld, batch=True, w=[P("LG")], out=LG[:], in_=io["ln_g"])
    S.dma(ld, batch=True, w=[P("LB")], out=LB[:], in_=io["ln_b"])
    S.dma(ld, batch=True, w=[P("ohp")], out=ohp[:], in_=io["ohprev"])
    S.dma(ld, batch=True, w=[P("idf")], out=idf[:], in_=io["ident"])
    S.dve("memset", w=[P("onesd")], ap=onesd[:], constant=1.0 / 256.0)
    S.dve("tensor_copy", r=[P("WPf")], w=[P("WPW")], out=WPW[:], in_=WPf[:])
    for cc in range(2):
        S.act("activation", r=[P("YB")], w=[P("YE")], out=YE[:, cc, :, 30:158], in_=YB[:, cc, :].rearrange("p (m t) -> p m t", t=128), func=AF.Copy)
        S.dve("tensor_scalar", r=[P("TLB"), P("ohp")], w=[P("YE")], out=YE[:, cc, :, 0:30], in0=TLB[:, cc, 0, :, :], scalar1=ohp[:, 0:1],
              scalar2=None, op0=ALU.mult)
        for r in (1, 2):
            S.dve("scalar_tensor_tensor", r=[P("TLB"), P("ohp"), P("YE")], w=[P("YE")], out=YE[:, cc, :, 0:30], in0=TLB[:, cc, r, :, :],
                  scalar=ohp[:, r:r + 1], in1=YE[:, cc, :, 0:30], op0=ALU.mult, op1=ALU.add)
        S.dve("scalar_tensor_tensor", r=[P("TLB"), P("ohp"), P("YE")], w=[P("YE")], out=YE[:, cc, 1:16, 0:30], in0=TLB[:, cc, 3, 0:15, :],
              scalar=ohp[:, 3:4], in1=YE[:, cc, 1:16, 0:30], op0=ALU.mult, op1=ALU.add)
    for cc in range(2):
        for tp in range(31):
            (S.dve if tp % 2 == 0 else S.pool)("tensor_scalar", r=[P("idf"), P("CW")], w=[P("DIAG%d_%d" % (cc, tp))], out=DIAG[:, cc, tp, :],
                                               in0=idf[:], scalar1=CW[:, cc, tp:tp + 1], scalar2=None, op0=ALU.mult)
    for cc in range(2):
        an = P("accA%d" % cc)
        for pc in range(4):
            k = (cc * 4 + pc) % 2
            for tp in range(31):
                S.pe("matmul", r=[P("DIAG%d_%d" % (cc, tp)), P("YE")], w=[P("ps%d" % k)], out=PS[k][:], lhsT=DIAG[:, cc, tp, :],
                     rhs=YE[:, cc, 4 * pc:4 * pc + 4, tp:tp + 128], start=(tp == 0), stop=(tp == 30))
            S.act("activation", r=[P("ps%d" % k), P("CBs")], w=[an], out=accA[:, cc, 4 * pc:4 * pc + 4, :].rearrange("p m t -> p (m t)"),
                  in_=PS[k][:], func=AF.Identity, bias=CBs[:, cc:cc + 1], scale=1.0)
        S.act("activation", r=[an], w=[P("SQ")], out=SQ[:, cc, :], in_=accA[:, cc].rearrange("p m t -> p (m t)"), func=AF.Square)
    for pc in range(4):
        cs = slice(pc * 512, (pc + 1) * 512)
        for cc in range(2):
            S.pe("matmul", r=[P("onesd"), P("accA%d" % cc)], w=[P("ps0")], out=PS[0][:], lhsT=onesd[:],
                 rhs=accA[:, cc].rearrange("p m t -> p (m t)")[:, cs], start=(cc == 0), stop=(cc == 1))
        for cc in range(2):
            S.pe("matmul", r=[P("onesd"), P("SQ")], w=[P("ps1")], out=PS[1][:], lhsT=onesd[:], rhs=SQ[:, cc, cs], start=(cc == 0), stop=(cc == 1))
        S.act("activation", r=[P("ps0")], w=[P("MEAN")], out=MEAN[:, cs], in_=PS[0][:], func=AF.Copy)
        S.act("activation", r=[P("ps1")], w=[P("MSQ")], out=MSQ[:, cs], in_=PS[1][:], func=AF.Copy)
    S.dve("tensor_tensor", r=[P("MEAN")], w=[P("SQ")], out=SQ[:, 0, :], in0=MEAN[:], in1=MEAN[:], op=ALU.mult)
    S.dve("tensor_tensor", r=[P("MSQ"), P("SQ")], w=[P("MSQ")], out=MSQ[:], in0=MSQ[:], in1=SQ[:, 0, :], op=ALU.subtract)
    S.dve("tensor_scalar", r=[P("MSQ")], w=[P("MSQ")], out=MSQ[:], in0=MSQ[:], scalar1=1e-6, scalar2=None, op0=ALU.add)
    S.act("activation", r=[P("MSQ")], w=[P("MSQ")], out=MSQ[:], in_=MSQ[:], func=AF.Sqrt)
    S.dve("reciprocal", r=[P("MSQ")], w=[P("MSQ")], out=MSQ[:], in_=MSQ[:])
    for cc in range(2):
        an = P("accA%d" % cc)
        a2 = accA[:, cc].rearrange("p m t -> p (m t)")
        S.dve("tensor_tensor", r=[an, P("MEAN")], w=[an], out=a2, in0=a2, in1=MEAN[:], op=ALU.subtract)
        S.dve("tensor_tensor", r=[an, P("MSQ")], w=[an], out=a2, in0=a2, in1=MSQ[:], op=ALU.mult)
        S.act("activation", r=[an, P("LG"), P("LB")], w=[P("ACTT")], out=ACTT[:, cc, :], in_=a2, func=AF.Silu, scale=LG[:, cc:cc + 1], bias=LB[:, cc:cc + 1])
    for m in range(16):
        p = m % 2
        k = 2 + p
        S.dma(P("gt%d" % p), w=[P("gtt%d" % p)], out=gtt[p][:], in_=io["gt"][m * 128:(m + 1) * 128, 256:512])
        for cc in range(2):
            S.pe("matmul", r=[P("ACTT"), P("WPW")], w=[P("ps%d" % k)], out=PS[k][:, 0:256], lhsT=ACTT[:, cc, m * 128:(m + 1) * 128], rhs=WPW[:, cc, :],
                 start=(cc == 0), stop=(cc == 1))
        S.act("activation", r=[P("ps%d" % k)], w=[P("ob%d" % p)], out=ob[p][:], in_=PS[k][:, 0:256], func=AF.Copy)
        S.dve("tensor_tensor", r=[P("ob%d" % p), P("gtt%d" % p)], w=["MIXB"], out=MIXB[:, m, 256:512], in0=ob[p][:], in1=gtt[p][:], op=ALU.mult)


def conv_consts(inputs, layer, r):
    cw = inputs["conv_w"][layer]
    col = lambda v: np.ascontiguousarray(v.reshape(2, 128).T, dtype=np.float32)
    ohp = np.zeros((128, 4), np.float32)
    ohp[:, (r - 1) % 4] = 1.0
    return dict(conv_w=np.ascontiguousarray(cw.T.reshape(2, 128, 31).transpose(1, 0, 2), dtype=np.float32),
                conv_b=col(inputs["conv_b"][layer]), ln_g=col(inputs["conv_ln_g"][layer]), ln_b=col(inputs["conv_ln_b"][layer]),
                conv_pw=np.ascontiguousarray(inputs["conv_pw"][layer]), ohprev=ohp)
```

```python
import numpy as np
import ml_dtypes
from contextlib import ExitStack
import concourse.bass as bass
import concourse.mybir as mybir
from concourse.bass_utils import run_bass_kernel_spmd

F32 = mybir.dt.float32
BF16 = mybir.dt.bfloat16
AF = mybir.ActivationFunctionType
ALU = mybir.AluOpType
AX = mybir.AxisListType
NPBF = ml_dtypes.bfloat16

ENGINES = ("pe", "act", "dve", "pool", "sp")


class Buf:
    __slots__ = ("name", "last_w", "readers")

    def __init__(self, name):
        self.name = name
        self.last_w = None
        self.readers = []


class Op:
    __slots__ = ("eng", "fn", "reads", "writes", "dma_key", "deps", "need_inc", "tok", "idx")

    def __init__(self, eng, fn, reads, writes, dma_key):
        self.eng = eng
        self.fn = fn
        self.reads = reads
        self.writes = writes
        self.dma_key = dma_key
        self.deps = []
        self.need_inc = False
        self.tok = None


class Sched:
    def __init__(self, nc, es):
        self.nc = nc
        self.es = es
        self.ops = []
        self.bufs = {}
        self.batch = set()
        self.cnt = {}
        self.sems = {}
        self.waited = {e: {} for e in ENGINES}

    def buf(self, name):
        b = self.bufs.get(name)
        if b is None:
            b = Buf(name)
            self.bufs[name] = b
        return b

    def _norm(self, lst):
        return [self.buf(b) if isinstance(b, str) else b for b in (lst or ())]

    def add(self, eng, meth, kw, reads=(), writes=(), dma_key=None):
        op = Op(eng, (meth, kw), self._norm(reads), self._norm(writes), dma_key)
        op.idx = len(self.ops)
        self.ops.append(op)
        return op

    def pe(self, meth, r=(), w=(), **kw):
        return self.add("pe", meth, kw, r, w)

    def act(self, meth, r=(), w=(), **kw):
        return self.add("act", meth, kw, r, w)

    def dve(self, meth, r=(), w=(), **kw):
        return self.add("dve", meth, kw, r, w)

    def pool(self, meth, r=(), w=(), **kw):
        return self.add("pool", meth, kw, r, w)

    def dma(self, key, r=(), w=(), eng="sp", batch=False, **kw):
        if batch:
            self.batch.add(key)
        return self.add(eng, "dma_start", kw, r, w, dma_key=key)

    def analyze(self):
        for op in self.ops:
            deps = set()
            for b in op.reads:
                if b.last_w is not None:
                    deps.add(b.last_w)
            for b in op.writes:
                if b.last_w is not None:
                    deps.add(b.last_w)
                for r in b.readers:
                    deps.add(r)
            deps.discard(op.idx)
            keep = []
            for d in deps:
                dop = self.ops[d]
                if dop.dma_key is not None and dop.dma_key == op.dma_key and dop.dma_key in self.batch:
                    continue
                if dop.dma_key is None and dop.eng == op.eng:
                    if op.eng == "pe" or op.dma_key is not None:
                        continue
                keep.append(d)
            op.deps = sorted(keep)
            for d in op.deps:
                self.ops[d].need_inc = True
            for b in op.reads:
                b.readers.append(op.idx)
            for b in op.writes:
                b.last_w = op.idx
                b.readers = []

    def flush(self):
        nc = self.nc
        self.analyze()
        per = {e: [op for op in self.ops if op.eng == e] for e in ENGINES}
        for e in ENGINES:
            comp = [op for op in per[e] if op.dma_key is None]
            if comp:
                comp[-1].need_inc = True
        cnt = self.cnt
        for op in self.ops:
            if op.dma_key is not None:
                k = ("dma", op.dma_key)
                cnt[k] = cnt.get(k, 0) + 16
                op.tok = (k, cnt[k])
            elif op.need_inc:
                k = ("eng", op.eng)
                cnt[k] = cnt.get(k, 0) + 1
                op.tok = (k, cnt[k])
        for op in self.ops:
            if op.dma_key is not None and op.dma_key in self.batch:
                op.tok = (op.tok[0], cnt[op.tok[0]])
        for k in sorted(cnt.keys()):
            if k not in self.sems:
                self.sems[k] = self.es.enter_context(nc.semaphore("s_%s_%s" % k))
        sems = self.sems
        ops = self.ops
        totals = dict(cnt)

        def run(eng_name, h):
            waited = self.waited[eng_name]
            for op in per[eng_name]:
                for d in op.deps:
                    k, v = ops[d].tok
                    if waited.get(k, 0) >= v:
                        continue
                    h.wait_ge(sems[k], v)
                    waited[k] = v
                ins = getattr(h, op.fn[0])(**op.fn[1])
                if op.tok is not None:
                    ins.then_inc(sems[op.tok[0]], 16 if op.dma_key is not None else 1)
            for k in sorted(totals.keys()):
                if waited.get(k, 0) < totals[k]:
                    h.wait_ge(sems[k], totals[k])
                    waited[k] = totals[k]

        with nc.Block() as block:
            block.sync(lambda h: run("sp", h))
            block.tensor(lambda h: run("pe", h))
            block.scalar(lambda h: run("act", h))
            block.vector(lambda h: run("dve", h))
            block.gpsimd(lambda h: run("pool", h))
        self.ops = []
        self.bufs = {}


D = 1024
DIN = 3612
NT = 2048
T = 8192
OFF = dict(fq=0, fk=256, fv=512, ff=768, fg=772, glu=1028, cg=1540, nq=1796, nkc=2308, nvc=2436,
           nks=2564, nvs=2692, nkw=2820, nvw=2948, ngl=3076, ng=3100)

A_OUTS = dict(qf=([2, 128, NT], BF16), kf=([2, 128, NT], BF16), yT=([2, 128, NT], BF16), qn=([4, 128, NT], BF16),
              kc=([128, NT], BF16), vc=([128, NT], BF16), ks=([128, NT], BF16), kw=([128, NT], BF16),
              vf=([NT, 320], BF16), vs=([NT, 160], BF16), vw=([NT, 160], BF16), lf=([NT, 4], F32),
              gt=([NT, D], BF16), gl=([NT, 24], F32))


def phase_a(S, nc, es, io, pfx="a"):
    P = lambda n: pfx + n
    sb = lambda n, s, d: es.enter_context(nc.sbuf_tensor(P(n), s, d))
    PS0b = es.enter_context(nc.psum_tensor(P("ps0"), [128, 1024], BF16))
    PS = [None] + [es.enter_context(nc.psum_tensor(P("ps%d" % i), [128, 512], F32)) for i in range(1, 8)]
    Wb = sb("Wb", [128, 8, DIN], BF16)
    wst = [sb("wst%d" % i, [128, DIN], F32) for i in range(2)]
    gcol = sb("gcol", [128, 8], F32)
    fb = sb("fb", [128, 4], F32)
    cosT = sb("cosT", [128, NT], F32)
    sinT = sb("sinT", [128, NT], F32)
    rmat_f = sb("rmat_f", [128, 128], F32)
    rmat = sb("rmat", [128, 128], BF16)
    idf = sb("idf", [128, 128], F32)
    idb = sb("idb", [128, 128], BF16)
    xt = [sb("xt%d" % i, [128, D], F32) for i in range(2)]
    sq = sb("sq", [128, D], BF16)
    ss = [sb("ss%d" % i, [128, 1], F32) for i in range(2)]
    hb = [sb("hb%d" % i, [128, D], BF16) for i in range(2)]
    hT = [sb("hT%d" % i, [128, 8, 512], BF16) for i in range(2)]
    fo = [sb("fo%d" % i, [128, 512], BF16) for i in range(4)]
    zc = [sb("zc%d" % i, [128, 512], BF16) for i in range(2)]
    t1 = [sb("t1%d" % i, [128, 512], F32) for i in range(2)]
    t2 = [sb("t2%d" % i, [128, 512], F32) for i in range(2)]
    sg = [sb("sg%d" % i, [128, 512], F32) for i in range(2)]
    vfo = [sb("vfo%d" % i, [128, 4, 80], BF16) for i in range(2)]
    vso = [sb("vso%d" % i, [128, 2, 80], BF16) for i in range(2)]
    vwo = [sb("vwo%d" % i, [128, 2, 80], BF16) for i in range(2)]
    lfo = [sb("lfo%d" % i, [128, 4], F32) for i in range(2)]
    lft = [sb("lft%d" % i, [128, 4], F32) for i in range(2)]
    glo = [sb("glo%d" % i, [128, 24], F32) for i in range(2)]
    gto = [sb("gto%d" % i, [128, D], BF16) for i in range(2)]
    out_keys = set()

    def store(src_name, dst, src):
        k = P("k_" + src_name)
        out_keys.add(k)
        S.dma(k, r=[P(src_name)], out=dst, in_=src)

    S.dma(P("c"), batch=True, w=[P("gcol")], out=gcol[:], in_=io["norm_g"])
    S.dma(P("c"), batch=True, w=[P("fb")], out=fb[:], in_=io["fox_b"].partition_broadcast(128))
    S.dma(P("c"), batch=True, w=[P("cos")], out=cosT[:], in_=io["cos"])
    S.dma(P("c"), batch=True, w=[P("sin")], out=sinT[:], in_=io["sin"])
    S.dma(P("c"), batch=True, w=[P("rmf")], out=rmat_f[:], in_=io["rmat"])
    S.dma(P("c"), batch=True, w=[P("idf")], out=idf[:], in_=io["ident"])
    S.dve("tensor_copy", r=[P("rmf")], w=[P("rmat")], out=rmat[:], in_=rmat_f[:])
    S.dve("tensor_copy", r=[P("idf")], w=[P("idb")], out=idb[:], in_=idf[:])
    for i in range(2):
        S.dve("memset", w=[P("vfo%d" % i)], ap=vfo[i][:], constant=1.0)
        S.dve("memset", w=[P("vso%d" % i)], ap=vso[i][:], constant=1.0)
        S.dve("memset", w=[P("vwo%d" % i)], ap=vwo[i][:], constant=1.0)
    for c in range(8):
        w = wst[c % 2]
        wn = P("wst%d" % (c % 2))
        S.dma(P("w%d" % (c % 2)), w=[wn], out=w[:], in_=io["w_in"][c * 128:(c + 1) * 128, :])
        S.act("activation", r=[wn, P("gcol")], w=[P("Wb%d" % c)], out=Wb[:, c, :], in_=w[:], func=AF.Copy, scale=gcol[:, c:c + 1])

    psrr = [1]

    def next_ps():
        k = psrr[0]
        psrr[0] = 1 + (psrr[0] % 7)
        return k

    foi = [0]
    ri = [0]
    for s in range(4):
        hTs = hT[s % 2]
        hTn = P("hT%d" % (s % 2))
        for q in range(4):
            tn = s * 4 + q
            p = tn % 2
            x_t = xt[p]
            xn = P("xt%d" % p)
            ssn = P("ss%d" % p)
            S.dma(P("x%d" % p), w=[xn], out=x_t[:], in_=io["x"][tn * 128:(tn + 1) * 128, :])
            S.act("activation", r=[xn], w=[P("sq"), ssn], out=sq[:], in_=x_t[:], func=AF.Square, accum_out=ss[p][:, 0:1])
            S.dve("tensor_scalar", r=[ssn], w=[ssn], out=ss[p][:], in0=ss[p][:], scalar1=1.0 / D, scalar2=1e-6,
                  op0=ALU.mult, op1=ALU.add)
            S.act("activation", r=[ssn], w=[ssn], out=ss[p][:], in_=ss[p][:], func=AF.Sqrt)
            S.dve("reciprocal", r=[ssn], w=[ssn], out=ss[p][:], in_=ss[p][:])
            S.dve("tensor_scalar", r=[xn, ssn], w=[P("hb%d" % p)], out=hb[p][:], in0=x_t[:], scalar1=ss[p][:, 0:1],
                  scalar2=None, op0=ALU.mult)
            for half in range(2):
                for cc in range(4):
                    c = half * 4 + cc
                    S.pe("transpose", r=[P("hb%d" % p), P("idb")], w=[P("ps0")], out=PS0b[:, cc * 128:(cc + 1) * 128],
                         in_=hb[p][:, c * 128:(c + 1) * 128], identity=idb[:])
                S.act("activation", r=[P("ps0")], w=[hTn], out=hTs[:, half * 4:(half + 1) * 4, q * 128:(q + 1) * 128],
                      in_=PS0b[:, 0:512].rearrange("p (c t) -> p c t", c=4), func=AF.Copy)
        cols = slice(s * 512, (s + 1) * 512)

        def fm(coff, k):
            for c in range(8):
                S.pe("matmul", r=[P("Wb%d" % c), hTn], w=[P("ps%d" % k)], out=PS[k][:, :], lhsT=Wb[:, c, coff:coff + 128],
                     rhs=hTs[:, c, :], start=(c == 0), stop=(c == 7))

        def plain_out(coff, dst):
            k = next_ps()
            fm(coff, k)
            i = foi[0] % 4
            foi[0] += 1
            S.act("activation", r=[P("ps%d" % k)], w=[P("fo%d" % i)], out=fo[i][:], in_=PS[k][:], func=AF.Copy)
            store("fo%d" % i, dst, fo[i][:])

        def rope_out(coff, dst):
            k = next_ps()
            fm(coff, k)
            k2 = next_ps()
            j = ri[0] % 2
            ri[0] += 1
            i = foi[0] % 4
            foi[0] += 1
            S.act("activation", r=[P("ps%d" % k)], w=[P("zc%d" % j)], out=zc[j][:], in_=PS[k][:], func=AF.Copy)
            S.pe("matmul", r=[P("rmat"), P("zc%d" % j)], w=[P("ps%d" % k2)], out=PS[k2][:], lhsT=rmat[:], rhs=zc[j][:],
                 start=True, stop=True)
            S.act("activation", r=[P("ps%d" % k)], w=[P("t1%d" % j)], out=t1[j][:], in_=PS[k][:], func=AF.Copy)
            S.act("activation", r=[P("ps%d" % k2)], w=[P("t2%d" % j)], out=t2[j][:], in_=PS[k2][:], func=AF.Copy)
            S.dve("tensor_tensor", r=[P("t1%d" % j), P("cos")], w=[P("t1%d" % j)], out=t1[j][:], in0=t1[j][:],
                  in1=cosT[:, cols], op=ALU.mult)
            S.pool("tensor_tensor", r=[P("t2%d" % j), P("sin")], w=[P("t2%d" % j)], out=t2[j][:], in0=t2[j][:],
                   in1=sinT[:, cols], op=ALU.mult)
            S.dve("tensor_tensor", r=[P("t1%d" % j), P("t2%d" % j)], w=[P("fo%d" % i)], out=fo[i][:], in0=t1[j][:],
                   in1=t2[j][:], op=ALU.add)
            store("fo%d" % i, dst, fo[i][:])

        for h2 in range(2):
            plain_out(OFF["fq"] + 128 * h2, io["qf"][h2, :, cols])
            plain_out(OFF["fk"] + 128 * h2, io["kf"][h2, :, cols])
        for h2 in range(2):
            ka = next_ps()
            fm(OFF["glu"] + 128 * h2, ka)
            kb = next_ps()
            fm(OFF["glu"] + 256 + 128 * h2, kb)
            i = foi[0] % 4
            foi[0] += 1
            S.act("activation", r=[P("ps%d" % kb)], w=[P("sg%d" % h2)], out=sg[h2][:], in_=PS[kb][:], func=AF.Sigmoid)
            S.act("activation", r=[P("ps%d" % ka)], w=[P("t1%d" % h2)], out=t1[h2][:], in_=PS[ka][:], func=AF.Copy)
            S.dve("tensor_tensor", r=[P("t1%d" % h2), P("sg%d" % h2)], w=[P("fo%d" % i)], out=fo[i][:], in0=t1[h2][:],
                  in1=sg[h2][:], op=ALU.mult)
            store("fo%d" % i, io["yT"][h2, :, cols], fo[i][:])
        for c4 in range(4):
            rope_out(OFF["nq"] + 128 * c4, io["qn"][c4, :, cols])
        plain_out(OFF["nkc"], io["kc"][:, cols])
        plain_out(OFF["nvc"], io["vc"][:, cols])
        rope_out(OFF["nks"], io["ks"][:, cols])
        rope_out(OFF["nkw"], io["kw"][:, cols])

        for q in range(4):
            tn = s * 4 + q
            p = tn % 2
            rows = slice(tn * 128, (tn + 1) * 128)

            def tm(c0, c1, k):
                for c in range(8):
                    S.pe("matmul", r=[P("Wb%d" % c), hTn], w=[P("ps%d" % k)], out=PS[k][:, 0:c1 - c0],
                         lhsT=hTs[:, c, q * 128:(q + 1) * 128], rhs=Wb[:, c, c0:c1], start=(c == 0), stop=(c == 7))

            k = next_ps()
            tm(512, 772, k)
            S.act("activation", r=[P("ps%d" % k)], w=[P("vfo%d" % p)], out=vfo[p][:, :, 0:64],
                  in_=PS[k][:, 0:256].rearrange("p (h d) -> p h d", h=4), func=AF.Copy)
            store("vfo%d" % p, io["vf"][rows, :], vfo[p][:].rearrange("p h d -> p (h d)"))
            lt = P("lft%d" % p)
            S.act("activation", r=[P("ps%d" % k)], w=[lt], out=lft[p][:], in_=PS[k][:, 256:260], func=AF.Copy)
            S.dve("tensor_tensor", r=[lt, P("fb")], w=[lt], out=lft[p][:], in0=lft[p][:], in1=fb[:], op=ALU.add)
            S.act("activation", r=[lt], w=[lt], out=lft[p][:], in_=lft[p][:], func=AF.Exp, scale=-1.0)
            S.act("activation", r=[lt], w=[lt], out=lft[p][:], in_=lft[p][:], func=AF.Ln, bias=1.0, scale=1.0)
            S.dve("tensor_scalar", r=[lt], w=[P("lfo%d" % p)], out=lfo[p][:], in0=lft[p][:], scalar1=-1.0, scalar2=None, op0=ALU.mult)
            store("lfo%d" % p, io["lf"][rows, :], lfo[p][:])
            for (c0, g0) in ((772, 0), (1540, 256)):
                k = next_ps()
                tm(c0, c0 + 256, k)
                S.act("activation", r=[P("ps%d" % k)], w=[P("gto%d" % p)], out=gto[p][:, g0:g0 + 256], in_=PS[k][:, 0:256], func=AF.Silu)
            k = next_ps()
            tm(3100, 3612, k)
            S.act("activation", r=[P("ps%d" % k)], w=[P("gto%d" % p)], out=gto[p][:, 512:1024], in_=PS[k][:, 0:512], func=AF.Silu)
            store("gto%d" % p, io["gt"][rows, :], gto[p][:])
            k = next_ps()
            tm(2692, 2820, k)
            S.act("activation", r=[P("ps%d" % k)], w=[P("vso%d" % p)], out=vso[p][:, :, 0:64],
                  in_=PS[k][:, 0:128].rearrange("p (h d) -> p h d", h=2), func=AF.Copy)
            store("vso%d" % p, io["vs"][rows, :], vso[p][:].rearrange("p h d -> p (h d)"))
            k = next_ps()
            tm(2948, 3100, k)
            S.act("activation", r=[P("ps%d" % k)], w=[P("vwo%d" % p)], out=vwo[p][:, :, 0:64],
                  in_=PS[k][:, 0:128].rearrange("p (h d) -> p h d", h=2), func=AF.Copy)
            store("vwo%d" % p, io["vw"][rows, :], vwo[p][:].rearrange("p h d -> p (h d)"))
            S.act("activation", r=[P("ps%d" % k)], w=[P("glo%d" % p)], out=glo[p][:], in_=PS[k][:, 128:152], func=AF.Sigmoid)
            store("glo%d" % p, io["gl"][rows, :], glo[p][:])
    return sorted(out_keys)


def own_positions(r):
    m = np.arange(16)[:, None]
    tl = np.arange(128)[None, :]
    return ((4 * m + r) * 128 + tl).reshape(-1)


def rope_tables(pos):
    inv = 500000.0 ** (-np.arange(0, 16, 2, dtype=np.float32) / 16.0)
    ang = pos.astype(np.float32)[None, :] * np.tile(inv, 2)[:, None].astype(np.float32)
    cos = np.ones((64, pos.size), np.float32)
    sin = np.zeros((64, pos.size), np.float32)
    cos[:16] = np.cos(ang)
    sin[:16] = np.sin(ang)
    return np.tile(cos, (2, 1)), np.tile(sin, (2, 1))


def rope_rmat():
    R = np.zeros((128, 128), np.float32)
    for hh in range(2):
        for d in range(8):
            R[hh * 64 + d + 8, hh * 64 + d] = -1.0
            R[hh * 64 + d, hh * 64 + d + 8] = 1.0
    return R


def build_a():
    nc = bass.Bass("TRN2", target_bir_lowering=False)
    io = {}
    io["x"] = nc.dram_tensor("x", [NT, D], F32, kind="ExternalInput").ap()
    io["w_in"] = nc.dram_tensor("w_in", [D, DIN], F32, kind="ExternalInput").ap()
    io["norm_g"] = nc.dram_tensor("norm_g", [128, 8], F32, kind="ExternalInput").ap()
    io["fox_b"] = nc.dram_tensor("fox_b", [4], F32, kind="ExternalInput").ap()
    io["cos"] = nc.dram_tensor("cos", [128, NT], F32, kind="ExternalInput").ap()
    io["sin"] = nc.dram_tensor("sin", [128, NT], F32, kind="ExternalInput").ap()
    io["rmat"] = nc.dram_tensor("rmat", [128, 128], F32, kind="ExternalInput").ap()
    io["ident"] = nc.dram_tensor("ident", [128, 128], F32, kind="ExternalInput").ap()
    for n, (shp, dt) in A_OUTS.items():
        io[n] = nc.dram_tensor(n, shp, dt, kind="ExternalOutput").ap()
    with ExitStack() as top:
        S = Sched(nc, top)
        with ExitStack() as es:
            phase_a(S, nc, es, io)
            S.flush()
    return nc


def out_scope(S, nc, es, io, MIXB, last, pfx="o"):
    P = lambda n: pfx + n
    sb = lambda n, s_, d: es.enter_context(nc.sbuf_tensor(P(n), s_, d))
    PS = [es.enter_context(nc.psum_tensor(P("ps%d" % i), [128, 512], F32)) for i in range(4)]
    PSTb = es.enter_context(nc.psum_tensor(P("pstb"), [128, 1024], BF16))
    WO = sb("WO", [128, 8, D], BF16)
    wst = [sb("wst%d" % i, [128, D], F32) for i in range(2)]
    idf = sb("idf", [128, 128], F32)
    idb = sb("idb", [128, 128], BF16)
    MXT = [sb("MXT%d" % i, [128, 8, 128], BF16) for i in range(2)]
    xt = [sb("xt%d" % i, [128, D], F32) for i in range(2)]
    yo = [sb("yo%d" % i, [128, D], F32) for i in range(2)]
    sq = sb("sq", [128, D], F32)
    ss = sb("ss", [128, 1], F32)
    fg = sb("fg", [128, D], F32)
    S.dma(P("c"), batch=True, w=[P("idf")], out=idf[:], in_=io["ident"])
    if last:
        S.dma(P("c"), batch=True, w=[P("fg")], out=fg[:], in_=io["final_g"].partition_broadcast(128))
    S.dve("tensor_copy", r=[P("idf")], w=[P("idb")], out=idb[:], in_=idf[:])
    for c in range(8):
        w, wn = wst[c % 2], P("wst%d" % (c % 2))
        S.dma(P("w%d" % (c % 2)), w=[wn], out=w[:], in_=io["w_out"][c * 128:(c + 1) * 128, :])
        if c % 2 == 0:
            S.dve("tensor_copy", r=[wn], w=[P("WO%d" % c)], out=WO[:, c, :], in_=w[:])
        else:
            S.act("activation", r=[wn], w=[P("WO%d" % c)], out=WO[:, c, :], in_=w[:], func=AF.Copy)
    for m in range(16):
        p = m % 2
        mx, mxn, x_t, xn, y_t, yn = MXT[p], P("MXT%d" % p), xt[p], P("xt%d" % p), yo[p], P("yo%d" % p)
        S.dma(P("x%d" % p), w=[xn], out=x_t[:], in_=io["x"][m * 128:(m + 1) * 128, :])
        for c in range(8):
            S.pe("transpose", r=["MIXB", P("idb")], w=[P("pstb")], out=PSTb[:, c * 128:(c + 1) * 128], in_=MIXB[:, m, c * 128:(c + 1) * 128],
                 identity=idb[:])
        S.act("activation", r=[P("pstb")], w=[mxn], out=mx[:].rearrange("f c q -> f (c q)"), in_=PSTb[:], func=AF.Copy)
        for n2 in range(2):
            k = (2 * m + n2) % 4
            for c in range(8):
                S.pe("matmul", r=[mxn, P("WO%d" % c)], w=[P("ps%d" % k)], out=PS[k][:], lhsT=mx[:, c, :], rhs=WO[:, c, n2 * 512:(n2 + 1) * 512],
                     start=(c == 0), stop=(c == 7))
            S.act("activation", r=[P("ps%d" % k)], w=[yn], out=y_t[:, n2 * 512:(n2 + 1) * 512], in_=PS[k][:], func=AF.Copy)
        S.dve("tensor_tensor", r=[yn, xn], w=[yn], out=y_t[:], in0=y_t[:], in1=x_t[:], op=ALU.add)
        if last:
            S.act("activation", r=[yn], w=[P("sq"), P("ss")], out=sq[:], in_=y_t[:], func=AF.Square, accum_out=ss[:, 0:1])
            S.dve("tensor_scalar", r=[P("ss")], w=[P("ss")], out=ss[:], in0=ss[:], scalar1=1.0 / D, scalar2=1e-6, op0=ALU.mult, op1=ALU.add)
            S.act("activation", r=[P("ss")], w=[P("ss")], out=ss[:], in_=ss[:], func=AF.Sqrt)
            S.dve("reciprocal", r=[P("ss")], w=[P("ss")], out=ss[:], in_=ss[:])
            S.dve("scalar_tensor_tensor", r=[yn, P("ss"), P("fg")], w=[yn], out=y_t[:], in0=y_t[:], scalar=ss[:, 0:1], in1=fg[:],
                  op0=ALU.mult, op1=ALU.mult)
        S.dma(P("o%d" % p), r=[yn], out=io["xo"][m * 128:(m + 1) * 128, :], in_=y_t[:])


B_INS = dict(kf_g=([2, 128, T], BF16), vf_g=([T, 320], BF16), lf_g=([T, 4], F32), qf=([2, 128, NT], BF16),
             kc_g=([128, T], BF16), vc_g=([128, T], BF16), ks_g=([128, T], BF16), kw_g=([128, T], BF16),
             vs_g=([T, 160], BF16), vw_g=([T, 160], BF16), qn=([4, 128, NT], BF16), gl=([NT, 24], F32), gt=([NT, D], BF16),
             yT=([2, 128, NT], BF16), tails_g=([4, 2, 128, 16, 30], BF16), x=([NT, D], F32), w_out=([D, D], F32), final_g=([D], F32),
             tri=([128, 128], F32), ones128=([128, 128], F32), onehot=([128, 4], F32), dmask=([128, 4, 128], BF16),
             wmask=([128, 8, 128], BF16), cmask=([16, 128, 4, 128], BF16), force=([16, 128, 128], F32), ee=([128, T], BF16),
             ident=([128, 128], F32), rmat=([128, 128], F32), cosC=([128, 512], F32), sinC=([128, 512], F32),
             ovl=([128, 4, 128], BF16), peT=([64, 2, 32], F32), w2=([128, 2, 2, 64], F32), w1k=([2048, 256], F32),
             w1v=([2048, 256], F32), conv_w=([128, 2, 31], F32), conv_b=([128, 2], F32), ln_g=([128, 2], F32), ln_b=([128, 2], F32),
             conv_pw=([256, 256], F32), ohprev=([128, 4], F32))


def phase_b(S, nc, top_unused, io, last):
  with ExitStack() as top:
    KCMP = top.enter_context(nc.sbuf_tensor("KCMP", [128, 512], BF16))
    VCA = top.enter_context(nc.sbuf_tensor("VCA", [128, 4, 2, 208], BF16))
    MIXB = top.enter_context(nc.sbuf_tensor("MIXB", [128, 16, D], BF16))
    for scope in (lambda es: cmp_scope(S, nc, es, io, KCMP, VCA), lambda es: conv_scope(S, nc, es, io, MIXB),
                  lambda es: fox_scope(S, nc, es, io, MIXB), lambda es: nsa_scope(S, nc, es, io, KCMP, VCA, MIXB),
                  lambda es: out_scope(S, nc, es, io, MIXB, last)):
        with ExitStack() as es:
            scope(es)
            S.flush()


def build_b(last):
    nc = bass.Bass("TRN2", target_bir_lowering=False)
    io = {n: nc.dram_tensor(n, shp, dt, kind="ExternalInput").ap() for n, (shp, dt) in B_INS.items()}
    io["xo"] = nc.dram_tensor("xo", [NT, D], F32, kind="ExternalOutput").ap()
    with ExitStack() as top:
        S = Sched(nc, top)
        phase_b(S, nc, top, io, last)
    return nc


def build_ba():
    nc = bass.Bass("TRN2", target_bir_lowering=False)
    io = {n: nc.dram_tensor(n, shp, dt, kind="ExternalInput").ap() for n, (shp, dt) in B_INS.items()}
    io["xo"] = nc.dram_tensor("xo", [NT, D], F32, kind="ExternalOutput").ap()
    ioa = dict(x=io["xo"], rmat=io["rmat"], ident=io["ident"])
    ioa["w_in"] = nc.dram_tensor("a_w_in", [D, DIN], F32, kind="ExternalInput").ap()
    ioa["norm_g"] = nc.dram_tensor("a_norm_g", [128, 8], F32, kind="ExternalInput").ap()
    ioa["fox_b"] = nc.dram_tensor("a_fox_b", [4], F32, kind="ExternalInput").ap()
    ioa["cos"] = nc.dram_tensor("a_cos", [128, NT], F32, kind="ExternalInput").ap()
    ioa["sin"] = nc.dram_tensor("a_sin", [128, NT], F32, kind="ExternalInput").ap()
    for n, (shp, dt) in A_OUTS.items():
        ioa[n] = nc.dram_tensor("a_" + n, shp, dt, kind="ExternalOutput").ap()
    with ExitStack() as top:
        S = Sched(nc, top)
        phase_b(S, nc, top, io, False)
        with ExitStack() as es:
            phase_a(S, nc, es, ioa, pfx="A")
            S.flush()
    return nc


def a_inputs(inputs, layer, x_own):
    maps = []
    for core in range(8):
        cos, sin = rope_tables(own_positions(core % 4))
        maps.append(dict(x=np.ascontiguousarray(x_own[core], dtype=np.float32), w_in=np.ascontiguousarray(inputs["w_in"][layer]),
                         norm_g=np.ascontiguousarray(inputs["norm_g"][layer].reshape(8, 128).T), fox_b=np.ascontiguousarray(inputs["fox_b"][layer]),
                         cos=cos, sin=sin, rmat=rope_rmat(), ident=np.eye(128, dtype=np.float32)))
    return maps


def b_inputs(inputs, layer, resA, x_own):
    maps = []
    cc = common_consts()
    ee = ee_const()
    cmpc = cmp_consts(inputs, layer)
    for core in range(8):
        b, r = core // 4, core % 4
        R = [resA[4 * b + rr] for rr in range(4)]
        cat = lambda n, ax: np.concatenate([np.asarray(R[rr][n]) for rr in range(4)], axis=ax)
        d = dict(kf_g=cat("kf", 2), vf_g=cat("vf", 0), lf_g=cat("lf", 0), kc_g=cat("kc", 1), vc_g=cat("vc", 1), ks_g=cat("ks", 1),
                 kw_g=cat("kw", 1), vs_g=cat("vs", 0), vw_g=cat("vw", 0))
        d["tails_g"] = np.ascontiguousarray(np.stack([np.asarray(R[rr]["yT"]).reshape(2, 128, 16, 128)[:, :, :, 98:128] for rr in range(4)], axis=0))
        own = resA[core]
        for n in ("qf", "qn", "gl", "gt", "yT"):
            d[n] = np.asarray(own[n])
        d["x"] = np.ascontiguousarray(x_own[core], dtype=np.float32)
        d["w_out"] = np.ascontiguousarray(inputs["w_out"][layer])
        d["final_g"] = np.ascontiguousarray(inputs["final_g"])
        d.update(cc)
        d.update(rank_consts(r))
        d.update(nsa_rank_consts(r))
        d.update(cmpc)
        d.update(conv_consts(inputs, layer, r))
        d["ee"] = ee
        d["ident"] = np.eye(128, dtype=np.float32)
        maps.append(d)
    return maps


def kernel(**inputs):
    inputs = {k: np.asarray(v) for k, v in inputs.items()}
    x = inputs["x"].astype(np.float32)
    x_own = [x[core // 4][own_positions(core % 4)] for core in range(8)]
    cores = list(range(8))
    resA0 = run_bass_kernel_spmd(build_a(), a_inputs(inputs, 0, x_own), core_ids=cores).results
    maps = b_inputs(inputs, 0, resA0, x_own)
    a1 = a_inputs(inputs, 1, x_own)
    for core in cores:
        for k in ("w_in", "norm_g", "fox_b", "cos", "sin"):
            maps[core]["a_" + k] = a1[core][k]
    resBA = run_bass_kernel_spmd(build_ba(), maps, core_ids=cores).results
    x1_own = [np.asarray(resBA[core]["xo"]) for core in cores]
    resA1 = [{k[2:]: v for k, v in resBA[core].items() if k.startswith("a_")} for core in cores]
    resB1 = run_bass_kernel_spmd(build_b(last=True), b_inputs(inputs, 1, resA1, x1_own), core_ids=cores).results
    out = np.empty((2, T, D), np.float32)
    for core in cores:
        out[core // 4][own_positions(core % 4)] = np.asarray(resB1[core]["xo"])
    return out


def cidx(j):
    return (j % 4) * 16 + j // 4


def gcol(j):
    return cidx(j) * 128


def fox_scope(S, nc, es, io, MIXB, pfx="f"):
    P = lambda n: pfx + n
    sb = lambda n, s, d: es.enter_context(nc.sbuf_tensor(P(n), s, d))
    PS = [es.enter_context(nc.psum_tensor(P("ps%d" % i), [128, 512], F32)) for i in range(4)]
    KF = sb("KF", [128, 2, T], BF16)
    VF = sb("VF", [128, 64, 320], BF16)
    QF = sb("QF", [128, 4, NT], BF16)
    LF = sb("LF", [128, 64, 4], F32)
    tri = sb("tri", [128, 128], F32)
    ones = sb("ones", [128, 128], F32)
    oneh = sb("oneh", [128, 4], F32)
    dm = sb("dm", [128, 4, 128], BF16)
    dm4 = sb("dm4", [128, 4, 4, 128], BF16)
    WI = sb("WI", [128, 64, 4], F32)
    TOT = sb("TOT", [128, 64, 4], F32)
    SA = sb("SA", [128, 64, 4], F32)
    SB_ = sb("SB", [128, 64, 4], F32)
    NC_ = sb("NC", [128, 64, 4], F32)
    ROWN = sb("ROWN", [128, 16, 4], F32)
    CB = [sb("CB%d" % i, [128, 64, 4], F32) for i in range(2)]
    PT = [sb("PT%d" % i, [128, 512], BF16) for i in range(3)]
    ofs = sb("ofs", [128, 4, 65], F32)
    rden = sb("rden", [128, 4], F32)
    onrm = sb("onrm", [128, 256], F32)
    gtt = [sb("gtt%d" % i, [128, 256], BF16) for i in range(2)]

    ld = P("ld")
    S.pool("memset", w=[P("QF")], ap=QF[:].rearrange("p h t -> p (h t)"), constant=0.0)
    for h2 in range(2):
        S.dma(ld, batch=True, w=[P("KF")], out=KF[:, h2, :], in_=io["kf_g"][h2])
    for h in range(4):
        pb = 64 * (h % 2)
        S.dma(ld, batch=True, w=[P("QF")], out=QF[pb:pb + 64, h, :], in_=io["qf"][h // 2, pb:pb + 64, :])
    for r in range(4):
        S.dma(ld, batch=True, w=[P("VF")], out=VF[:, r * 16:(r + 1) * 16, :],
              in_=io["vf_g"][r * NT:(r + 1) * NT, :].rearrange("(c t) w -> t c w", t=128))
    for r in range(4):
        S.dma(ld, batch=True, w=[P("LF")], out=LF[:].rearrange("t (m r) h -> t m r h", r=4)[:, :, r, :],
              in_=io["lf_g"][r * NT:(r + 1) * NT, :].rearrange("(m t) h -> t m h", t=128))
    S.dma(ld, batch=True, w=[P("tri")], out=tri[:], in_=io["tri"])
    S.dma(ld, batch=True, w=[P("ones")], out=ones[:], in_=io["ones128"])
    S.dma(ld, batch=True, w=[P("oneh")], out=oneh[:], in_=io["onehot"])
    S.dma(ld, batch=True, w=[P("dm")], out=dm[:], in_=io["dmask"])
    for h in range(4):
        S.pool("tensor_copy", r=[P("dm")], w=[P("dm4")], out=dm4[:, :, h, :], in_=dm[:])

    LFf = LF[:].rearrange("t j h -> t (j h)")
    S.pe("matmul", r=[P("tri"), P("LF")], w=[P("ps0")], out=PS[0][:, 0:256], lhsT=tri[:], rhs=LFf, start=True, stop=True)
    S.pe("matmul", r=[P("ones"), P("LF")], w=[P("ps1")], out=PS[1][:, 0:256], lhsT=ones[:], rhs=LFf, start=True, stop=True)
    S.act("activation", r=[P("ps0")], w=[P("WI")], out=WI[:].rearrange("t j h -> t (j h)"), in_=PS[0][:, 0:256], func=AF.Copy)
    S.act("activation", r=[P("ps1")], w=[P("TOT")], out=TOT[:].rearrange("t j h -> t (j h)"), in_=PS[1][:, 0:256], func=AF.Copy)
    src, srcn = TOT, P("TOT")
    d = 1
    pp = [(SA, P("SA")), (SB_, P("SB"))]
    k = 0
    while d < 64:
        dst, dstn = pp[k % 2]
        S.dve("tensor_tensor", r=[srcn], w=[dstn], out=dst[:, d:, :], in0=src[:, d:, :], in1=src[:, :64 - d, :], op=ALU.add)
        S.dve("tensor_copy", r=[srcn], w=[dstn], out=dst[:, :d, :], in_=src[:, :d, :])
        src, srcn = dst, dstn
        d *= 2
        k += 1
    INCL, INCLn = src, srcn
    S.dve("tensor_tensor", r=[P("WI"), INCLn], w=[P("NC")], out=NC_[:], in0=WI[:], in1=INCL[:], op=ALU.add)
    S.dve("tensor_tensor", r=[P("NC"), P("TOT")], w=[P("NC")], out=NC_[:], in0=TOT[:], in1=NC_[:], op=ALU.subtract)
    I4 = INCL[:].rearrange("t (m r) h -> t m r h", r=4)
    S.dve("tensor_scalar", r=[INCLn, P("oneh")], w=[P("ROWN")], out=ROWN[:], in0=I4[:, :, 0, :], scalar1=oneh[:, 0:1],
          scalar2=None, op0=ALU.mult)
    for r in range(1, 4):
        S.dve("scalar_tensor_tensor", r=[INCLn, P("oneh"), P("ROWN")], w=[P("ROWN")], out=ROWN[:], in0=I4[:, :, r, :],
              scalar=oneh[:, r:r + 1], in1=ROWN[:], op0=ALU.mult, op1=ALU.add)

    st_i = [0]
    for m in range(16):
        J = 4 * m + 4
        cb, cbn = CB[m % 2], P("CB%d" % (m % 2))
        for h in range(4):
            S.dve("tensor_scalar", r=[P("NC"), P("ROWN")], w=[cbn + "_%d" % h], out=cb[:, 0:J, h], in0=NC_[:, 0:J, h],
                  scalar1=ROWN[:, m, h:h + 1], scalar2=0.0, op0=ALU.add, op1=ALU.min)
        for j in range(J):
            sp = st_i[0] % 2
            pt, ptn = PT[st_i[0] % 3], P("PT%d" % (st_i[0] % 3))
            st_i[0] += 1
            psn = P("ps%d" % sp)
            for pr in range(2):
                S.pe("matmul", r=[P("KF"), P("QF")], w=[psn], out=PS[sp][:, pr * 256:(pr + 1) * 256],
                     lhsT=KF[:, pr, gcol(j):gcol(j) + 128], rhs=QF[:, 2 * pr:2 * pr + 2, m * 128:(m + 1) * 128], start=True, stop=True)
            pth = [ptn + "_%d" % h for h in range(4)]
            for h in range(4):
                S.act("activation", r=[psn, cbn + "_%d" % h], w=[pth[h]], out=pt[:, h * 128:(h + 1) * 128], in_=PS[sp][:, h * 128:(h + 1) * 128],
                      func=AF.Exp, bias=cb[:, j, h:h + 1], scale=0.125)
            if j >= 4 * m:
                S.dve("tensor_tensor", r=pth + [P("dm4")], w=pth, out=pt[:], in0=pt[:],
                      in1=dm4[:, j - 4 * m, :, :].rearrange("k h q -> k (h q)"), op=ALU.mult)
            for h in range(4):
                S.pe("matmul", r=[pth[h], P("VF")], w=[P("ps2")], out=PS[2][:, h * 65:(h + 1) * 65], lhsT=pt[:, h * 128:(h + 1) * 128],
                     rhs=VF[:, cidx(j), h * 80:h * 80 + 65], start=(j == 0 and h == 0), stop=(j == J - 1 and h == 3))
        S.act("activation", r=[P("ps2")], w=[P("ofs")], out=ofs[:].rearrange("q h d -> q (h d)"), in_=PS[2][:, 0:260], func=AF.Copy)
        S.dve("reciprocal", r=[P("ofs")], w=[P("rden")], out=rden[:], in_=ofs[:, :, 64])
        for h in range(4):
            S.dve("tensor_scalar", r=[P("ofs"), P("rden")], w=[P("onrm%d" % h)], out=onrm[:, h * 64:(h + 1) * 64], in0=ofs[:, h, 0:64],
                  scalar1=rden[:, h:h + 1], scalar2=None, op0=ALU.mult)
        gp = m % 2
        S.dma(P("gt%d" % gp), w=[P("gtt%d" % gp)], out=gtt[gp][:], in_=io["gt"][m * 128:(m + 1) * 128, 0:256])
        S.dve("tensor_tensor", r=[P("onrm%d" % h) for h in range(4)] + [P("gtt%d" % gp)], w=["MIXB"], out=MIXB[:, m, 0:256], in0=onrm[:],
              in1=gtt[gp][:], op=ALU.mult)


def rank_consts(r):
    k = np.arange(128)[:, None]
    q = np.arange(128)[None, :]
    dmask = np.zeros((128, 4, 128), np.float32)
    for c in range(4):
        if c < r:
            dmask[:, c, :] = 1.0
        elif c == r:
            dmask[:, c, :] = (k <= q)
    onehot = np.zeros((128, 4), np.float32)
    onehot[:, r] = 1.0
    return dict(dmask=dmask.astype(NPBF), onehot=onehot)


def common_consts():
    k = np.arange(128)[:, None]
    q = np.arange(128)[None, :]
    return dict(tri=(k <= q).astype(np.float32), ones128=np.ones((128, 128), np.float32))


def cmp_scope(S, nc, es, io, KCMP, VCA, pfx="c"):
    P = lambda n: pfx + n
    sb = lambda n, s, d: es.enter_context(nc.sbuf_tensor(P(n), s, d))
    PS = [es.enter_context(nc.psum_tensor(P("ps%d" % i), [128, 512], F32)) for i in range(4)]
    KCN = [sb("KCN%d" % i, [128, T], BF16) for i in range(2)]
    W1st = sb("W1st", [128, 32, 256], F32)
    W1Z = [sb("W1Z%d" % g, [128, 32, 256], BF16) for g in range(2)]
    PEf = sb("PEf", [128, 2, 32], F32)
    PEB = sb("PEB", [128, 2, 32], BF16)
    W2f = sb("W2f", [128, 2, 2, 64], F32)
    W2KZ = sb("W2KZ", [128, 2, 2, 128], BF16)
    W2V = sb("W2V", [128, 2, 64], BF16)
    BH = sb("BH", [128, 2], F32)
    HID = [[sb("HID%d%d" % (kv, g), [128, 2, 512], BF16) for g in range(2)] for kv in range(2)]
    X = sb("X", [128, 512], F32)
    X2 = sb("X2", [128, 512], F32)
    U = sb("U", [128, 512], F32)
    SG = sb("SG", [128, 512], F32)
    KC0 = sb("KC0", [128, 512], F32)
    KZ = sb("KZ", [128, 512], BF16)
    RT = sb("RT", [128, 512], F32)
    cosC = sb("cosC", [128, 512], F32)
    sinC = sb("sinC", [128, 512], F32)
    rmf = sb("rmf", [128, 128], F32)
    rmb = sb("rmb", [128, 128], BF16)
    ovf = sb("ovf", [128, 4, 128], BF16)

    ld = P("ld")
    for kv, nm in ((0, "kc_g"), (1, "vc_g")):
        for r in range(4):
            S.dma(ld, batch=True, w=[P("KCN%d" % kv)], out=KCN[kv][:].rearrange("d (m r t) -> d m r t", r=4, t=128)[:, :, r, :],
                  in_=io[nm][:, r * NT:(r + 1) * NT].rearrange("d (m t) -> d m t", t=128))
    S.dma(ld, batch=True, w=[P("PEf")], out=PEf[0:64, :, :], in_=io["peT"])
    S.dma(ld, batch=True, w=[P("PEf")], out=PEf[64:128, :, :], in_=io["peT"])
    S.dma(ld, batch=True, w=[P("W2f")], out=W2f[:], in_=io["w2"])
    S.dma(ld, batch=True, w=[P("cosC")], out=cosC[:], in_=io["cosC"])
    S.dma(ld, batch=True, w=[P("sinC")], out=sinC[:], in_=io["sinC"])
    S.dma(ld, batch=True, w=[P("rmf")], out=rmf[:], in_=io["rmat"])
    S.dma(ld, batch=True, w=[P("ovf")], out=ovf[:], in_=io["ovl"])
    S.dve("tensor_copy", r=[P("PEf")], w=[P("PEB")], out=PEB[:], in_=PEf[:])
    S.dve("tensor_copy", r=[P("rmf")], w=[P("rmb")], out=rmb[:], in_=rmf[:])
    S.pool("memset", w=[P("W2KZ")], ap=W2KZ[:], constant=0.0)
    for g in range(2):
        S.pool("memset", w=[P("W1Z%d" % g)], ap=W1Z[g][:], constant=0.0)
        S.dve("tensor_copy", r=[P("W2f"), P("W2KZ")], w=[P("W2KZ")], out=W2KZ[:, g, :, 64 * g:64 * g + 64], in_=W2f[:, 0, :, :])
        for kv in range(2):
            S.pool("memset", w=[P("HID%d%d" % (kv, g))], ap=HID[kv][g][:], constant=0.0)
    S.dve("tensor_copy", r=[P("W2f")], w=[P("W2V")], out=W2V[:], in_=W2f[:, 1, :, :])
    S.pool("memset", w=[P("KC0")], ap=KC0[:], constant=0.0)
    S.pool("memset", w=["VCA"], ap=VCA[:], constant=1.0)
    for g in range(2):
        S.pool("tensor_copy", r=[P("ovf"), "VCA"], w=["VCA"], out=VCA[:, :, g, 80:208], in_=ovf[:])

    for kv, wn in ((0, "w1k"), (1, "w1v")):
        for half in range(2):
            S.dma(P("w1"), w=[P("W1st")], out=W1st[64 * half:64 * half + 64, :, :],
                  in_=io[wn].rearrange("(l d) h -> d l h", d=64))
        for g in range(2):
            S.dve("tensor_copy", r=[P("W1st"), P("W1Z%d" % g)], w=[P("W1Z%d" % g)], out=W1Z[g][64 * g:64 * g + 64, :, :],
                  in_=W1st[64 * g:64 * g + 64, :, :])
        for hc in range(2):
            for l in range(32):
                S.pe("matmul", r=[P("W1Z0"), P("PEB")], w=[P("ps3")], out=PS[3][:, hc:hc + 1], lhsT=W1Z[0][:, l, hc * 128:(hc + 1) * 128],
                     rhs=PEB[:, kv, l:l + 1], start=(hc == 0 and l == 0), stop=(hc == 1 and l == 31))
        S.act("activation", r=[P("ps3")], w=[P("BH")], out=BH[:], in_=PS[3][:, 0:2], func=AF.Copy)
        for g in range(2):
            for hc in range(2):
                k = (g * 2 + hc) % 3
                psn = P("ps%d" % k)
                for l in range(32):
                    S.pe("matmul", r=[P("W1Z%d" % g), P("KCN%d" % kv)], w=[psn], out=PS[k][:, 0:511],
                         lhsT=W1Z[g][:, l, hc * 128:(hc + 1) * 128], rhs=KCN[kv][:, l:l + 16 * 510 + 1:16], start=(l == 0), stop=(l == 31))
                S.act("activation", r=[psn, P("BH")], w=[P("X")], out=X[:, 0:511], in_=PS[k][:, 0:511], func=AF.Identity,
                      bias=BH[:, hc:hc + 1], scale=1.0)
                S.dve("tensor_tensor", r=[P("X")], w=[P("X2")], out=X2[:, 0:511], in0=X[:, 0:511], in1=X[:, 0:511], op=ALU.mult)
                S.dve("tensor_scalar", r=[P("X2")], w=[P("X2")], out=X2[:, 0:511], in0=X2[:, 0:511], scalar1=0.044715, scalar2=1.0,
                      op0=ALU.mult, op1=ALU.add)
                S.dve("tensor_tensor", r=[P("X2"), P("X")], w=[P("U")], out=U[:, 0:511], in0=X2[:, 0:511], in1=X[:, 0:511], op=ALU.mult)
                S.act("activation", r=[P("U")], w=[P("SG")], out=SG[:, 0:511], in_=U[:, 0:511], func=AF.Sigmoid, scale=1.5957691216057308)
                S.dve("tensor_tensor", r=[P("X"), P("SG")], w=[P("HID%d%d" % (kv, g))], out=HID[kv][g][:, hc, 0:511], in0=X[:, 0:511],
                      in1=SG[:, 0:511], op=ALU.mult)
    n = 0
    for g in range(2):
        for hc in range(2):
            S.pe("matmul", r=[P("W2KZ"), P("HID0%d" % g)], w=[P("ps0")], out=PS[0][:, 0:511], lhsT=W2KZ[:, g, hc, :],
                 rhs=HID[0][g][:, hc, 0:511], start=(n == 0), stop=(n == 3))
            n += 1
    S.act("activation", r=[P("ps0")], w=[P("KC0")], out=KC0[:, 0:511], in_=PS[0][:, 0:511], func=AF.Copy)
    S.dve("tensor_copy", r=[P("KC0")], w=[P("KZ")], out=KZ[:], in_=KC0[:])
    S.pe("matmul", r=[P("rmb"), P("KZ")], w=[P("ps1")], out=PS[1][:], lhsT=rmb[:], rhs=KZ[:], start=True, stop=True)
    S.act("activation", r=[P("ps1")], w=[P("RT")], out=RT[:], in_=PS[1][:], func=AF.Copy)
    S.dve("tensor_tensor", r=[P("KC0"), P("cosC")], w=[P("KC0")], out=KC0[:], in0=KC0[:], in1=cosC[:], op=ALU.mult)
    S.dve("tensor_tensor", r=[P("RT"), P("sinC")], w=[P("RT")], out=RT[:], in0=RT[:], in1=sinC[:], op=ALU.mult)
    S.dve("tensor_tensor", r=[P("KC0"), P("RT")], w=["KCMP"], out=KCMP[:], in0=KC0[:], in1=RT[:], op=ALU.add)
    for c in range(4):
        for g in range(2):
            k = 2 + (c * 2 + g) % 2
            for hc in range(2):
                S.pe("matmul", r=[P("HID1%d" % g), P("W2V")], w=[P("ps%d" % k)], out=PS[k][:, 0:64], lhsT=HID[1][g][:, hc, c * 128:(c + 1) * 128],
                     rhs=W2V[:, hc, :], start=(hc == 0), stop=(hc == 1))
            S.act("activation", r=[P("ps%d" % k), "VCA"], w=["VCA"], out=VCA[:, c, g, 0:64], in_=PS[k][:, 0:64], func=AF.Copy)


def cmp_consts(inputs, layer):
    n = np.arange(512)
    cosC, sinC = rope_tables(16 * n + 31)
    nn = np.arange(512)[:, None]
    bb = np.arange(128)[None, :]
    ovl = ((16 * nn < 64 * bb + 64) & (16 * nn + 31 >= 64 * bb) & (nn < 511)).astype(np.float32)
    peT = np.stack([inputs["cmp_pe_k"][layer].T, inputs["cmp_pe_v"][layer].T], axis=1)
    w2 = np.stack([inputs["cmp_k_w2"][layer].reshape(2, 128, 64), inputs["cmp_v_w2"][layer].reshape(2, 128, 64)], axis=0)
    w2 = np.ascontiguousarray(w2.transpose(2, 0, 1, 3))
    return dict(cosC=cosC, sinC=sinC, ovl=np.ascontiguousarray(ovl.reshape(4, 128, 128).transpose(1, 0, 2)).astype(NPBF),
                peT=np.ascontiguousarray(peT, dtype=np.float32), w2=w2.astype(np.float32),
                w1k=np.ascontiguousarray(inputs["cmp_k_w1"][layer]), w1v=np.ascontiguousarray(inputs["cmp_v_w1"][layer]),
                rmat=rope_rmat())


def nsa_scope(S, nc, es, io, KCMP, VCA, MIXB, pfx="n"):
    P = lambda n: pfx + n
    sb = lambda n, s, d: es.enter_context(nc.sbuf_tensor(P(n), s, d))
    PS = [es.enter_context(nc.psum_tensor(P("ps%d" % i), [128, 512], F32)) for i in range(6)]
    PSTb = es.enter_context(nc.psum_tensor(P("pstb"), [128, 1024], BF16))
    KS = sb("KS", [128, T], BF16)
    VS = sb("VS", [128, 64, 160], BF16)
    EE = sb("EE", [128, T], BF16)
    QNZ = sb("QNZ", [128, 2, 4, NT], BF16)
    KWm = [sb("KWm%d" % i, [128, 8, 128], BF16) for i in range(2)]
    VWm = [sb("VWm%d" % i, [128, 8, 160], BF16) for i in range(2)]
    dm = sb("dm", [128, 4, 128], BF16)
    wm = sb("wm", [128, 8, 128], BF16)
    cm = [sb("cm%d" % i, [128, 4, 128], BF16) for i in range(2)]
    fc = [sb("fc%d" % i, [128, 128], F32) for i in range(2)]
    glt = [sb("glt%d" % i, [128, 24], F32) for i in range(2)]
    gtt = [sb("gtt%d" % i, [128, 512], BF16) for i in range(2)]
    idf = sb("idf", [128, 128], F32)
    idb = sb("idb", [128, 128], BF16)
    PT = [sb("PT%d" % i, [128, 512], BF16) for i in range(3)]
    ocs = sb("ocs", [128, 4, 208], F32)
    oss = sb("oss", [128, 4, 65], F32)
    ows = sb("ows", [128, 4, 65], F32)
    rd = sb("rd", [128, 3, 4], F32)
    fac = sb("fac", [128, 3, 4], F32)
    imp = sb("imp", [128, 128], F32)
    wk = sb("wk", [128, 128], F32)
    mx = sb("mx", [128, 8], F32)
    MBq = sb("MBq", [128, 128], BF16)
    MBT = sb("MBT", [128, 4, 128], BF16)
    acc = sb("acc", [128, 4, 64], F32)
    tmp = sb("tmp", [128, 4, 64], F32)

    def bc(ap, h=4):
        return ap.unsqueeze(1).to_broadcast([128, h, ap.shape[-1]])

    ld = P("ld")
    S.dma(ld, batch=True, w=[P("KS")], out=KS[:], in_=io["ks_g"])
    S.dma(ld, batch=True, w=[P("EE")], out=EE[:], in_=io["ee"])
    for r in range(4):
        S.dma(ld, batch=True, w=[P("VS")], out=VS[:, r * 16:(r + 1) * 16, :],
              in_=io["vs_g"][r * NT:(r + 1) * NT, :].rearrange("(c t) w -> t c w", t=128))
    S.pool("memset", w=[P("QNZ")], ap=QNZ[:].rearrange("p g h t -> p (g h t)"), constant=0.0)
    for hh in range(8):
        g, h = hh // 4, hh % 4
        src = 64 * (hh % 2)
        S.dma(ld, batch=True, w=[P("QNZ")], out=QNZ[64 * g:64 * g + 64, g, h, :], in_=io["qn"][hh // 2, src:src + 64, :])
    S.dma(ld, batch=True, w=[P("dm")], out=dm[:], in_=io["dmask"])
    S.dma(ld, batch=True, w=[P("wm")], out=wm[:], in_=io["wmask"])
    S.dma(ld, batch=True, w=[P("idf")], out=idf[:], in_=io["ident"])
    S.dve("tensor_copy", r=[P("idf")], w=[P("idb")], out=idb[:], in_=idf[:])

    units = []
    load_fns = []

    def add_unit(**kw):
        units.append(kw)

    def emit_s(i, u):
        sp = i % 2
        psn = P("ps%d" % sp)
        for fn in u.get("pre", ()):
            fn()
        S.pe("matmul", r=list(u["rk"]) + [P("QNZ")], w=[psn], out=PS[sp][:], lhsT=u["lhsT"], rhs=u["qrhs"], start=True,
             stop=(u.get("bias_mm") is None))
        if u.get("bias_mm") is not None:
            S.pe("matmul", r=[P("EE"), P("MBT")], w=[psn], out=PS[sp][:], lhsT=u["bias_mm"], rhs=MBT[:].rearrange("b h q -> b (h q)"),
                 start=False, stop=True)

    def emit_rest(i, u):
        sp = i % 2
        psn = P("ps%d" % sp)
        pt, ptn = PT[i % 3], P("PT%d" % (i % 3))
        S.act("activation", r=[psn], w=[ptn], out=pt[:], in_=PS[sp][:], func=AF.Exp, scale=0.125)
        if u.get("mask") is not None:
            mk, mn = u["mask"]
            S.pool("tensor_tensor", r=[ptn, mn], w=[ptn], out=pt[:].rearrange("k (h q) -> k h q", h=4),
                   in0=pt[:].rearrange("k (h q) -> k h q", h=4), in1=bc(mk), op=ALU.mult)
        obank, ocols, first, last = u["obank"], u["ocols"], u["first"], u["last"]
        for h in range(4):
            ob, oc = obank(h), ocols(h)
            S.pe("matmul", r=[ptn] + list(u["vk"]), w=[P("ps%d" % ob)], out=PS[ob][:, oc[0]:oc[1]], lhsT=pt[:, h * 128:(h + 1) * 128],
                 rhs=u["vrhs"], start=(first and (h == 0 or (ob != obank(0) and h == 2))),
                 stop=(last and (h == 3 or (ob != obank(3) and h == 1))))
        for fn in u.get("post", ()):
            fn()

    for m in range(16):
        p = m % 2
        cmn, fcn, gln, gtn, kwn, vwn = P("cm%d" % p), P("fc%d" % p), P("glt%d" % p), P("gtt%d" % p), P("KWm%d" % p), P("VWm%d" % p)

        def loads(m=m, p=p, cmn=cmn, fcn=fcn, gln=gln, gtn=gtn, kwn=kwn, vwn=vwn):
            S.dma(P("cm%d" % p), w=[cmn], out=cm[p][:], in_=io["cmask"][m])
            S.dma(P("fc%d" % p), w=[fcn], out=fc[p][:], in_=io["force"][m])
            S.dma(P("gl%d" % p), w=[gln], out=glt[p][:], in_=io["gl"][m * 128:(m + 1) * 128, :])
            S.dma(P("gt%d" % p), w=[gtn], out=gtt[p][:], in_=io["gt"][m * 128:(m + 1) * 128, 512:1024])
            for mm in ((m - 1, m) if m > 0 else (m,)):
                s0 = 4 * (mm - m + 1)
                S.dma(P("kw%d" % p), w=[kwn], out=KWm[p][:, s0:s0 + 4, :],
                      in_=io["kw_g"].rearrange("d (r c t) -> d r c t", r=4, t=128)[:, :, mm, :])
                S.dma(P("vw%d" % p), w=[vwn], out=VWm[p][:, s0:s0 + 4, :],
                      in_=io["vw_g"].rearrange("(r c t) w -> t r c w", r=4, t=128)[:, :, mm, :])

        load_fns.append(loads)
    load_fns[0]()
    for m in range(16):
        p = m % 2
        cmn, fcn, gln, gtn, kwn, vwn = P("cm%d" % p), P("fc%d" % p), P("glt%d" % p), P("gtt%d" % p), P("KWm%d" % p), P("VWm%d" % p)
        nxt_loads = load_fns[m + 1] if m + 1 < 16 else None
        qs = slice(m * 128, (m + 1) * 128)
        for g in range(2):
            qrhs = QNZ[:, g, :, qs]

            def cmp_post(p=p, fcn=fcn):
                S.act("activation", r=[P("ps2")], w=[P("ocs")], out=ocs[:, 0:2, :].rearrange("q h w -> q (h w)"), in_=PS[2][:, 0:416], func=AF.Copy)
                S.act("activation", r=[P("ps3")], w=[P("ocs")], out=ocs[:, 2:4, :].rearrange("q h w -> q (h w)"), in_=PS[3][:, 0:416], func=AF.Copy)
                S.dve("tensor_scalar", r=[P("ocs")], w=[P("rd0")], out=rd[:, 0, :], in0=ocs[:, :, 64], scalar1=1e-30, scalar2=None, op0=ALU.max)
                S.dve("reciprocal", r=[P("rd0")], w=[P("rd0")], out=rd[:, 0, :], in_=rd[:, 0, :])
                S.dve("tensor_scalar", r=[P("ocs"), P("rd0")], w=[P("imp")], out=imp[:], in0=ocs[:, 0, 80:208], scalar1=rd[:, 0, 0:1],
                      scalar2=None, op0=ALU.mult)
                for h in range(1, 4):
                    S.dve("scalar_tensor_tensor", r=[P("ocs"), P("rd0"), P("imp")], w=[P("imp")], out=imp[:], in0=ocs[:, h, 80:208],
                          scalar=rd[:, 0, h:h + 1], in1=imp[:], op0=ALU.mult, op1=ALU.add)
                S.dve("tensor_tensor", r=[P("imp"), fcn], w=[P("imp")], out=imp[:], in0=imp[:], in1=fc[p][:], op=ALU.max)
                S.dve("max", r=[P("imp")], w=[P("mx")], out=mx[:], in_=imp[:])
                S.dve("match_replace", r=[P("mx"), P("imp")], w=[P("wk")], out=wk[:], in_to_replace=mx[:], in_values=imp[:], imm_value=-1.0)
                S.dve("max", r=[P("wk")], w=[P("mx")], out=mx[:], in_=wk[:])
                S.dve("tensor_scalar", r=[P("imp"), P("mx")], w=[P("wk")], out=wk[:], in0=imp[:], scalar1=mx[:, 7:8], scalar2=None, op0=ALU.is_ge)
                S.dve("tensor_scalar", r=[P("wk")], w=[P("MBq")], out=MBq[:], in0=wk[:], scalar1=-1.0, scalar2=30000.0, op0=ALU.add, op1=ALU.mult)

            def slc_pre():
                S.pe("transpose", r=[P("MBq"), P("idb")], w=[P("pstb")], out=PSTb[:, 0:128], in_=MBq[:], identity=idb[:])
                S.act("activation", r=[P("pstb")], w=[P("MBT")], out=MBT[:], in_=bc(PSTb[:, 0:128]), func=AF.Copy)

            def win_post():
                S.act("activation", r=[P("ps5")], w=[P("ows")], out=ows[:].rearrange("q h w -> q (h w)"), in_=PS[5][:, 0:260], func=AF.Copy)
                S.dve("reciprocal", r=[P("ows")], w=[P("rd2")], out=rd[:, 2, :], in_=ows[:, :, 64])

            def slc_post(m=m, g=g, p=p, gln=gln, gtn=gtn):
                S.act("activation", r=[P("ps4")], w=[P("oss")], out=oss[:].rearrange("q h w -> q (h w)"), in_=PS[4][:, 0:260], func=AF.Copy)
                S.dve("reciprocal", r=[P("oss")], w=[P("rd1")], out=rd[:, 1, :], in_=oss[:, :, 64])
                S.dve("tensor_tensor", r=[P("rd0"), P("rd1"), P("rd2"), gln], w=[P("fac")], out=fac[:], in0=rd[:],
                      in1=glt[p][:, g * 12:g * 12 + 12].rearrange("q (h b) -> q b h", b=3), op=ALU.mult)
                srcs = ((ocs, P("ocs")), (oss, P("oss")), (ows, P("ows")))
                for br in range(3):
                    o_t, o_n = srcs[br]
                    dst, dstn = (acc, P("acc")) if br == 0 else (tmp, P("tmp"))
                    S.dve("tensor_tensor", r=[o_n, P("fac")], w=[dstn], out=dst[:], in0=o_t[:, :, 0:64],
                          in1=fac[:, br, :].unsqueeze(2).to_broadcast([128, 4, 64]), op=ALU.mult)
                    if br > 0:
                        S.dve("tensor_tensor", r=[P("acc"), P("tmp")], w=[P("acc")], out=acc[:], in0=acc[:], in1=tmp[:], op=ALU.add)
                S.dve("tensor_tensor", r=[P("acc"), gtn], w=["MIXB"], out=MIXB[:, m, 512 + g * 256:512 + g * 256 + 256],
                      in0=acc[:].rearrange("q h d -> q (h d)"), in1=gtt[p][:, g * 256:g * 256 + 256], op=ALU.mult)

            for c in range(4):
                add_unit(lhsT=KCMP[:, c * 128:(c + 1) * 128], rk=["KCMP"], qrhs=qrhs, mask=(cm[p][:, c, :], cmn), vrhs=VCA[:, c, g, :], vk=["VCA"],
                         obank=(lambda h: 2 + h // 2), ocols=(lambda h: ((h % 2) * 208, (h % 2) * 208 + 208)), first=(c == 0), last=(c == 3),
                         post=(([nxt_loads] if (g == 0 and c == 0 and nxt_loads is not None) else []) + ([cmp_post] if c == 3 else [])))
            cl = list(range(8)) if m > 0 else list(range(4, 8))
            for c in cl:
                add_unit(lhsT=KWm[p][:, c, :], rk=[kwn], qrhs=qrhs, mask=(wm[:, c, :], P("wm")), vrhs=VWm[p][:, c, g * 80:g * 80 + 65], vk=[vwn],
                         obank=(lambda h: 5), ocols=(lambda h: (h * 65, h * 65 + 65)), first=(c == cl[0]), last=(c == cl[-1]),
                         post=([win_post] if c == cl[-1] else []))
            J = 4 * m + 4
            for j in range(J):
                add_unit(lhsT=KS[:, gcol(j):gcol(j) + 128], rk=[P("KS")], qrhs=qrhs,
                         mask=((dm[:, j - 4 * m, :], P("dm")) if j >= 4 * m else None), vrhs=VS[:, cidx(j), g * 80:g * 80 + 65], vk=[P("VS")],
                         obank=(lambda h: 4), ocols=(lambda h: (h * 65, h * 65 + 65)), first=(j == 0), last=(j == J - 1),
                         bias_mm=EE[:, gcol(j):gcol(j) + 128], pre=([slc_pre] if j == 0 else []), post=([slc_post] if j == J - 1 else []))

    for i, u in enumerate(units):
        emit_s(i, u)
        if i > 0:
            emit_rest(i - 1, units[i - 1])
    emit_rest(len(units) - 1, units[-1])


def nsa_rank_consts(r):
    k = np.arange(128)[:, None]
    q = np.arange(128)[None, :]
    wmask = np.zeros((128, 8, 128), np.float32)
    for c in range(8):
        d = (4 + r) - c
        if d == 4:
            wmask[:, c, :] = (k > q)
        elif 1 <= d <= 3:
            wmask[:, c, :] = 1.0
        elif d == 0:
            wmask[:, c, :] = (k <= q)
    cmask = np.zeros((16, 128, 4, 128), np.float32)
    force = np.zeros((16, 128, 128), np.float32)
    nl = np.arange(128)[:, None]
    b = np.arange(128)[None, :]
    for m in range(16):
        tq = (4 * m + r) * 128 + np.arange(128)
        for c in range(4):
            n = 128 * c + nl
            cmask[m, :, c, :] = ((16 * n + 31) <= tq[None, :]) & (n < 511)
        cur = (tq // 64)[:, None]
        force[m] = 1e4 * (b == 0) + 2e4 * (b == cur) + 3e4 * (b == cur - 1)
    return dict(wmask=wmask.astype(NPBF), cmask=cmask.astype(NPBF), force=force.astype(np.float32))


def ee_const():
    ee = np.zeros((128, T), np.float32)
    for j in range(64):
        for half in range(2):
            ee[2 * j + half, gcol(j) + 64 * half: gcol(j) + 64 * half + 64] = 1.0
    return ee.astype(NPBF)


def conv_scope(S, nc, es, io, MIXB, pfx="v"):
    P = lambda n: pfx + n
    sb = lambda n, s, d: es.enter_context(nc.sbuf_tensor(P(n), s, d))
    PS = [es.enter_context(nc.psum_tensor(P("ps%d" % i), [128, 512], F32)) for i in range(4)]
    YB = sb("YB", [128, 2, NT], BF16)
    TLB = sb("TLB", [128, 2, 4, 16, 30], BF16)
    YE = sb("YE", [128, 2, 16, 158], BF16)
    DIAG = sb("DIAG", [128, 2, 31, 128], BF16)
    idf = sb("idf", [128, 128], F32)
    accA = sb("accA", [128, 2, 16, 128], F32)
    SQ = sb("SQ", [128, 2, NT], F32)
    MEAN = sb("MEAN", [128, NT], F32)
    MSQ = sb("MSQ", [128, NT], F32)
    CW = sb("CW", [128, 2, 31], F32)
    CBs = sb("CBs", [128, 2], F32)
    LG = sb("LG", [128, 2], F32)
    LB = sb("LB", [128, 2], F32)
    ohp = sb("ohp", [128, 4], F32)
    onesd = sb("onesd", [128, 128], F32)
    WPf = sb("WPf", [128, 2, 256], F32)
    WPW = sb("WPW", [128, 2, 256], BF16)
    ACTT = sb("ACTT", [128, 2, NT], BF16)
    ob = [sb("ob%d" % i, [128, 256], F32) for i in range(2)]
    gtt = [sb("gtt%d" % i, [128, 256], BF16) for i in range(2)]

    ld = P("ld")
    for cc in range(2):
        S.dma(ld, batch=True, w=[P("YB")], out=YB[:, cc, :], in_=io["yT"][cc])
        S.dma(ld, batch=True, w=[P("WPf")], out=WPf[:, cc, :], in_=io["conv_pw"][cc * 128:(cc + 1) * 128, :])
        for r in range(4):
            S.dma(ld, batch=True, w=[P("TLB")], out=TLB[:, cc, r, :, :], in_=io["tails_g"][r, cc])
    S.dma(ld, batch=True, w=[P("CW")], out=CW[:], in_=io["conv_w"])
    S.dma(ld, batch=True, w=[P("CBs")], out=CBs[:], in_=io["conv_b"])
    S.dma(ld, batch=True, w=[P("LG")], out=LG[:], in_=io["ln_g"])
    S.dma(ld, batch=True, w=[P("LB")], out=LB[:], in_=io["ln_b"])
    S.dma(ld, batch=True, w=[P("ohp")], out=ohp[:], in_=io["ohprev"])
    S.dma(ld, batch=True, w=[P("idf")], out=idf[:], in_=io["ident"])
    S.dve("memset", w=[P("onesd")], ap=onesd[:], constant=1.0 / 256.0)
    S.dve("tensor_copy", r=[P("WPf")], w=[P("WPW")], out=WPW[:], in_=WPf[:])
    for cc in range(2):
        S.act("activation", r=[P("YB")], w=[P("YE")], out=YE[:, cc, :, 30:158], in_=YB[:, cc, :].rearrange("p (m t) -> p m t", t=128), func=AF.Copy)
        S.dve("tensor_scalar", r=[P("TLB"), P("ohp")], w=[P("YE")], out=YE[:, cc, :, 0:30], in0=TLB[:, cc, 0, :, :], scalar1=ohp[:, 0:1],
              scalar2=None, op0=ALU.mult)
        for r in (1, 2):
            S.dve("scalar_tensor_tensor", r=[P("TLB"), P("ohp"), P("YE")], w=[P("YE")], out=YE[:, cc, :, 0:30], in0=TLB[:, cc, r, :, :],
                  scalar=ohp[:, r:r + 1], in1=YE[:, cc, :, 0:30], op0=ALU.mult, op1=ALU.add)
        S.dve("scalar_tensor_tensor", r=[P("TLB"), P("ohp"), P("YE")], w=[P("YE")], out=YE[:, cc, 1:16, 0:30], in0=TLB[:, cc, 3, 0:15, :],
              scalar=ohp[:, 3:4], in1=YE[:, cc, 1:16, 0:30], op0=ALU.mult, op1=ALU.add)
    for cc in range(2):
        for tp in range(31):
            (S.dve if tp % 2 == 0 else S.pool)("tensor_scalar", r=[P("idf"), P("CW")], w=[P("DIAG%d_%d" % (cc, tp))], out=DIAG[:, cc, tp, :],
                                               in0=idf[:], scalar1=CW[:, cc, tp:tp + 1], scalar2=None, op0=ALU.mult)
    for cc in range(2):
        an = P("accA%d" % cc)
        for pc in range(4):
            k = (cc * 4 + pc) % 2
            for tp in range(31):
                S.pe("matmul", r=[P("DIAG%d_%d" % (cc, tp)), P("YE")], w=[P("ps%d" % k)], out=PS[k][:], lhsT=DIAG[:, cc, tp, :],
                     rhs=YE[:, cc, 4 * pc:4 * pc + 4, tp:tp + 128], start=(tp == 0), stop=(tp == 30))
            S.act("activation", r=[P("ps%d" % k), P("CBs")], w=[an], out=accA[:, cc, 4 * pc:4 * pc + 4, :].rearrange("p m t -> p (m t)"),
                  in_=PS[k][:], func=AF.Identity, bias=CBs[:, cc:cc + 1], scale=1.0)
        S.act("activation", r=[an], w=[P("SQ")], out=SQ[:, cc, :], in_=accA[:, cc].rearrange("p m t -> p (m t)"), func=AF.Square)
    for pc in range(4):
        cs = slice(pc * 512, (pc + 1) * 512)
        for cc in range(2):
            S.pe("matmul", r=[P("onesd"), P("accA%d" % cc)], w=[P("ps0")], out=PS[0][:], lhsT=onesd[:],
                 rhs=accA[:, cc].rearrange("p m t -> p (m t)")[:, cs], start=(cc == 0), stop=(cc == 1))
        for cc in range(2):
            S.pe("matmul", r=[P("onesd"), P("SQ")], w=[P("ps1")], out=PS[1][:], lhsT=onesd[:], rhs=SQ[:, cc, cs], start=(cc == 0), stop=(cc == 1))
        S.act("activation", r=[P("ps0")], w=[P("MEAN")], out=MEAN[:, cs], in_=PS[0][:], func=AF.Copy)
        S.act("activation", r=[P("ps1")], w=[P("MSQ")], out=MSQ[:, cs], in_=PS[1][:], func=AF.Copy)
    S.dve("tensor_tensor", r=[P("MEAN")], w=[P("SQ")], out=SQ[:, 0, :], in0=MEAN[:], in1=MEAN[:], op=ALU.mult)
    S.dve("tensor_tensor", r=[P("MSQ"), P("SQ")], w=[P("MSQ")], out=MSQ[:], in0=MSQ[:], in1=SQ[:, 0, :], op=ALU.subtract)
    S.dve("tensor_scalar", r=[P("MSQ")], w=[P("MSQ")], out=MSQ[:], in0=MSQ[:], scalar1=1e-6, scalar2=None, op0=ALU.add)
    S.act("activation", r=[P("MSQ")], w=[P("MSQ")], out=MSQ[:], in_=MSQ[:], func=AF.Sqrt)
    S.dve("reciprocal", r=[P("MSQ")], w=[P("MSQ")], out=MSQ[:], in_=MSQ[:])
    for cc in range(2):
        an = P("accA%d" % cc)
        a2 = accA[:, cc].rearrange("p m t -> p (m t)")
        S.dve("tensor_tensor", r=[an, P("MEAN")], w=[an], out=a2, in0=a2, in1=MEAN[:], op=ALU.subtract)
        S.dve("tensor_tensor", r=[an, P("MSQ")], w=[an], out=a2, in0=a2, in1=MSQ[:], op=ALU.mult)
        S.act("activation", r=[an, P("LG"), P("LB")], w=[P("ACTT")], out=ACTT[:, cc, :], in_=a2, func=AF.Silu, scale=LG[:, cc:cc + 1], bias=LB[:, cc:cc + 1])
    for m in range(16):
        p = m % 2
        k = 2 + p
        S.dma(P("gt%d" % p), w=[P("gtt%d" % p)], out=gtt[p][:], in_=io["gt"][m * 128:(m + 1) * 128, 256:512])
        for cc in range(2):
            S.pe("matmul", r=[P("ACTT"), P("WPW")], w=[P("ps%d" % k)], out=PS[k][:, 0:256], lhsT=ACTT[:, cc, m * 128:(m + 1) * 128], rhs=WPW[:, cc, :],
                 start=(cc == 0), stop=(cc == 1))
        S.act("activation", r=[P("ps%d" % k)], w=[P("ob%d" % p)], out=ob[p][:], in_=PS[k][:, 0:256], func=AF.Copy)
        S.dve("tensor_tensor", r=[P("ob%d" % p), P("gtt%d" % p)], w=["MIXB"], out=MIXB[:, m, 256:512], in0=ob[p][:], in1=gtt[p][:], op=ALU.mult)


def conv_consts(inputs, layer, r):
    cw = inputs["conv_w"][layer]
    col = lambda v: np.ascontiguousarray(v.reshape(2, 128).T, dtype=np.float32)
    ohp = np.zeros((128, 4), np.float32)
    ohp[:, (r - 1) % 4] = 1.0
    return dict(conv_w=np.ascontiguousarray(cw.T.reshape(2, 128, 31).transpose(1, 0, 2), dtype=np.float32),
                conv_b=col(inputs["conv_b"][layer]), ln_g=col(inputs["conv_ln_g"][layer]), ln_b=col(inputs["conv_ln_b"][layer]),
                conv_pw=np.ascontiguousarray(inputs["conv_pw"][layer]), ohprev=ohp)
```

```python
import numpy as np
import ml_dtypes
from contextlib import ExitStack
import concourse.bass as bass
import concourse.mybir as mybir
from concourse.bass_utils import run_bass_kernel_spmd

F32 = mybir.dt.float32
BF16 = mybir.dt.bfloat16
AF = mybir.ActivationFunctionType
ALU = mybir.AluOpType
AX = mybir.AxisListType
NPBF = ml_dtypes.bfloat16

ENGINES = ("pe", "act", "dve", "pool", "sp")


class Buf:
    __slots__ = ("name", "last_w", "readers")

    def __init__(self, name):
        self.name = name
        self.last_w = None
        self.readers = []


class Op:
    __slots__ = ("eng", "fn", "reads", "writes", "dma_key", "deps", "need_inc", "tok", "idx")

    def __init__(self, eng, fn, reads, writes, dma_key):
        self.eng = eng
        self.fn = fn
        self.reads = reads
        self.writes = writes
        self.dma_key = dma_key
        self.deps = []
        self.need_inc = False
        self.tok = None


class Sched:
    def __init__(self, nc, es):
        self.nc = nc
        self.es = es
        self.ops = []
        self.bufs = {}
        self.batch = set()
        self.cnt = {}
        self.sems = {}
        self.waited = {e: {} for e in ENGINES}

    def buf(self, name):
        b = self.bufs.get(name)
        if b is None:
            b = Buf(name)
            self.bufs[name] = b
        return b

    def _norm(self, lst):
        return [self.buf(b) if isinstance(b, str) else b for b in (lst or ())]

    def add(self, eng, meth, kw, reads=(), writes=(), dma_key=None):
        op = Op(eng, (meth, kw), self._norm(reads), self._norm(writes), dma_key)
        op.idx = len(self.ops)
        self.ops.append(op)
        return op

    def pe(self, meth, r=(), w=(), **kw):
        return self.add("pe", meth, kw, r, w)

    def act(self, meth, r=(), w=(), **kw):
        return self.add("act", meth, kw, r, w)

    def dve(self, meth, r=(), w=(), **kw):
        return self.add("dve", meth, kw, r, w)

    def pool(self, meth, r=(), w=(), **kw):
        return self.add("pool", meth, kw, r, w)

    def dma(self, key, r=(), w=(), eng="sp", batch=False, **kw):
        if batch:
            self.batch.add(key)
        return self.add(eng, "dma_start", kw, r, w, dma_key=key)

    def analyze(self):
        for op in self.ops:
            deps = set()
            for b in op.reads:
                if b.last_w is not None:
                    deps.add(b.last_w)
            for b in op.writes:
                if b.last_w is not None:
                    deps.add(b.last_w)
                for r in b.readers:
                    deps.add(r)
            deps.discard(op.idx)
            keep = []
            for d in deps:
                dop = self.ops[d]
                if dop.dma_key is not None and dop.dma_key == op.dma_key and dop.dma_key in self.batch:
                    continue
                if dop.dma_key is None and dop.eng == op.eng:
                    if op.eng == "pe" or op.dma_key is not None:
                        continue
                keep.append(d)
            op.deps = sorted(keep)
            for d in op.deps:
                self.ops[d].need_inc = True
            for b in op.reads:
                b.readers.append(op.idx)
            for b in op.writes:
                b.last_w = op.idx
                b.readers = []

    def flush(self):
        nc = self.nc
        self.analyze()
        per = {e: [op for op in self.ops if op.eng == e] for e in ENGINES}
        for e in ENGINES:
            comp = [op for op in per[e] if op.dma_key is None]
            if comp:
                comp[-1].need_inc = True
        cnt = self.cnt
        for op in self.ops:
            if op.dma_key is not None:
                k = ("dma", op.dma_key)
                cnt[k] = cnt.get(k, 0) + 16
                op.tok = (k, cnt[k])
            elif op.need_inc:
                k = ("eng", op.eng)
                cnt[k] = cnt.get(k, 0) + 1
                op.tok = (k, cnt[k])
        for op in self.ops:
            if op.dma_key is not None and op.dma_key in self.batch:
                op.tok = (op.tok[0], cnt[op.tok[0]])
        for k in sorted(cnt.keys()):
            if k not in self.sems:
                self.sems[k] = self.es.enter_context(nc.semaphore("s_%s_%s" % k))
        sems = self.sems
        ops = self.ops
        totals = dict(cnt)

        def run(eng_name, h):
            waited = self.waited[eng_name]
            for op in per[eng_name]:
                for d in op.deps:
                    k, v = ops[d].tok
                    if waited.get(k, 0) >= v:
                        continue
                    h.wait_ge(sems[k], v)
                    waited[k] = v
                ins = getattr(h, op.fn[0])(**op.fn[1])
                if op.tok is not None:
                    ins.then_inc(sems[op.tok[0]], 16 if op.dma_key is not None else 1)
            for k in sorted(totals.keys()):
                if waited.get(k, 0) < totals[k]:
                    h.wait_ge(sems[k], totals[k])
                    waited[k] = totals[k]

        with nc.Block() as block:
            block.sync(lambda h: run("sp", h))
            block.tensor(lambda h: run("pe", h))
            block.scalar(lambda h: run("act", h))
            block.vector(lambda h: run("dve", h))
            block.gpsimd(lambda h: run("pool", h))
        self.ops = []
        self.bufs = {}


D = 1024
DIN = 3612
NT = 2048
T = 8192
OFF = dict(fq=0, fk=256, fv=512, ff=768, fg=772, glu=1028, cg=1540, nq=1796, nkc=2308, nvc=2436,
           nks=2564, nvs=2692, nkw=2820, nvw=2948, ngl=3076, ng=3100)

A_OUTS = dict(qf=([2, 128, NT], BF16), kf=([2, 128, NT], BF16), yT=([2, 128, NT], BF16), qn=([4, 128, NT], BF16),
              kc=([128, NT], BF16), vc=([128, NT], BF16), ks=([128, NT], BF16), kw=([128, NT], BF16),
              vf=([NT, 320], BF16), vs=([NT, 160], BF16), vw=([NT, 160], BF16), lf=([NT, 4], F32),
              gt=([NT, D], BF16), gl=([NT, 24], F32))


def phase_a(S, nc, es, io, pfx="a"):
    P = lambda n: pfx + n
    sb = lambda n, s, d: es.enter_context(nc.sbuf_tensor(P(n), s, d))
    PS0b = es.enter_context(nc.psum_tensor(P("ps0"), [128, 1024], BF16))
    PS = [None] + [es.enter_context(nc.psum_tensor(P("ps%d" % i), [128, 512], F32)) for i in range(1, 8)]
    Wb = sb("Wb", [128, 8, DIN], BF16)
    wst = [sb("wst%d" % i, [128, DIN], F32) for i in range(2)]
    gcol = sb("gcol", [128, 8], F32)
    fb = sb("fb", [128, 4], F32)
    cosT = sb("cosT", [128, NT], F32)
    sinT = sb("sinT", [128, NT], F32)
    rmat_f = sb("rmat_f", [128, 128], F32)
    rmat = sb("rmat", [128, 128], BF16)
    idf = sb("idf", [128, 128], F32)
    idb = sb("idb", [128, 128], BF16)
    xt = [sb("xt%d" % i, [128, D], F32) for i in range(2)]
    sq = sb("sq", [128, D], BF16)
    ss = [sb("ss%d" % i, [128, 1], F32) for i in range(2)]
    hb = [sb("hb%d" % i, [128, D], BF16) for i in range(2)]
    hT = [sb("hT%d" % i, [128, 8, 512], BF16) for i in range(2)]
    fo = [sb("fo%d" % i, [128, 512], BF16) for i in range(4)]
    zc = [sb("zc%d" % i, [128, 512], BF16) for i in range(2)]
    t1 = [sb("t1%d" % i, [128, 512], F32) for i in range(2)]
    t2 = [sb("t2%d" % i, [128, 512], F32) for i in range(2)]
    sg = [sb("sg%d" % i, [128, 512], F32) for i in range(2)]
    vfo = [sb("vfo%d" % i, [128, 4, 80], BF16) for i in range(2)]
    vso = [sb("vso%d" % i, [128, 2, 80], BF16) for i in range(2)]
    vwo = [sb("vwo%d" % i, [128, 2, 80], BF16) for i in range(2)]
    lfo = [sb("lfo%d" % i, [128, 4], F32) for i in range(2)]
    lft = [sb("lft%d" % i, [128, 4], F32) for i in range(2)]
    glo = [sb("glo%d" % i, [128, 24], F32) for i in range(2)]
    gto = [sb("gto%d" % i, [128, D], BF16) for i in range(2)]
    out_keys = set()

    def store(src_name, dst, src):
        k = P("k_" + src_name)
        out_keys.add(k)
        S.dma(k, r=[P(src_name)], out=dst, in_=src)

    S.dma(P("c"), batch=True, w=[P("gcol")], out=gcol[:], in_=io["norm_g"])
    S.dma(P("c"), batch=True, w=[P("fb")], out=fb[:], in_=io["fox_b"].partition_broadcast(128))
    S.dma(P("c"), batch=True, w=[P("cos")], out=cosT[:], in_=io["cos"])
    S.dma(P("c"), batch=True, w=[P("sin")], out=sinT[:], in_=io["sin"])
    S.dma(P("c"), batch=True, w=[P("rmf")], out=rmat_f[:], in_=io["rmat"])
    S.dma(P("c"), batch=True, w=[P("idf")], out=idf[:], in_=io["ident"])
    S.dve("tensor_copy", r=[P("rmf")], w=[P("rmat")], out=rmat[:], in_=rmat_f[:])
    S.dve("tensor_copy", r=[P("idf")], w=[P("idb")], out=idb[:], in_=idf[:])
    for i in range(2):
        S.dve("memset", w=[P("vfo%d" % i)], ap=vfo[i][:], constant=1.0)
        S.dve("memset", w=[P("vso%d" % i)], ap=vso[i][:], constant=1.0)
        S.dve("memset", w=[P("vwo%d" % i)], ap=vwo[i][:], constant=1.0)
    for c in range(8):
        w = wst[c % 2]
        wn = P("wst%d" % (c % 2))
        S.dma(P("w%d" % (c % 2)), w=[wn], out=w[:], in_=io["w_in"][c * 128:(c + 1) * 128, :])
        S.act("activation", r=[wn, P("gcol")], w=[P("Wb%d" % c)], out=Wb[:, c, :], in_=w[:], func=AF.Copy, scale=gcol[:, c:c + 1])

    psrr = [1]

    def next_ps():
        k = psrr[0]
        psrr[0] = 1 + (psrr[0] % 7)
        return k

    foi = [0]
    ri = [0]
    for s in range(4):
        hTs = hT[s % 2]
        hTn = P("hT%d" % (s % 2))
        for q in range(4):
            tn = s * 4 + q
            p = tn % 2
            x_t = xt[p]
            xn = P("xt%d" % p)
            ssn = P("ss%d" % p)
            S.dma(P("x%d" % p), w=[xn], out=x_t[:], in_=io["x"][tn * 128:(tn + 1) * 128, :])
            S.act("activation", r=[xn], w=[P("sq"), ssn], out=sq[:], in_=x_t[:], func=AF.Square, accum_out=ss[p][:, 0:1])
            S.dve("tensor_scalar", r=[ssn], w=[ssn], out=ss[p][:], in0=ss[p][:], scalar1=1.0 / D, scalar2=1e-6,
                  op0=ALU.mult, op1=ALU.add)
            S.act("activation", r=[ssn], w=[ssn], out=ss[p][:], in_=ss[p][:], func=AF.Sqrt)
            S.dve("reciprocal", r=[ssn], w=[ssn], out=ss[p][:], in_=ss[p][:])
            S.dve("tensor_scalar", r=[xn, ssn], w=[P("hb%d" % p)], out=hb[p][:], in0=x_t[:], scalar1=ss[p][:, 0:1],
                  scalar2=None, op0=ALU.mult)
            for half in range(2):
                for cc in range(4):
                    c = half * 4 + cc
                    S.pe("transpose", r=[P("hb%d" % p), P("idb")], w=[P("ps0")], out=PS0b[:, cc * 128:(cc + 1) * 128],
                         in_=hb[p][:, c * 128:(c + 1) * 128], identity=idb[:])
                S.act("activation", r=[P("ps0")], w=[hTn], out=hTs[:, half * 4:(half + 1) * 4, q * 128:(q + 1) * 128],
                      in_=PS0b[:, 0:512].rearrange("p (c t) -> p c t", c=4), func=AF.Copy)
        cols = slice(s * 512, (s + 1) * 512)

        def fm(coff, k):
            for c in range(8):
                S.pe("matmul", r=[P("Wb%d" % c), hTn], w=[P("ps%d" % k)], out=PS[k][:, :], lhsT=Wb[:, c, coff:coff + 128],
                     rhs=hTs[:, c, :], start=(c == 0), stop=(c == 7))

        def plain_out(coff, dst):
            k = next_ps()
            fm(coff, k)
            i = foi[0] % 4
            foi[0] += 1
            S.act("activation", r=[P("ps%d" % k)], w=[P("fo%d" % i)], out=fo[i][:], in_=PS[k][:], func=AF.Copy)
            store("fo%d" % i, dst, fo[i][:])

        def rope_out(coff, dst):
            k = next_ps()
            fm(coff, k)
            k2 = next_ps()
            j = ri[0] % 2
            ri[0] += 1
            i = foi[0] % 4
            foi[0] += 1
            S.act("activation", r=[P("ps%d" % k)], w=[P("zc%d" % j)], out=zc[j][:], in_=PS[k][:], func=AF.Copy)
            S.pe("matmul", r=[P("rmat"), P("zc%d" % j)], w=[P("ps%d" % k2)], out=PS[k2][:], lhsT=rmat[:], rhs=zc[j][:],
                 start=True, stop=True)
            S.act("activation", r=[P("ps%d" % k)], w=[P("t1%d" % j)], out=t1[j][:], in_=PS[k][:], func=AF.Copy)
            S.act("activation", r=[P("ps%d" % k2)], w=[P("t2%d" % j)], out=t2[j][:], in_=PS[k2][:], func=AF.Copy)
            S.dve("tensor_tensor", r=[P("t1%d" % j), P("cos")], w=[P("t1%d" % j)], out=t1[j][:], in0=t1[j][:],
                  in1=cosT[:, cols], op=ALU.mult)
            S.pool("tensor_tensor", r=[P("t2%d" % j), P("sin")], w=[P("t2%d" % j)], out=t2[j][:], in0=t2[j][:],
                   in1=sinT[:, cols], op=ALU.mult)
            S.dve("tensor_tensor", r=[P("t1%d" % j), P("t2%d" % j)], w=[P("fo%d" % i)], out=fo[i][:], in0=t1[j][:],
                   in1=t2[j][:], op=ALU.add)
            store("fo%d" % i, dst, fo[i][:])

        for h2 in range(2):
            plain_out(OFF["fq"] + 128 * h2, io["qf"][h2, :, cols])
            plain_out(OFF["fk"] + 128 * h2, io["kf"][h2, :, cols])
        for h2 in range(2):
            ka = next_ps()
            fm(OFF["glu"] + 128 * h2, ka)
            kb = next_ps()
            fm(OFF["glu"] + 256 + 128 * h2, kb)
            i = foi[0] % 4
            foi[0] += 1
            S.act("activation", r=[P("ps%d" % kb)], w=[P("sg%d" % h2)], out=sg[h2][:], in_=PS[kb][:], func=AF.Sigmoid)
            S.act("activation", r=[P("ps%d" % ka)], w=[P("t1%d" % h2)], out=t1[h2][:], in_=PS[ka][:], func=AF.Copy)
            S.dve("tensor_tensor", r=[P("t1%d" % h2), P("sg%d" % h2)], w=[P("fo%d" % i)], out=fo[i][:], in0=t1[h2][:],
                  in1=sg[h2][:], op=ALU.mult)
            store("fo%d" % i, io["yT"][h2, :, cols], fo[i][:])
        for c4 in range(4):
            rope_out(OFF["nq"] + 128 * c4, io["qn"][c4, :, cols])
        plain_out(OFF["nkc"], io["kc"][:, cols])
        plain_out(OFF["nvc"], io["vc"][:, cols])
        rope_out(OFF["nks"], io["ks"][:, cols])
        rope_out(OFF["nkw"], io["kw"][:, cols])

        for q in range(4):
            tn = s * 4 + q
            p = tn % 2
            rows = slice(tn * 128, (tn + 1) * 128)

            def tm(c0, c1, k):
                for c in range(8):
                    S.pe("matmul", r=[P("Wb%d" % c), hTn], w=[P("ps%d" % k)], out=PS[k][:, 0:c1 - c0],
                         lhsT=hTs[:, c, q * 128:(q + 1) * 128], rhs=Wb[:, c, c0:c1], start=(c == 0), stop=(c == 7))

            k = next_ps()
            tm(512, 772, k)
            S.act("activation", r=[P("ps%d" % k)], w=[P("vfo%d" % p)], out=vfo[p][:, :, 0:64],
                  in_=PS[k][:, 0:256].rearrange("p (h d) -> p h d", h=4), func=AF.Copy)
            store("vfo%d" % p, io["vf"][rows, :], vfo[p][:].rearrange("p h d -> p (h d)"))
            lt = P("lft%d" % p)
            S.act("activation", r=[P("ps%d" % k)], w=[lt], out=lft[p][:], in_=PS[k][:, 256:260], func=AF.Copy)
            S.dve("tensor_tensor", r=[lt, P("fb")], w=[lt], out=lft[p][:], in0=lft[p][:], in1=fb[:], op=ALU.add)
            S.act("activation", r=[lt], w=[lt], out=lft[p][:], in_=lft[p][:], func=AF.Exp, scale=-1.0)
            S.act("activation", r=[lt], w=[lt], out=lft[p][:], in_=lft[p][:], func=AF.Ln, bias=1.0, scale=1.0)
            S.dve("tensor_scalar", r=[lt], w=[P("lfo%d" % p)], out=lfo[p][:], in0=lft[p][:], scalar1=-1.0, scalar2=None, op0=ALU.mult)
            store("lfo%d" % p, io["lf"][rows, :], lfo[p][:])
            for (c0, g0) in ((772, 0), (1540, 256)):
                k = next_ps()
                tm(c0, c0 + 256, k)
                S.act("activation", r=[P("ps%d" % k)], w=[P("gto%d" % p)], out=gto[p][:, g0:g0 + 256], in_=PS[k][:, 0:256], func=AF.Silu)
            k = next_ps()
            tm(3100, 3612, k)
            S.act("activation", r=[P("ps%d" % k)], w=[P("gto%d" % p)], out=gto[p][:, 512:1024], in_=PS[k][:, 0:512], func=AF.Silu)
            store("gto%d" % p, io["gt"][rows, :], gto[p][:])
            k = next_ps()
            tm(2692, 2820, k)
            S.act("activation", r=[P("ps%d" % k)], w=[P("vso%d" % p)], out=vso[p][:, :, 0:64],
                  in_=PS[k][:, 0:128].rearrange("p (h d) -> p h d", h=2), func=AF.Copy)
            store("vso%d" % p, io["vs"][rows, :], vso[p][:].rearrange("p h d -> p (h d)"))
            k = next_ps()
            tm(2948, 3100, k)
            S.act("activation", r=[P("ps%d" % k)], w=[P("vwo%d" % p)], out=vwo[p][:, :, 0:64],
                  in_=PS[k][:, 0:128].rearrange("p (h d) -> p h d", h=2), func=AF.Copy)
            store("vwo%d" % p, io["vw"][rows, :], vwo[p][:].rearrange("p h d -> p (h d)"))
            S.act("activation", r=[P("ps%d" % k)], w=[P("glo%d" % p)], out=glo[p][:], in_=PS[k][:, 128:152], func=AF.Sigmoid)
            store("glo%d" % p, io["gl"][rows, :], glo[p][:])
    return sorted(out_keys)


def own_positions(r):
    m = np.arange(16)[:, None]
    tl = np.arange(128)[None, :]
    return ((4 * m + r) * 128 + tl).reshape(-1)


def rope_tables(pos):
    inv = 500000.0 ** (-np.arange(0, 16, 2, dtype=np.float32) / 16.0)
    ang = pos.astype(np.float32)[None, :] * np.tile(inv, 2)[:, None].astype(np.float32)
    cos = np.ones((64, pos.size), np.float32)
    sin = np.zeros((64, pos.size), np.float32)
    cos[:16] = np.cos(ang)
    sin[:16] = np.sin(ang)
    return np.tile(cos, (2, 1)), np.tile(sin, (2, 1))


def rope_rmat():
    R = np.zeros((128, 128), np.float32)
    for hh in range(2):
        for d in range(8):
            R[hh * 64 + d + 8, hh * 64 + d] = -1.0
            R[hh * 64 + d, hh * 64 + d + 8] = 1.0
    return R


def build_a():
    nc = bass.Bass("TRN2", target_bir_lowering=False)
    io = {}
    io["x"] = nc.dram_tensor("x", [NT, D], F32, kind="ExternalInput").ap()
    io["w_in"] = nc.dram_tensor("w_in", [D, DIN], F32, kind="ExternalInput").ap()
    io["norm_g"] = nc.dram_tensor("norm_g", [128, 8], F32, kind="ExternalInput").ap()
    io["fox_b"] = nc.dram_tensor("fox_b", [4], F32, kind="ExternalInput").ap()
    io["cos"] = nc.dram_tensor("cos", [128, NT], F32, kind="ExternalInput").ap()
    io["sin"] = nc.dram_tensor("sin", [128, NT], F32, kind="ExternalInput").ap()
    io["rmat"] = nc.dram_tensor("rmat", [128, 128], F32, kind="ExternalInput").ap()
    io["ident"] = nc.dram_tensor("ident", [128, 128], F32, kind="ExternalInput").ap()
    for n, (shp, dt) in A_OUTS.items():
        io[n] = nc.dram_tensor(n, shp, dt, kind="ExternalOutput").ap()
    with ExitStack() as top:
        S = Sched(nc, top)
        with ExitStack() as es:
            phase_a(S, nc, es, io)
            S.flush()
    return nc


def out_scope(S, nc, es, io, MIXB, last, pfx="o"):
    P = lambda n: pfx + n
    sb = lambda n, s_, d: es.enter_context(nc.sbuf_tensor(P(n), s_, d))
    PS = [es.enter_context(nc.psum_tensor(P("ps%d" % i), [128, 512], F32)) for i in range(4)]
    PSTb = es.enter_context(nc.psum_tensor(P("pstb"), [128, 1024], BF16))
    WO = sb("WO", [128, 8, D], BF16)
    wst = [sb("wst%d" % i, [128, D], F32) for i in range(2)]
    idf = sb("idf", [128, 128], F32)
    idb = sb("idb", [128, 128], BF16)
    MXT = [sb("MXT%d" % i, [128, 8, 128], BF16) for i in range(2)]
    xt = [sb("xt%d" % i, [128, D], F32) for i in range(2)]
    yo = [sb("yo%d" % i, [128, D], F32) for i in range(2)]
    sq = sb("sq", [128, D], F32)
    ss = sb("ss", [128, 1], F32)
    fg = sb("fg", [128, D], F32)
    S.dma(P("c"), batch=True, w=[P("idf")], out=idf[:], in_=io["ident"])
    if last:
        S.dma(P("c"), batch=True, w=[P("fg")], out=fg[:], in_=io["final_g"].partition_broadcast(128))
    S.dve("tensor_copy", r=[P("idf")], w=[P("idb")], out=idb[:], in_=idf[:])
    for c in range(8):
        w, wn = wst[c % 2], P("wst%d" % (c % 2))
        S.dma(P("w%d" % (c % 2)), w=[wn], out=w[:], in_=io["w_out"][c * 128:(c + 1) * 128, :])
        if c % 2 == 0:
            S.dve("tensor_copy", r=[wn], w=[P("WO%d" % c)], out=WO[:, c, :], in_=w[:])
        else:
            S.act("activation", r=[wn], w=[P("WO%d" % c)], out=WO[:, c, :], in_=w[:], func=AF.Copy)
    for m in range(16):
        p = m % 2
        mx, mxn, x_t, xn, y_t, yn = MXT[p], P("MXT%d" % p), xt[p], P("xt%d" % p), yo[p], P("yo%d" % p)
        S.dma(P("x%d" % p), w=[xn], out=x_t[:], in_=io["x"][m * 128:(m + 1) * 128, :])
        for c in range(8):
            S.pe("transpose", r=["MIXB", P("idb")], w=[P("pstb")], out=PSTb[:, c * 128:(c + 1) * 128], in_=MIXB[:, m, c * 128:(c + 1) * 128],
                 identity=idb[:])
        S.act("activation", r=[P("pstb")], w=[mxn], out=mx[:].rearrange("f c q -> f (c q)"), in_=PSTb[:], func=AF.Copy)
        for n2 in range(2):
            k = (2 * m + n2) % 4
            for c in range(8):
                S.pe("matmul", r=[mxn, P("WO%d" % c)], w=[P("ps%d" % k)], out=PS[k][:], lhsT=mx[:, c, :], rhs=WO[:, c, n2 * 512:(n2 + 1) * 512],
                     start=(c == 0), stop=(c == 7))
            S.act("activation", r=[P("ps%d" % k)], w=[yn], out=y_t[:, n2 * 512:(n2 + 1) * 512], in_=PS[k][:], func=AF.Copy)
        S.dve("tensor_tensor", r=[yn, xn], w=[yn], out=y_t[:], in0=y_t[:], in1=x_t[:], op=ALU.add)
        if last:
            S.act("activation", r=[yn], w=[P("sq"), P("ss")], out=sq[:], in_=y_t[:], func=AF.Square, accum_out=ss[:, 0:1])
            S.dve("tensor_scalar", r=[P("ss")], w=[P("ss")], out=ss[:], in0=ss[:], scalar1=1.0 / D, scalar2=1e-6, op0=ALU.mult, op1=ALU.add)
            S.act("activation", r=[P("ss")], w=[P("ss")], out=ss[:], in_=ss[:], func=AF.Sqrt)
            S.dve("reciprocal", r=[P("ss")], w=[P("ss")], out=ss[:], in_=ss[:])
            S.dve("scalar_tensor_tensor", r=[yn, P("ss"), P("fg")], w=[yn], out=y_t[:], in0=y_t[:], scalar=ss[:, 0:1], in1=fg[:],
                  op0=ALU.mult, op1=ALU.mult)
        S.dma(P("o%d" % p), r=[yn], out=io["xo"][m * 128:(m + 1) * 128, :], in_=y_t[:])


B_INS = dict(kf_g=([2, 128, T], BF16), vf_g=([T, 320], BF16), lf_g=([T, 4], F32), qf=([2, 128, NT], BF16),
             kc_g=([128, T], BF16), vc_g=([128, T], BF16), ks_g=([128, T], BF16), kw_g=([128, T], BF16),
             vs_g=([T, 160], BF16), vw_g=([T, 160], BF16), qn=([4, 128, NT], BF16), gl=([NT, 24], F32), gt=([NT, D], BF16),
             yT=([2, 128, NT], BF16), tails_g=([4, 2, 128, 16, 30], BF16), x=([NT, D], F32), w_out=([D, D], F32), final_g=([D], F32),
             tri=([128, 128], F32), ones128=([128, 128], F32), onehot=([128, 4], F32), dmask=([128, 4, 128], BF16),
             wmask=([128, 8, 128], BF16), cmask=([16, 128, 4, 128], BF16), force=([16, 128, 128], F32), ee=([128, T], BF16),
             ident=([128, 128], F32), rmat=([128, 128], F32), cosC=([128, 512], F32), sinC=([128, 512], F32),
             ovl=([128, 4, 128], BF16), peT=([64, 2, 32], F32), w2=([128, 2, 2, 64], F32), w1k=([2048, 256], F32),
             w1v=([2048, 256], F32), conv_w=([128, 2, 31], F32), conv_b=([128, 2], F32), ln_g=([128, 2], F32), ln_b=([128, 2], F32),
             conv_pw=([256, 256], F32), ohprev=([128, 4], F32))


def phase_b(S, nc, top_unused, io, last):
  with ExitStack() as top:
    KCMP = top.enter_context(nc.sbuf_tensor("KCMP", [128, 512], BF16))
    VCA = top.enter_context(nc.sbuf_tensor("VCA", [128, 4, 2, 208], BF16))
    MIXB = top.enter_context(nc.sbuf_tensor("MIXB", [128, 16, D], BF16))
    for scope in (lambda es: cmp_scope(S, nc, es, io, KCMP, VCA), lambda es: conv_scope(S, nc, es, io, MIXB),
                  lambda es: fox_scope(S, nc, es, io, MIXB), lambda es: nsa_scope(S, nc, es, io, KCMP, VCA, MIXB),
                  lambda es: out_scope(S, nc, es, io, MIXB, last)):
        with ExitStack() as es:
            scope(es)
            S.flush()


def build_b(last):
    nc = bass.Bass("TRN2", target_bir_lowering=False)
    io = {n: nc.dram_tensor(n, shp, dt, kind="ExternalInput").ap() for n, (shp, dt) in B_INS.items()}
    io["xo"] = nc.dram_tensor("xo", [NT, D], F32, kind="ExternalOutput").ap()
    with ExitStack() as top:
        S = Sched(nc, top)
        phase_b(S, nc, top, io, last)
    return nc


def build_ba():
    nc = bass.Bass("TRN2", target_bir_lowering=False)
    io = {n: nc.dram_tensor(n, shp, dt, kind="ExternalInput").ap() for n, (shp, dt) in B_INS.items()}
    io["xo"] = nc.dram_tensor("xo", [NT, D], F32, kind="ExternalOutput").ap()
    ioa = dict(x=io["xo"], rmat=io["rmat"], ident=io["ident"])
    ioa["w_in"] = nc.dram_tensor("a_w_in", [D, DIN], F32, kind="ExternalInput").ap()
    ioa["norm_g"] = nc.dram_tensor("a_norm_g", [128, 8], F32, kind="ExternalInput").ap()
    ioa["fox_b"] = nc.dram_tensor("a_fox_b", [4], F32, kind="ExternalInput").ap()
    ioa["cos"] = nc.dram_tensor("a_cos", [128, NT], F32, kind="ExternalInput").ap()
    ioa["sin"] = nc.dram_tensor("a_sin", [128, NT], F32, kind="ExternalInput").ap()
    for n, (shp, dt) in A_OUTS.items():
        ioa[n] = nc.dram_tensor("a_" + n, shp, dt, kind="ExternalOutput").ap()
    with ExitStack() as top:
        S = Sched(nc, top)
        phase_b(S, nc, top, io, False)
        with ExitStack() as es:
            phase_a(S, nc, es, ioa, pfx="A")
            S.flush()
    return nc


def a_inputs(inputs, layer, x_own):
    maps = []
    for core in range(8):
        cos, sin = rope_tables(own_positions(core % 4))
        maps.append(dict(x=np.ascontiguousarray(x_own[core], dtype=np.float32), w_in=np.ascontiguousarray(inputs["w_in"][layer]),
                         norm_g=np.ascontiguousarray(inputs["norm_g"][layer].reshape(8, 128).T), fox_b=np.ascontiguousarray(inputs["fox_b"][layer]),
                         cos=cos, sin=sin, rmat=rope_rmat(), ident=np.eye(128, dtype=np.float32)))
    return maps


def b_inputs(inputs, layer, resA, x_own):
    maps = []
    cc = common_consts()
    ee = ee_const()
    cmpc = cmp_consts(inputs, layer)
    for core in range(8):
        b, r = core // 4, core % 4
        R = [resA[4 * b + rr] for rr in range(4)]
        cat = lambda n, ax: np.concatenate([np.asarray(R[rr][n]) for rr in range(4)], axis=ax)
        d = dict(kf_g=cat("kf", 2), vf_g=cat("vf", 0), lf_g=cat("lf", 0), kc_g=cat("kc", 1), vc_g=cat("vc", 1), ks_g=cat("ks", 1),
                 kw_g=cat("kw", 1), vs_g=cat("vs", 0), vw_g=cat("vw", 0))
        d["tails_g"] = np.ascontiguousarray(np.stack([np.asarray(R[rr]["yT"]).reshape(2, 128, 16, 128)[:, :, :, 98:128] for rr in range(4)], axis=0))
        own = resA[core]
        for n in ("qf", "qn", "gl", "gt", "yT"):
            d[n] = np.asarray(own[n])
        d["x"] = np.ascontiguousarray(x_own[core], dtype=np.float32)
        d["w_out"] = np.ascontiguousarray(inputs["w_out"][layer])
        d["final_g"] = np.ascontiguousarray(inputs["final_g"])
        d.update(cc)
        d.update(rank_consts(r))
        d.update(nsa_rank_consts(r))
        d.update(cmpc)
        d.update(conv_consts(inputs, layer, r))
        d["ee"] = ee
        d["ident"] = np.eye(128, dtype=np.float32)
        maps.append(d)
    return maps


def kernel(**inputs):
    inputs = {k: np.asarray(v) for k, v in inputs.items()}
    x = inputs["x"].astype(np.float32)
    x_own = [x[core // 4][own_positions(core % 4)] for core in range(8)]
    cores = list(range(8))
    resA0 = run_bass_kernel_spmd(build_a(), a_inputs(inputs, 0, x_own), core_ids=cores).results
    maps = b_inputs(inputs, 0, resA0, x_own)
    a1 = a_inputs(inputs, 1, x_own)
    for core in cores:
        for k in ("w_in", "norm_g", "fox_b", "cos", "sin"):
            maps[core]["a_" + k] = a1[core][k]
    resBA = run_bass_kernel_spmd(build_ba(), maps, core_ids=cores).results
    x1_own = [np.asarray(resBA[core]["xo"]) for core in cores]
    resA1 = [{k[2:]: v for k, v in resBA[core].items() if k.startswith("a_")} for core in cores]
    resB1 = run_bass_kernel_spmd(build_b(last=True), b_inputs(inputs, 1, resA1, x1_own), core_ids=cores).results
    out = np.empty((2, T, D), np.float32)
    for core in cores:
        out[core // 4][own_positions(core % 4)] = np.asarray(resB1[core]["xo"])
    return out


def cidx(j):
    return (j % 4) * 16 + j // 4


def gcol(j):
    return cidx(j) * 128


def fox_scope(S, nc, es, io, MIXB, pfx="f"):
    P = lambda n: pfx + n
    sb = lambda n, s, d: es.enter_context(nc.sbuf_tensor(P(n), s, d))
    PS = [es.enter_context(nc.psum_tensor(P("ps%d" % i), [128, 512], F32)) for i in range(4)]
    KF = sb("KF", [128, 2, T], BF16)
    VF = sb("VF", [128, 64, 320], BF16)
    QF = sb("QF", [128, 4, NT], BF16)
    LF = sb("LF", [128, 64, 4], F32)
    tri = sb("tri", [128, 128], F32)
    ones = sb("ones", [128, 128], F32)
    oneh = sb("oneh", [128, 4], F32)
    dm = sb("dm", [128, 4, 128], BF16)
    dm4 = sb("dm4", [128, 4, 4, 128], BF16)
    WI = sb("WI", [128, 64, 4], F32)
    TOT = sb("TOT", [128, 64, 4], F32)
    SA = sb("SA", [128, 64, 4], F32)
    SB_ = sb("SB", [128, 64, 4], F32)
    NC_ = sb("NC", [128, 64, 4], F32)
    ROWN = sb("ROWN", [128, 16, 4], F32)
    CB = [sb("CB%d" % i, [128, 64, 4], F32) for i in range(2)]
    PT = [sb("PT%d" % i, [128, 512], BF16) for i in range(3)]
    ofs = sb("ofs", [128, 4, 65], F32)
    rden = sb("rden", [128, 4], F32)
    onrm = sb("onrm", [128, 256], F32)
    gtt = [sb("gtt%d" % i, [128, 256], BF16) for i in range(2)]

    ld = P("ld")
    S.pool("memset", w=[P("QF")], ap=QF[:].rearrange("p h t -> p (h t)"), constant=0.0)
    for h2 in range(2):
        S.dma(ld, batch=True, w=[P("KF")], out=KF[:, h2, :], in_=io["kf_g"][h2])
    for h in range(4):
        pb = 64 * (h % 2)
        S.dma(ld, batch=True, w=[P("QF")], out=QF[pb:pb + 64, h, :], in_=io["qf"][h // 2, pb:pb + 64, :])
    for r in range(4):
        S.dma(ld, batch=True, w=[P("VF")], out=VF[:, r * 16:(r + 1) * 16, :],
              in_=io["vf_g"][r * NT:(r + 1) * NT, :].rearrange("(c t) w -> t c w", t=128))
    for r in range(4):
        S.dma(ld, batch=True, w=[P("LF")], out=LF[:].rearrange("t (m r) h -> t m r h", r=4)[:, :, r, :],
              in_=io["lf_g"][r * NT:(r + 1) * NT, :].rearrange("(m t) h -> t m h", t=128))
    S.dma(ld, batch=True, w=[P("tri")], out=tri[:], in_=io["tri"])
    S.dma(ld, batch=True, w=[P("ones")], out=ones[:], in_=io["ones128"])
    S.dma(ld, batch=True, w=[P("oneh")], out=oneh[:], in_=io["onehot"])
    S.dma(ld, batch=True, w=[P("dm")], out=dm[:], in_=io["dmask"])
    for h in range(4):
        S.pool("tensor_copy", r=[P("dm")], w=[P("dm4")], out=dm4[:, :, h, :], in_=dm[:])

    LFf = LF[:].rearrange("t j h -> t (j h)")
    S.pe("matmul", r=[P("tri"), P("LF")], w=[P("ps0")], out=PS[0][:, 0:256], lhsT=tri[:], rhs=LFf, start=True, stop=True)
    S.pe("matmul", r=[P("ones"), P("LF")], w=[P("ps1")], out=PS[1][:, 0:256], lhsT=ones[:], rhs=LFf, start=True, stop=True)
    S.act("activation", r=[P("ps0")], w=[P("WI")], out=WI[:].rearrange("t j h -> t (j h)"), in_=PS[0][:, 0:256], func=AF.Copy)
    S.act("activation", r=[P("ps1")], w=[P("TOT")], out=TOT[:].rearrange("t j h -> t (j h)"), in_=PS[1][:, 0:256], func=AF.Copy)
    src, srcn = TOT, P("TOT")
    d = 1
    pp = [(SA, P("SA")), (SB_, P("SB"))]
    k = 0
    while d < 64:
        dst, dstn = pp[k % 2]
        S.dve("tensor_tensor", r=[srcn], w=[dstn], out=dst[:, d:, :], in0=src[:, d:, :], in1=src[:, :64 - d, :], op=ALU.add)
        S.dve("tensor_copy", r=[srcn], w=[dstn], out=dst[:, :d, :], in_=src[:, :d, :])
        src, srcn = dst, dstn
        d *= 2
        k += 1
    INCL, INCLn = src, srcn
    S.dve("tensor_tensor", r=[P("WI"), INCLn], w=[P("NC")], out=NC_[:], in0=WI[:], in1=INCL[:], op=ALU.add)
    S.dve("tensor_tensor", r=[P("NC"), P("TOT")], w=[P("NC")], out=NC_[:], in0=TOT[:], in1=NC_[:], op=ALU.subtract)
    I4 = INCL[:].rearrange("t (m r) h -> t m r h", r=4)
    S.dve("tensor_scalar", r=[INCLn, P("oneh")], w=[P("ROWN")], out=ROWN[:], in0=I4[:, :, 0, :], scalar1=oneh[:, 0:1],
          scalar2=None, op0=ALU.mult)
    for r in range(1, 4):
        S.dve("scalar_tensor_tensor", r=[INCLn, P("oneh"), P("ROWN")], w=[P("ROWN")], out=ROWN[:], in0=I4[:, :, r, :],
              scalar=oneh[:, r:r + 1], in1=ROWN[:], op0=ALU.mult, op1=ALU.add)

    units = []
    for m in range(16):
        J = 4 * m + 4
        for j in range(J):
            units.append((m, j, J))

    def emit_s(i, u):
        m, j, J = u
        sp = i % 2
        psn = P("ps%d" % sp)
        if j == 0:
            cb, cbn = CB[m % 2], P("CB%d" % (m % 2))
            for h in range(4):
                S.dve("tensor_scalar", r=[P("NC"), P("ROWN")], w=[cbn + "_%d" % h], out=cb[:, 0:J, h], in0=NC_[:, 0:J, h],
                      scalar1=ROWN[:, m, h:h + 1], scalar2=0.0, op0=ALU.add, op1=ALU.min)
        for pr in range(2):
            S.pe("matmul", r=[P("KF"), P("QF")], w=[psn], out=PS[sp][:, pr * 256:(pr + 1) * 256],
                 lhsT=KF[:, pr, gcol(j):gcol(j) + 128], rhs=QF[:, 2 * pr:2 * pr + 2, m * 128:(m + 1) * 128], start=True, stop=True)

    def emit_rest(i, u):
        m, j, J = u
        sp = i % 2
        psn = P("ps%d" % sp)
        cb, cbn = CB[m % 2], P("CB%d" % (m % 2))
        pt, ptn = PT[i % 3], P("PT%d" % (i % 3))
        pth = [ptn + "_%d" % h for h in range(4)]
        for h in range(4):
            S.act("activation", r=[psn, cbn + "_%d" % h], w=[pth[h]], out=pt[:, h * 128:(h + 1) * 128], in_=PS[sp][:, h * 128:(h + 1) * 128],
                  func=AF.Exp, bias=cb[:, j, h:h + 1], scale=0.125)
        if j >= 4 * m:
            S.dve("tensor_tensor", r=pth + [P("dm4")], w=pth, out=pt[:], in0=pt[:],
                  in1=dm4[:, j - 4 * m, :, :].rearrange("k h q -> k (h q)"), op=ALU.mult)
        for h in range(4):
            S.pe("matmul", r=[pth[h], P("VF")], w=[P("ps2")], out=PS[2][:, h * 65:(h + 1) * 65], lhsT=pt[:, h * 128:(h + 1) * 128],
                 rhs=VF[:, cidx(j), h * 80:h * 80 + 65], start=(j == 0 and h == 0), stop=(j == J - 1 and h == 3))
        if j == J - 1:
            S.act("activation", r=[P("ps2")], w=[P("ofs")], out=ofs[:].rearrange("q h d -> q (h d)"), in_=PS[2][:, 0:260], func=AF.Copy)
            S.dve("reciprocal", r=[P("ofs")], w=[P("rden")], out=rden[:], in_=ofs[:, :, 64])
            for h in range(4):
                S.dve("tensor_scalar", r=[P("ofs"), P("rden")], w=[P("onrm%d" % h)], out=onrm[:, h * 64:(h + 1) * 64], in0=ofs[:, h, 0:64],
                      scalar1=rden[:, h:h + 1], scalar2=None, op0=ALU.mult)
            gp = m % 2
            S.dma(P("gt%d" % gp), w=[P("gtt%d" % gp)], out=gtt[gp][:], in_=io["gt"][m * 128:(m + 1) * 128, 0:256])
            S.dve("tensor_tensor", r=[P("onrm%d" % h) for h in range(4)] + [P("gtt%d" % gp)], w=["MIXB"], out=MIXB[:, m, 0:256], in0=onrm[:],
                  in1=gtt[gp][:], op=ALU.mult)

    for i, u in enumerate(units):
        emit_s(i, u)
        if i > 0:
            emit_rest(i - 1, units[i - 1])
    emit_rest(len(units) - 1, units[-1])


def rank_consts(r):
    k = np.arange(128)[:, None]
    q = np.arange(128)[None, :]
    dmask = np.zeros((128, 4, 128), np.float32)
    for c in range(4):
        if c < r:
            dmask[:, c, :] = 1.0
        elif c == r:
            dmask[:, c, :] = (k <= q)
    onehot = np.zeros((128, 4), np.float32)
    onehot[:, r] = 1.0
    return dict(dmask=dmask.astype(NPBF), onehot=onehot)


def common_consts():
    k = np.arange(128)[:, None]
    q = np.arange(128)[None, :]
    return dict(tri=(k <= q).astype(np.float32), ones128=np.ones((128, 128), np.float32))


def cmp_scope(S, nc, es, io, KCMP, VCA, pfx="c"):
    P = lambda n: pfx + n
    sb = lambda n, s, d: es.enter_context(nc.sbuf_tensor(P(n), s, d))
    PS = [es.enter_context(nc.psum_tensor(P("ps%d" % i), [128, 512], F32)) for i in range(4)]
    KCN = [sb("KCN%d" % i, [128, T], BF16) for i in range(2)]
    W1st = sb("W1st", [128, 32, 256], F32)
    W1Z = [sb("W1Z%d" % g, [128, 32, 256], BF16) for g in range(2)]
    PEf = sb("PEf", [128, 2, 32], F32)
    PEB = sb("PEB", [128, 2, 32], BF16)
    W2f = sb("W2f", [128, 2, 2, 64], F32)
    W2KZ = sb("W2KZ", [128, 2, 2, 128], BF16)
    W2V = sb("W2V", [128, 2, 64], BF16)
    BH = sb("BH", [128, 2], F32)
    HID = [[sb("HID%d%d" % (kv, g), [128, 2, 512], BF16) for g in range(2)] for kv in range(2)]
    X = sb("X", [128, 512], F32)
    X2 = sb("X2", [128, 512], F32)
    U = sb("U", [128, 512], F32)
    SG = sb("SG", [128, 512], F32)
    KC0 = sb("KC0", [128, 512], F32)
    KZ = sb("KZ", [128, 512], BF16)
    RT = sb("RT", [128, 512], F32)
    cosC = sb("cosC", [128, 512], F32)
    sinC = sb("sinC", [128, 512], F32)
    rmf = sb("rmf", [128, 128], F32)
    rmb = sb("rmb", [128, 128], BF16)
    ovf = sb("ovf", [128, 4, 128], BF16)

    ld = P("ld")
    for kv, nm in ((0, "kc_g"), (1, "vc_g")):
        for r in range(4):
            S.dma(ld, batch=True, w=[P("KCN%d" % kv)], out=KCN[kv][:].rearrange("d (m r t) -> d m r t", r=4, t=128)[:, :, r, :],
                  in_=io[nm][:, r * NT:(r + 1) * NT].rearrange("d (m t) -> d m t", t=128))
    S.dma(ld, batch=True, w=[P("PEf")], out=PEf[0:64, :, :], in_=io["peT"])
    S.dma(ld, batch=True, w=[P("PEf")], out=PEf[64:128, :, :], in_=io["peT"])
    S.dma(ld, batch=True, w=[P("W2f")], out=W2f[:], in_=io["w2"])
    S.dma(ld, batch=True, w=[P("cosC")], out=cosC[:], in_=io["cosC"])
    S.dma(ld, batch=True, w=[P("sinC")], out=sinC[:], in_=io["sinC"])
    S.dma(ld, batch=True, w=[P("rmf")], out=rmf[:], in_=io["rmat"])
    S.dma(ld, batch=True, w=[P("ovf")], out=ovf[:], in_=io["ovl"])
    S.dve("tensor_copy", r=[P("PEf")], w=[P("PEB")], out=PEB[:], in_=PEf[:])
    S.dve("tensor_copy", r=[P("rmf")], w=[P("rmb")], out=rmb[:], in_=rmf[:])
    S.pool("memset", w=[P("W2KZ")], ap=W2KZ[:], constant=0.0)
    for g in range(2):
        S.pool("memset", w=[P("W1Z%d" % g)], ap=W1Z[g][:], constant=0.0)
        S.dve("tensor_copy", r=[P("W2f"), P("W2KZ")], w=[P("W2KZ")], out=W2KZ[:, g, :, 64 * g:64 * g + 64], in_=W2f[:, 0, :, :])
        for kv in range(2):
            S.pool("memset", w=[P("HID%d%d" % (kv, g))], ap=HID[kv][g][:], constant=0.0)
    S.dve("tensor_copy", r=[P("W2f")], w=[P("W2V")], out=W2V[:], in_=W2f[:, 1, :, :])
    S.pool("memset", w=[P("KC0")], ap=KC0[:], constant=0.0)
    S.pool("memset", w=["VCA"], ap=VCA[:], constant=1.0)
    for g in range(2):
        S.pool("tensor_copy", r=[P("ovf"), "VCA"], w=["VCA"], out=VCA[:, :, g, 80:208], in_=ovf[:])

    for kv, wn in ((0, "w1k"), (1, "w1v")):
        for half in range(2):
            S.dma(P("w1"), w=[P("W1st")], out=W1st[64 * half:64 * half + 64, :, :],
                  in_=io[wn].rearrange("(l d) h -> d l h", d=64))
        for g in range(2):
            S.dve("tensor_copy", r=[P("W1st"), P("W1Z%d" % g)], w=[P("W1Z%d" % g)], out=W1Z[g][64 * g:64 * g + 64, :, :],
                  in_=W1st[64 * g:64 * g + 64, :, :])
        for hc in range(2):
            for l in range(32):
                S.pe("matmul", r=[P("W1Z0"), P("PEB")], w=[P("ps3")], out=PS[3][:, hc:hc + 1], lhsT=W1Z[0][:, l, hc * 128:(hc + 1) * 128],
                     rhs=PEB[:, kv, l:l + 1], start=(hc == 0 and l == 0), stop=(hc == 1 and l == 31))
        S.act("activation", r=[P("ps3")], w=[P("BH")], out=BH[:], in_=PS[3][:, 0:2], func=AF.Copy)
        for g in range(2):
            for hc in range(2):
                k = (g * 2 + hc) % 3
                psn = P("ps%d" % k)
                for l in range(32):
                    S.pe("matmul", r=[P("W1Z%d" % g), P("KCN%d" % kv)], w=[psn], out=PS[k][:, 0:511],
                         lhsT=W1Z[g][:, l, hc * 128:(hc + 1) * 128], rhs=KCN[kv][:, l:l + 16 * 510 + 1:16], start=(l == 0), stop=(l == 31))
                S.act("activation", r=[psn, P("BH")], w=[P("X")], out=X[:, 0:511], in_=PS[k][:, 0:511], func=AF.Identity,
                      bias=BH[:, hc:hc + 1], scale=1.0)
                S.dve("tensor_tensor", r=[P("X")], w=[P("X2")], out=X2[:, 0:511], in0=X[:, 0:511], in1=X[:, 0:511], op=ALU.mult)
                S.dve("tensor_scalar", r=[P("X2")], w=[P("X2")], out=X2[:, 0:511], in0=X2[:, 0:511], scalar1=0.044715, scalar2=1.0,
                      op0=ALU.mult, op1=ALU.add)
                S.dve("tensor_tensor", r=[P("X2"), P("X")], w=[P("U")], out=U[:, 0:511], in0=X2[:, 0:511], in1=X[:, 0:511], op=ALU.mult)
                S.act("activation", r=[P("U")], w=[P("SG")], out=SG[:, 0:511], in_=U[:, 0:511], func=AF.Sigmoid, scale=1.5957691216057308)
                S.dve("tensor_tensor", r=[P("X"), P("SG")], w=[P("HID%d%d" % (kv, g))], out=HID[kv][g][:, hc, 0:511], in0=X[:, 0:511],
                      in1=SG[:, 0:511], op=ALU.mult)
    n = 0
    for g in range(2):
        for hc in range(2):
            S.pe("matmul", r=[P("W2KZ"), P("HID0%d" % g)], w=[P("ps0")], out=PS[0][:, 0:511], lhsT=W2KZ[:, g, hc, :],
                 rhs=HID[0][g][:, hc, 0:511], start=(n == 0), stop=(n == 3))
            n += 1
    S.act("activation", r=[P("ps0")], w=[P("KC0")], out=KC0[:, 0:511], in_=PS[0][:, 0:511], func=AF.Copy)
    S.dve("tensor_copy", r=[P("KC0")], w=[P("KZ")], out=KZ[:], in_=KC0[:])
    S.pe("matmul", r=[P("rmb"), P("KZ")], w=[P("ps1")], out=PS[1][:], lhsT=rmb[:], rhs=KZ[:], start=True, stop=True)
    S.act("activation", r=[P("ps1")], w=[P("RT")], out=RT[:], in_=PS[1][:], func=AF.Copy)
    S.dve("tensor_tensor", r=[P("KC0"), P("cosC")], w=[P("KC0")], out=KC0[:], in0=KC0[:], in1=cosC[:], op=ALU.mult)
    S.dve("tensor_tensor", r=[P("RT"), P("sinC")], w=[P("RT")], out=RT[:], in0=RT[:], in1=sinC[:], op=ALU.mult)
    S.dve("tensor_tensor", r=[P("KC0"), P("RT")], w=["KCMP"], out=KCMP[:], in0=KC0[:], in1=RT[:], op=ALU.add)
    for c in range(4):
        for g in range(2):
            k = 2 + (c * 2 + g) % 2
            for hc in range(2):
                S.pe("matmul", r=[P("HID1%d" % g), P("W2V")], w=[P("ps%d" % k)], out=PS[k][:, 0:64], lhsT=HID[1][g][:, hc, c * 128:(c + 1) * 128],
                     rhs=W2V[:, hc, :], start=(hc == 0), stop=(hc == 1))
            S.act("activation", r=[P("ps%d" % k), "VCA"], w=["VCA"], out=VCA[:, c, g, 0:64], in_=PS[k][:, 0:64], func=AF.Copy)


def cmp_consts(inputs, layer):
    n = np.arange(512)
    cosC, sinC = rope_tables(16 * n + 31)
    nn = np.arange(512)[:, None]
    bb = np.arange(128)[None, :]
    ovl = ((16 * nn < 64 * bb + 64) & (16 * nn + 31 >= 64 * bb) & (nn < 511)).astype(np.float32)
    peT = np.stack([inputs["cmp_pe_k"][layer].T, inputs["cmp_pe_v"][layer].T], axis=1)
    w2 = np.stack([inputs["cmp_k_w2"][layer].reshape(2, 128, 64), inputs["cmp_v_w2"][layer].reshape(2, 128, 64)], axis=0)
    w2 = np.ascontiguousarray(w2.transpose(2, 0, 1, 3))
    return dict(cosC=cosC, sinC=sinC, ovl=np.ascontiguousarray(ovl.reshape(4, 128, 128).transpose(1, 0, 2)).astype(NPBF),
                peT=np.ascontiguousarray(peT, dtype=np.float32), w2=w2.astype(np.float32),
                w1k=np.ascontiguousarray(inputs["cmp_k_w1"][layer]), w1v=np.ascontiguousarray(inputs["cmp_v_w1"][layer]),
                rmat=rope_rmat())


def nsa_scope(S, nc, es, io, KCMP, VCA, MIXB, pfx="n"):
    P = lambda n: pfx + n
    sb = lambda n, s, d: es.enter_context(nc.sbuf_tensor(P(n), s, d))
    PS = [es.enter_context(nc.psum_tensor(P("ps%d" % i), [128, 512], F32)) for i in range(6)]
    PSTb = es.enter_context(nc.psum_tensor(P("pstb"), [128, 1024], BF16))
    KS = sb("KS", [128, T], BF16)
    VS = sb("VS", [128, 64, 160], BF16)
    EE = sb("EE", [128, T], BF16)
    QNZ = sb("QNZ", [128, 2, 4, NT], BF16)
    KWm = [sb("KWm%d" % i, [128, 8, 128], BF16) for i in range(2)]
    VWm = [sb("VWm%d" % i, [128, 8, 160], BF16) for i in range(2)]
    dm = sb("dm", [128, 4, 128], BF16)
    wm = sb("wm", [128, 8, 128], BF16)
    cm = [sb("cm%d" % i, [128, 4, 128], BF16) for i in range(2)]
    fc = [sb("fc%d" % i, [128, 128], F32) for i in range(2)]
    glt = [sb("glt%d" % i, [128, 24], F32) for i in range(2)]
    gtt = [sb("gtt%d" % i, [128, 512], BF16) for i in range(2)]
    idf = sb("idf", [128, 128], F32)
    idb = sb("idb", [128, 128], BF16)
    PT = [sb("PT%d" % i, [128, 512], BF16) for i in range(3)]
    ocs = sb("ocs", [128, 4, 208], F32)
    oss = sb("oss", [128, 4, 65], F32)
    ows = sb("ows", [128, 4, 65], F32)
    rd = sb("rd", [128, 3, 4], F32)
    fac = sb("fac", [128, 3, 4], F32)
    imp = sb("imp", [128, 128], F32)
    wk = sb("wk", [128, 128], F32)
    mx = sb("mx", [128, 8], F32)
    MBq = sb("MBq", [128, 128], BF16)
    MBT = sb("MBT", [128, 4, 128], BF16)
    acc = sb("acc", [128, 4, 64], F32)
    tmp = sb("tmp", [128, 4, 64], F32)

    def bc(ap, h=4):
        return ap.unsqueeze(1).to_broadcast([128, h, ap.shape[-1]])

    ld = P("ld")
    S.dma(ld, batch=True, w=[P("KS")], out=KS[:], in_=io["ks_g"])
    S.dma(ld, batch=True, w=[P("EE")], out=EE[:], in_=io["ee"])
    for r in range(4):
        S.dma(ld, batch=True, w=[P("VS")], out=VS[:, r * 16:(r + 1) * 16, :],
              in_=io["vs_g"][r * NT:(r + 1) * NT, :].rearrange("(c t) w -> t c w", t=128))
    S.pool("memset", w=[P("QNZ")], ap=QNZ[:].rearrange("p g h t -> p (g h t)"), constant=0.0)
    for hh in range(8):
        g, h = hh // 4, hh % 4
        src = 64 * (hh % 2)
        S.dma(ld, batch=True, w=[P("QNZ")], out=QNZ[64 * g:64 * g + 64, g, h, :], in_=io["qn"][hh // 2, src:src + 64, :])
    S.dma(ld, batch=True, w=[P("dm")], out=dm[:], in_=io["dmask"])
    S.dma(ld, batch=True, w=[P("wm")], out=wm[:], in_=io["wmask"])
    S.dma(ld, batch=True, w=[P("idf")], out=idf[:], in_=io["ident"])
    S.dve("tensor_copy", r=[P("idf")], w=[P("idb")], out=idb[:], in_=idf[:])

    units = []
    load_fns = []

    def add_unit(**kw):
        units.append(kw)

    def emit_s(i, u):
        sp = i % 2
        psn = P("ps%d" % sp)
        for fn in u.get("pre", ()):
            fn()
        S.pe("matmul", r=list(u["rk"]) + [P("QNZ")], w=[psn], out=PS[sp][:], lhsT=u["lhsT"], rhs=u["qrhs"], start=True,
             stop=(u.get("bias_mm") is None))
        if u.get("bias_mm") is not None:
            S.pe("matmul", r=[P("EE"), P("MBT")], w=[psn], out=PS[sp][:], lhsT=u["bias_mm"], rhs=MBT[:].rearrange("b h q -> b (h q)"),
                 start=False, stop=True)

    def emit_rest(i, u):
        sp = i % 2
        psn = P("ps%d" % sp)
        pt, ptn = PT[i % 3], P("PT%d" % (i % 3))
        S.act("activation", r=[psn], w=[ptn], out=pt[:], in_=PS[sp][:], func=AF.Exp, scale=0.125)
        if u.get("mask") is not None:
            mk, mn = u["mask"]
            S.pool("tensor_tensor", r=[ptn, mn], w=[ptn], out=pt[:].rearrange("k (h q) -> k h q", h=4),
                   in0=pt[:].rearrange("k (h q) -> k h q", h=4), in1=bc(mk), op=ALU.mult)
        obank, ocols, first, last = u["obank"], u["ocols"], u["first"], u["last"]
        for h in range(4):
            ob, oc = obank(h), ocols(h)
            S.pe("matmul", r=[ptn] + list(u["vk"]), w=[P("ps%d" % ob)], out=PS[ob][:, oc[0]:oc[1]], lhsT=pt[:, h * 128:(h + 1) * 128],
                 rhs=u["vrhs"], start=(first and (h == 0 or (ob != obank(0) and h == 2))),
                 stop=(last and (h == 3 or (ob != obank(3) and h == 1))))
        for fn in u.get("post", ()):
            fn()

    for m in range(16):
        p = m % 2
        cmn, fcn, gln, gtn, kwn, vwn = P("cm%d" % p), P("fc%d" % p), P("glt%d" % p), P("gtt%d" % p), P("KWm%d" % p), P("VWm%d" % p)

        def loads(m=m, p=p, cmn=cmn, fcn=fcn, gln=gln, gtn=gtn, kwn=kwn, vwn=vwn):
            S.dma(P("cm%d" % p), w=[cmn], out=cm[p][:], in_=io["cmask"][m])
            S.dma(P("fc%d" % p), w=[fcn], out=fc[p][:], in_=io["force"][m])
            S.dma(P("gl%d" % p), w=[gln], out=glt[p][:], in_=io["gl"][m * 128:(m + 1) * 128, :])
            S.dma(P("gt%d" % p), w=[gtn], out=gtt[p][:], in_=io["gt"][m * 128:(m + 1) * 128, 512:1024])
            for mm in ((m - 1, m) if m > 0 else (m,)):
                s0 = 4 * (mm - m + 1)
                S.dma(P("kw%d" % p), w=[kwn], out=KWm[p][:, s0:s0 + 4, :],
                      in_=io["kw_g"].rearrange("d (r c t) -> d r c t", r=4, t=128)[:, :, mm, :])
                S.dma(P("vw%d" % p), w=[vwn], out=VWm[p][:, s0:s0 + 4, :],
                      in_=io["vw_g"].rearrange("(r c t) w -> t r c w", r=4, t=128)[:, :, mm, :])

        load_fns.append(loads)
    load_fns[0]()
    for m in range(16):
        p = m % 2
        cmn, fcn, gln, gtn, kwn, vwn = P("cm%d" % p), P("fc%d" % p), P("glt%d" % p), P("gtt%d" % p), P("KWm%d" % p), P("VWm%d" % p)
        nxt_loads = load_fns[m + 1] if m + 1 < 16 else None
        qs = slice(m * 128, (m + 1) * 128)
        for g in range(2):
            qrhs = QNZ[:, g, :, qs]

            def cmp_post(p=p, fcn=fcn):
                S.act("activation", r=[P("ps2")], w=[P("ocs")], out=ocs[:, 0:2, :].rearrange("q h w -> q (h w)"), in_=PS[2][:, 0:416], func=AF.Copy)
                S.act("activation", r=[P("ps3")], w=[P("ocs")], out=ocs[:, 2:4, :].rearrange("q h w -> q (h w)"), in_=PS[3][:, 0:416], func=AF.Copy)
                S.dve("tensor_scalar", r=[P("ocs")], w=[P("rd0")], out=rd[:, 0, :], in0=ocs[:, :, 64], scalar1=1e-30, scalar2=None, op0=ALU.max)
                S.dve("reciprocal", r=[P("rd0")], w=[P("rd0")], out=rd[:, 0, :], in_=rd[:, 0, :])
                S.dve("tensor_scalar", r=[P("ocs"), P("rd0")], w=[P("imp")], out=imp[:], in0=ocs[:, 0, 80:208], scalar1=rd[:, 0, 0:1],
                      scalar2=None, op0=ALU.mult)
                for h in range(1, 4):
                    S.dve("scalar_tensor_tensor", r=[P("ocs"), P("rd0"), P("imp")], w=[P("imp")], out=imp[:], in0=ocs[:, h, 80:208],
                          scalar=rd[:, 0, h:h + 1], in1=imp[:], op0=ALU.mult, op1=ALU.add)
                S.dve("tensor_tensor", r=[P("imp"), fcn], w=[P("imp")], out=imp[:], in0=imp[:], in1=fc[p][:], op=ALU.max)
                S.dve("max", r=[P("imp")], w=[P("mx")], out=mx[:], in_=imp[:])
                S.dve("match_replace", r=[P("mx"), P("imp")], w=[P("wk")], out=wk[:], in_to_replace=mx[:], in_values=imp[:], imm_value=-1.0)
                S.dve("max", r=[P("wk")], w=[P("mx")], out=mx[:], in_=wk[:])
                S.dve("tensor_scalar", r=[P("imp"), P("mx")], w=[P("wk")], out=wk[:], in0=imp[:], scalar1=mx[:, 7:8], scalar2=None, op0=ALU.is_ge)
                S.dve("tensor_scalar", r=[P("wk")], w=[P("MBq")], out=MBq[:], in0=wk[:], scalar1=-1.0, scalar2=30000.0, op0=ALU.add, op1=ALU.mult)

            def slc_pre():
                S.pe("transpose", r=[P("MBq"), P("idb")], w=[P("pstb")], out=PSTb[:, 0:128], in_=MBq[:], identity=idb[:])
                S.act("activation", r=[P("pstb")], w=[P("MBT")], out=MBT[:], in_=bc(PSTb[:, 0:128]), func=AF.Copy)

            def win_post():
                S.act("activation", r=[P("ps5")], w=[P("ows")], out=ows[:].rearrange("q h w -> q (h w)"), in_=PS[5][:, 0:260], func=AF.Copy)
                S.dve("reciprocal", r=[P("ows")], w=[P("rd2")], out=rd[:, 2, :], in_=ows[:, :, 64])

            def slc_post(m=m, g=g, p=p, gln=gln, gtn=gtn):
                S.act("activation", r=[P("ps4")], w=[P("oss")], out=oss[:].rearrange("q h w -> q (h w)"), in_=PS[4][:, 0:260], func=AF.Copy)
                S.dve("reciprocal", r=[P("oss")], w=[P("rd1")], out=rd[:, 1, :], in_=oss[:, :, 64])
                S.dve("tensor_tensor", r=[P("rd0"), P("rd1"), P("rd2"), gln], w=[P("fac")], out=fac[:], in0=rd[:],
                      in1=glt[p][:, g * 12:g * 12 + 12].rearrange("q (h b) -> q b h", b=3), op=ALU.mult)
                srcs = ((ocs, P("ocs")), (oss, P("oss")), (ows, P("ows")))
                for br in range(3):
                    o_t, o_n = srcs[br]
                    dst, dstn = (acc, P("acc")) if br == 0 else (tmp, P("tmp"))
                    S.dve("tensor_tensor", r=[o_n, P("fac")], w=[dstn], out=dst[:], in0=o_t[:, :, 0:64],
                          in1=fac[:, br, :].unsqueeze(2).to_broadcast([128, 4, 64]), op=ALU.mult)
                    if br > 0:
                        S.dve("tensor_tensor", r=[P("acc"), P("tmp")], w=[P("acc")], out=acc[:], in0=acc[:], in1=tmp[:], op=ALU.add)
                S.dve("tensor_tensor", r=[P("acc"), gtn], w=["MIXB"], out=MIXB[:, m, 512 + g * 256:512 + g * 256 + 256],
                      in0=acc[:].rearrange("q h d -> q (h d)"), in1=gtt[p][:, g * 256:g * 256 + 256], op=ALU.mult)

            for c in range(4):
                add_unit(lhsT=KCMP[:, c * 128:(c + 1) * 128], rk=["KCMP"], qrhs=qrhs, mask=(cm[p][:, c, :], cmn), vrhs=VCA[:, c, g, :], vk=["VCA"],
                         obank=(lambda h: 2 + h // 2), ocols=(lambda h: ((h % 2) * 208, (h % 2) * 208 + 208)), first=(c == 0), last=(c == 3),
                         post=(([nxt_loads] if (g == 0 and c == 0 and nxt_loads is not None) else []) + ([cmp_post] if c == 3 else [])))
            cl = list(range(8)) if m > 0 else list(range(4, 8))
            for c in cl:
                add_unit(lhsT=KWm[p][:, c, :], rk=[kwn], qrhs=qrhs, mask=(wm[:, c, :], P("wm")), vrhs=VWm[p][:, c, g * 80:g * 80 + 65], vk=[vwn],
                         obank=(lambda h: 5), ocols=(lambda h: (h * 65, h * 65 + 65)), first=(c == cl[0]), last=(c == cl[-1]),
                         post=([win_post] if c == cl[-1] else []))
            J = 4 * m + 4
            for j in range(J):
                add_unit(lhsT=KS[:, gcol(j):gcol(j) + 128], rk=[P("KS")], qrhs=qrhs,
                         mask=((dm[:, j - 4 * m, :], P("dm")) if j >= 4 * m else None), vrhs=VS[:, cidx(j), g * 80:g * 80 + 65], vk=[P("VS")],
                         obank=(lambda h: 4), ocols=(lambda h: (h * 65, h * 65 + 65)), first=(j == 0), last=(j == J - 1),
                         bias_mm=EE[:, gcol(j):gcol(j) + 128], pre=([slc_pre] if j == 0 else []), post=([slc_post] if j == J - 1 else []))

    for i, u in enumerate(units):
        emit_s(i, u)
        if i > 0:
            emit_rest(i - 1, units[i - 1])
    emit_rest(len(units) - 1, units[-1])


def nsa_rank_consts(r):
    k = np.arange(128)[:, None]
    q = np.arange(128)[None, :]
    wmask = np.zeros((128, 8, 128), np.float32)
    for c in range(8):
        d = (4 + r) - c
        if d == 4:
            wmask[:, c, :] = (k > q)
        elif 1 <= d <= 3:
            wmask[:, c, :] = 1.0
        elif d == 0:
            wmask[:, c, :] = (k <= q)
    cmask = np.zeros((16, 128, 4, 128), np.float32)
    force = np.zeros((16, 128, 128), np.float32)
    nl = np.arange(128)[:, None]
    b = np.arange(128)[None, :]
    for m in range(16):
        tq = (4 * m + r) * 128 + np.arange(128)
        for c in range(4):
            n = 128 * c + nl
            cmask[m, :, c, :] = ((16 * n + 31) <= tq[None, :]) & (n < 511)
        cur = (tq // 64)[:, None]
        force[m] = 1e4 * (b == 0) + 2e4 * (b == cur) + 3e4 * (b == cur - 1)
    return dict(wmask=wmask.astype(NPBF), cmask=cmask.astype(NPBF), force=force.astype(np.float32))


def ee_const():
    ee = np.zeros((128, T), np.float32)
    for j in range(64):
        for half in range(2):
            ee[2 * j + half, gcol(j) + 64 * half: gcol(j) + 64 * half + 64] = 1.0
    return ee.astype(NPBF)


def conv_scope(S, nc, es, io, MIXB, pfx="v"):
    P = lambda n: pfx + n
    sb = lambda n, s, d: es.enter_context(nc.sbuf_tensor(P(n), s, d))
    PS = [es.enter_context(nc.psum_tensor(P("ps%d" % i), [128, 512], F32)) for i in range(4)]
    YB = sb("YB", [128, 2, NT], BF16)
    TLB = sb("TLB", [128, 2, 4, 16, 30], BF16)
    YE = sb("YE", [128, 2, 16, 158], BF16)
    DIAG = sb("DIAG", [128, 2, 31, 128], BF16)
    idf = sb("idf", [128, 128], F32)
    accA = sb("accA", [128, 2, 16, 128], F32)
    SQ = sb("SQ", [128, 2, NT], F32)
    MEAN = sb("MEAN", [128, NT], F32)
    MSQ = sb("MSQ", [128, NT], F32)
    CW = sb("CW", [128, 2, 31], F32)
    CBs = sb("CBs", [128, 2], F32)
    LG = sb("LG", [128, 2], F32)
    LB = sb("LB", [128, 2], F32)
    ohp = sb("ohp", [128, 4], F32)
    onesd = sb("onesd", [128, 128], F32)
    WPf = sb("WPf", [128, 2, 256], F32)
    WPW = sb("WPW", [128, 2, 256], BF16)
    ACTT = sb("ACTT", [128, 2, NT], BF16)
    ob = [sb("ob%d" % i, [128, 256], F32) for i in range(2)]
    gtt = [sb("gtt%d" % i, [128, 256], BF16) for i in range(2)]

    ld = P("ld")
    for cc in range(2):
        S.dma(ld, batch=True, w=[P("YB")], out=YB[:, cc, :], in_=io["yT"][cc])
        S.dma(ld, batch=True, w=[P("WPf")], out=WPf[:, cc, :], in_=io["conv_pw"][cc * 128:(cc + 1) * 128, :])
        for r in range(4):
            S.dma(ld, batch=True, w=[P("TLB")], out=TLB[:, cc, r, :, :], in_=io["tails_g"][r, cc])
    S.dma(ld, batch=True, w=[P("CW")], out=CW[:], in_=io["conv_w"])
    S.dma(ld, batch=True, w=[P("CBs")], out=CBs[:], in_=io["conv_b"])
    S.dma(ld, batch=True, w=[P("LG")], out=LG[:], in_=io["ln_g"])
    S.dma(ld, batch=True, w=[P("LB")], out=LB[:], in_=io["ln_b"])
    S.dma(ld, batch=True, w=[P("ohp")], out=ohp[:], in_=io["ohprev"])
    S.dma(ld, batch=True, w=[P("idf")], out=idf[:], in_=io["ident"])
    S.dve("memset", w=[P("onesd")], ap=onesd[:], constant=1.0 / 256.0)
    S.dve("tensor_copy", r=[P("WPf")], w=[P("WPW")], out=WPW[:], in_=WPf[:])
    for cc in range(2):
        S.act("activation", r=[P("YB")], w=[P("YE")], out=YE[:, cc, :, 30:158], in_=YB[:, cc, :].rearrange("p (m t) -> p m t", t=128), func=AF.Copy)
        S.dve("tensor_scalar", r=[P("TLB"), P("ohp")], w=[P("YE")], out=YE[:, cc, :, 0:30], in0=TLB[:, cc, 0, :, :], scalar1=ohp[:, 0:1],
              scalar2=None, op0=ALU.mult)
        for r in (1, 2):
            S.dve("scalar_tensor_tensor", r=[P("TLB"), P("ohp"), P("YE")], w=[P("YE")], out=YE[:, cc, :, 0:30], in0=TLB[:, cc, r, :, :],
                  scalar=ohp[:, r:r + 1], in1=YE[:, cc, :, 0:30], op0=ALU.mult, op1=ALU.add)
        S.dve("scalar_tensor_tensor", r=[P("TLB"), P("ohp"), P("YE")], w=[P("YE")], out=YE[:, cc, 1:16, 0:30], in0=TLB[:, cc, 3, 0:15, :],
              scalar=ohp[:, 3:4], in1=YE[:, cc, 1:16, 0:30], op0=ALU.mult, op1=ALU.add)
    for cc in range(2):
        for tp in range(31):
            (S.dve if tp % 2 == 0 else S.pool)("tensor_scalar", r=[P("idf"), P("CW")], w=[P("DIAG%d_%d" % (cc, tp))], out=DIAG[:, cc, tp, :],
                                               in0=idf[:], scalar1=CW[:, cc, tp:tp + 1], scalar2=None, op0=ALU.mult)
    for cc in range(2):
        an = P("accA%d" % cc)
        for pc in range(4):
            k = (cc * 4 + pc) % 2
            for tp in range(31):
                S.pe("matmul", r=[P("DIAG%d_%d" % (cc, tp)), P("YE")], w=[P("ps%d" % k)], out=PS[k][:], lhsT=DIAG[:, cc, tp, :],
                     rhs=YE[:, cc, 4 * pc:4 * pc + 4, tp:tp + 128], start=(tp == 0), stop=(tp == 30))
            S.act("activation", r=[P("ps%d" % k), P("CBs")], w=[an], out=accA[:, cc, 4 * pc:4 * pc + 4, :].rearrange("p m t -> p (m t)"),
                  in_=PS[k][:], func=AF.Identity, bias=CBs[:, cc:cc + 1], scale=1.0)
        S.act("activation", r=[an], w=[P("SQ")], out=SQ[:, cc, :], in_=accA[:, cc].rearrange("p m t -> p (m t)"), func=AF.Square)
    for pc in range(4):
        cs = slice(pc * 512, (pc + 1) * 512)
        for cc in range(2):
            S.pe("matmul", r=[P("onesd"), P("accA%d" % cc)], w=[P("ps0")], out=PS[0][:], lhsT=onesd[:],
                 rhs=accA[:, cc].rearrange("p m t -> p (m t)")[:, cs], start=(cc == 0), stop=(cc == 1))
        for cc in range(2):
            S.pe("matmul", r=[P("onesd"), P("SQ")], w=[P("ps1")], out=PS[1][:], lhsT=onesd[:], rhs=SQ[:, cc, cs], start=(cc == 0), stop=(cc == 1))
        S.act("activation", r=[P("ps0")], w=[P("MEAN")], out=MEAN[:, cs], in_=PS[0][:], func=AF.Copy)
        S.act("activation", r=[P("ps1")], w=[P("MSQ")], out=MSQ[:, cs], in_=PS[1][:], func=AF.Copy)
    S.dve("tensor_tensor", r=[P("MEAN")], w=[P("SQ")], out=SQ[:, 0, :], in0=MEAN[:], in1=MEAN[:], op=ALU.mult)
    S.dve("tensor_tensor", r=[P("MSQ"), P("SQ")], w=[P("MSQ")], out=MSQ[:], in0=MSQ[:], in1=SQ[:, 0, :], op=ALU.subtract)
    S.dve("tensor_scalar", r=[P("MSQ")], w=[P("MSQ")], out=MSQ[:], in0=MSQ[:], scalar1=1e-6, scalar2=None, op0=ALU.add)
    S.act("activation", r=[P("MSQ")], w=[P("MSQ")], out=MSQ[:], in_=MSQ[:], func=AF.Sqrt)
    S.dve("reciprocal", r=[P("MSQ")], w=[P("MSQ")], out=MSQ[:], in_=MSQ[:])
    for cc in range(2):
        an = P("accA%d" % cc)
        a2 = accA[:, cc].rearrange("p m t -> p (m t)")
        S.dve("tensor_tensor", r=[an, P("MEAN")], w=[an], out=a2, in0=a2, in1=MEAN[:], op=ALU.subtract)
        S.dve("tensor_tensor", r=[an, P("MSQ")], w=[an], out=a2, in0=a2, in1=MSQ[:], op=ALU.mult)
        S.act("activation", r=[an, P("LG"), P("LB")], w=[P("ACTT")], out=ACTT[:, cc, :], in_=a2, func=AF.Silu, scale=LG[:, cc:cc + 1], bias=LB[:, cc:cc + 1])
    for m in range(16):
        p = m % 2
        k = 2 + p
        S.dma(P("gt%d" % p), w=[P("gtt%d" % p)], out=gtt[p][:], in_=io["gt"][m * 128:(m + 1) * 128, 256:512])
        for cc in range(2):
            S.pe("matmul", r=[P("ACTT"), P("WPW")], w=[P("ps%d" % k)], out=PS[k][:, 0:256], lhsT=ACTT[:, cc, m * 128:(m + 1) * 128], rhs=WPW[:, cc, :],
                 start=(cc == 0), stop=(cc == 1))
        S.act("activation", r=[P("ps%d" % k)], w=[P("ob%d" % p)], out=ob[p][:], in_=PS[k][:, 0:256], func=AF.Copy)
        S.dve("tensor_tensor", r=[P("ob%d" % p), P("gtt%d" % p)], w=["MIXB"], out=MIXB[:, m, 256:512], in0=ob[p][:], in1=gtt[p][:], op=ALU.mult)


def conv_consts(inputs, layer, r):
    cw = inputs["conv_w"][layer]
    col = lambda v: np.ascontiguousarray(v.reshape(2, 128).T, dtype=np.float32)
    ohp = np.zeros((128, 4), np.float32)
    ohp[:, (r - 1) % 4] = 1.0
    return dict(conv_w=np.ascontiguousarray(cw.T.reshape(2, 128, 31).transpose(1, 0, 2), dtype=np.float32),
                conv_b=col(inputs["conv_b"][layer]), ln_g=col(inputs["conv_ln_g"][layer]), ln_b=col(inputs["conv_ln_b"][layer]),
                conv_pw=np.ascontiguousarray(inputs["conv_pw"][layer]), ohprev=ohp)
```

```python
import numpy as np
import ml_dtypes
from contextlib import ExitStack
import concourse.bass as bass
import concourse.mybir as mybir
from concourse.bass_utils import run_bass_kernel_spmd

F32 = mybir.dt.float32
BF16 = mybir.dt.bfloat16
AF = mybir.ActivationFunctionType
ALU = mybir.AluOpType
AX = mybir.AxisListType
NPBF = ml_dtypes.bfloat16

ENGINES = ("pe", "act", "dve", "pool", "sp")


class Buf:
    __slots__ = ("name", "last_w", "readers")

    def __init__(self, name):
        self.name = name
        self.last_w = None
        self.readers = []


class Op:
    __slots__ = ("eng", "fn", "reads", "writes", "dma_key", "deps", "need_inc", "tok", "idx")

    def __init__(self, eng, fn, reads, writes, dma_key):
        self.eng = eng
        self.fn = fn
        self.reads = reads
        self.writes = writes
        self.dma_key = dma_key
        self.deps = []
        self.need_inc = False
        self.tok = None


class Sched:
    def __init__(self, nc, es):
        self.nc = nc
        self.es = es
        self.ops = []
        self.bufs = {}
        self.batch = set()
        self.cnt = {}
        self.sems = {}
        self.waited = {e: {} for e in ENGINES}

    def buf(self, name):
        b = self.bufs.get(name)
        if b is None:
            b = Buf(name)
            self.bufs[name] = b
        return b

    def _norm(self, lst):
        return [self.buf(b) if isinstance(b, str) else b for b in (lst or ())]

    def add(self, eng, meth, kw, reads=(), writes=(), dma_key=None):
        op = Op(eng, (meth, kw), self._norm(reads), self._norm(writes), dma_key)
        op.idx = len(self.ops)
        self.ops.append(op)
        return op

    def pe(self, meth, r=(), w=(), **kw):
        return self.add("pe", meth, kw, r, w)

    def act(self, meth, r=(), w=(), **kw):
        return self.add("act", meth, kw, r, w)

    def dve(self, meth, r=(), w=(), **kw):
        return self.add("dve", meth, kw, r, w)

    def pool(self, meth, r=(), w=(), **kw):
        return self.add("pool", meth, kw, r, w)

    def dma(self, key, r=(), w=(), eng="sp", batch=False, **kw):
        if batch:
            self.batch.add(key)
        return self.add(eng, "dma_start", kw, r, w, dma_key=key)

    def analyze(self):
        for op in self.ops:
            deps = set()
            for b in op.reads:
                if b.last_w is not None:
                    deps.add(b.last_w)
            for b in op.writes:
                if b.last_w is not None:
                    deps.add(b.last_w)
                for r in b.readers:
                    deps.add(r)
            deps.discard(op.idx)
            keep = []
            for d in deps:
                dop = self.ops[d]
                if dop.dma_key is not None and dop.dma_key == op.dma_key and dop.dma_key in self.batch:
                    continue
                if dop.dma_key is None and dop.eng == op.eng:
                    if op.eng == "pe" or op.dma_key is not None:
                        continue
                keep.append(d)
            op.deps = sorted(keep)
            for d in op.deps:
                self.ops[d].need_inc = True
            for b in op.reads:
                b.readers.append(op.idx)
            for b in op.writes:
                b.last_w = op.idx
                b.readers = []

    def flush(self):
        nc = self.nc
        self.analyze()
        per = {e: [op for op in self.ops if op.eng == e] for e in ENGINES}
        for e in ENGINES:
            comp = [op for op in per[e] if op.dma_key is None]
            if comp:
                comp[-1].need_inc = True
        cnt = self.cnt
        for op in self.ops:
            if op.dma_key is not None:
                k = ("dma", op.dma_key)
                cnt[k] = cnt.get(k, 0) + 16
                op.tok = (k, cnt[k])
            elif op.need_inc:
                k = ("eng", op.eng)
                cnt[k] = cnt.get(k, 0) + 1
                op.tok = (k, cnt[k])
        for op in self.ops:
            if op.dma_key is not None and op.dma_key in self.batch:
                op.tok = (op.tok[0], cnt[op.tok[0]])
        for k in sorted(cnt.keys()):
            if k not in self.sems:
                self.sems[k] = self.es.enter_context(nc.semaphore("s_%s_%s" % k))
        sems = self.sems
        ops = self.ops
        totals = dict(cnt)

        def run(eng_name, h):
            waited = self.waited[eng_name]
            for op in per[eng_name]:
                for d in op.deps:
                    k, v = ops[d].tok
                    if waited.get(k, 0) >= v:
                        continue
                    h.wait_ge(sems[k], v)
                    waited[k] = v
                ins = getattr(h, op.fn[0])(**op.fn[1])
                if op.tok is not None:
                    ins.then_inc(sems[op.tok[0]], 16 if op.dma_key is not None else 1)
            for k in sorted(totals.keys()):
                if waited.get(k, 0) < totals[k]:
                    h.wait_ge(sems[k], totals[k])
                    waited[k] = totals[k]

        with nc.Block() as block:
            block.sync(lambda h: run("sp", h))
            block.tensor(lambda h: run("pe", h))
            block.scalar(lambda h: run("act", h))
            block.vector(lambda h: run("dve", h))
            block.gpsimd(lambda h: run("pool", h))
        self.ops = []
        self.bufs = {}


D = 1024
DIN = 3612
NT = 2048
T = 8192
OFF = dict(fq=0, fk=256, fv=512, ff=768, fg=772, glu=1028, cg=1540, nq=1796, nkc=2308, nvc=2436,
           nks=2564, nvs=2692, nkw=2820, nvw=2948, ngl=3076, ng=3100)

A_OUTS = dict(qf=([2, 128, NT], BF16), kf=([2, 128, NT], BF16), yT=([2, 128, NT], BF16), qn=([4, 128, NT], BF16),
              kc=([128, NT], BF16), vc=([128, NT], BF16), ks=([128, NT], BF16), kw=([128, NT], BF16),
              vf=([NT, 320], BF16), vs=([NT, 160], BF16), vw=([NT, 160], BF16), lf=([NT, 4], F32),
              gt=([NT, D], BF16), gl=([NT, 24], F32))


def phase_a(S, nc, es, io, pfx="a"):
    P = lambda n: pfx + n
    sb = lambda n, s, d: es.enter_context(nc.sbuf_tensor(P(n), s, d))
    PS0b = es.enter_context(nc.psum_tensor(P("ps0"), [128, 1024], BF16))
    PS = [None] + [es.enter_context(nc.psum_tensor(P("ps%d" % i), [128, 512], F32)) for i in range(1, 8)]
    Wb = sb("Wb", [128, 8, DIN], BF16)
    wst = [sb("wst%d" % i, [128, DIN], F32) for i in range(2)]
    gcol = sb("gcol", [128, 8], F32)
    fb = sb("fb", [128, 4], F32)
    cosT = sb("cosT", [128, NT], F32)
    sinT = sb("sinT", [128, NT], F32)
    rmat_f = sb("rmat_f", [128, 128], F32)
    rmat = sb("rmat", [128, 128], BF16)
    idf = sb("idf", [128, 128], F32)
    idb = sb("idb", [128, 128], BF16)
    xt = [sb("xt%d" % i, [128, D], F32) for i in range(2)]
    sq = sb("sq", [128, D], BF16)
    ss = [sb("ss%d" % i, [128, 1], F32) for i in range(2)]
    hb = [sb("hb%d" % i, [128, D], BF16) for i in range(2)]
    hT = [sb("hT%d" % i, [128, 8, 512], BF16) for i in range(2)]
    fo = [sb("fo%d" % i, [128, 512], BF16) for i in range(4)]
    zc = [sb("zc%d" % i, [128, 512], BF16) for i in range(2)]
    t1 = [sb("t1%d" % i, [128, 512], F32) for i in range(2)]
    t2 = [sb("t2%d" % i, [128, 512], F32) for i in range(2)]
    sg = [sb("sg%d" % i, [128, 512], F32) for i in range(2)]
    vfo = [sb("vfo%d" % i, [128, 4, 80], BF16) for i in range(2)]
    vso = [sb("vso%d" % i, [128, 2, 80], BF16) for i in range(2)]
    vwo = [sb("vwo%d" % i, [128, 2, 80], BF16) for i in range(2)]
    lfo = [sb("lfo%d" % i, [128, 4], F32) for i in range(2)]
    lft = [sb("lft%d" % i, [128, 4], F32) for i in range(2)]
    glo = [sb("glo%d" % i, [128, 24], F32) for i in range(2)]
    gto = [sb("gto%d" % i, [128, D], BF16) for i in range(2)]
    out_keys = set()

    def store(src_name, dst, src):
        k = P("k_" + src_name)
        out_keys.add(k)
        S.dma(k, r=[P(src_name)], out=dst, in_=src)

    S.dma(P("c"), batch=True, w=[P("gcol")], out=gcol[:], in_=io["norm_g"])
    S.dma(P("c"), batch=True, w=[P("fb")], out=fb[:], in_=io["fox_b"].partition_broadcast(128))
    S.dma(P("c"), batch=True, w=[P("cos")], out=cosT[:], in_=io["cos"])
    S.dma(P("c"), batch=True, w=[P("sin")], out=sinT[:], in_=io["sin"])
    S.dma(P("c"), batch=True, w=[P("rmf")], out=rmat_f[:], in_=io["rmat"])
    S.dma(P("c"), batch=True, w=[P("idf")], out=idf[:], in_=io["ident"])
    S.dve("tensor_copy", r=[P("rmf")], w=[P("rmat")], out=rmat[:], in_=rmat_f[:])
    S.dve("tensor_copy", r=[P("idf")], w=[P("idb")], out=idb[:], in_=idf[:])
    for i in range(2):
        S.dve("memset", w=[P("vfo%d" % i)], ap=vfo[i][:], constant=1.0)
        S.dve("memset", w=[P("vso%d" % i)], ap=vso[i][:], constant=1.0)
        S.dve("memset", w=[P("vwo%d" % i)], ap=vwo[i][:], constant=1.0)
    for c in range(8):
        w = wst[c % 2]
        wn = P("wst%d" % (c % 2))
        S.dma(P("w%d" % (c % 2)), w=[wn], out=w[:], in_=io["w_in"][c * 128:(c + 1) * 128, :])
        S.act("activation", r=[wn, P("gcol")], w=[P("Wb%d" % c)], out=Wb[:, c, :], in_=w[:], func=AF.Copy, scale=gcol[:, c:c + 1])

    psrr = [1]

    def next_ps():
        k = psrr[0]
        psrr[0] = 1 + (psrr[0] % 7)
        return k

    foi = [0]
    ri = [0]
    for s in range(4):
        hTs = hT[s % 2]
        hTn = P("hT%d" % (s % 2))
        for q in range(4):
            tn = s * 4 + q
            p = tn % 2
            x_t = xt[p]
            xn = P("xt%d" % p)
            ssn = P("ss%d" % p)
            S.dma(P("x%d" % p), w=[xn], out=x_t[:], in_=io["x"][tn * 128:(tn + 1) * 128, :])
            S.act("activation", r=[xn], w=[P("sq"), ssn], out=sq[:], in_=x_t[:], func=AF.Square, accum_out=ss[p][:, 0:1])
            S.dve("tensor_scalar", r=[ssn], w=[ssn], out=ss[p][:], in0=ss[p][:], scalar1=1.0 / D, scalar2=1e-6,
                  op0=ALU.mult, op1=ALU.add)
            S.act("activation", r=[ssn], w=[ssn], out=ss[p][:], in_=ss[p][:], func=AF.Sqrt)
            S.dve("reciprocal", r=[ssn], w=[ssn], out=ss[p][:], in_=ss[p][:])
            S.dve("tensor_scalar", r=[xn, ssn], w=[P("hb%d" % p)], out=hb[p][:], in0=x_t[:], scalar1=ss[p][:, 0:1],
                  scalar2=None, op0=ALU.mult)
            for half in range(2):
                for cc in range(4):
                    c = half * 4 + cc
                    S.pe("transpose", r=[P("hb%d" % p), P("idb")], w=[P("ps0")], out=PS0b[:, cc * 128:(cc + 1) * 128],
                         in_=hb[p][:, c * 128:(c + 1) * 128], identity=idb[:])
                S.act("activation", r=[P("ps0")], w=[hTn], out=hTs[:, half * 4:(half + 1) * 4, q * 128:(q + 1) * 128],
                      in_=PS0b[:, 0:512].rearrange("p (c t) -> p c t", c=4), func=AF.Copy)
        cols = slice(s * 512, (s + 1) * 512)

        def fm(coff, k):
            for c in range(8):
                S.pe("matmul", r=[P("Wb%d" % c), hTn], w=[P("ps%d" % k)], out=PS[k][:, :], lhsT=Wb[:, c, coff:coff + 128],
                     rhs=hTs[:, c, :], start=(c == 0), stop=(c == 7))

        def plain_out(coff, dst):
            k = next_ps()
            fm(coff, k)
            i = foi[0] % 4
            foi[0] += 1
            S.act("activation", r=[P("ps%d" % k)], w=[P("fo%d" % i)], out=fo[i][:], in_=PS[k][:], func=AF.Copy)
            store("fo%d" % i, dst, fo[i][:])

        def rope_out(coff, dst):
            k = next_ps()
            fm(coff, k)
            k2 = next_ps()
            j = ri[0] % 2
            ri[0] += 1
            i = foi[0] % 4
            foi[0] += 1
            S.act("activation", r=[P("ps%d" % k)], w=[P("zc%d" % j)], out=zc[j][:], in_=PS[k][:], func=AF.Copy)
            S.pe("matmul", r=[P("rmat"), P("zc%d" % j)], w=[P("ps%d" % k2)], out=PS[k2][:], lhsT=rmat[:], rhs=zc[j][:],
                 start=True, stop=True)
            S.act("activation", r=[P("ps%d" % k)], w=[P("t1%d" % j)], out=t1[j][:], in_=PS[k][:], func=AF.Copy)
            S.act("activation", r=[P("ps%d" % k2)], w=[P("t2%d" % j)], out=t2[j][:], in_=PS[k2][:], func=AF.Copy)
            S.dve("tensor_tensor", r=[P("t1%d" % j), P("cos")], w=[P("t1%d" % j)], out=t1[j][:], in0=t1[j][:],
                  in1=cosT[:, cols], op=ALU.mult)
            S.pool("tensor_tensor", r=[P("t2%d" % j), P("sin")], w=[P("t2%d" % j)], out=t2[j][:], in0=t2[j][:],
                   in1=sinT[:, cols], op=ALU.mult)
            S.dve("tensor_tensor", r=[P("t1%d" % j), P("t2%d" % j)], w=[P("fo%d" % i)], out=fo[i][:], in0=t1[j][:],
                   in1=t2[j][:], op=ALU.add)
            store("fo%d" % i, dst, fo[i][:])

        for h2 in range(2):
            plain_out(OFF["fq"] + 128 * h2, io["qf"][h2, :, cols])
            plain_out(OFF["fk"] + 128 * h2, io["kf"][h2, :, cols])
        for h2 in range(2):
            ka = next_ps()
            fm(OFF["glu"] + 128 * h2, ka)
            kb = next_ps()
            fm(OFF["glu"] + 256 + 128 * h2, kb)
            i = foi[0] % 4
            foi[0] += 1
            S.act("activation", r=[P("ps%d" % kb)], w=[P("sg%d" % h2)], out=sg[h2][:], in_=PS[kb][:], func=AF.Sigmoid)
            S.act("activation", r=[P("ps%d" % ka)], w=[P("t1%d" % h2)], out=t1[h2][:], in_=PS[ka][:], func=AF.Copy)
            S.dve("tensor_tensor", r=[P("t1%d" % h2), P("sg%d" % h2)], w=[P("fo%d" % i)], out=fo[i][:], in0=t1[h2][:],
                  in1=sg[h2][:], op=ALU.mult)
            store("fo%d" % i, io["yT"][h2, :, cols], fo[i][:])
        for c4 in range(4):
            rope_out(OFF["nq"] + 128 * c4, io["qn"][c4, :, cols])
        plain_out(OFF["nkc"], io["kc"][:, cols])
        plain_out(OFF["nvc"], io["vc"][:, cols])
        rope_out(OFF["nks"], io["ks"][:, cols])
        rope_out(OFF["nkw"], io["kw"][:, cols])

        for q in range(4):
            tn = s * 4 + q
            p = tn % 2
            rows = slice(tn * 128, (tn + 1) * 128)

            def tm(c0, c1, k):
                for c in range(8):
                    S.pe("matmul", r=[P("Wb%d" % c), hTn], w=[P("ps%d" % k)], out=PS[k][:, 0:c1 - c0],
                         lhsT=hTs[:, c, q * 128:(q + 1) * 128], rhs=Wb[:, c, c0:c1], start=(c == 0), stop=(c == 7))

            k = next_ps()
            tm(512, 772, k)
            S.act("activation", r=[P("ps%d" % k)], w=[P("vfo%d" % p)], out=vfo[p][:, :, 0:64],
                  in_=PS[k][:, 0:256].rearrange("p (h d) -> p h d", h=4), func=AF.Copy)
            store("vfo%d" % p, io["vf"][rows, :], vfo[p][:].rearrange("p h d -> p (h d)"))
            lt = P("lft%d" % p)
            S.act("activation", r=[P("ps%d" % k)], w=[lt], out=lft[p][:], in_=PS[k][:, 256:260], func=AF.Copy)
            S.dve("tensor_tensor", r=[lt, P("fb")], w=[lt], out=lft[p][:], in0=lft[p][:], in1=fb[:], op=ALU.add)
            S.act("activation", r=[lt], w=[lt], out=lft[p][:], in_=lft[p][:], func=AF.Exp, scale=-1.0)
            S.act("activation", r=[lt], w=[lt], out=lft[p][:], in_=lft[p][:], func=AF.Ln, bias=1.0, scale=1.0)
            S.dve("tensor_scalar", r=[lt], w=[P("lfo%d" % p)], out=lfo[p][:], in0=lft[p][:], scalar1=-1.0, scalar2=None, op0=ALU.mult)
            store("lfo%d" % p, io["lf"][rows, :], lfo[p][:])
            for (c0, g0) in ((772, 0), (1540, 256)):
                k = next_ps()
                tm(c0, c0 + 256, k)
                S.act("activation", r=[P("ps%d" % k)], w=[P("gto%d" % p)], out=gto[p][:, g0:g0 + 256], in_=PS[k][:, 0:256], func=AF.Silu)
            k = next_ps()
            tm(3100, 3612, k)
            S.act("activation", r=[P("ps%d" % k)], w=[P("gto%d" % p)], out=gto[p][:, 512:1024], in_=PS[k][:, 0:512], func=AF.Silu)
            store("gto%d" % p, io["gt"][rows, :], gto[p][:])
            k = next_ps()
            tm(2692, 2820, k)
            S.act("activation", r=[P("ps%d" % k)], w=[P("vso%d" % p)], out=vso[p][:, :, 0:64],
                  in_=PS[k][:, 0:128].rearrange("p (h d) -> p h d", h=2), func=AF.Copy)
            store("vso%d" % p, io["vs"][rows, :], vso[p][:].rearrange("p h d -> p (h d)"))
            k = next_ps()
            tm(2948, 3100, k)
            S.act("activation", r=[P("ps%d" % k)], w=[P("vwo%d" % p)], out=vwo[p][:, :, 0:64],
                  in_=PS[k][:, 0:128].rearrange("p (h d) -> p h d", h=2), func=AF.Copy)
            store("vwo%d" % p, io["vw"][rows, :], vwo[p][:].rearrange("p h d -> p (h d)"))
            S.act("activation", r=[P("ps%d" % k)], w=[P("glo%d" % p)], out=glo[p][:], in_=PS[k][:, 128:152], func=AF.Sigmoid)
            store("glo%d" % p, io["gl"][rows, :], glo[p][:])
    return sorted(out_keys)


def own_positions(r):
    m = np.arange(16)[:, None]
    tl = np.arange(128)[None, :]
    return ((4 * m + r) * 128 + tl).reshape(-1)


def rope_tables(pos):
    inv = 500000.0 ** (-np.arange(0, 16, 2, dtype=np.float32) / 16.0)
    ang = pos.astype(np.float32)[None, :] * np.tile(inv, 2)[:, None].astype(np.float32)
    cos = np.ones((64, pos.size), np.float32)
    sin = np.zeros((64, pos.size), np.float32)
    cos[:16] = np.cos(ang)
    sin[:16] = np.sin(ang)
    return np.tile(cos, (2, 1)), np.tile(sin, (2, 1))


def rope_rmat():
    R = np.zeros((128, 128), np.float32)
    for hh in range(2):
        for d in range(8):
            R[hh * 64 + d + 8, hh * 64 + d] = -1.0
            R[hh * 64 + d, hh * 64 + d + 8] = 1.0
    return R


def build_a():
    nc = bass.Bass("TRN2", target_bir_lowering=False)
    io = {}
    io["x"] = nc.dram_tensor("x", [NT, D], F32, kind="ExternalInput").ap()
    io["w_in"] = nc.dram_tensor("w_in", [D, DIN], F32, kind="ExternalInput").ap()
    io["norm_g"] = nc.dram_tensor("norm_g", [128, 8], F32, kind="ExternalInput").ap()
    io["fox_b"] = nc.dram_tensor("fox_b", [4], F32, kind="ExternalInput").ap()
    io["cos"] = nc.dram_tensor("cos", [128, NT], F32, kind="ExternalInput").ap()
    io["sin"] = nc.dram_tensor("sin", [128, NT], F32, kind="ExternalInput").ap()
    io["rmat"] = nc.dram_tensor("rmat", [128, 128], F32, kind="ExternalInput").ap()
    io["ident"] = nc.dram_tensor("ident", [128, 128], F32, kind="ExternalInput").ap()
    for n, (shp, dt) in A_OUTS.items():
        io[n] = nc.dram_tensor(n, shp, dt, kind="ExternalOutput").ap()
    with ExitStack() as top:
        S = Sched(nc, top)
        with ExitStack() as es:
            phase_a(S, nc, es, io)
            S.flush()
    return nc


def out_scope(S, nc, es, io, MIXB, last, pfx="o"):
    P = lambda n: pfx + n
    sb = lambda n, s_, d: es.enter_context(nc.sbuf_tensor(P(n), s_, d))
    PS = [es.enter_context(nc.psum_tensor(P("ps%d" % i), [128, 512], F32)) for i in range(4)]
    PSTb = es.enter_context(nc.psum_tensor(P("pstb"), [128, 1024], BF16))
    WO = sb("WO", [128, 8, D], BF16)
    wst = [sb("wst%d" % i, [128, D], F32) for i in range(2)]
    idf = sb("idf", [128, 128], F32)
    idb = sb("idb", [128, 128], BF16)
    MXT = [sb("MXT%d" % i, [128, 8, 128], BF16) for i in range(2)]
    xt = [sb("xt%d" % i, [128, D], F32) for i in range(2)]
    yo = [sb("yo%d" % i, [128, D], F32) for i in range(2)]
    sq = sb("sq", [128, D], F32)
    ss = sb("ss", [128, 1], F32)
    fg = sb("fg", [128, D], F32)
    S.dma(P("c"), batch=True, w=[P("idf")], out=idf[:], in_=io["ident"])
    if last:
        S.dma(P("c"), batch=True, w=[P("fg")], out=fg[:], in_=io["final_g"].partition_broadcast(128))
    S.dve("tensor_copy", r=[P("idf")], w=[P("idb")], out=idb[:], in_=idf[:])
    for c in range(8):
        w, wn = wst[c % 2], P("wst%d" % (c % 2))
        S.dma(P("w%d" % (c % 2)), w=[wn], out=w[:], in_=io["w_out"][c * 128:(c + 1) * 128, :])
        if c % 2 == 0:
            S.dve("tensor_copy", r=[wn], w=[P("WO%d" % c)], out=WO[:, c, :], in_=w[:])
        else:
            S.act("activation", r=[wn], w=[P("WO%d" % c)], out=WO[:, c, :], in_=w[:], func=AF.Copy)
    for m in range(16):
        p = m % 2
        mx, mxn, x_t, xn, y_t, yn = MXT[p], P("MXT%d" % p), xt[p], P("xt%d" % p), yo[p], P("yo%d" % p)
        S.dma(P("x%d" % p), w=[xn], out=x_t[:], in_=io["x"][m * 128:(m + 1) * 128, :])
        for c in range(8):
            S.pe("transpose", r=["MIXB", P("idb")], w=[P("pstb")], out=PSTb[:, c * 128:(c + 1) * 128], in_=MIXB[:, m, c * 128:(c + 1) * 128],
                 identity=idb[:])
        S.act("activation", r=[P("pstb")], w=[mxn], out=mx[:].rearrange("f c q -> f (c q)"), in_=PSTb[:], func=AF.Copy)
        for n2 in range(2):
            k = (2 * m + n2) % 4
            for c in range(8):
                S.pe("matmul", r=[mxn, P("WO%d" % c)], w=[P("ps%d" % k)], out=PS[k][:], lhsT=mx[:, c, :], rhs=WO[:, c, n2 * 512:(n2 + 1) * 512],
                     start=(c == 0), stop=(c == 7))
            S.act("activation", r=[P("ps%d" % k)], w=[yn], out=y_t[:, n2 * 512:(n2 + 1) * 512], in_=PS[k][:], func=AF.Copy)
        S.dve("tensor_tensor", r=[yn, xn], w=[yn], out=y_t[:], in0=y_t[:], in1=x_t[:], op=ALU.add)
        if last:
            S.act("activation", r=[yn], w=[P("sq"), P("ss")], out=sq[:], in_=y_t[:], func=AF.Square, accum_out=ss[:, 0:1])
            S.dve("tensor_scalar", r=[P("ss")], w=[P("ss")], out=ss[:], in0=ss[:], scalar1=1.0 / D, scalar2=1e-6, op0=ALU.mult, op1=ALU.add)
            S.act("activation", r=[P("ss")], w=[P("ss")], out=ss[:], in_=ss[:], func=AF.Sqrt)
            S.dve("reciprocal", r=[P("ss")], w=[P("ss")], out=ss[:], in_=ss[:])
            S.dve("scalar_tensor_tensor", r=[yn, P("ss"), P("fg")], w=[yn], out=y_t[:], in0=y_t[:], scalar=ss[:, 0:1], in1=fg[:],
                  op0=ALU.mult, op1=ALU.mult)
        S.dma(P("o%d" % p), r=[yn], out=io["xo"][m * 128:(m + 1) * 128, :], in_=y_t[:])


B_INS = dict(kf_g=([2, 128, T], BF16), vf_g=([T, 320], BF16), lf_g=([T, 4], F32), qf=([2, 128, NT], BF16),
             kc_g=([128, T], BF16), vc_g=([128, T], BF16), ks_g=([128, T], BF16), kw_g=([128, T], BF16),
             vs_g=([T, 160], BF16), vw_g=([T, 160], BF16), qn=([4, 128, NT], BF16), gl=([NT, 24], F32), gt=([NT, D], BF16),
             yT=([2, 128, NT], BF16), tails_g=([4, 2, 128, 16, 30], BF16), x=([NT, D], F32), w_out=([D, D], F32), final_g=([D], F32),
             tri=([128, 128], F32), ones128=([128, 128], F32), onehot=([128, 4], F32), dmask=([128, 4, 128], BF16),
             wmask=([128, 8, 128], BF16), cmask=([16, 128, 4, 128], BF16), force=([16, 128, 128], F32), ee=([128, T], BF16),
             ident=([128, 128], F32), rmat=([128, 128], F32), cosC=([128, 512], F32), sinC=([128, 512], F32),
             ovl=([128, 4, 128], BF16), peT=([64, 2, 32], F32), w2=([128, 2, 2, 64], F32), w1k=([2048, 256], F32),
             w1v=([2048, 256], F32), conv_w=([128, 2, 31], F32), conv_b=([128, 2], F32), ln_g=([128, 2], F32), ln_b=([128, 2], F32),
             conv_pw=([256, 256], F32), ohprev=([128, 4], F32))


def phase_b(S, nc, top_unused, io, last):
  with ExitStack() as top:
    KCMP = top.enter_context(nc.sbuf_tensor("KCMP", [128, 512], BF16))
    VCA = top.enter_context(nc.sbuf_tensor("VCA", [128, 4, 2, 208], BF16))
    MIXB = top.enter_context(nc.sbuf_tensor("MIXB", [128, 16, D], BF16))
    for scope in (lambda es: cmp_scope(S, nc, es, io, KCMP, VCA), lambda es: conv_scope(S, nc, es, io, MIXB),
                  lambda es: fox_scope(S, nc, es, io, MIXB), lambda es: nsa_scope(S, nc, es, io, KCMP, VCA, MIXB),
                  lambda es: out_scope(S, nc, es, io, MIXB, last)):
        with ExitStack() as es:
            scope(es)
            S.flush()


def build_b(last):
    nc = bass.Bass("TRN2", target_bir_lowering=False)
    io = {n: nc.dram_tensor(n, shp, dt, kind="ExternalInput").ap() for n, (shp, dt) in B_INS.items()}
    io["xo"] = nc.dram_tensor("xo", [NT, D], F32, kind="ExternalOutput").ap()
    with ExitStack() as top:
        S = Sched(nc, top)
        phase_b(S, nc, top, io, last)
    return nc


def build_ba():
    nc = bass.Bass("TRN2", target_bir_lowering=False)
    io = {n: nc.dram_tensor(n, shp, dt, kind="ExternalInput").ap() for n, (shp, dt) in B_INS.items()}
    io["xo"] = nc.dram_tensor("xo", [NT, D], F32, kind="ExternalOutput").ap()
    ioa = dict(x=io["xo"], rmat=io["rmat"], ident=io["ident"])
    ioa["w_in"] = nc.dram_tensor("a_w_in", [D, DIN], F32, kind="ExternalInput").ap()
    ioa["norm_g"] = nc.dram_tensor("a_norm_g", [128, 8], F32, kind="ExternalInput").ap()
    ioa["fox_b"] = nc.dram_tensor("a_fox_b", [4], F32, kind="ExternalInput").ap()
    ioa["cos"] = nc.dram_tensor("a_cos", [128, NT], F32, kind="ExternalInput").ap()
    ioa["sin"] = nc.dram_tensor("a_sin", [128, NT], F32, kind="ExternalInput").ap()
    for n, (shp, dt) in A_OUTS.items():
        ioa[n] = nc.dram_tensor("a_" + n, shp, dt, kind="ExternalOutput").ap()
    with ExitStack() as top:
        S = Sched(nc, top)
        phase_b(S, nc, top, io, False)
        with ExitStack() as es:
            phase_a(S, nc, es, ioa, pfx="A")
            S.flush()
    return nc


def a_inputs(inputs, layer, x_own):
    maps = []
    for core in range(8):
        cos, sin = rope_tables(own_positions(core % 4))
        maps.append(dict(x=np.ascontiguousarray(x_own[core], dtype=np.float32), w_in=np.ascontiguousarray(inputs["w_in"][layer]),
                         norm_g=np.ascontiguousarray(inputs["norm_g"][layer].reshape(8, 128).T), fox_b=np.ascontiguousarray(inputs["fox_b"][layer]),
                         cos=cos, sin=sin, rmat=rope_rmat(), ident=np.eye(128, dtype=np.float32)))
    return maps


def b_inputs(inputs, layer, resA, x_own):
    maps = []
    cc = common_consts()
    ee = ee_const()
    cmpc = cmp_consts(inputs, layer)
    for core in range(8):
        b, r = core // 4, core % 4
        R = [resA[4 * b + rr] for rr in range(4)]
        cat = lambda n, ax: np.concatenate([np.asarray(R[rr][n]) for rr in range(4)], axis=ax)
        d = dict(kf_g=cat("kf", 2), vf_g=cat("vf", 0), lf_g=cat("lf", 0), kc_g=cat("kc", 1), vc_g=cat("vc", 1), ks_g=cat("ks", 1),
                 kw_g=cat("kw", 1), vs_g=cat("vs", 0), vw_g=cat("vw", 0))
        d["tails_g"] = np.ascontiguousarray(np.stack([np.asarray(R[rr]["yT"]).reshape(2, 128, 16, 128)[:, :, :, 98:128] for rr in range(4)], axis=0))
        own = resA[core]
        for n in ("qf", "qn", "gl", "gt", "yT"):
            d[n] = np.asarray(own[n])
        d["x"] = np.ascontiguousarray(x_own[core], dtype=np.float32)
        d["w_out"] = np.ascontiguousarray(inputs["w_out"][layer])
        d["final_g"] = np.ascontiguousarray(inputs["final_g"])
        d.update(cc)
        d.update(rank_consts(r))
        d.update(nsa_rank_consts(r))
        d.update(cmpc)
        d.update(conv_consts(inputs, layer, r))
        d["ee"] = ee
        d["ident"] = np.eye(128, dtype=np.float32)
        maps.append(d)
    return maps


def kernel(**inputs):
    inputs = {k: np.asarray(v) for k, v in inputs.items()}
    x = inputs["x"].astype(np.float32)
    x_own = [x[core // 4][own_positions(core % 4)] for core in range(8)]
    cores = list(range(8))
    resA0 = run_bass_kernel_spmd(build_a(), a_inputs(inputs, 0, x_own), core_ids=cores).results
    maps = b_inputs(inputs, 0, resA0, x_own)
    a1 = a_inputs(inputs, 1, x_own)
    for core in cores:
        for k in ("w_in", "norm_g", "fox_b", "cos", "sin"):
            maps[core]["a_" + k] = a1[core][k]
    resBA = run_bass_kernel_spmd(build_ba(), maps, core_ids=cores).results
    x1_own = [np.asarray(resBA[core]["xo"]) for core in cores]
    resA1 = [{k[2:]: v for k, v in resBA[core].items() if k.startswith("a_")} for core in cores]
    resB1 = run_bass_kernel_spmd(build_b(last=True), b_inputs(inputs, 1, resA1, x1_own), core_ids=cores).results
    out = np.empty((2, T, D), np.float32)
    for core in cores:
        out[core // 4][own_positions(core % 4)] = np.asarray(resB1[core]["xo"])
    return out


def cidx(j):
    return (j % 4) * 16 + j // 4


def gcol(j):
    return cidx(j) * 128


def fox_scope(S, nc, es, io, MIXB, pfx="f"):
    P = lambda n: pfx + n
    sb = lambda n, s, d: es.enter_context(nc.sbuf_tensor(P(n), s, d))
    PS = [es.enter_context(nc.psum_tensor(P("ps%d" % i), [128, 512], F32)) for i in range(4)]
    KF = sb("KF", [128, 2, T], BF16)
    VF = sb("VF", [128, 64, 320], BF16)
    QF = sb("QF", [128, 4, NT], BF16)
    LF = sb("LF", [128, 64, 4], F32)
    tri = sb("tri", [128, 128], F32)
    ones = sb("ones", [128, 128], F32)
    oneh = sb("oneh", [128, 4], F32)
    dm = sb("dm", [128, 4, 128], BF16)
    dm4 = sb("dm4", [128, 4, 4, 128], BF16)
    WI = sb("WI", [128, 64, 4], F32)
    TOT = sb("TOT", [128, 64, 4], F32)
    SA = sb("SA", [128, 64, 4], F32)
    SB_ = sb("SB", [128, 64, 4], F32)
    NC_ = sb("NC", [128, 64, 4], F32)
    ROWN = sb("ROWN", [128, 16, 4], F32)
    CB = [sb("CB%d" % i, [128, 64, 4], F32) for i in range(2)]
    PT = [sb("PT%d" % i, [128, 512], BF16) for i in range(3)]
    ofs = sb("ofs", [128, 4, 65], F32)
    rden = sb("rden", [128, 4], F32)
    onrm = sb("onrm", [128, 256], F32)
    gtt = [sb("gtt%d" % i, [128, 256], BF16) for i in range(2)]

    ld = P("ld")
    S.pool("memset", w=[P("QF")], ap=QF[:].rearrange("p h t -> p (h t)"), constant=0.0)
    for h2 in range(2):
        S.dma(ld, batch=True, w=[P("KF")], out=KF[:, h2, :], in_=io["kf_g"][h2])
    for h in range(4):
        pb = 64 * (h % 2)
        S.dma(ld, batch=True, w=[P("QF")], out=QF[pb:pb + 64, h, :], in_=io["qf"][h // 2, pb:pb + 64, :])
    for r in range(4):
        S.dma(ld, batch=True, w=[P("VF")], out=VF[:, r * 16:(r + 1) * 16, :],
              in_=io["vf_g"][r * NT:(r + 1) * NT, :].rearrange("(c t) w -> t c w", t=128))
    for r in range(4):
        S.dma(ld, batch=True, w=[P("LF")], out=LF[:].rearrange("t (m r) h -> t m r h", r=4)[:, :, r, :],
              in_=io["lf_g"][r * NT:(r + 1) * NT, :].rearrange("(m t) h -> t m h", t=128))
    S.dma(ld, batch=True, w=[P("tri")], out=tri[:], in_=io["tri"])
    S.dma(ld, batch=True, w=[P("ones")], out=ones[:], in_=io["ones128"])
    S.dma(ld, batch=True, w=[P("oneh")], out=oneh[:], in_=io["onehot"])
    S.dma(ld, batch=True, w=[P("dm")], out=dm[:], in_=io["dmask"])
    for h in range(4):
        S.pool("tensor_copy", r=[P("dm")], w=[P("dm4")], out=dm4[:, :, h, :], in_=dm[:])

    LFf = LF[:].rearrange("t j h -> t (j h)")
    S.pe("matmul", r=[P("tri"), P("LF")], w=[P("ps0")], out=PS[0][:, 0:256], lhsT=tri[:], rhs=LFf, start=True, stop=True)
    S.pe("matmul", r=[P("ones"), P("LF")], w=[P("ps1")], out=PS[1][:, 0:256], lhsT=ones[:], rhs=LFf, start=True, stop=True)
    S.act("activation", r=[P("ps0")], w=[P("WI")], out=WI[:].rearrange("t j h -> t (j h)"), in_=PS[0][:, 0:256], func=AF.Copy)
    S.act("activation", r=[P("ps1")], w=[P("TOT")], out=TOT[:].rearrange("t j h -> t (j h)"), in_=PS[1][:, 0:256], func=AF.Copy)
    src, srcn = TOT, P("TOT")
    d = 1
    pp = [(SA, P("SA")), (SB_, P("SB"))]
    k = 0
    while d < 64:
        dst, dstn = pp[k % 2]
        S.dve("tensor_tensor", r=[srcn], w=[dstn], out=dst[:, d:, :], in0=src[:, d:, :], in1=src[:, :64 - d, :], op=ALU.add)
        S.dve("tensor_copy", r=[srcn], w=[dstn], out=dst[:, :d, :], in_=src[:, :d, :])
        src, srcn = dst, dstn
        d *= 2
        k += 1
    INCL, INCLn = src, srcn
    S.dve("tensor_tensor", r=[P("WI"), INCLn], w=[P("NC")], out=NC_[:], in0=WI[:], in1=INCL[:], op=ALU.add)
    S.dve("tensor_tensor", r=[P("NC"), P("TOT")], w=[P("NC")], out=NC_[:], in0=TOT[:], in1=NC_[:], op=ALU.subtract)
    I4 = INCL[:].rearrange("t (m r) h -> t m r h", r=4)
    S.dve("tensor_scalar", r=[INCLn, P("oneh")], w=[P("ROWN")], out=ROWN[:], in0=I4[:, :, 0, :], scalar1=oneh[:, 0:1],
          scalar2=None, op0=ALU.mult)
    for r in range(1, 4):
        S.dve("scalar_tensor_tensor", r=[INCLn, P("oneh"), P("ROWN")], w=[P("ROWN")], out=ROWN[:], in0=I4[:, :, r, :],
              scalar=oneh[:, r:r + 1], in1=ROWN[:], op0=ALU.mult, op1=ALU.add)

    units = []
    for m in range(16):
        J = 4 * m + 4
        for j in range(J):
            units.append((m, j, J))

    def emit_s(i, u):
        m, j, J = u
        sp = i % 2
        psn = P("ps%d" % sp)
        if j == 0:
            cb, cbn = CB[m % 2], P("CB%d" % (m % 2))
            for h in range(4):
                S.dve("tensor_scalar", r=[P("NC"), P("ROWN")], w=[cbn + "_%d" % h], out=cb[:, 0:J, h], in0=NC_[:, 0:J, h],
                      scalar1=ROWN[:, m, h:h + 1], scalar2=0.0, op0=ALU.add, op1=ALU.min)
        for pr in range(2):
            S.pe("matmul", r=[P("KF"), P("QF")], w=[psn], out=PS[sp][:, pr * 256:(pr + 1) * 256],
                 lhsT=KF[:, pr, gcol(j):gcol(j) + 128], rhs=QF[:, 2 * pr:2 * pr + 2, m * 128:(m + 1) * 128], start=True, stop=True)

    def emit_rest(i, u):
        m, j, J = u
        sp = i % 2
        psn = P("ps%d" % sp)
        cb, cbn = CB[m % 2], P("CB%d" % (m % 2))
        pt, ptn = PT[i % 3], P("PT%d" % (i % 3))
        pth = [ptn + "_%d" % h for h in range(4)]
        for h in range(4):
            S.act("activation", r=[psn, cbn + "_%d" % h], w=[pth[h]], out=pt[:, h * 128:(h + 1) * 128], in_=PS[sp][:, h * 128:(h + 1) * 128],
                  func=AF.Exp, bias=cb[:, j, h:h + 1], scale=0.125)
        if j >= 4 * m:
            S.dve("tensor_tensor", r=pth + [P("dm4")], w=pth, out=pt[:], in0=pt[:],
                  in1=dm4[:, j - 4 * m, :, :].rearrange("k h q -> k (h q)"), op=ALU.mult)
        for h in range(4):
            S.pe("matmul", r=[pth[h], P("VF")], w=[P("ps2")], out=PS[2][:, h * 65:(h + 1) * 65], lhsT=pt[:, h * 128:(h + 1) * 128],
                 rhs=VF[:, cidx(j), h * 80:h * 80 + 65], start=(j == 0 and h == 0), stop=(j == J - 1 and h == 3))
        if j == J - 1:
            S.act("activation", r=[P("ps2")], w=[P("ofs")], out=ofs[:].rearrange("q h d -> q (h d)"), in_=PS[2][:, 0:260], func=AF.Copy)
            S.dve("reciprocal", r=[P("ofs")], w=[P("rden")], out=rden[:], in_=ofs[:, :, 64])
            for h in range(4):
                S.dve("tensor_scalar", r=[P("ofs"), P("rden")], w=[P("onrm%d" % h)], out=onrm[:, h * 64:(h + 1) * 64], in0=ofs[:, h, 0:64],
                      scalar1=rden[:, h:h + 1], scalar2=None, op0=ALU.mult)
            gp = m % 2
            S.dma(P("gt%d" % gp), w=[P("gtt%d" % gp)], out=gtt[gp][:], in_=io["gt"][m * 128:(m + 1) * 128, 0:256])
            S.dve("tensor_tensor", r=[P("onrm%d" % h) for h in range(4)] + [P("gtt%d" % gp)], w=["MIXB"], out=MIXB[:, m, 0:256], in0=onrm[:],
                  in1=gtt[gp][:], op=ALU.mult)

    for i, u in enumerate(units):
        emit_s(i, u)
        if i > 0:
            emit_rest(i - 1, units[i - 1])
    emit_rest(len(units) - 1, units[-1])


def rank_consts(r):
    k = np.arange(128)[:, None]
    q = np.arange(128)[None, :]
    dmask = np.zeros((128, 4, 128), np.float32)
    for c in range(4):
        if c < r:
            dmask[:, c, :] = 1.0
        elif c == r:
            dmask[:, c, :] = (k <= q)
    onehot = np.zeros((128, 4), np.float32)
    onehot[:, r] = 1.0
    return dict(dmask=dmask.astype(NPBF), onehot=onehot)


def common_consts():
    k = np.arange(128)[:, None]
    q = np.arange(128)[None, :]
    return dict(tri=(k <= q).astype(np.float32), ones128=np.ones((128, 128), np.float32))


def cmp_scope(S, nc, es, io, KCMP, VCA, pfx="c"):
    P = lambda n: pfx + n
    sb = lambda n, s, d: es.enter_context(nc.sbuf_tensor(P(n), s, d))
    PS = [es.enter_context(nc.psum_tensor(P("ps%d" % i), [128, 512], F32)) for i in range(4)]
    KCN = [sb("KCN%d" % i, [128, T], BF16) for i in range(2)]
    KCS = [sb("KCS%d" % i, [128, 16, 512], BF16) for i in range(2)]
    W1st = sb("W1st", [128, 32, 256], F32)
    W1Z = [sb("W1Z%d" % g, [128, 32, 256], BF16) for g in range(2)]
    PEf = sb("PEf", [128, 2, 32], F32)
    PEB = sb("PEB", [128, 2, 32], BF16)
    W2f = sb("W2f", [128, 2, 2, 64], F32)
    W2KZ = sb("W2KZ", [128, 2, 2, 128], BF16)
    W2V = sb("W2V", [128, 2, 64], BF16)
    BH = sb("BH", [128, 2], F32)
    HID = [[sb("HID%d%d" % (kv, g), [128, 2, 512], BF16) for g in range(2)] for kv in range(2)]
    X = sb("X", [128, 512], F32)
    X2 = sb("X2", [128, 512], F32)
    U = sb("U", [128, 512], F32)
    SG = sb("SG", [128, 512], F32)
    KC0 = sb("KC0", [128, 512], F32)
    KZ = sb("KZ", [128, 512], BF16)
    RT = sb("RT", [128, 512], F32)
    cosC = sb("cosC", [128, 512], F32)
    sinC = sb("sinC", [128, 512], F32)
    rmf = sb("rmf", [128, 128], F32)
    rmb = sb("rmb", [128, 128], BF16)
    ovf = sb("ovf", [128, 4, 128], BF16)

    ld = P("ld")
    for kv, nm in ((0, "kc_g"), (1, "vc_g")):
        for r in range(4):
            S.dma(ld, batch=True, w=[P("KCN%d" % kv)], out=KCN[kv][:].rearrange("d (m r t) -> d m r t", r=4, t=128)[:, :, r, :],
                  in_=io[nm][:, r * NT:(r + 1) * NT].rearrange("d (m t) -> d m t", t=128))
    S.dma(ld, batch=True, w=[P("PEf")], out=PEf[0:64, :, :], in_=io["peT"])
    S.dma(ld, batch=True, w=[P("PEf")], out=PEf[64:128, :, :], in_=io["peT"])
    S.dma(ld, batch=True, w=[P("W2f")], out=W2f[:], in_=io["w2"])
    S.dma(ld, batch=True, w=[P("cosC")], out=cosC[:], in_=io["cosC"])
    S.dma(ld, batch=True, w=[P("sinC")], out=sinC[:], in_=io["sinC"])
    S.dma(ld, batch=True, w=[P("rmf")], out=rmf[:], in_=io["rmat"])
    S.dma(ld, batch=True, w=[P("ovf")], out=ovf[:], in_=io["ovl"])
    S.dve("tensor_copy", r=[P("PEf")], w=[P("PEB")], out=PEB[:], in_=PEf[:])
    S.dve("tensor_copy", r=[P("rmf")], w=[P("rmb")], out=rmb[:], in_=rmf[:])
    for kv in range(2):
        for sft in range(16):
            src = KCN[kv][:, sft:sft + 16 * 511 + 1:16]
            if sft % 3 == 0:
                S.act("activation", r=[P("KCN%d" % kv)], w=[P("KCS%d_%d" % (kv, sft))], out=KCS[kv][:, sft, :], in_=src, func=AF.Copy)
            elif sft % 3 == 1:
                S.dve("tensor_copy", r=[P("KCN%d" % kv)], w=[P("KCS%d_%d" % (kv, sft))], out=KCS[kv][:, sft, :], in_=src)
            else:
                S.pool("tensor_copy", r=[P("KCN%d" % kv)], w=[P("KCS%d_%d" % (kv, sft))], out=KCS[kv][:, sft, :], in_=src)
    S.pool("memset", w=[P("W2KZ")], ap=W2KZ[:], constant=0.0)
    for g in range(2):
        S.pool("memset", w=[P("W1Z%d" % g)], ap=W1Z[g][:], constant=0.0)
        S.dve("tensor_copy", r=[P("W2f"), P("W2KZ")], w=[P("W2KZ")], out=W2KZ[:, g, :, 64 * g:64 * g + 64], in_=W2f[:, 0, :, :])
        for kv in range(2):
            S.pool("memset", w=[P("HID%d%d" % (kv, g))], ap=HID[kv][g][:], constant=0.0)
    S.dve("tensor_copy", r=[P("W2f")], w=[P("W2V")], out=W2V[:], in_=W2f[:, 1, :, :])
    S.pool("memset", w=[P("KC0")], ap=KC0[:], constant=0.0)
    S.pool("memset", w=["VCA"], ap=VCA[:], constant=1.0)
    for g in range(2):
        S.pool("tensor_copy", r=[P("ovf"), "VCA"], w=["VCA"], out=VCA[:, :, g, 80:208], in_=ovf[:])

    for kv, wn in ((0, "w1k"), (1, "w1v")):
        for half in range(2):
            S.dma(P("w1"), w=[P("W1st")], out=W1st[64 * half:64 * half + 64, :, :],
                  in_=io[wn].rearrange("(l d) h -> d l h", d=64))
        for g in range(2):
            S.dve("tensor_copy", r=[P("W1st"), P("W1Z%d" % g)], w=[P("W1Z%d" % g)], out=W1Z[g][64 * g:64 * g + 64, :, :],
                  in_=W1st[64 * g:64 * g + 64, :, :])
        for hc in range(2):
            for l in range(32):
                S.pe("matmul", r=[P("W1Z0"), P("PEB")], w=[P("ps3")], out=PS[3][:, hc:hc + 1], lhsT=W1Z[0][:, l, hc * 128:(hc + 1) * 128],
                     rhs=PEB[:, kv, l:l + 1], start=(hc == 0 and l == 0), stop=(hc == 1 and l == 31))
        S.act("activation", r=[P("ps3")], w=[P("BH")], out=BH[:], in_=PS[3][:, 0:2], func=AF.Copy)
        for g in range(2):
            for hc in range(2):
                k = (g * 2 + hc) % 3
                psn = P("ps%d" % k)
                for l in range(32):
                    S.pe("matmul", r=[P("W1Z%d" % g), P("KCS%d_%d" % (kv, l % 16))], w=[psn], out=PS[k][:, 0:511],
                         lhsT=W1Z[g][:, l, hc * 128:(hc + 1) * 128], rhs=KCS[kv][:, l % 16, l // 16:l // 16 + 511], start=(l == 0), stop=(l == 31))
                S.act("activation", r=[psn, P("BH")], w=[P("X")], out=X[:, 0:511], in_=PS[k][:, 0:511], func=AF.Identity,
                      bias=BH[:, hc:hc + 1], scale=1.0)
                S.dve("tensor_tensor", r=[P("X")], w=[P("X2")], out=X2[:, 0:511], in0=X[:, 0:511], in1=X[:, 0:511], op=ALU.mult)
                S.dve("tensor_scalar", r=[P("X2")], w=[P("X2")], out=X2[:, 0:511], in0=X2[:, 0:511], scalar1=0.044715, scalar2=1.0,
                      op0=ALU.mult, op1=ALU.add)
                S.dve("tensor_tensor", r=[P("X2"), P("X")], w=[P("U")], out=U[:, 0:511], in0=X2[:, 0:511], in1=X[:, 0:511], op=ALU.mult)
                S.act("activation", r=[P("U")], w=[P("SG")], out=SG[:, 0:511], in_=U[:, 0:511], func=AF.Sigmoid, scale=1.5957691216057308)
                S.dve("tensor_tensor", r=[P("X"), P("SG")], w=[P("HID%d%d" % (kv, g))], out=HID[kv][g][:, hc, 0:511], in0=X[:, 0:511],
                      in1=SG[:, 0:511], op=ALU.mult)
    n = 0
    for g in range(2):
        for hc in range(2):
            S.pe("matmul", r=[P("W2KZ"), P("HID0%d" % g)], w=[P("ps0")], out=PS[0][:, 0:511], lhsT=W2KZ[:, g, hc, :],
                 rhs=HID[0][g][:, hc, 0:511], start=(n == 0), stop=(n == 3))
            n += 1
    S.act("activation", r=[P("ps0")], w=[P("KC0")], out=KC0[:, 0:511], in_=PS[0][:, 0:511], func=AF.Copy)
    S.dve("tensor_copy", r=[P("KC0")], w=[P("KZ")], out=KZ[:], in_=KC0[:])
    S.pe("matmul", r=[P("rmb"), P("KZ")], w=[P("ps1")], out=PS[1][:], lhsT=rmb[:], rhs=KZ[:], start=True, stop=True)
    S.act("activation", r=[P("ps1")], w=[P("RT")], out=RT[:], in_=PS[1][:], func=AF.Copy)
    S.dve("tensor_tensor", r=[P("KC0"), P("cosC")], w=[P("KC0")], out=KC0[:], in0=KC0[:], in1=cosC[:], op=ALU.mult)
    S.dve("tensor_tensor", r=[P("RT"), P("sinC")], w=[P("RT")], out=RT[:], in0=RT[:], in1=sinC[:], op=ALU.mult)
    S.dve("tensor_tensor", r=[P("KC0"), P("RT")], w=["KCMP"], out=KCMP[:], in0=KC0[:], in1=RT[:], op=ALU.add)
    for c in range(4):
        for g in range(2):
            k = 2 + (c * 2 + g) % 2
            for hc in range(2):
                S.pe("matmul", r=[P("HID1%d" % g), P("W2V")], w=[P("ps%d" % k)], out=PS[k][:, 0:64], lhsT=HID[1][g][:, hc, c * 128:(c + 1) * 128],
                     rhs=W2V[:, hc, :], start=(hc == 0), stop=(hc == 1))
            S.act("activation", r=[P("ps%d" % k), "VCA"], w=["VCA"], out=VCA[:, c, g, 0:64], in_=PS[k][:, 0:64], func=AF.Copy)


def cmp_consts(inputs, layer):
    n = np.arange(512)
    cosC, sinC = rope_tables(16 * n + 31)
    nn = np.arange(512)[:, None]
    bb = np.arange(128)[None, :]
    ovl = ((16 * nn < 64 * bb + 64) & (16 * nn + 31 >= 64 * bb) & (nn < 511)).astype(np.float32)
    peT = np.stack([inputs["cmp_pe_k"][layer].T, inputs["cmp_pe_v"][layer].T], axis=1)
    w2 = np.stack([inputs["cmp_k_w2"][layer].reshape(2, 128, 64), inputs["cmp_v_w2"][layer].reshape(2, 128, 64)], axis=0)
    w2 = np.ascontiguousarray(w2.transpose(2, 0, 1, 3))
    return dict(cosC=cosC, sinC=sinC, ovl=np.ascontiguousarray(ovl.reshape(4, 128, 128).transpose(1, 0, 2)).astype(NPBF),
                peT=np.ascontiguousarray(peT, dtype=np.float32), w2=w2.astype(np.float32),
                w1k=np.ascontiguousarray(inputs["cmp_k_w1"][layer]), w1v=np.ascontiguousarray(inputs["cmp_v_w1"][layer]),
                rmat=rope_rmat())


def nsa_scope(S, nc, es, io, KCMP, VCA, MIXB, pfx="n"):
    P = lambda n: pfx + n
    sb = lambda n, s, d: es.enter_context(nc.sbuf_tensor(P(n), s, d))
    PS = [es.enter_context(nc.psum_tensor(P("ps%d" % i), [128, 512], F32)) for i in range(6)]
    PSTb = es.enter_context(nc.psum_tensor(P("pstb"), [128, 1024], BF16))
    KS = sb("KS", [128, T], BF16)
    VS = sb("VS", [128, 64, 160], BF16)
    EE = sb("EE", [128, T], BF16)
    QNZ = sb("QNZ", [128, 2, 4, NT], BF16)
    KWm = [sb("KWm%d" % i, [128, 8, 128], BF16) for i in range(2)]
    VWm = [sb("VWm%d" % i, [128, 8, 160], BF16) for i in range(2)]
    dm = sb("dm", [128, 4, 128], BF16)
    wm = sb("wm", [128, 8, 128], BF16)
    cm = [sb("cm%d" % i, [128, 4, 128], BF16) for i in range(2)]
    fc = [sb("fc%d" % i, [128, 128], F32) for i in range(2)]
    glt = [sb("glt%d" % i, [128, 24], F32) for i in range(2)]
    gtt = [sb("gtt%d" % i, [128, 512], BF16) for i in range(2)]
    idf = sb("idf", [128, 128], F32)
    idb = sb("idb", [128, 128], BF16)
    PT = [sb("PT%d" % i, [128, 512], BF16) for i in range(3)]
    ocs = sb("ocs", [128, 4, 208], F32)
    oss = sb("oss", [128, 4, 65], F32)
    ows = sb("ows", [128, 4, 65], F32)
    rd = sb("rd", [128, 3, 4], F32)
    fac = sb("fac", [128, 3, 4], F32)
    imp = sb("imp", [128, 128], F32)
    wk = sb("wk", [128, 128], F32)
    mx = sb("mx", [128, 8], F32)
    MBq = sb("MBq", [128, 128], BF16)
    MBT = sb("MBT", [128, 4, 128], BF16)
    acc = sb("acc", [128, 4, 64], F32)
    tmp = sb("tmp", [128, 4, 64], F32)

    def bc(ap, h=4):
        return ap.unsqueeze(1).to_broadcast([128, h, ap.shape[-1]])

    ld = P("ld")
    S.dma(ld, batch=True, w=[P("KS")], out=KS[:], in_=io["ks_g"])
    S.dma(ld, batch=True, w=[P("EE")], out=EE[:], in_=io["ee"])
    for r in range(4):
        S.dma(ld, batch=True, w=[P("VS")], out=VS[:, r * 16:(r + 1) * 16, :],
              in_=io["vs_g"][r * NT:(r + 1) * NT, :].rearrange("(c t) w -> t c w", t=128))
    S.pool("memset", w=[P("QNZ")], ap=QNZ[:].rearrange("p g h t -> p (g h t)"), constant=0.0)
    for hh in range(8):
        g, h = hh // 4, hh % 4
        src = 64 * (hh % 2)
        S.dma(ld, batch=True, w=[P("QNZ")], out=QNZ[64 * g:64 * g + 64, g, h, :], in_=io["qn"][hh // 2, src:src + 64, :])
    S.dma(ld, batch=True, w=[P("dm")], out=dm[:], in_=io["dmask"])
    S.dma(ld, batch=True, w=[P("wm")], out=wm[:], in_=io["wmask"])
    S.dma(ld, batch=True, w=[P("idf")], out=idf[:], in_=io["ident"])
    S.dve("tensor_copy", r=[P("idf")], w=[P("idb")], out=idb[:], in_=idf[:])

    units = []
    load_fns = []

    def add_unit(**kw):
        units.append(kw)

    def emit_s(i, u):
        sp = i % 2
        psn = P("ps%d" % sp)
        for fn in u.get("pre", ()):
            fn()
        S.pe("matmul", r=list(u["rk"]) + [P("QNZ")], w=[psn], out=PS[sp][:], lhsT=u["lhsT"], rhs=u["qrhs"], start=True,
             stop=(u.get("bias_mm") is None))
        if u.get("bias_mm") is not None:
            S.pe("matmul", r=[P("EE"), P("MBT")], w=[psn], out=PS[sp][:], lhsT=u["bias_mm"], rhs=MBT[:].rearrange("b h q -> b (h q)"),
                 start=False, stop=True)

    def emit_rest(i, u):
        sp = i % 2
        psn = P("ps%d" % sp)
        pt, ptn = PT[i % 3], P("PT%d" % (i % 3))
        S.act("activation", r=[psn], w=[ptn], out=pt[:], in_=PS[sp][:], func=AF.Exp, scale=0.125)
        if u.get("mask") is not None:
            mk, mn = u["mask"]
            S.pool("tensor_tensor", r=[ptn, mn], w=[ptn], out=pt[:].rearrange("k (h q) -> k h q", h=4),
                   in0=pt[:].rearrange("k (h q) -> k h q", h=4), in1=bc(mk), op=ALU.mult)
        obank, ocols, first, last = u["obank"], u["ocols"], u["first"], u["last"]
        for h in range(4):
            ob, oc = obank(h), ocols(h)
            S.pe("matmul", r=[ptn] + list(u["vk"]), w=[P("ps%d" % ob)], out=PS[ob][:, oc[0]:oc[1]], lhsT=pt[:, h * 128:(h + 1) * 128],
                 rhs=u["vrhs"], start=(first and (h == 0 or (ob != obank(0) and h == 2))),
                 stop=(last and (h == 3 or (ob != obank(3) and h == 1))))
        for fn in u.get("post", ()):
            fn()

    for m in range(16):
        p = m % 2
        cmn, fcn, gln, gtn, kwn, vwn = P("cm%d" % p), P("fc%d" % p), P("glt%d" % p), P("gtt%d" % p), P("KWm%d" % p), P("VWm%d" % p)

        def loads(m=m, p=p, cmn=cmn, fcn=fcn, gln=gln, gtn=gtn, kwn=kwn, vwn=vwn):
            S.dma(P("cm%d" % p), w=[cmn], out=cm[p][:], in_=io["cmask"][m])
            S.dma(P("fc%d" % p), w=[fcn], out=fc[p][:], in_=io["force"][m])
            S.dma(P("gl%d" % p), w=[gln], out=glt[p][:], in_=io["gl"][m * 128:(m + 1) * 128, :])
            S.dma(P("gt%d" % p), w=[gtn], out=gtt[p][:], in_=io["gt"][m * 128:(m + 1) * 128, 512:1024])
            for mm in ((m - 1, m) if m > 0 else (m,)):
                s0 = 4 * (mm - m + 1)
                S.dma(P("kw%d" % p), w=[kwn], out=KWm[p][:, s0:s0 + 4, :],
                      in_=io["kw_g"].rearrange("d (r c t) -> d r c t", r=4, t=128)[:, :, mm, :])
                S.dma(P("vw%d" % p), w=[vwn], out=VWm[p][:, s0:s0 + 4, :],
                      in_=io["vw_g"].rearrange("(r c t) w -> t r c w", r=4, t=128)[:, :, mm, :])

        load_fns.append(loads)
    load_fns[0]()
    for m in range(16):
        p = m % 2
        cmn, fcn, gln, gtn, kwn, vwn = P("cm%d" % p), P("fc%d" % p), P("glt%d" % p), P("gtt%d" % p), P("KWm%d" % p), P("VWm%d" % p)
        nxt_loads = load_fns[m + 1] if m + 1 < 16 else None
        qs = slice(m * 128, (m + 1) * 128)
        for g in range(2):
            qrhs = QNZ[:, g, :, qs]

            def cmp_post(p=p, fcn=fcn):
                S.act("activation", r=[P("ps2")], w=[P("ocs")], out=ocs[:, 0:2, :].rearrange("q h w -> q (h w)"), in_=PS[2][:, 0:416], func=AF.Copy)
                S.act("activation", r=[P("ps3")], w=[P("ocs")], out=ocs[:, 2:4, :].rearrange("q h w -> q (h w)"), in_=PS[3][:, 0:416], func=AF.Copy)
                S.dve("tensor_scalar", r=[P("ocs")], w=[P("rd0")], out=rd[:, 0, :], in0=ocs[:, :, 64], scalar1=1e-30, scalar2=None, op0=ALU.max)
                S.dve("reciprocal", r=[P("rd0")], w=[P("rd0")], out=rd[:, 0, :], in_=rd[:, 0, :])
                S.dve("tensor_scalar", r=[P("ocs"), P("rd0")], w=[P("imp")], out=imp[:], in0=ocs[:, 0, 80:208], scalar1=rd[:, 0, 0:1],
                      scalar2=None, op0=ALU.mult)
                for h in range(1, 4):
                    S.dve("scalar_tensor_tensor", r=[P("ocs"), P("rd0"), P("imp")], w=[P("imp")], out=imp[:], in0=ocs[:, h, 80:208],
                          scalar=rd[:, 0, h:h + 1], in1=imp[:], op0=ALU.mult, op1=ALU.add)
                S.dve("tensor_tensor", r=[P("imp"), fcn], w=[P("imp")], out=imp[:], in0=imp[:], in1=fc[p][:], op=ALU.max)
                S.dve("max", r=[P("imp")], w=[P("mx")], out=mx[:], in_=imp[:])
                S.dve("match_replace", r=[P("mx"), P("imp")], w=[P("wk")], out=wk[:], in_to_replace=mx[:], in_values=imp[:], imm_value=-1.0)
                S.dve("max", r=[P("wk")], w=[P("mx")], out=mx[:], in_=wk[:])
                S.dve("tensor_scalar", r=[P("imp"), P("mx")], w=[P("wk")], out=wk[:], in0=imp[:], scalar1=mx[:, 7:8], scalar2=None, op0=ALU.is_ge)
                S.dve("tensor_scalar", r=[P("wk")], w=[P("MBq")], out=MBq[:], in0=wk[:], scalar1=-1.0, scalar2=30000.0, op0=ALU.add, op1=ALU.mult)

            def slc_pre():
                S.pe("transpose", r=[P("MBq"), P("idb")], w=[P("pstb")], out=PSTb[:, 0:128], in_=MBq[:], identity=idb[:])
                S.act("activation", r=[P("pstb")], w=[P("MBT")], out=MBT[:], in_=bc(PSTb[:, 0:128]), func=AF.Copy)

            def win_post():
                S.act("activation", r=[P("ps5")], w=[P("ows")], out=ows[:].rearrange("q h w -> q (h w)"), in_=PS[5][:, 0:260], func=AF.Copy)
                S.dve("reciprocal", r=[P("ows")], w=[P("rd2")], out=rd[:, 2, :], in_=ows[:, :, 64])

            def slc_post(m=m, g=g, p=p, gln=gln, gtn=gtn):
                S.act("activation", r=[P("ps4")], w=[P("oss")], out=oss[:].rearrange("q h w -> q (h w)"), in_=PS[4][:, 0:260], func=AF.Copy)
                S.dve("reciprocal", r=[P("oss")], w=[P("rd1")], out=rd[:, 1, :], in_=oss[:, :, 64])
                S.dve("tensor_tensor", r=[P("rd0"), P("rd1"), P("rd2"), gln], w=[P("fac")], out=fac[:], in0=rd[:],
                      in1=glt[p][:, g * 12:g * 12 + 12].rearrange("q (h b) -> q b h", b=3), op=ALU.mult)
                srcs = ((ocs, P("ocs")), (oss, P("oss")), (ows, P("ows")))
                for br in range(3):
                    o_t, o_n = srcs[br]
                    dst, dstn = (acc, P("acc")) if br == 0 else (tmp, P("tmp"))
                    S.dve("tensor_tensor", r=[o_n, P("fac")], w=[dstn], out=dst[:], in0=o_t[:, :, 0:64],
                          in1=fac[:, br, :].unsqueeze(2).to_broadcast([128, 4, 64]), op=ALU.mult)
                    if br > 0:
                        S.dve("tensor_tensor", r=[P("acc"), P("tmp")], w=[P("acc")], out=acc[:], in0=acc[:], in1=tmp[:], op=ALU.add)
                S.dve("tensor_tensor", r=[P("acc"), gtn], w=["MIXB"], out=MIXB[:, m, 512 + g * 256:512 + g * 256 + 256],
                      in0=acc[:].rearrange("q h d -> q (h d)"), in1=gtt[p][:, g * 256:g * 256 + 256], op=ALU.mult)

            for c in range(4):
                add_unit(lhsT=KCMP[:, c * 128:(c + 1) * 128], rk=["KCMP"], qrhs=qrhs, mask=(cm[p][:, c, :], cmn), vrhs=VCA[:, c, g, :], vk=["VCA"],
                         obank=(lambda h: 2 + h // 2), ocols=(lambda h: ((h % 2) * 208, (h % 2) * 208 + 208)), first=(c == 0), last=(c == 3),
                         post=(([nxt_loads] if (g == 0 and c == 0 and nxt_loads is not None) else []) + ([cmp_post] if c == 3 else [])))
            cl = list(range(8)) if m > 0 else list(range(4, 8))
            for c in cl:
                add_unit(lhsT=KWm[p][:, c, :], rk=[kwn], qrhs=qrhs, mask=(wm[:, c, :], P("wm")), vrhs=VWm[p][:, c, g * 80:g * 80 + 65], vk=[vwn],
                         obank=(lambda h: 5), ocols=(lambda h: (h * 65, h * 65 + 65)), first=(c == cl[0]), last=(c == cl[-1]),
                         post=([win_post] if c == cl[-1] else []))
            J = 4 * m + 4
            for j in range(J):
                add_unit(lhsT=KS[:, gcol(j):gcol(j) + 128], rk=[P("KS")], qrhs=qrhs,
                         mask=((dm[:, j - 4 * m, :], P("dm")) if j >= 4 * m else None), vrhs=VS[:, cidx(j), g * 80:g * 80 + 65], vk=[P("VS")],
                         obank=(lambda h: 4), ocols=(lambda h: (h * 65, h * 65 + 65)), first=(j == 0), last=(j == J - 1),
                         bias_mm=EE[:, gcol(j):gcol(j) + 128], pre=([slc_pre] if j == 0 else []), post=([slc_post] if j == J - 1 else []))

    for i, u in enumerate(units):
        emit_s(i, u)
        if i > 0:
            emit_rest(i - 1, units[i - 1])
    emit_rest(len(units) - 1, units[-1])


def nsa_rank_consts(r):
    k = np.arange(128)[:, None]
    q = np.arange(128)[None, :]
    wmask = np.zeros((128, 8, 128), np.float32)
    for c in range(8):
        d = (4 + r) - c
        if d == 4:
            wmask[:, c, :] = (k > q)
        elif 1 <= d <= 3:
            wmask[:, c, :] = 1.0
        elif d == 0:
            wmask[:, c, :] = (k <= q)
    cmask = np.zeros((16, 128, 4, 128), np.float32)
    force = np.zeros((16, 128, 128), np.float32)
    nl = np.arange(128)[:, None]
    b = np.arange(128)[None, :]
    for m in range(16):
        tq = (4 * m + r) * 128 + np.arange(128)
        for c in range(4):
            n = 128 * c + nl
            cmask[m, :, c, :] = ((16 * n + 31) <= tq[None, :]) & (n < 511)
        cur = (tq // 64)[:, None]
        force[m] = 1e4 * (b == 0) + 2e4 * (b == cur) + 3e4 * (b == cur - 1)
    return dict(wmask=wmask.astype(NPBF), cmask=cmask.astype(NPBF), force=force.astype(np.float32))


def ee_const():
    ee = np.zeros((128, T), np.float32)
    for j in range(64):
        for half in range(2):
            ee[2 * j + half, gcol(j) + 64 * half: gcol(j) + 64 * half + 64] = 1.0
    return ee.astype(NPBF)


def conv_scope(S, nc, es, io, MIXB, pfx="v"):
    P = lambda n: pfx + n
    sb = lambda n, s, d: es.enter_context(nc.sbuf_tensor(P(n), s, d))
    PS = [es.enter_context(nc.psum_tensor(P("ps%d" % i), [128, 512], F32)) for i in range(4)]
    YB = sb("YB", [128, 2, NT], BF16)
    TLB = sb("TLB", [128, 2, 4, 16, 30], BF16)
    YE = sb("YE", [128, 2, 16, 158], BF16)
    DIAG = sb("DIAG", [128, 2, 31, 128], BF16)
    idf = sb("idf", [128, 128], F32)
    accA = sb("accA", [128, 2, 16, 128], F32)
    SQ = sb("SQ", [128, 2, NT], F32)
    MEAN = sb("MEAN", [128, NT], F32)
    MSQ = sb("MSQ", [128, NT], F32)
    CW = sb("CW", [128, 2, 31], F32)
    CBs = sb("CBs", [128, 2], F32)
    LG = sb("LG", [128, 2], F32)
    LB = sb("LB", [128, 2], F32)
    ohp = sb("ohp", [128, 4], F32)
    onesd = sb("onesd", [128, 128], F32)
    WPf = sb("WPf", [128, 2, 256], F32)
    WPW = sb("WPW", [128, 2, 256], BF16)
    ACTT = sb("ACTT", [128, 2, NT], BF16)
    ob = [sb("ob%d" % i, [128, 256], F32) for i in range(2)]
    gtt = [sb("gtt%d" % i, [128, 256], BF16) for i in range(2)]

    ld = P("ld")
    for cc in range(2):
        S.dma(ld, batch=True, w=[P("YB")], out=YB[:, cc, :], in_=io["yT"][cc])
        S.dma(ld, batch=True, w=[P("WPf")], out=WPf[:, cc, :], in_=io["conv_pw"][cc * 128:(cc + 1) * 128, :])
        for r in range(4):
            S.dma(ld, batch=True, w=[P("TLB")], out=TLB[:, cc, r, :, :], in_=io["tails_g"][r, cc])
    S.dma(ld, batch=True, w=[P("CW")], out=CW[:], in_=io["conv_w"])
    S.dma(ld, batch=True, w=[P("CBs")], out=CBs[:], in_=io["conv_b"])
    S.dma(ld, batch=True, w=[P("LG")], out=LG[:], in_=io["ln_g"])
    S.dma(ld, batch=True, w=[P("LB")], out=LB[:], in_=io["ln_b"])
    S.dma(ld, batch=True, w=[P("ohp")], out=ohp[:], in_=io["ohprev"])
    S.dma(ld, batch=True, w=[P("idf")], out=idf[:], in_=io["ident"])
    S.dve("memset", w=[P("onesd")], ap=onesd[:], constant=1.0 / 256.0)
    S.dve("tensor_copy", r=[P("WPf")], w=[P("WPW")], out=WPW[:], in_=WPf[:])
    for cc in range(2):
        S.act("activation", r=[P("YB")], w=[P("YE")], out=YE[:, cc, :, 30:158], in_=YB[:, cc, :].rearrange("p (m t) -> p m t", t=128), func=AF.Copy)
        S.dve("tensor_scalar", r=[P("TLB"), P("ohp")], w=[P("YE")], out=YE[:, cc, :, 0:30], in0=TLB[:, cc, 0, :, :], scalar1=ohp[:, 0:1],
              scalar2=None, op0=ALU.mult)
        for r in (1, 2):
            S.dve("scalar_tensor_tensor", r=[P("TLB"), P("ohp"), P("YE")], w=[P("YE")], out=YE[:, cc, :, 0:30], in0=TLB[:, cc, r, :, :],
                  scalar=ohp[:, r:r + 1], in1=YE[:, cc, :, 0:30], op0=ALU.mult, op1=ALU.add)
        S.dve("scalar_tensor_tensor", r=[P("TLB"), P("ohp"), P("YE")], w=[P("YE")], out=YE[:, cc, 1:16, 0:30], in0=TLB[:, cc, 3, 0:15, :],
              scalar=ohp[:, 3:4], in1=YE[:, cc, 1:16, 0:30], op0=ALU.mult, op1=ALU.add)
    for cc in range(2):
        for tp in range(31):
            (S.dve if tp % 2 == 0 else S.pool)("tensor_scalar", r=[P("idf"), P("CW")], w=[P("DIAG%d_%d" % (cc, tp))], out=DIAG[:, cc, tp, :],
                                               in0=idf[:], scalar1=CW[:, cc, tp:tp + 1], scalar2=None, op0=ALU.mult)
    for cc in range(2):
        an = P("accA%d" % cc)
        for pc in range(4):
            k = (cc * 4 + pc) % 2
            for tp in range(31):
                S.pe("matmul", r=[P("DIAG%d_%d" % (cc, tp)), P("YE")], w=[P("ps%d" % k)], out=PS[k][:], lhsT=DIAG[:, cc, tp, :],
                     rhs=YE[:, cc, 4 * pc:4 * pc + 4, tp:tp + 128], start=(tp == 0), stop=(tp == 30))
            S.act("activation", r=[P("ps%d" % k), P("CBs")], w=[an], out=accA[:, cc, 4 * pc:4 * pc + 4, :].rearrange("p m t -> p (m t)"),
                  in_=PS[k][:], func=AF.Identity, bias=CBs[:, cc:cc + 1], scale=1.0)
        S.act("activation", r=[an], w=[P("SQ")], out=SQ[:, cc, :], in_=accA[:, cc].rearrange("p m t -> p (m t)"), func=AF.Square)
    for pc in range(4):
        cs = slice(pc * 512, (pc + 1) * 512)
        for cc in range(2):
            S.pe("matmul", r=[P("onesd"), P("accA%d" % cc)], w=[P("ps0")], out=PS[0][:], lhsT=onesd[:],
                 rhs=accA[:, cc].rearrange("p m t -> p (m t)")[:, cs], start=(cc == 0), stop=(cc == 1))
        for cc in range(2):
            S.pe("matmul", r=[P("onesd"), P("SQ")], w=[P("ps1")], out=PS[1][:], lhsT=onesd[:], rhs=SQ[:, cc, cs], start=(cc == 0), stop=(cc == 1))
        S.act("activation", r=[P("ps0")], w=[P("MEAN")], out=MEAN[:, cs], in_=PS[0][:], func=AF.Copy)
        S.act("activation", r=[P("ps1")], w=[P("MSQ")], out=MSQ[:, cs], in_=PS[1][:], func=AF.Copy)
    S.dve("tensor_tensor", r=[P("MEAN")], w=[P("SQ")], out=SQ[:, 0, :], in0=MEAN[:], in1=MEAN[:], op=ALU.mult)
    S.dve("tensor_tensor", r=[P("MSQ"), P("SQ")], w=[P("MSQ")], out=MSQ[:], in0=MSQ[:], in1=SQ[:, 0, :], op=ALU.subtract)
    S.dve("tensor_scalar", r=[P("MSQ")], w=[P("MSQ")], out=MSQ[:], in0=MSQ[:], scalar1=1e-6, scalar2=None, op0=ALU.add)
    S.act("activation", r=[P("MSQ")], w=[P("MSQ")], out=MSQ[:], in_=MSQ[:], func=AF.Sqrt)
    S.dve("reciprocal", r=[P("MSQ")], w=[P("MSQ")], out=MSQ[:], in_=MSQ[:])
    for cc in range(2):
        an = P("accA%d" % cc)
        a2 = accA[:, cc].rearrange("p m t -> p (m t)")
        S.dve("tensor_tensor", r=[an, P("MEAN")], w=[an], out=a2, in0=a2, in1=MEAN[:], op=ALU.subtract)
        S.dve("tensor_tensor", r=[an, P("MSQ")], w=[an], out=a2, in0=a2, in1=MSQ[:], op=ALU.mult)
        S.act("activation", r=[an, P("LG"), P("LB")], w=[P("ACTT")], out=ACTT[:, cc, :], in_=a2, func=AF.Silu, scale=LG[:, cc:cc + 1], bias=LB[:, cc:cc + 1])
    for m in range(16):
        p = m % 2
        k = 2 + p
        S.dma(P("gt%d" % p), w=[P("gtt%d" % p)], out=gtt[p][:], in_=io["gt"][m * 128:(m + 1) * 128, 256:512])
        for cc in range(2):
            S.pe("matmul", r=[P("ACTT"), P("WPW")], w=[P("ps%d" % k)], out=PS[k][:, 0:256], lhsT=ACTT[:, cc, m * 128:(m + 1) * 128], rhs=WPW[:, cc, :],
                 start=(cc == 0), stop=(cc == 1))
        S.act("activation", r=[P("ps%d" % k)], w=[P("ob%d" % p)], out=ob[p][:], in_=PS[k][:, 0:256], func=AF.Copy)
        S.dve("tensor_tensor", r=[P("ob%d" % p), P("gtt%d" % p)], w=["MIXB"], out=MIXB[:, m, 256:512], in0=ob[p][:], in1=gtt[p][:], op=ALU.mult)


def conv_consts(inputs, layer, r):
    cw = inputs["conv_w"][layer]
    col = lambda v: np.ascontiguousarray(v.reshape(2, 128).T, dtype=np.float32)
    ohp = np.zeros((128, 4), np.float32)
    ohp[:, (r - 1) % 4] = 1.0
    return dict(conv_w=np.ascontiguousarray(cw.T.reshape(2, 128, 31).transpose(1, 0, 2), dtype=np.float32),
                conv_b=col(inputs["conv_b"][layer]), ln_g=col(inputs["conv_ln_g"][layer]), ln_b=col(inputs["conv_ln_b"][layer]),
                conv_pw=np.ascontiguousarray(inputs["conv_pw"][layer]), ohprev=ohp)
```
